# Optimizing a Trainium2 kernel written in Bass

```python
import math
import jax, jax.numpy as jnp
from jax import lax
import numpy as np

D_MODEL = 1024
BATCH = 4
SEQ = 4096
DEPTH = 1
DEC_BATCH = 32
DEC_SEQ = 8
PAST_LEN = 8192
PAGE_SIZE = 128

N_HEADS = 8
QK_DIM = 64
V_DIM = 2 * QK_DIM
ATTN_WIDTH = N_HEADS * V_DIM
Q_WIDTH = N_HEADS * 2 * QK_DIM
LRU_WIDTH = D_MODEL
N_LRU_BLOCKS = 8
LRU_BLOCK = LRU_WIDTH // N_LRU_BLOCKS
CONV_WIDTH = 4
LRU_C = 8.0
D_FF = -(-8 * D_MODEL // (3 * 256)) * 256
PLE_DIM = 256
Q_BLOCK = 128
EPS = 1e-6
IN_SIZES = (Q_WIDTH, Q_WIDTH, ATTN_WIDTH, LRU_WIDTH, LRU_WIDTH, D_MODEL, D_MODEL)
IN_COLS = sum(IN_SIZES)

kernel_name = "hybrid_diffattn_rglru_decode_step"


def rmsnorm(x, g):
    xf = x.astype(jnp.float32)
    y = xf * lax.rsqrt(jnp.mean(xf * xf, axis=-1, keepdims=True) + EPS) * g.astype(jnp.float32)
    return y.astype(x.dtype)


def diff_attn(q, k, v, q_pos, k_pos, lam):
    s = jnp.einsum('bqhcd,bkhcd->bchqk', q, k).astype(jnp.float32) * (QK_DIM ** -0.5)
    mask = k_pos[None, :] <= q_pos[:, None]
    s = jnp.where(mask, s, -jnp.inf)
    pr = jax.nn.softmax(s, axis=-1)
    w = pr[:, 0] - lam * pr[:, 1]
    return jnp.einsum('bhqk,bkhd->bqhd', w.astype(v.dtype), v)


def prompt_attend(q, k, v, lam):
    b, t = q.shape[0], q.shape[1]
    nb = t // Q_BLOCK
    qb = q.reshape(b, nb, Q_BLOCK, N_HEADS, 2, QK_DIM).transpose(1, 0, 2, 3, 4, 5)
    k_pos = jnp.arange(t)

    def blk(args):
        i, qi = args
        q_pos = i * Q_BLOCK + jnp.arange(Q_BLOCK)
        return diff_attn(qi, k, v, q_pos, k_pos, lam)

    out = lax.map(blk, (jnp.arange(nb), qb))
    return out.transpose(1, 0, 2, 3, 4).reshape(b, t, N_HEADS, V_DIM)


def make_sample_attend(ck, cv, page_table):
    def attend(q, k, v, lam):
        db, n_pages = page_table.shape
        past = n_pages * PAGE_SIZE
        t = q.shape[1]
        past_k = ck[page_table].reshape(db, past, N_HEADS, 2, QK_DIM)
        past_v = cv[page_table].reshape(db, past, N_HEADS, V_DIM)
        k_all = jnp.concatenate([past_k.astype(k.dtype), k], axis=1)
        v_all = jnp.concatenate([past_v.astype(v.dtype), v], axis=1)
        q_pos = past + jnp.arange(t)
        k_pos = jnp.arange(past + t)
        return diff_attn(q, k_all, v_all, q_pos, k_pos, lam)
    return attend


def causal_conv(x, prev, w, b):
    t = x.shape[1]
    xp = jnp.concatenate([prev.astype(x.dtype), x], axis=1)
    out = b
    for j in range(CONV_WIDTH):
        out = out + xp[:, j:j + t] * w[j]
    return out, xp[:, -(CONV_WIDTH - 1):]


def rglru(x, h0, w_a, b_a, w_x, b_x, lru_lambda):
    b, t, wd = x.shape
    xb = x.reshape(b, t, N_LRU_BLOCKS, LRU_BLOCK)
    r = jax.nn.sigmoid(jnp.einsum('btni,nij->btnj', xb, w_a).astype(jnp.float32) + b_a.astype(jnp.float32)).reshape(b, t, wd)
    ig = jax.nn.sigmoid(jnp.einsum('btni,nij->btnj', xb, w_x).astype(jnp.float32) + b_x.astype(jnp.float32)).reshape(b, t, wd)
    log_a = -LRU_C * r * jax.nn.softplus(-lru_lambda.astype(jnp.float32))
    a = jnp.exp(log_a)
    u = jnp.sqrt(-jnp.expm1(2.0 * log_a)) * (ig * x.astype(jnp.float32))

    def step(h, inp):
        a_t, u_t = inp
        h = a_t * h + u_t
        return h, h

    h_t, hs = lax.scan(step, h0.astype(jnp.float32), (a.transpose(1, 0, 2), u.transpose(1, 0, 2)))
    return hs.transpose(1, 0, 2).astype(x.dtype), h_t


def trunk_layer(h, p_l, attend, conv_prev, h_prev, lw, lam_init):
    b, t, _ = h.shape
    u = rmsnorm(h, lw['g_mix'])
    z = u @ lw['w_in']
    offs = [int(s) for s in np.cumsum(IN_SIZES)[:-1]]
    q, k, v, xr, yr, ga, gr = jnp.split(z, offs, axis=-1)
    q = q.reshape(b, t, N_HEADS, 2, QK_DIM)
    k = k.reshape(b, t, N_HEADS, 2, QK_DIM)
    v = v.reshape(b, t, N_HEADS, V_DIM)
    f32 = jnp.float32
    lam = (jnp.exp(jnp.sum(lw['lq1'].astype(f32) * lw['lk1'].astype(f32)))
           - jnp.exp(jnp.sum(lw['lq2'].astype(f32) * lw['lk2'].astype(f32))) + lam_init)
    o_a = attend(q, k, v, lam)
    o_a = rmsnorm(o_a, lw['g_subln']) * (1.0 - lam_init)
    o_a = o_a.reshape(b, t, ATTN_WIDTH) @ lw['w_attn_br']
    xc, conv_new = causal_conv(xr, conv_prev, lw['w_conv'], lw['b_conv'])
    hs, h_new = rglru(xc, h_prev, lw['w_gate_a'], lw['b_gate_a'], lw['w_gate_x'], lw['b_gate_x'], lw['lru_lambda'])
    o_r = (hs * jax.nn.gelu(yr)) @ lw['w_rec_br']
    m = jax.nn.sigmoid(ga) * o_a + jax.nn.sigmoid(gr) * o_r
    h = h + m @ lw['w_out']
    u2 = rmsnorm(h, lw['g_ffn'])
    h = h + (jax.nn.silu(u2 @ lw['w_ffn_gate']) * (u2 @ lw['w_ffn_up'])) @ lw['w_ffn_down']
    g = jax.nn.sigmoid(rmsnorm(h, lw['g_ple']) @ lw['w_ple_gate'])
    h = h + g * (p_l @ lw['w_ple_proj'])
    return h, k, v, conv_new, h_new.astype(h.dtype)


def setup_inputs(seed: int = 0) -> dict:
    key = jax.random.key(seed)
    ks = iter(jax.random.split(key, 48))
    nrm = lambda shape, s=1.0: s * jax.random.normal(next(ks), shape, jnp.float32)
    n_pages = PAST_LEN // PAGE_SIZE
    n_used = DEC_BATCH * n_pages
    n_phys = (5 * n_used) // 4
    page_table = jax.random.permutation(next(ks), n_phys)[:n_used].reshape(DEC_BATCH, n_pages).astype(jnp.int32)
    gain = lambda shape: 1.0 + nrm(shape, 0.02)
    a0 = jax.random.uniform(next(ks), (DEPTH, LRU_WIDTH), jnp.float32, 0.9, 0.999) ** (1.0 / LRU_C)
    lru_lambda = jnp.log(a0) - jnp.log1p(-a0)
    return {
        'x_prompt': nrm((BATCH, SEQ, D_MODEL)),
        'x_sample': nrm((DEC_BATCH, DEC_SEQ, D_MODEL)),
        'p_prompt': nrm((DEPTH, BATCH, SEQ, PLE_DIM)),
        'p_sample': nrm((DEPTH, DEC_BATCH, DEC_SEQ, PLE_DIM)),
        'cache_k': nrm((DEPTH, n_phys, PAGE_SIZE, N_HEADS, 2, QK_DIM)),
        'cache_v': nrm((DEPTH, n_phys, PAGE_SIZE, N_HEADS, V_DIM)),
        'page_table': page_table,
        'state_conv': nrm((DEPTH, DEC_BATCH, CONV_WIDTH - 1, LRU_WIDTH)),
        'state_h': nrm((DEPTH, DEC_BATCH, LRU_WIDTH), 0.5),
        'g_mix': gain((DEPTH, D_MODEL)),
        'w_in': nrm((DEPTH, D_MODEL, IN_COLS), D_MODEL ** -0.5),
        'lambda_q1': nrm((DEPTH, QK_DIM), 0.1),
        'lambda_k1': nrm((DEPTH, QK_DIM), 0.1),
        'lambda_q2': nrm((DEPTH, QK_DIM), 0.1),
        'lambda_k2': nrm((DEPTH, QK_DIM), 0.1),
        'g_subln': gain((DEPTH, V_DIM)),
        'w_attn_br': nrm((DEPTH, ATTN_WIDTH, D_MODEL), ATTN_WIDTH ** -0.5),
        'w_conv': nrm((DEPTH, CONV_WIDTH, LRU_WIDTH), CONV_WIDTH ** -0.5),
        'b_conv': nrm((DEPTH, LRU_WIDTH), 0.01),
        'w_gate_a': nrm((DEPTH, N_LRU_BLOCKS, LRU_BLOCK, LRU_BLOCK), LRU_BLOCK ** -0.5),
        'b_gate_a': nrm((DEPTH, N_LRU_BLOCKS, LRU_BLOCK), 0.1),
        'w_gate_x': nrm((DEPTH, N_LRU_BLOCKS, LRU_BLOCK, LRU_BLOCK), LRU_BLOCK ** -0.5),
        'b_gate_x': nrm((DEPTH, N_LRU_BLOCKS, LRU_BLOCK), 0.1),
        'lru_lambda': lru_lambda,
        'w_rec_br': nrm((DEPTH, LRU_WIDTH, D_MODEL), LRU_WIDTH ** -0.5),
        'w_out': nrm((DEPTH, D_MODEL, D_MODEL), D_MODEL ** -0.5),
        'g_ffn': gain((DEPTH, D_MODEL)),
        'w_ffn_gate': nrm((DEPTH, D_MODEL, D_FF), D_MODEL ** -0.5),
        'w_ffn_up': nrm((DEPTH, D_MODEL, D_FF), D_MODEL ** -0.5),
        'w_ffn_down': nrm((DEPTH, D_FF, D_MODEL), D_FF ** -0.5),
        'g_ple': gain((DEPTH, D_MODEL)),
        'w_ple_gate': nrm((DEPTH, D_MODEL, D_MODEL), D_MODEL ** -0.5),
        'w_ple_proj': nrm((DEPTH, PLE_DIM, D_MODEL), PLE_DIM ** -0.5),
        'g_final': gain((D_MODEL,)),
    }


def reference(x_prompt, x_sample, p_prompt, p_sample, cache_k, cache_v, page_table, state_conv, state_h,
              g_mix, w_in, lambda_q1, lambda_k1, lambda_q2, lambda_k2, g_subln, w_attn_br,
              w_conv, b_conv, w_gate_a, b_gate_a, w_gate_x, b_gate_x, lru_lambda, w_rec_br,
              w_out, g_ffn, w_ffn_gate, w_ffn_up, w_ffn_down, g_ple, w_ple_gate, w_ple_proj, g_final):
    hp, hs = x_prompt, x_sample
    kp_l, vp_l, cp_l, hp_l, ks_l, vs_l, cs_l, hs_l = [], [], [], [], [], [], [], []
    bp = x_prompt.shape[0]
    for l in range(DEPTH):
        lw = dict(g_mix=g_mix[l], w_in=w_in[l], lq1=lambda_q1[l], lk1=lambda_k1[l], lq2=lambda_q2[l],
                  lk2=lambda_k2[l], g_subln=g_subln[l], w_attn_br=w_attn_br[l], w_conv=w_conv[l],
                  b_conv=b_conv[l], w_gate_a=w_gate_a[l], b_gate_a=b_gate_a[l], w_gate_x=w_gate_x[l],
                  b_gate_x=b_gate_x[l], lru_lambda=lru_lambda[l], w_rec_br=w_rec_br[l], w_out=w_out[l],
                  g_ffn=g_ffn[l], w_ffn_gate=w_ffn_gate[l], w_ffn_up=w_ffn_up[l], w_ffn_down=w_ffn_down[l],
                  g_ple=g_ple[l], w_ple_gate=w_ple_gate[l], w_ple_proj=w_ple_proj[l])
        lam_init = 0.8 - 0.6 * math.exp(-0.3 * l)
        conv0 = jnp.zeros((bp, CONV_WIDTH - 1, LRU_WIDTH), x_prompt.dtype)
        h0 = jnp.zeros((bp, LRU_WIDTH), jnp.float32)
        hp, kp, vp, cp, hpn = trunk_layer(hp, p_prompt[l], prompt_attend, conv0, h0, lw, lam_init)
        attend_s = make_sample_attend(cache_k[l], cache_v[l], page_table)
        hs, ksn, vsn, csn, hsn = trunk_layer(hs, p_sample[l], attend_s, state_conv[l], state_h[l], lw, lam_init)
        kp_l.append(kp); vp_l.append(vp); cp_l.append(cp); hp_l.append(hpn)
        ks_l.append(ksn); vs_l.append(vsn); cs_l.append(csn); hs_l.append(hsn)
    y_prompt = rmsnorm(hp, g_final)
    y_sample = rmsnorm(hs, g_final)
    return (y_prompt, y_sample,
            jnp.stack(kp_l), jnp.stack(vp_l), jnp.stack(cp_l), jnp.stack(hp_l),
            jnp.stack(ks_l), jnp.stack(vs_l), jnp.stack(cs_l), jnp.stack(hs_l))
```

```python
import math
from contextlib import ExitStack
import numpy as np
import concourse.bass as bass
import concourse.mybir as mybir
from concourse.bass_utils import run_bass_kernel_spmd

F32 = mybir.dt.float32
BF16 = mybir.dt.bfloat16
I32 = mybir.dt.int32
AF = mybir.ActivationFunctionType
ALU = mybir.AluOpType
AX = mybir.AxisListType

D = 1024
NH = 8
KC = 8
DFF = 2816
FC = 22
EPS = 1e-6
LAM_INIT = 0.8 - 0.6 * math.exp(-0.3 * 0)


class Buf:
    __slots__ = ("w", "r", "excl")

    def __init__(self, excl=False):
        self.w = None
        self.r = {}
        self.excl = excl


def PBuf():
    return Buf(True)


class Sched:
    def __init__(self, nc, es, nds=20):
        self.nc = nc
        self.sems = []
        self.E = {}
        for name, e in (("pe", nc.tensor), ("act", nc.scalar), ("dve", nc.vector),
                        ("pool", nc.gpsimd), ("sp", nc.sync)):
            sid = self._new_sem(es, name)
            self.E[name] = {"e": e, "sid": sid, "cnt": 0, "waited": {}}
        self.dq = {}
        for q in ("sp", "pool", "act"):
            self.dq[q] = {"sids": [self._new_sem(es, f"d{q}{i}") for i in range(nds)], "next": 0}
        self.dval = {}
        self.out_toks = []

    def _new_sem(self, es, name):
        s = es.enter_context(self.nc.semaphore(name))
        self.sems.append(s)
        return len(self.sems) - 1

    def _deps(self, R, W):
        deps = {}

        def add(t):
            if t is None:
                return
            if deps.get(t[0], 0) < t[1]:
                deps[t[0]] = t[1]
        for b in R:
            add(b.w)
            if b.excl:
                for s, v in b.r.items():
                    add((s, v))
        for b in W:
            add(b.w)
            for s, v in b.r.items():
                add((s, v))
        return deps

    def _emit_waits(self, en, deps, skip_own=False):
        E = self.E[en]
        for s, v in deps.items():
            if skip_own and s == E["sid"]:
                continue
            if E["waited"].get(s, 0) < v:
                E["e"].wait_ge(self.sems[s], v)
                E["waited"][s] = v

    def _record(self, tok, R, W):
        for b in R:
            if b.r.get(tok[0], 0) < tok[1]:
                b.r[tok[0]] = tok[1]
        for b in W:
            b.w = tok
            b.r = {}

    def op(self, en, fn, R=(), W=()):
        E = self.E[en]
        self._emit_waits(en, self._deps(R, W), skip_own=(en == "pe"))
        ins = fn()
        E["cnt"] += 1
        ins.then_inc(self.sems[E["sid"]], 1)
        self._record((E["sid"], E["cnt"]), R, W)

    def mm(self, fns, R=(), W=()):
        E = self.E["pe"]
        self._emit_waits("pe", self._deps(R, W), skip_own=True)
        ins = None
        for fn in fns:
            ins = fn()
        E["cnt"] += 1
        ins.then_inc(self.sems[E["sid"]], 1)
        self._record((E["sid"], E["cnt"]), R, W)

    def dma(self, q, fn, R=(), W=(), is_out=False):
        Q = self.dq[q]
        sid = Q["sids"][Q["next"]]
        Q["next"] = (Q["next"] + 1) % len(Q["sids"])
        deps = self._deps(R, W)
        prev = self.dval.get(sid, 0)
        if prev:
            if deps.get(sid, 0) < prev:
                deps[sid] = prev
        self._emit_waits(q, deps)
        ins = fn(self.E[q]["e"])
        val = prev + 16
        self.dval[sid] = val
        ins.then_inc(self.sems[sid], 16)
        tok = (sid, val)
        self._record(tok, R, W)
        if is_out:
            self.out_toks.append(tok)

    def barrier(self):
        toks = {}
        for en, E in self.E.items():
            if E["cnt"]:
                toks[E["sid"]] = E["cnt"]
        for sid, v in self.dval.items():
            toks[sid] = v
        for en in self.E:
            self._emit_waits(en, toks)

    def finish(self):
        toks = {}
        for t in self.out_toks:
            if toks.get(t[0], 0) < t[1]:
                toks[t[0]] = t[1]
        self._emit_waits("sp", toks)


def build(cfg):
    TP, TO, NS = cfg["TP"], cfg["TO"], cfg["NS"]
    TF = TP + TO
    NT = TF + NS
    NTT = TF // 128
    OT0 = TP // 128
    NOT = TO // 128
    NQG = TO // 512
    NOS = TO + NS
    nc = bass.Bass("TRN2", target_bir_lowering=False)

    def din(name, shape, dt=F32):
        return nc.dram_tensor(name, list(shape), dt, kind="ExternalInput").ap()

    def dout(name, shape, dt=F32):
        return nc.dram_tensor(name, list(shape), dt, kind="ExternalOutput").ap()

    xf = din("xf", [TF, D])
    xs = din("xs", [NS, D])
    pown = din("pown", [TO, 256])
    psm = din("psm", [NS, 256])
    flag = din("flag", [128, 1])
    sconv = din("sconv", [12, D])
    sh = din("sh", [4, D])
    w_in = din("w_in", [D, 7 * D])
    w_attn = din("w_attn", [D, D])
    w_rec = din("w_rec", [D, D])
    w_out = din("w_out", [D, D])
    w_fg = din("w_fg", [D, DFF])
    w_fu = din("w_fu", [D, DFF])
    w_fd = din("w_fd", [DFF, D])
    w_pg = din("w_pg", [D, D])
    w_pp = din("w_pp", [256, D])
    w_ga = din("w_ga", [8, 128, 128])
    w_gx = din("w_gx", [8, 128, 128])
    vecs = din("vecs", [8, D])
    wconv = din("wconv", [4, D])
    gsub = din("gsub", [1, 128])
    lamv = din("lamv", [4, 64])
    ident = din("ident", [128, 128])
    tri = din("tri", [128, 128])
    on_s_in = din("on_s", [NS, D])

    y_own = dout("y_own", [TO, D])
    y_s = dout("y_s", [NS, D])
    k_own = dout("k_own", [TO, D])
    v_own = dout("v_own", [TO, D])
    k_s = dout("k_s", [NS, D])
    v_s = dout("v_s", [NS, D])
    conv_p = dout("conv_p", [3, D])
    h_p = dout("h_p", [1, D])
    conv_s = dout("conv_s", [12, D])
    h_s = dout("h_s", [4, D])
    kT_d = nc.dram_tensor("kT_d", [NH, 128, TF], BF16, kind="Internal").ap()
    v_d = nc.dram_tensor("v_d", [NH, 128, NTT, 130], BF16, kind="Internal").ap()

    es = ExitStack()
    S = Sched(nc, es)

    def sb(name, shape, dt=F32, stack=None):
        return (stack or es).enter_context(nc.sbuf_tensor(name, list(shape), dt))

    def pst(name, shape, dt=F32, stack=None):
        return (stack or es).enter_context(nc.psum_tensor(name, list(shape), dt))

    identb = sb("identb", [128, 128], BF16)
    identf = sb("identf", [128, 128], F32)
    trib = sb("trib", [128, 128], BF16)
    gb = sb("gb", [128, 4, D], F32)
    gsubb = sb("gsubb", [128, 128], F32)
    vT = sb("vT", [128, 4, KC], F32)
    wcT = sb("wcT", [128, 4, KC], F32)
    clam = sb("clam", [128, KC], F32)
    clam2 = sb("clam2", [128, KC], F32)
    epsb = sb("epsb", [128, 1], F32)
    flg = sb("flg", [128, 1], F32)
    lamb = sb("lamb", [128, 4, 64], F32)
    neglam = sb("neglam", [128, 1], F32)
    tmpc = sb("tmpc", [128, 4], F32)
    B_const = Buf()
    ld = []
    S.dma("pool", lambda e: e.dma_start(out=identb[:], in_=ident), W=[B_const])
    S.dma("sp", lambda e: e.dma_start(out=identf[:], in_=ident), W=[B_const])
    S.dma("pool", lambda e: e.dma_start(out=trib[:], in_=tri), W=[B_const])
    for i in range(4):
        S.dma("sp", lambda e, i=i: e.dma_start(out=gb[:, i, :], in_=vecs[i:i + 1, :].partition_broadcast(128)), W=[B_const])
    S.dma("sp", lambda e: e.dma_start(out=gsubb[:], in_=gsub.partition_broadcast(128)), W=[B_const])
    with nc.allow_non_contiguous_dma(reason="tiny parameter vectors to feature-major"):
        for i in range(4):
            S.dma("sp", lambda e, i=i: e.dma_start(out=vT[:, i, :], in_=vecs[4 + i, :].rearrange("(c p) -> p c", p=128)), W=[B_const])
            S.dma("sp", lambda e, i=i: e.dma_start(out=wcT[:, i, :], in_=wconv[i, :].rearrange("(c p) -> p c", p=128)), W=[B_const])
    S.dma("sp", lambda e: e.dma_start(out=flg[:], in_=flag), W=[B_const])
    S.dma("sp", lambda e: e.dma_start(out=lamb[:].rearrange("p a b -> p (a b)"),
                                      in_=lamv.rearrange("a b -> (a b)").rearrange("(o n) -> o n", o=1).partition_broadcast(128)), W=[B_const])
    S.op("dve", lambda: nc.vector.memset(epsb[:], EPS), W=[B_const])
    scr64 = sb("scr64", [128, 64], F32)
    S.op("dve", lambda: nc.vector.tensor_tensor(out=scr64[:], in0=lamb[:, 0, :], in1=lamb[:, 1, :], op=ALU.mult), R=[B_const], W=[B_const])
    S.op("dve", lambda: nc.vector.reduce_sum(out=tmpc[:, 0:1], in_=scr64[:], axis=AX.X), R=[B_const], W=[B_const])
    S.op("dve", lambda: nc.vector.tensor_tensor(out=scr64[:], in0=lamb[:, 2, :], in1=lamb[:, 3, :], op=ALU.mult), R=[B_const], W=[B_const])
    S.op("dve", lambda: nc.vector.reduce_sum(out=tmpc[:, 1:2], in_=scr64[:], axis=AX.X), R=[B_const], W=[B_const])
    S.op("act", lambda: nc.scalar.activation(out=tmpc[:, 2:4], in_=tmpc[:, 0:2], func=AF.Exp), R=[B_const], W=[B_const])
    S.op("dve", lambda: nc.vector.tensor_tensor(out=neglam[:], in0=tmpc[:, 3:4], in1=tmpc[:, 2:3], op=ALU.subtract), R=[B_const], W=[B_const])
    S.op("dve", lambda: nc.vector.tensor_scalar(out=neglam[:], in0=neglam[:], scalar1=-LAM_INIT, scalar2=None, op0=ALU.add), R=[B_const], W=[B_const])
    S.op("dve", lambda: nc.vector.tensor_scalar(out=gsubb[:], in0=gsubb[:], scalar1=1.0 - LAM_INIT, scalar2=None, op0=ALU.mult), R=[B_const], W=[B_const])
    S.op("act", lambda: nc.scalar.activation(out=clam[:], in_=vT[:, 0, :], func=AF.Exp, scale=-1.0), R=[B_const], W=[B_const])
    S.op("dve", lambda: nc.vector.tensor_scalar(out=clam[:], in0=clam[:], scalar1=1.0, scalar2=None, op0=ALU.add), R=[B_const], W=[B_const])
    S.op("act", lambda: nc.scalar.activation(out=clam[:], in_=clam[:], func=AF.Ln), R=[B_const], W=[B_const])
    S.op("dve", lambda: nc.vector.tensor_scalar(out=clam2[:], in0=clam[:], scalar1=-16.0, scalar2=None, op0=ALU.mult), R=[B_const], W=[B_const])
    S.op("dve", lambda: nc.vector.tensor_scalar(out=clam[:], in0=clam[:], scalar1=-8.0, scalar2=None, op0=ALU.mult), R=[B_const], W=[B_const])
    S.barrier()
    if cfg.get("stop") == 0:
        S.finish(); es.close(); return nc

    uT = sb("uT", [128, KC, NT], BF16)
    B_uT = [Buf() for _ in range(NTT + 1)]

    def rmsnorm_rows(stk, x_ap, P, gidx, out_ap, Bx, Bo, tag):
        junk = rmsnorm_rows.junk
        ss = rmsnorm_rows.ss
        Bs = rmsnorm_rows.Bs
        S.op("act", lambda: nc.scalar.activation(out=junk[:P, :], in_=x_ap, func=AF.Square, accum_out=ss[:P, 0:1]), R=[Bx], W=[Bs])
        S.op("act", lambda: nc.scalar.activation(out=ss[:P, 1:2], in_=ss[:P, 0:1], func=AF.Sqrt, scale=1.0 / D, bias=epsb[:P, 0:1]), R=[Bs], W=[Bs])
        S.op("dve", lambda: nc.vector.reciprocal(out=ss[:P, 2:3], in_=ss[:P, 1:2]), R=[Bs], W=[Bs])
        S.op("dve", lambda: nc.vector.scalar_tensor_tensor(out=out_ap, in0=x_ap, scalar=ss[:P, 2:3], in1=gb[:P, gidx, :],
                                                             op0=ALU.mult, op1=ALU.mult), R=[Bx, Bs], W=[Bo])
    rmsnorm_rows.junk = sb("nrm_junk", [128, D], F32)
    rmsnorm_rows.ss = sb("nrm_ss", [128, 4], F32)
    rmsnorm_rows.Bs = Buf()

    def transpose_to_uT(stk, xn, P, col0, Bxn, Bu, psT, BpsT):
        S.mm([lambda kc=kc: nc.tensor.transpose(out=psT[:, kc, :P], in_=xn[:P, kc * 128:(kc + 1) * 128], identity=identb[:P, :P])
              for kc in range(KC)], R=[Bxn], W=[BpsT])
        S.op("act", lambda: nc.scalar.copy(out=uT[:, :, col0:col0 + P], in_=psT[:, :, :P]), R=[BpsT], W=[Bu])

    with ExitStack() as ph:
        xt = [sb(f"a0_x{i}", [128, D], F32, ph) for i in range(2)]
        Bxt = [Buf(), Buf()]
        xn = [sb(f"a0_xn{i}", [128, D], BF16, ph) for i in range(2)]
        Bxn = [Buf(), Buf()]
        psT = [pst(f"a0_ps{i}", [128, KC, 128], BF16, ph) for i in range(2)]
        BpsT = [PBuf(), PBuf()]
        for t in range(NTT + 1):
            i = t % 2
            P = 128 if t < NTT else NS
            src = xf[t * 128:(t + 1) * 128, :] if t < NTT else xs
            S.dma("sp", lambda e, i=i, P=P, src=src: e.dma_start(out=xt[i][:P, :], in_=src), W=[Bxt[i]])
            rmsnorm_rows(ph, xt[i][:P, :], P, 0, xn[i][:P, :], Bxt[i], Bxn[i], "a0")
            transpose_to_uT(ph, xn[i], P, t * 128, Bxn[i], B_uT[t], psT[i], BpsT[i])
        S.barrier()

    if cfg.get("stop") == 1:
        S.finish(); es.close(); return nc
    kTs = sb("kTs", [128, NH, NS], BF16)
    B_kTs = Buf()
    with ExitStack() as ph:
        wk = [sb(f"a1_wk{i}", [128, KC, 512], BF16, ph) for i in range(2)]
        Bwk = [Buf(), Buf()]
        wv = [sb(f"a1_wv{i}", [128, KC, 512], BF16, ph) for i in range(2)]
        Bwv = [Buf(), Buf()]
        kst = [sb(f"a1_kst{i}", [128, 512], BF16, ph) for i in range(2)]
        Bkst = [Buf(), Buf()]
        vst = [sb(f"a1_vst{i}", [128, 4, 130], BF16, ph) for i in range(2)]
        Bvst = [Buf(), Buf()]
        fst = [sb(f"a1_fst{i}", [128, 512], F32, ph) for i in range(4)]
        Bfst = [Buf() for _ in range(4)]
        ps = [pst(f"a1_ps{i}", [128, 512], F32, ph) for i in range(4)]
        Bps = [PBuf() for _ in range(4)]
        ones_c = sb("a1_ones", [128, 1], F32, ph)
        B1 = Buf()
        S.op("dve", lambda: nc.vector.memset(ones_c[:], 1.0), W=[B1])
        pi = 0
        fi = 0
        for hh in range(2):
            wi = hh % 2
            for kc in range(KC):
                S.dma("pool", lambda e, wi=wi, hh=hh, kc=kc: e.dma_start(
                    out=wk[wi][:, kc, :], in_=w_in[kc * 128:(kc + 1) * 128, D + hh * 512:D + (hh + 1) * 512]), W=[Bwk[wi]])
                S.dma("pool", lambda e, wi=wi, hh=hh, kc=kc: e.dma_start(
                    out=wv[wi][:, kc, :], in_=w_in[kc * 128:(kc + 1) * 128, 2 * D + hh * 512:2 * D + (hh + 1) * 512]), W=[Bwv[wi]])
            for hl in range(4 if cfg.get("stop") != 2 else 0):
                h = hh * 4 + hl
                c0 = 0
                while c0 < NT:
                    cw = min(512, NT - c0)
                    tiles = sorted(set([min(c // 128, NTT) for c in range(c0, c0 + cw, 128)]))
                    p = pi % 4
                    pi += 1
                    S.mm([lambda kc=kc, p=p, c0=c0, cw=cw, hl=hl, wi=wi: nc.tensor.matmul(
                        ps[p][:, :cw], lhsT=wk[wi][:, kc, hl * 128:(hl + 1) * 128], rhs=uT[:, kc, c0:c0 + cw],
                        start=(kc == 0), stop=(kc == KC - 1)) for kc in range(KC)],
                        R=[Bwk[wi]] + [B_uT[t] for t in tiles], W=[Bps[p]])
                    if c0 < TF:
                        ki = (pi) % 2
                        S.op("act", lambda ki=ki, p=p, cw=cw: nc.scalar.copy(out=kst[ki][:, :cw], in_=ps[p][:, :cw]), R=[Bps[p]], W=[Bkst[ki]])
                        if not cfg.get("noscr"):
                            S.dma("sp", lambda e, ki=ki, h=h, c0=c0, cw=cw: e.dma_start(out=kT_d[h, :, c0:c0 + cw], in_=kst[ki][:, :cw]), R=[Bkst[ki]])
                    else:
                        S.op("act", lambda p=p, cw=cw, h=h: nc.scalar.copy(out=kTs[:, h, :], in_=ps[p][:, :cw]), R=[Bps[p]], W=[B_kTs])
                    c0 += cw
            for t in range(NTT + 1 if cfg.get("stop") not in (2, 3) else 0):
                P = 128 if t < NTT else NS
                col0 = t * 128
                own = t >= OT0
                p = pi % 4
                pi += 1
                S.mm([lambda kc=kc, p=p, P=P, col0=col0, wi=wi: nc.tensor.matmul(
                    ps[p][:P, :], lhsT=uT[:, kc, col0:col0 + P], rhs=wv[wi][:, kc, :],
                    start=(kc == 0), stop=(kc == KC - 1)) for kc in range(KC)], R=[Bwv[wi], B_uT[t]], W=[Bps[p]])
                if t < NTT:
                    vi = t % 2
                    S.op("dve", lambda vi=vi, p=p: nc.vector.tensor_copy(out=vst[vi][:, :, 0:128], in_=ps[p][:].rearrange("p (h d) -> p h d", h=4)),
                         R=[Bps[p]], W=[Bvst[vi]])
                    onesrc = ones_c if own else flg
                    for hl in range(4):
                        S.op("dve", lambda vi=vi, hl=hl, onesrc=onesrc: nc.vector.tensor_copy(out=vst[vi][:, hl, 128:129], in_=onesrc[:, 0:1]),
                             R=[B1, B_const], W=[Bvst[vi]])
                    if not cfg.get("noscr"):
                        S.dma("sp", lambda e, vi=vi, t=t, hh=hh: e.dma_start(out=v_d[hh * 4:(hh + 1) * 4, :, t, :].rearrange("h p d -> p h d"), in_=vst[vi][:]),
                              R=[Bvst[vi]])
                if own and cfg.get("stop") != 4:
                    f = fi % 4
                    fi += 1
                    if cfg.get("stop") == 7 and P == 128:
                        S.op("act", lambda f=f, p=p, P=P: nc.scalar.copy(out=fst[f][:P, :], in_=ps[p][:P, :]), R=[Bps[p]], W=[Bfst[f]])
                    elif cfg.get("stop") in (5, 7):
                        S.op("dve", lambda f=f, p=p, P=P: nc.vector.memset(fst[f][:P, :], 1.0), R=[Bps[p]], W=[Bfst[f]])
                        if P == 128 and cfg.get("stop") == 5:
                            S.op("dve", lambda f=f, col0=col0: nc.vector.tensor_copy(out=fst[f][:, 0:128], in_=uT[:, 0, col0:col0 + 128]), R=[B_uT[t]], W=[Bfst[f]])
                            S.op("dve", lambda f=f, col0=col0, wi=wi: nc.vector.tensor_copy(out=fst[f][:, 128:256], in_=wv[wi][:, 0, 0:128]), R=[Bwv[wi]], W=[Bfst[f]])
                    else:
                        S.op("act", lambda f=f, p=p, P=P: nc.scalar.copy(out=fst[f][:P, :], in_=ps[p][:P, :]), R=[Bps[p]], W=[Bfst[f]])
                    dst = v_own[(t - OT0) * 128:(t - OT0 + 1) * 128, hh * 512:(hh + 1) * 512] if t < NTT else v_s[:, hh * 512:(hh + 1) * 512]
                    S.dma("sp", lambda e, f=f, P=P, dst=dst: e.dma_start(out=dst, in_=fst[f][:P, :]), R=[Bfst[f]], is_out=True)
                    if cfg.get("stop") in (5, 7):
                        continue
                    p = pi % 4
                    pi += 1
                    S.mm([lambda kc=kc, p=p, P=P, col0=col0, wi=wi: nc.tensor.matmul(
                        ps[p][:P, :], lhsT=uT[:, kc, col0:col0 + P], rhs=wk[wi][:, kc, :],
                        start=(kc == 0), stop=(kc == KC - 1)) for kc in range(KC)], R=[Bwk[wi], B_uT[t]], W=[Bps[p]])
                    f = fi % 4
                    fi += 1
                    S.op("dve", lambda f=f, p=p, P=P: nc.vector.tensor_copy(out=fst[f][:P, :], in_=ps[p][:P, :]), R=[Bps[p]], W=[Bfst[f]])
                    dst = k_own[(t - OT0) * 128:(t - OT0 + 1) * 128, hh * 512:(hh + 1) * 512] if t < NTT else k_s[:, hh * 512:(hh + 1) * 512]
                    S.dma("sp", lambda e, f=f, P=P, dst=dst: e.dma_start(out=dst, in_=fst[f][:P, :]), R=[Bfst[f]], is_out=True)
        S.barrier()

    if cfg.get("stop") == 11:
        S.finish(); es.close(); return nc
    mTs = sb("mTs", [128, KC, NS], BF16)
    actTs = sb("actTs", [128, 8, NS], BF16)
    es3 = ExitStack()
    on_tm = sb("on_tm", [128, NOT + 1, D], BF16, es3)
    B_on = [Buf() for _ in range(NOT + 1)]
    qTs = sb("qTs", [128, NH, NS], BF16, es3)
    B_qTs = Buf()
    NPB = TP // 128
    with ExitStack() as ph:
        wq = [sb(f"b_wq{i}", [128, KC, 128], BF16, ph) for i in range(2)]
        Bwq = [Buf(), Buf()]
        kTh = [sb(f"b_kT{i}", [128, TF], BF16, ph) for i in range(2)]
        BkTh = [Buf(), Buf()]
        vh = [sb(f"b_v{i}", [128, NTT, 130], BF16, ph) for i in range(2)]
        Bvh = [Buf(), Buf()]
        qTh = [sb(f"b_q{i}", [128, NOS], BF16, ph) for i in range(2)]
        BqTh = [Buf(), Buf()]
        PT = [sb(f"b_PT{i}", [128, 2, 512], BF16, ph) for i in range(3)]
        BPT = [Buf() for _ in range(3)]
        pS = [pst(f"b_pS{i}", [128, 2, 512], F32, ph) for i in range(2)]
        BpS = [PBuf(), PBuf()]
        acc = [pst(f"b_acc{i}", [128, 512], F32, ph) for i in range(4)]
        Bacc = [PBuf() for _ in range(4)]
        rcp = sb("b_rcp", [128, 4], F32, ph)
        o1 = sb("b_o1", [128, 128], F32, ph)
        o2 = sb("b_o2", [128, 128], F32, ph)
        junk = sb("b_junk", [128, 128], F32, ph)
        Bfin = Buf()
        scnt = 0
        pcnt = 0
        for h in range(NH):
            hb = h % 2
            for kc in range(KC):
                S.dma("pool", lambda e, hb=hb, kc=kc, h=h: e.dma_start(out=wq[hb][:, kc, :], in_=w_in[kc * 128:(kc + 1) * 128, h * 128:(h + 1) * 128]), W=[Bwq[hb]])
            S.dma("sp", lambda e, hb=hb, h=h: e.dma_start(out=kTh[hb][:, :], in_=kT_d[h]), W=[BkTh[hb]])
            S.dma("sp", lambda e, hb=hb, h=h: e.dma_start(out=vh[hb][:, :, :], in_=v_d[h]), W=[Bvh[hb]])
            c0 = 0
            while c0 < NOS:
                cw = min(512, NOS - c0)
                sb_ = scnt % 2; scnt += 1
                tiles = sorted(set(min((TP + c0 + o) // 128, NTT) for o in range(0, cw, 128)))
                S.mm([lambda kc=kc, sb_=sb_, c0=c0, cw=cw, hb=hb: nc.tensor.matmul(pS[sb_][:, 0, :cw], lhsT=wq[hb][:, kc, :], rhs=uT[:, kc, TP + c0:TP + c0 + cw],
                                                                                  start=(kc == 0), stop=(kc == KC - 1)) for kc in range(KC)],
                     R=[Bwq[hb]] + [B_uT[t] for t in tiles], W=[BpS[sb_]])
                S.op("dve", lambda sb_=sb_, c0=c0, cw=cw, hb=hb: nc.vector.tensor_scalar(out=qTh[hb][:, c0:c0 + cw], in0=pS[sb_][:, 0, :cw], scalar1=0.125, scalar2=None, op0=ALU.mult),
                     R=[BpS[sb_]], W=[BqTh[hb]])
                c0 += cw
            S.op("pool", lambda hb=hb, h=h: nc.gpsimd.tensor_copy(out=qTs[:, h, :], in_=qTh[hb][:, TO:TO + NS]), R=[BqTh[hb]], W=[B_qTs])
            for s_ in range(NQG):
                nkb = NPB + 4 * (s_ + 1)
                for j in range(nkb):
                    jj = j - (NPB + 4 * s_)
                    qb0 = max(jj, 0)
                    q0 = qb0 * 128
                    sb_ = scnt % 2; scnt += 1
                    pb = pcnt % 3; pcnt += 1
                    S.mm([lambda c=c, sb_=sb_, j=j, q0=q0, s_=s_, hb=hb: nc.tensor.matmul(
                        pS[sb_][:, c, q0:512], lhsT=kTh[hb][64 * c:64 * c + 64, j * 128:(j + 1) * 128],
                        rhs=qTh[hb][64 * c:64 * c + 64, s_ * 512 + q0:(s_ + 1) * 512], start=True, stop=True) for c in range(2)],
                        R=[BkTh[hb], BqTh[hb]], W=[BpS[sb_]])
                    S.op("act", lambda sb_=sb_, pb=pb, q0=q0: nc.scalar.activation(out=PT[pb][:, :, q0:512], in_=pS[sb_][:, :, q0:512], func=AF.Exp),
                         R=[BpS[sb_]], W=[BPT[pb]])
                    if jj >= 0:
                        for c in range(2):
                            S.op("dve", lambda pb=pb, c=c, q0=q0: nc.vector.tensor_tensor(out=PT[pb][:, c, q0:q0 + 128], in0=PT[pb][:, c, q0:q0 + 128], in1=trib[:, :], op=ALU.mult),
                                 R=[BPT[pb], B_const], W=[BPT[pb]])
                    fns = []
                    for qb in range(qb0, 4):
                        last = (j == NPB + 4 * s_ + qb)
                        for c in range(2):
                            fns.append(lambda qb=qb, c=c, pb=pb, j=j, hb=hb, last=last: nc.tensor.matmul(
                                acc[qb][:, c * 256:c * 256 + 129], lhsT=PT[pb][:, c, qb * 128:(qb + 1) * 128], rhs=vh[hb][:, j, 0:129],
                                start=(j == 0 and c == 0), stop=last, skip_group_check=True))
                    S.mm(fns, R=[BPT[pb], Bvh[hb]], W=[Bacc[qb] for qb in range(qb0, 4)])
                for qb in range(4):
                    ti = s_ * 4 + qb
                    S.op("dve", lambda qb=qb: nc.vector.reciprocal(out=rcp[:, 0:1], in_=acc[qb][:, 128:129]), R=[Bacc[qb]], W=[Bfin])
                    S.op("dve", lambda qb=qb: nc.vector.reciprocal(out=rcp[:, 1:2], in_=acc[qb][:, 384:385]), R=[Bacc[qb]], W=[Bfin])
                    S.op("dve", lambda: nc.vector.tensor_tensor(out=rcp[:, 1:2], in0=rcp[:, 1:2], in1=neglam[:, :], op=ALU.mult), R=[Bfin, B_const], W=[Bfin])
                    S.op("dve", lambda qb=qb: nc.vector.tensor_scalar(out=o1[:, :], in0=acc[qb][:, 0:128], scalar1=rcp[:, 0:1], scalar2=None, op0=ALU.mult), R=[Bacc[qb], Bfin], W=[Bfin])
                    S.op("dve", lambda qb=qb: nc.vector.scalar_tensor_tensor(out=o2[:, :], in0=acc[qb][:, 256:384], scalar=rcp[:, 1:2], in1=o1[:, :], op0=ALU.mult, op1=ALU.add),
                         R=[Bacc[qb], Bfin], W=[Bfin])
                    S.op("act", lambda: nc.scalar.activation(out=junk[:, :], in_=o2[:, :], func=AF.Square, accum_out=rcp[:, 2:3]), R=[Bfin], W=[Bfin])
                    S.op("act", lambda: nc.scalar.activation(out=rcp[:, 3:4], in_=rcp[:, 2:3], func=AF.Sqrt, scale=1.0 / 128, bias=epsb[:, 0:1]), R=[Bfin, B_const], W=[Bfin])
                    S.op("dve", lambda: nc.vector.reciprocal(out=rcp[:, 3:4], in_=rcp[:, 3:4]), R=[Bfin], W=[Bfin])
                    S.op("dve", lambda ti=ti, h=h: nc.vector.scalar_tensor_tensor(out=on_tm[:, ti, h * 128:(h + 1) * 128], in0=o2[:, :], scalar=rcp[:, 3:4], in1=gsubb[:, :],
                                                                                op0=ALU.mult, op1=ALU.mult), R=[Bfin, B_const], W=[B_on[ti]])
        S.barrier()
    if cfg.get("stop") == 20:
        with ExitStack() as ph:
            dbg = sb("dbg", [128, D], F32, ph)
            Bd = Buf()
            for ti in range(NOT):
                S.op("dve", lambda ti=ti: nc.vector.tensor_copy(out=dbg[:, :], in_=on_tm[:, ti, :]), R=[B_on[ti]], W=[Bd])
                S.dma("sp", lambda e, ti=ti: e.dma_start(out=y_own[ti * 128:(ti + 1) * 128, :], in_=dbg[:, :]), R=[Bd], is_out=True)
            S.barrier()
        S.finish(); es.close(); return nc

    if cfg.get("stop") == 10:
        S.finish(); es.close(); return nc
    NOS = TO + NS
    es2 = ExitStack()
    hsT = sb("hsT", [128, KC, NOS], BF16, es2)
    B_hsT = [Buf() for _ in range(KC)]
    with ExitStack() as ph:
        PW = min(512, TP)
        NPC = TF // PW
        onesb = sb("c_ones", [128, 1], F32, ph)
        wga = sb("c_wga", [128, 8, 128], BF16, ph)
        wgx = sb("c_wgx", [128, 8, 128], BF16, ph)
        Bwg = Buf()
        S.op("dve", lambda: nc.vector.memset(onesb[:], 1.0), W=[Bwg])
        for n in range(8):
            S.dma("pool", lambda e, n=n: e.dma_start(out=wga[:, n, :], in_=w_ga[n]), W=[Bwg])
            S.dma("pool", lambda e, n=n: e.dma_start(out=wgx[:, n, :], in_=w_gx[n]), W=[Bwg])
        scT = sb("c_scT", [128, KC, 12], F32, ph)
        shT = sb("c_shT", [128, KC, 4], F32, ph)
        Bst = Buf()
        with ExitStack() as ph0:
            sc_tm = sb("c_sctm", [12, D], F32, ph0)
            sh_tm = sb("c_shtm", [4, D], F32, ph0)
            S.dma("sp", lambda e: e.dma_start(out=sc_tm[:], in_=sconv), W=[Bst])
            S.dma("sp", lambda e: e.dma_start(out=sh_tm[:], in_=sh), W=[Bst])
            pss = pst("c_pss", [128, 16], F32, ph0)
            Bpss = PBuf()
            for kc in range(KC):
                S.mm([lambda kc=kc: nc.tensor.transpose(out=pss[:, 0:12], in_=sc_tm[:12, kc * 128:(kc + 1) * 128], identity=identf[:12, :12])], R=[Bst, B_const], W=[Bpss])
                S.op("dve", lambda kc=kc: nc.vector.tensor_copy(out=scT[:, kc, :], in_=pss[:, 0:12]), R=[Bpss], W=[Bst])
                S.mm([lambda kc=kc: nc.tensor.transpose(out=pss[:, 12:16], in_=sh_tm[:4, kc * 128:(kc + 1) * 128], identity=identf[:4, :4])], R=[Bst, B_const], W=[Bpss])
                S.op("dve", lambda kc=kc: nc.vector.tensor_copy(out=shT[:, kc, :], in_=pss[:, 12:16]), R=[Bpss], W=[Bst])
            S.barrier()
        convT = sb("c_convT", [128, KC, 3], F32, ph)
        hlT = sb("c_hlT", [128, KC, 1], F32, ph)
        convsT = sb("c_convsT", [128, KC, 12], F32, ph)
        hsl = sb("c_hsl", [128, KC, 4], F32, ph)
        Bcol = Buf()
        wx = [sb(f"c_wx{i}", [128, KC, 128], BF16, ph) for i in range(2)]
        Bwx = [Buf(), Buf()]
        NB = 2
        xp = [sb(f"c_xp{i}", [128, 3 + PW], F32, ph) for i in range(NB)]
        xc = [sb(f"c_xc{i}", [128, PW], F32, ph) for i in range(NB)]
        xcb = [sb(f"c_xcb{i}", [128, PW], BF16, ph) for i in range(NB)]
        rr = [sb(f"c_r{i}", [128, PW], F32, ph) for i in range(NB)]
        ig = [sb(f"c_ig{i}", [128, PW], F32, ph) for i in range(NB)]
        am = [sb(f"c_am{i}", [128, PW], F32, ph) for i in range(NB)]
        aa = [sb(f"c_aa{i}", [128, PW], F32, ph) for i in range(NB)]
        hh_ = [sb(f"c_h{i}", [128, PW], F32, ph) for i in range(NB)]
        carry = sb("c_carry", [128, 1], F32, ph)
        Bxp = [Buf() for _ in range(NB)]; Bxc = [Buf() for _ in range(NB)]; Bxcb = [Buf() for _ in range(NB)]
        Br = [Buf() for _ in range(NB)]; Big = [Buf() for _ in range(NB)]; Bam = [Buf() for _ in range(NB)]
        Baa = [Buf() for _ in range(NB)]; Bh = [Buf() for _ in range(NB)]; Bcarry = Buf()
        psx = [pst(f"c_psx{i}", [128, 512], F32, ph) for i in range(2)]
        Bpsx = [PBuf() for _ in range(2)]
        psg = [pst(f"c_psg{i}", [128, 512], F32, ph) for i in range(4)]
        Bpsg = [PBuf() for _ in range(4)]
        xps = sb("c_xps", [128, 4, 11], F32, ph)
        xcs = sb("c_xcs", [128, 4, 8], F32, ph)
        xcsb = sb("c_xcsb", [128, 32], BF16, ph)
        rs_ = sb("c_rs", [128, 32], F32, ph)
        igs = sb("c_igs", [128, 32], F32, ph)
        ams = sb("c_ams", [128, 32], F32, ph)
        aas = sb("c_aas", [128, 32], F32, ph)
        hss = sb("c_hss", [128, 4, 8], F32, ph)
        Bsm = Buf()
        gcnt = 0
        xcnt = 0
        it = 0

        def lru_core(j, xp_ap3, xc_t, xcb_t, r_t, ig_t, am_t, aa_t, W_, Bxp_, Bxc_, Bxcb_, Br_, Big_, Bam_, Baa_, shape3=None):
            nonlocal gcnt
            S.op("pool", lambda: nc.gpsimd.tensor_scalar(out=xc_t, in0=xp_ap3(0), scalar1=wcT[:, 0, j:j + 1], scalar2=vT[:, 1, j:j + 1],
                                                          op0=ALU.mult, op1=ALU.add), R=[Bxp_, B_const], W=[Bxc_])
            for jj in range(1, 4):
                S.op("dve", lambda jj=jj: nc.vector.scalar_tensor_tensor(out=xc_t, in0=xp_ap3(jj), scalar=wcT[:, jj, j:j + 1], in1=xc_t,
                                                                            op0=ALU.mult, op1=ALU.add), R=[Bxp_, Bxc_], W=[Bxc_])
            xc2 = xc_t if shape3 is None else xc_t.rearrange("p a b -> p (a b)")
            S.op("pool", lambda: nc.gpsimd.tensor_copy(out=xcb_t, in_=xc2), R=[Bxc_], W=[Bxcb_])
            c0 = 0
            while c0 < W_:
                cw = min(512, W_ - c0)
                g0 = gcnt % 4; g1 = (gcnt + 1) % 4; gcnt += 2
                S.mm([lambda g0=g0, c0=c0, cw=cw: nc.tensor.matmul(psg[g0][:, :cw], lhsT=wga[:, j, :], rhs=xcb_t[:, c0:c0 + cw], start=True, stop=True)],
                     R=[Bwg, Bxcb_], W=[Bpsg[g0]])
                S.mm([lambda g1=g1, c0=c0, cw=cw: nc.tensor.matmul(psg[g1][:, :cw], lhsT=wgx[:, j, :], rhs=xcb_t[:, c0:c0 + cw], start=True, stop=True)],
                     R=[Bwg, Bxcb_], W=[Bpsg[g1]])
                S.op("act", lambda g0=g0, c0=c0, cw=cw: nc.scalar.activation(out=r_t[:, c0:c0 + cw], in_=psg[g0][:, :cw], func=AF.Sigmoid, bias=vT[:, 2, j:j + 1]),
                     R=[Bpsg[g0], B_const], W=[Br_])
                S.op("act", lambda g1=g1, c0=c0, cw=cw: nc.scalar.activation(out=ig_t[:, c0:c0 + cw], in_=psg[g1][:, :cw], func=AF.Sigmoid, bias=vT[:, 3, j:j + 1]),
                     R=[Bpsg[g1], B_const], W=[Big_])
                c0 += cw
            S.op("act", lambda: nc.scalar.activation(out=aa_t, in_=r_t, func=AF.Exp, scale=clam[:, j:j + 1]), R=[Br_, B_const], W=[Baa_])
            S.op("act", lambda: nc.scalar.activation(out=am_t, in_=r_t, func=AF.Exp, scale=clam2[:, j:j + 1]), R=[Br_, B_const], W=[Bam_])
            S.op("act", lambda: nc.scalar.activation(out=am_t, in_=am_t, func=AF.Sqrt, scale=-1.0, bias=onesb[:, 0:1]), R=[Bam_, Bwg], W=[Bam_])
            S.op("dve", lambda: nc.vector.tensor_tensor(out=ig_t, in0=ig_t, in1=xc2, op=ALU.mult), R=[Big_, Bxc_], W=[Big_])
            S.op("dve", lambda: nc.vector.tensor_tensor(out=ig_t, in0=ig_t, in1=am_t, op=ALU.mult), R=[Big_, Bam_], W=[Big_])

        for j in range(KC):
            wi = j % 2
            for kc in range(KC):
                S.dma("pool", lambda e, wi=wi, kc=kc, j=j: e.dma_start(out=wx[wi][:, kc, :], in_=w_in[kc * 128:(kc + 1) * 128, 3 * D + j * 128:3 * D + (j + 1) * 128]), W=[Bwx[wi]])
            for pc in range(NPC):
                b = it % NB
                it += 1
                col0 = pc * PW
                if pc == 0:
                    S.op("pool", lambda b=b: nc.gpsimd.memset(xp[b][:, 0:3], 0.0), W=[Bxp[b]])
                else:
                    pb = (it - 2) % NB
                    S.op("pool", lambda b=b, pb=pb: nc.gpsimd.tensor_copy(out=xp[b][:, 0:3], in_=xp[pb][:, PW:PW + 3]), R=[Bxp[pb]], W=[Bxp[b]])
                c0 = 0
                while c0 < PW:
                    x_ = xcnt % 2; xcnt += 1
                    tiles = sorted(set((col0 + c0 + o) // 128 for o in range(0, 512, 128)))
                    S.mm([lambda kc=kc, x_=x_, c0=c0, col0=col0, wi=wi: nc.tensor.matmul(psx[x_][:, :], lhsT=wx[wi][:, kc, :], rhs=uT[:, kc, col0 + c0:col0 + c0 + 512],
                                                                                       start=(kc == 0), stop=(kc == KC - 1)) for kc in range(KC)],
                         R=[Bwx[wi]] + [B_uT[t] for t in tiles], W=[Bpsx[x_]])
                    S.op("act", lambda x_=x_, b=b, c0=c0: nc.scalar.copy(out=xp[b][:, 3 + c0:3 + c0 + 512], in_=psx[x_][:, :]), R=[Bpsx[x_]], W=[Bxp[b]])
                    c0 += 512
                lru_core(j, lambda jj, b=b: xp[b][:, jj:jj + PW], xc[b][:, :], xcb[b][:, :], rr[b][:, :], ig[b][:, :], am[b][:, :], aa[b][:, :], PW,
                         Bxp[b], Bxc[b], Bxcb[b], Br[b], Big[b], Bam[b], Baa[b])
                if pc == 0:
                    init = 0.0
                    Rc = []
                else:
                    pb = (it - 2) % NB
                    if col0 == TP:
                        S.op("dve", lambda pb=pb: nc.vector.tensor_tensor(out=carry[:], in0=hh_[pb][:, PW - 1:PW], in1=flg[:], op=ALU.mult), R=[Bh[pb], B_const], W=[Bcarry])
                        init = carry[:, 0:1]
                        Rc = [Bcarry]
                    else:
                        init = hh_[pb][:, PW - 1:PW]
                        Rc = [Bh[pb]]
                S.op("dve", lambda b=b, init=init: nc.vector.tensor_tensor_scan(out=hh_[b][:, :], data0=aa[b][:, :], data1=ig[b][:, :], initial=init,
                                                                                 op0=ALU.mult, op1=ALU.add), R=[Baa[b], Big[b]] + Rc, W=[Bh[b]])
                if col0 >= TP:
                    o0 = col0 - TP
                    S.op("pool", lambda b=b, o0=o0, j=j: nc.gpsimd.tensor_copy(out=hsT[:, j, o0:o0 + PW], in_=hh_[b][:, :]), R=[Bh[b]], W=[B_hsT[j]])
                if pc == NPC - 1:
                    S.op("pool", lambda b=b, j=j: nc.gpsimd.tensor_copy(out=convT[:, j, :], in_=xp[b][:, PW:PW + 3]), R=[Bxp[b]], W=[Bcol])
                    S.op("pool", lambda b=b, j=j: nc.gpsimd.tensor_copy(out=hlT[:, j, :], in_=hh_[b][:, PW - 1:PW]), R=[Bh[b]], W=[Bcol])
            x_ = xcnt % 2; xcnt += 1
            S.mm([lambda kc=kc, x_=x_, wi=wi: nc.tensor.matmul(psx[x_][:, :NS], lhsT=wx[wi][:, kc, :], rhs=uT[:, kc, TF:TF + NS],
                                                              start=(kc == 0), stop=(kc == KC - 1)) for kc in range(KC)],
                 R=[Bwx[wi], B_uT[NTT]], W=[Bpsx[x_]])
            S.op("act", lambda x_=x_: nc.scalar.copy(out=xps[:, :, 3:11], in_=psx[x_][:, :NS].rearrange("p (a b) -> p a b", a=4)), R=[Bpsx[x_]], W=[Bsm])
            S.op("pool", lambda j=j: nc.gpsimd.tensor_copy(out=xps[:, :, 0:3], in_=scT[:, j, :].rearrange("p (a b) -> p a b", a=4)), R=[Bst], W=[Bsm])
            lru_core(j, lambda jj: xps[:, :, jj:jj + 8], xcs[:, :, :], xcsb[:, :], rs_[:, :], igs[:, :], ams[:, :], aas[:, :], NS,
                     Bsm, Bsm, Bsm, Bsm, Bsm, Bsm, Bsm, shape3=True)
            for sq in range(4):
                S.op("dve", lambda sq=sq, j=j: nc.vector.tensor_tensor_scan(out=hss[:, sq, :], data0=aas[:, sq * 8:(sq + 1) * 8], data1=igs[:, sq * 8:(sq + 1) * 8],
                                                                              initial=shT[:, j, sq:sq + 1], op0=ALU.mult, op1=ALU.add), R=[Bsm, Bst], W=[Bsm])
            S.op("pool", lambda j=j: nc.gpsimd.tensor_copy(out=hsT[:, j, TO:TO + NS], in_=hss[:].rearrange("p a b -> p (a b)")), R=[Bsm], W=[B_hsT[j]])
            S.op("pool", lambda j=j: nc.gpsimd.tensor_copy(out=convsT[:, j, :].rearrange("p (a b) -> p a b", a=4), in_=xps[:, :, 8:11]), R=[Bsm], W=[Bcol])
            S.op("pool", lambda j=j: nc.gpsimd.tensor_copy(out=hsl[:, j, :], in_=hss[:, :, 7]), R=[Bsm], W=[Bcol])
        otm = sb("c_otm", [12, D], F32, ph)
        Botm = Buf()
        pso = pst("c_pso", [12, 512], F32, ph)
        Bpso = PBuf()
        for (src, n, k, dst) in ((convT, 3, 0, conv_p), (hlT, 1, 1, h_p), (convsT, 12, 2, conv_s), (hsl, 4, 3, h_s)):
            for half in range(2):
                S.mm([lambda kc=kc, src=src, n=n, half=half: nc.tensor.transpose(out=pso[:n, (kc % 4) * 128:(kc % 4 + 1) * 128], in_=src[:, kc, :], identity=identf[:, :])
                      for kc in range(half * 4, half * 4 + 4)], R=[Bcol, B_const], W=[Bpso])
                S.op("dve", lambda n=n, k=k, half=half: nc.vector.tensor_copy(out=otm[:n, half * 512:(half + 1) * 512], in_=pso[:n, :]), R=[Bpso], W=[Botm])
            S.dma("sp", lambda e, n=n, k=k, dst=dst: e.dma_start(out=dst, in_=otm[:n, :]), R=[Botm], is_out=True)
        S.barrier()

    if cfg.get("stop") == 12:
        S.finish(); es.close(); return nc
    S.dma("pool", lambda e: e.dma_start(out=on_tm[:NS, NOT, :], in_=on_s_in), W=[B_on[NOT]])

    def col_chunks():
        c0 = 0
        while c0 < NOS:
            cw = min(512, NOS - c0)
            tiles = sorted(set(min((TP + c0 + o) // 128, NTT) for o in range(0, cw, 128)))
            yield c0, cw, tiles
            c0 += cw

    with ExitStack() as ph:
        wy = [sb(f"d1_wy{i}", [128, KC, 128], BF16, ph) for i in range(2)]
        Bwy = [Buf(), Buf()]
        ps = [pst(f"d1_ps{i}", [128, 512], F32, ph) for i in range(4)]
        Bps = [PBuf() for _ in range(4)]
        NBF = 2
        xs_ = [sb(f"d1_x{i}", [128, 512], F32, ph) for i in range(NBF)]
        t1 = [sb(f"d1_t{i}", [128, 512], F32, ph) for i in range(NBF)]
        sg = [sb(f"d1_s{i}", [128, 512], F32, ph) for i in range(NBF)]
        Bx = [Buf() for _ in range(NBF)]; Bt = [Buf() for _ in range(NBF)]; Bsg = [Buf() for _ in range(NBF)]
        cnt = 0
        for n in range(KC):
            wi = n % 2
            for kc in range(KC):
                S.dma("pool", lambda e, wi=wi, kc=kc, n=n: e.dma_start(out=wy[wi][:, kc, :], in_=w_in[kc * 128:(kc + 1) * 128, 4 * D + n * 128:4 * D + (n + 1) * 128]), W=[Bwy[wi]])
            for c0, cw, tiles in col_chunks():
                p = cnt % 4; b = cnt % NBF; cnt += 1
                S.mm([lambda kc=kc, p=p, c0=c0, cw=cw, wi=wi: nc.tensor.matmul(ps[p][:, :cw], lhsT=wy[wi][:, kc, :], rhs=uT[:, kc, TP + c0:TP + c0 + cw],
                                                                              start=(kc == 0), stop=(kc == KC - 1)) for kc in range(KC)],
                     R=[Bwy[wi]] + [B_uT[t] for t in tiles], W=[Bps[p]])
                S.op("act", lambda p=p, b=b, cw=cw: nc.scalar.copy(out=xs_[b][:, :cw], in_=ps[p][:, :cw]), R=[Bps[p]], W=[Bx[b]])
                S.op("pool", lambda b=b, cw=cw: nc.gpsimd.tensor_tensor(out=t1[b][:, :cw], in0=xs_[b][:, :cw], in1=xs_[b][:, :cw], op=ALU.mult), R=[Bx[b]], W=[Bt[b]])
                S.op("pool", lambda b=b, cw=cw: nc.gpsimd.tensor_scalar(out=t1[b][:, :cw], in0=t1[b][:, :cw], scalar1=0.044715, scalar2=1.0, op0=ALU.mult, op1=ALU.add), R=[Bt[b]], W=[Bt[b]])
                S.op("pool", lambda b=b, cw=cw: nc.gpsimd.tensor_tensor(out=t1[b][:, :cw], in0=t1[b][:, :cw], in1=xs_[b][:, :cw], op=ALU.mult), R=[Bt[b], Bx[b]], W=[Bt[b]])
                S.op("act", lambda b=b, cw=cw: nc.scalar.activation(out=sg[b][:, :cw], in_=t1[b][:, :cw], func=AF.Sigmoid, scale=1.5957691216057308), R=[Bt[b]], W=[Bsg[b]])
                S.op("dve", lambda b=b, cw=cw: nc.vector.tensor_tensor(out=sg[b][:, :cw], in0=sg[b][:, :cw], in1=xs_[b][:, :cw], op=ALU.mult), R=[Bsg[b], Bx[b]], W=[Bsg[b]])
                S.op("dve", lambda b=b, cw=cw, n=n, c0=c0: nc.vector.tensor_tensor(out=hsT[:, n, c0:c0 + cw], in0=hsT[:, n, c0:c0 + cw], in1=sg[b][:, :cw], op=ALU.mult),
                     R=[Bsg[b], B_hsT[n]], W=[B_hsT[n]])
        S.barrier()

    B_mT = [Buf() for _ in range(NOT + 1)]

    def mT_cols(kc_or_all, c0, cw):
        if c0 < TO:
            return uT[:, kc_or_all, c0:c0 + cw]
        return mTs[:, kc_or_all, 0:cw]

    with ExitStack() as ph:
        wr = [sb(f"d2_wr{i}", [128, KC, 128], BF16, ph) for i in range(2)]
        wgr = [sb(f"d2_wgr{i}", [128, KC, 128], BF16, ph) for i in range(2)]
        Bw = [Buf(), Buf()]
        ps = [pst(f"d2_ps{i}", [128, 512], F32, ph) for i in range(4)]
        Bps = [PBuf() for _ in range(4)]
        sgr = [sb(f"d2_sgr{i}", [128, 512], F32, ph) for i in range(2)]
        Bsgr = [Buf(), Buf()]
        cnt = 0
        for m in range(KC):
            wi = m % 2
            for kc in range(KC):
                r0, r1 = kc * 128, (kc + 1) * 128
                S.dma("pool", lambda e, wi=wi, kc=kc, m=m, r0=r0, r1=r1: e.dma_start(out=wr[wi][:, kc, :], in_=w_rec[r0:r1, m * 128:(m + 1) * 128]), W=[Bw[wi]])
                S.dma("pool", lambda e, wi=wi, kc=kc, m=m, r0=r0, r1=r1: e.dma_start(out=wgr[wi][:, kc, :], in_=w_in[r0:r1, 6 * D + m * 128:6 * D + (m + 1) * 128]), W=[Bw[wi]])
            for c0, cw, tiles in col_chunks():
                b = cnt % 2; cnt += 1
                pA, pB = (2 * b, 2 * b + 1)
                otiles = sorted(set(min((c0 + o) // 128, NOT) for o in range(0, cw, 128)))
                S.mm([lambda kc=kc: nc.tensor.matmul(ps[pA][:, :cw], lhsT=wr[wi][:, kc, :], rhs=hsT[:, kc, c0:c0 + cw], start=(kc == 0), stop=(kc == KC - 1)) for kc in range(KC)],
                     R=[Bw[wi]] + B_hsT, W=[Bps[pA]])
                S.mm([lambda kc=kc: nc.tensor.matmul(ps[pB][:, :cw], lhsT=wgr[wi][:, kc, :], rhs=uT[:, kc, TP + c0:TP + c0 + cw], start=(kc == 0), stop=(kc == KC - 1)) for kc in range(KC)],
                     R=[Bw[wi]] + [B_uT[t] for t in tiles], W=[Bps[pB]])
                S.op("act", lambda: nc.scalar.activation(out=sgr[b][:, :cw], in_=ps[pB][:, :cw], func=AF.Sigmoid), R=[Bps[pB]], W=[Bsgr[b]])
                S.op("dve", lambda: nc.vector.tensor_tensor(out=mT_cols(m, c0, cw), in0=ps[pA][:, :cw], in1=sgr[b][:, :cw], op=ALU.mult),
                     R=[Bps[pA], Bsgr[b]], W=[B_mT[t] for t in otiles])
        S.barrier()
    es2.close()

    with ExitStack() as ph:
        wa = sb("d2_wa", [128, KC, D], BF16, ph)
        wga_ = sb("d2_wga", [128, KC, D], BF16, ph)
        Bw = Buf()
        for kc in range(KC):
            for hf in range(2):
                S.dma("pool", lambda e, kc=kc, hf=hf: e.dma_start(out=wa[:, kc, hf * 512:(hf + 1) * 512], in_=w_attn[kc * 128:(kc + 1) * 128, hf * 512:(hf + 1) * 512]), W=[Bw])
                S.dma("pool", lambda e, kc=kc, hf=hf: e.dma_start(out=wga_[:, kc, hf * 512:(hf + 1) * 512],
                                                                   in_=w_in[kc * 128:(kc + 1) * 128, 5 * D + hf * 512:5 * D + (hf + 1) * 512]), W=[Bw])
        oaT = [sb(f"d2_oaT{i}", [128, KC, 512], BF16, ph) for i in range(2)]
        BoaT = [Buf(), Buf()]
        psT = [pst(f"d2_psT{i}", [128, KC, 128], BF16, ph) for i in range(2)]
        BpsT = [PBuf(), PBuf()]
        ps = [pst(f"d2b_ps{i}", [128, 512], F32, ph) for i in range(4)]
        Bps = [PBuf() for _ in range(4)]
        sga = [sb(f"d2_sga{i}", [128, 512], F32, ph) for i in range(2)]
        ta_ = [sb(f"d2_ta{i}", [128, 512], F32, ph) for i in range(2)]
        Bsga = [Buf(), Buf()]; Bta = [Buf(), Buf()]
        cnt = 0
        tcnt = 0
        for ci, (c0, cw, tiles) in enumerate(col_chunks()):
            ob = ci % 2
            otiles = sorted(set(min((c0 + o) // 128, NOT) for o in range(0, cw, 128)))
            for t in otiles:
                P = 128 if t < NOT else NS
                i = tcnt % 2; tcnt += 1
                lc = t * 128 - c0
                S.mm([lambda kc=kc, i=i, t=t, P=P: nc.tensor.transpose(out=psT[i][:, kc, :P], in_=on_tm[:P, t, kc * 128:(kc + 1) * 128], identity=identb[:P, :P])
                      for kc in range(KC)], R=[B_on[t], B_const], W=[BpsT[i]])
                S.op("act", lambda i=i, ob=ob, lc=lc, P=P: nc.scalar.copy(out=oaT[ob][:, :, lc:lc + P], in_=psT[i][:, :, :P]), R=[BpsT[i]], W=[BoaT[ob]])
            for m in range(KC):
                b = cnt % 2; cnt += 1
                pC, pD = (2 * b, 2 * b + 1)
                S.mm([lambda kc=kc: nc.tensor.matmul(ps[pC][:, :cw], lhsT=wa[:, kc, m * 128:(m + 1) * 128], rhs=oaT[ob][:, kc, :cw], start=(kc == 0), stop=(kc == KC - 1)) for kc in range(KC)],
                     R=[Bw, BoaT[ob]], W=[Bps[pC]])
                S.mm([lambda kc=kc: nc.tensor.matmul(ps[pD][:, :cw], lhsT=wga_[:, kc, m * 128:(m + 1) * 128], rhs=uT[:, kc, TP + c0:TP + c0 + cw], start=(kc == 0), stop=(kc == KC - 1)) for kc in range(KC)],
                     R=[Bw] + [B_uT[t] for t in tiles], W=[Bps[pD]])
                S.op("act", lambda: nc.scalar.activation(out=sga[b][:, :cw], in_=ps[pD][:, :cw], func=AF.Sigmoid), R=[Bps[pD]], W=[Bsga[b]])
                S.op("dve", lambda: nc.vector.tensor_tensor(out=ta_[b][:, :cw], in0=ps[pC][:, :cw], in1=sga[b][:, :cw], op=ALU.mult), R=[Bps[pC], Bsga[b]], W=[Bta[b]])
                S.op("pool", lambda: nc.gpsimd.tensor_tensor(out=mT_cols(m, c0, cw), in0=mT_cols(m, c0, cw), in1=ta_[b][:, :cw], op=ALU.add),
                     R=[Bta[b]] + [B_mT[t] for t in otiles], W=[B_mT[t] for t in otiles])
        S.barrier()
    es3.close()

    hres = sb("hres", [128, NOT + 1, D], F32)
    B_h = [Buf() for _ in range(NOT + 1)]

    def tile_rows(t):
        return 128 if t < NOT else NS

    def mT_tile(kc, t):
        if t < NOT:
            return uT[:, kc, t * 128:(t + 1) * 128]
        return mTs[:, kc, :]

    def norm_to_uT(ph_, t, gidx, xnb, Bxnb, psT, BpsT):
        P = tile_rows(t)
        i = t % 2
        rmsnorm_rows(ph_, hres[:P, t, :], P, gidx, xnb[i][:P, :], B_h[t], Bxnb[i], "d")
        transpose_to_uT(ph_, xnb[i], P, TP + t * 128, Bxnb[i], B_uT[OT0 + t], psT[i], BpsT[i])

    with ExitStack() as ph:
        wo = sb("d3_wo", [128, KC, D], BF16, ph)
        Bwo = Buf()
        for kc in range(KC):
            for hf in range(2):
                S.dma("pool", lambda e, kc=kc, hf=hf: e.dma_start(out=wo[:, kc, hf * 512:(hf + 1) * 512], in_=w_out[kc * 128:(kc + 1) * 128, hf * 512:(hf + 1) * 512]), W=[Bwo])
        xt = [sb(f"d3_x{i}", [128, D], F32, ph) for i in range(2)]
        Bxt = [Buf(), Buf()]
        xnb = [sb(f"d3_xn{i}", [128, D], BF16, ph) for i in range(2)]
        Bxnb = [Buf(), Buf()]
        psT = [pst(f"d3_psT{i}", [128, KC, 128], BF16, ph) for i in range(2)]
        BpsT = [PBuf(), PBuf()]
        ps = [pst(f"d3_ps{i}", [128, 512], F32, ph) for i in range(4)]
        Bps = [PBuf() for _ in range(4)]
        cnt = 0
        for t in range(NOT + 1):
            P = tile_rows(t)
            i = t % 2
            src = xf[TP + t * 128:TP + (t + 1) * 128, :] if t < NOT else xs
            S.dma("sp", lambda e, i=i, P=P, src=src: e.dma_start(out=xt[i][:P, :], in_=src), W=[Bxt[i]])
            for hf in range(2):
                p = cnt % 4; cnt += 1
                S.mm([lambda kc=kc, p=p, t=t, P=P, hf=hf: nc.tensor.matmul(ps[p][:P, :], lhsT=mT_tile(kc, t), rhs=wo[:, kc, hf * 512:(hf + 1) * 512],
                                                                          start=(kc == 0), stop=(kc == KC - 1)) for kc in range(KC)], R=[Bwo, B_mT[t]], W=[Bps[p]])
                S.op("dve", lambda p=p, t=t, P=P, hf=hf, i=i: nc.vector.tensor_tensor(out=hres[:P, t, hf * 512:(hf + 1) * 512], in0=ps[p][:P, :], in1=xt[i][:P, hf * 512:(hf + 1) * 512], op=ALU.add),
                     R=[Bps[p], Bxt[i]], W=[B_h[t]])
            norm_to_uT(ph, t, 1, xnb, Bxnb, psT, BpsT)
        S.barrier()

    B_act = [Buf() for _ in range(NOT + 1)]

    def act_cols(fl, c0, cw):
        if c0 < TO:
            return uT[:, fl, c0:c0 + cw]
        return actTs[:, fl, 0:cw]

    def act_tile(fl, t):
        if t < NOT:
            return uT[:, fl, t * 128:(t + 1) * 128]
        return actTs[:, fl, :]

    with ExitStack() as ph:
        wg = [sb(f"d4_wg{i}", [128, KC, 128], BF16, ph) for i in range(2)]
        wu = [sb(f"d4_wu{i}", [128, KC, 128], BF16, ph) for i in range(2)]
        Bwgu = [Buf(), Buf()]
        wd = sb("d4_wd", [128, 8, D], BF16, ph)
        Bwd = Buf()
        psg = [pst(f"d4_pg{i}", [128, 512], F32, ph) for i in range(2)]
        psu = [pst(f"d4_pu{i}", [128, 512], F32, ph) for i in range(2)]
        psd = [pst(f"d4_pd{i}", [128, 512], F32, ph) for i in range(4)]
        Bpsg = [PBuf(), PBuf()]; Bpsu = [PBuf(), PBuf()]; Bpsd = [PBuf() for _ in range(4)]
        sl = [sb(f"d4_sl{i}", [128, 512], F32, ph) for i in range(2)]
        Bsl = [Buf(), Buf()]
        parts = [(0, 8), (8, 15), (15, 22)]
        cnt = 0
        dcnt = 0
        wcnt = 0
        for (f0, f1) in parts:
            nf = f1 - f0
            for fl in range(nf):
                for hf in range(2):
                    S.dma("pool", lambda e, fl=fl, f0=f0, hf=hf: e.dma_start(out=wd[:, fl, hf * 512:(hf + 1) * 512],
                                                                              in_=w_fd[(f0 + fl) * 128:(f0 + fl + 1) * 128, hf * 512:(hf + 1) * 512]), W=[Bwd])
            for fl in range(nf):
                fc = f0 + fl
                wi = wcnt % 2; wcnt += 1
                for kc in range(KC):
                    S.dma("pool", lambda e, wi=wi, kc=kc, fc=fc: e.dma_start(out=wg[wi][:, kc, :], in_=w_fg[kc * 128:(kc + 1) * 128, fc * 128:(fc + 1) * 128]), W=[Bwgu[wi]])
                    S.dma("pool", lambda e, wi=wi, kc=kc, fc=fc: e.dma_start(out=wu[wi][:, kc, :], in_=w_fu[kc * 128:(kc + 1) * 128, fc * 128:(fc + 1) * 128]), W=[Bwgu[wi]])
                for c0, cw, tiles in col_chunks():
                    b = cnt % 2; cnt += 1
                    otiles = sorted(set(min((c0 + o) // 128, NOT) for o in range(0, cw, 128)))
                    S.mm([lambda kc=kc: nc.tensor.matmul(psg[b][:, :cw], lhsT=wg[wi][:, kc, :], rhs=uT[:, kc, TP + c0:TP + c0 + cw], start=(kc == 0), stop=(kc == KC - 1)) for kc in range(KC)],
                         R=[Bwgu[wi]] + [B_uT[t] for t in tiles], W=[Bpsg[b]])
                    S.mm([lambda kc=kc: nc.tensor.matmul(psu[b][:, :cw], lhsT=wu[wi][:, kc, :], rhs=uT[:, kc, TP + c0:TP + c0 + cw], start=(kc == 0), stop=(kc == KC - 1)) for kc in range(KC)],
                         R=[Bwgu[wi]] + [B_uT[t] for t in tiles], W=[Bpsu[b]])
                    S.op("act", lambda: nc.scalar.activation(out=sl[b][:, :cw], in_=psg[b][:, :cw], func=AF.Silu), R=[Bpsg[b]], W=[Bsl[b]])
                    S.op("dve", lambda: nc.vector.tensor_tensor(out=act_cols(fl, c0, cw), in0=psu[b][:, :cw], in1=sl[b][:, :cw], op=ALU.mult),
                         R=[Bpsu[b], Bsl[b]], W=[B_act[t] for t in otiles])
            for t in range(NOT + 1):
                P = tile_rows(t)
                for hf in range(2):
                    p = dcnt % 4; dcnt += 1
                    S.mm([lambda fl=fl, p=p, t=t, P=P, hf=hf: nc.tensor.matmul(psd[p][:P, :], lhsT=act_tile(fl, t), rhs=wd[:, fl, hf * 512:(hf + 1) * 512],
                                                                              start=(fl == 0), stop=(fl == nf - 1)) for fl in range(nf)], R=[Bwd, B_act[t]], W=[Bpsd[p]])
                    S.op("dve", lambda p=p, t=t, P=P, hf=hf: nc.vector.tensor_tensor(out=hres[:P, t, hf * 512:(hf + 1) * 512], in0=psd[p][:P, :], in1=hres[:P, t, hf * 512:(hf + 1) * 512], op=ALU.add),
                         R=[Bpsd[p], B_h[t]], W=[B_h[t]])
        S.barrier()

    with ExitStack() as ph:
        wpg = sb("d5_wpg", [128, KC, D], BF16, ph)
        wpp = sb("d5_wpp", [128, 2, D], BF16, ph)
        Bwp = Buf()
        for kc in range(KC):
            for hf in range(2):
                S.dma("pool", lambda e, kc=kc, hf=hf: e.dma_start(out=wpg[:, kc, hf * 512:(hf + 1) * 512], in_=w_pg[kc * 128:(kc + 1) * 128, hf * 512:(hf + 1) * 512]), W=[Bwp])
        for c in range(2):
            for hf in range(2):
                S.dma("pool", lambda e, c=c, hf=hf: e.dma_start(out=wpp[:, c, hf * 512:(hf + 1) * 512], in_=w_pp[c * 128:(c + 1) * 128, hf * 512:(hf + 1) * 512]), W=[Bwp])
        xnb = [sb(f"d5_xn{i}", [128, D], BF16, ph) for i in range(2)]
        Bxnb = [Buf(), Buf()]
        psT = [pst(f"d5_psT{i}", [128, KC, 128], BF16, ph) for i in range(2)]
        BpsT = [PBuf(), PBuf()]
        pt_ = [sb(f"d5_p{i}", [128, 256], F32, ph) for i in range(2)]
        ptb = [sb(f"d5_pb{i}", [128, 256], BF16, ph) for i in range(2)]
        pT = [sb(f"d5_pT{i}", [128, 2, 128], BF16, ph) for i in range(2)]
        Bpt = [Buf(), Buf()]; Bptb = [Buf(), Buf()]; BpT = [Buf(), Buf()]
        psg = [pst(f"d5_pg{i}", [128, 512], F32, ph) for i in range(2)]
        psp = [pst(f"d5_pp{i}", [128, 512], F32, ph) for i in range(2)]
        Bpsg = [PBuf(), PBuf()]; Bpsp = [PBuf(), PBuf()]
        sg = [sb(f"d5_sg{i}", [128, 512], F32, ph) for i in range(2)]
        Bsg = [Buf(), Buf()]
        yt = [sb(f"d5_y{i}", [128, D], F32, ph) for i in range(2)]
        Byt = [Buf(), Buf()]
        cnt = 0
        for t in range(NOT + 1):
            P = tile_rows(t)
            i = t % 2
            norm_to_uT(ph, t, 2, xnb, Bxnb, psT, BpsT)
            src = pown[t * 128:(t + 1) * 128, :] if t < NOT else psm
            S.dma("sp", lambda e, i=i, P=P, src=src: e.dma_start(out=pt_[i][:P, :], in_=src), W=[Bpt[i]])
            S.op("pool", lambda i=i, P=P: nc.gpsimd.tensor_copy(out=ptb[i][:P, :], in_=pt_[i][:P, :]), R=[Bpt[i]], W=[Bptb[i]])
            S.mm([lambda c=c, i=i, P=P: nc.tensor.transpose(out=psT[i][:, c, :P], in_=ptb[i][:P, c * 128:(c + 1) * 128], identity=identb[:P, :P]) for c in range(2)],
                 R=[Bptb[i], B_const], W=[BpsT[i]])
            S.op("act", lambda i=i, P=P: nc.scalar.copy(out=pT[i][:, :, :P], in_=psT[i][:, 0:2, :P]), R=[BpsT[i]], W=[BpT[i]])
            col0 = TP + t * 128
            for hf in range(2):
                b = cnt % 2; cnt += 1
                S.mm([lambda kc=kc, b=b, P=P, hf=hf, col0=col0: nc.tensor.matmul(psg[b][:P, :], lhsT=uT[:, kc, col0:col0 + P], rhs=wpg[:, kc, hf * 512:(hf + 1) * 512],
                                                                                start=(kc == 0), stop=(kc == KC - 1)) for kc in range(KC)], R=[Bwp, B_uT[OT0 + t]], W=[Bpsg[b]])
                S.mm([lambda c=c, b=b, P=P, hf=hf, i=i: nc.tensor.matmul(psp[b][:P, :], lhsT=pT[i][:, c, :P], rhs=wpp[:, c, hf * 512:(hf + 1) * 512],
                                                                        start=(c == 0), stop=(c == 1)) for c in range(2)], R=[Bwp, BpT[i]], W=[Bpsp[b]])
                S.op("act", lambda b=b, P=P: nc.scalar.activation(out=sg[b][:P, :], in_=psg[b][:P, :], func=AF.Sigmoid), R=[Bpsg[b]], W=[Bsg[b]])
                S.op("dve", lambda b=b, P=P: nc.vector.tensor_tensor(out=sg[b][:P, :], in0=psp[b][:P, :], in1=sg[b][:P, :], op=ALU.mult), R=[Bpsp[b], Bsg[b]], W=[Bsg[b]])
                S.op("pool", lambda b=b, P=P, t=t, hf=hf: nc.gpsimd.tensor_tensor(out=hres[:P, t, hf * 512:(hf + 1) * 512], in0=hres[:P, t, hf * 512:(hf + 1) * 512], in1=sg[b][:P, :], op=ALU.add),
                     R=[Bsg[b], B_h[t]], W=[B_h[t]])
            rmsnorm_rows(ph, hres[:P, t, :], P, 3, yt[i][:P, :], B_h[t], Byt[i], "y")
            dst = y_own[t * 128:(t + 1) * 128, :] if t < NOT else y_s
            S.dma("sp", lambda e, i=i, P=P, dst=dst: e.dma_start(out=dst, in_=yt[i][:P, :]), R=[Byt[i]], is_out=True)
        S.barrier()

    S.finish()
    es.close()
    return nc


def build_sample_attn(cfgb):
    NSEQ, NPG, NPHYS = cfgb["NSEQ"], cfgb["NPG"], cfgb["NPHYS"]
    NTK = NSEQ * 8
    NTL = (NTK + 127) // 128
    nc = bass.Bass("TRN2", target_bir_lowering=False)

    def din(name, shape, dt=F32):
        return nc.dram_tensor(name, list(shape), dt, kind="ExternalInput").ap()

    xs = din("xs", [NTK, D])
    ck = din("ck", [NPHYS * 128, D])
    cv = din("cv", [NPHYS * 128, D])
    ptab = din("ptab", [1, NSEQ * NPG], I32)
    iot = din("iot", [128, 1], I32)
    w_in = din("w_in", [D, 7 * D])
    gmix = din("gmix", [1, D])
    gsub = din("gsub", [1, 128])
    lamv = din("lamv", [4, 64])
    ident = din("ident", [128, 128])
    maskn = din("maskn", [8, 128])
    hmask = din("hmask", [128, 8])
    cpos = din("cpos", [128, 64])
    cneg = din("cneg", [128, 64])
    on_all = nc.dram_tensor("on_all", [NTK, D], F32, kind="ExternalOutput").ap()

    es = ExitStack()
    S = Sched(nc, es)

    def sb(name, shape, dt=F32, stack=None):
        return (stack or es).enter_context(nc.sbuf_tensor(name, list(shape), dt))

    def pst(name, shape, dt=F32, stack=None):
        return (stack or es).enter_context(nc.psum_tensor(name, list(shape), dt))

    identb = sb("identb", [128, 128], BF16)
    gb = sb("gb", [128, D], F32)
    gsubb = sb("gsubb", [128, 128], F32)
    epsb = sb("epsb", [128, 1], F32)
    lamb = sb("lamb", [128, 4, 64], F32)
    neglam = sb("neglam", [128, 1], F32)
    tmpc = sb("tmpc", [128, 4], F32)
    scr64 = sb("scr64", [128, 64], F32)
    maskb = sb("maskb", [8, 128], BF16)
    hm = sb("hm", [128, 8], F32)
    comb = sb("comb", [128, 64], F32)
    cng = sb("cng", [128, 64], F32)
    ones2 = sb("ones2", [128, 2], BF16)
    ptb = sb("ptb", [128, NSEQ * NPG], I32)
    iott = sb("iott", [128, 1], I32)
    idx = sb("idx", [128, NSEQ * NPG], I32)
    Bc = Buf()
    S.dma("pool", lambda e: e.dma_start(out=identb[:], in_=ident), W=[Bc])
    S.dma("pool", lambda e: e.dma_start(out=maskb[:], in_=maskn), W=[Bc])
    S.dma("sp", lambda e: e.dma_start(out=gb[:], in_=gmix.partition_broadcast(128)), W=[Bc])
    S.dma("sp", lambda e: e.dma_start(out=gsubb[:], in_=gsub.partition_broadcast(128)), W=[Bc])
    S.dma("sp", lambda e: e.dma_start(out=hm[:], in_=hmask), W=[Bc])
    S.dma("sp", lambda e: e.dma_start(out=comb[:], in_=cpos), W=[Bc])
    S.dma("sp", lambda e: e.dma_start(out=cng[:], in_=cneg), W=[Bc])
    S.dma("sp", lambda e: e.dma_start(out=ptb[:], in_=ptab.partition_broadcast(128)), W=[Bc])
    S.dma("sp", lambda e: e.dma_start(out=iott[:], in_=iot), W=[Bc])
    S.dma("sp", lambda e: e.dma_start(out=lamb[:].rearrange("p a b -> p (a b)"),
                                      in_=lamv.rearrange("a b -> (a b)").rearrange("(o n) -> o n", o=1).partition_broadcast(128)), W=[Bc])
    S.op("dve", lambda: nc.vector.memset(epsb[:], EPS), W=[Bc])
    S.op("dve", lambda: nc.vector.memset(ones2[:], 1.0), W=[Bc])
    S.op("dve", lambda: nc.vector.tensor_scalar(out=idx[:], in0=ptb[:], scalar1=128, scalar2=iott[:, 0:1], op0=ALU.mult, op1=ALU.add), R=[Bc], W=[Bc])
    S.op("dve", lambda: nc.vector.tensor_tensor(out=scr64[:], in0=lamb[:, 0, :], in1=lamb[:, 1, :], op=ALU.mult), R=[Bc], W=[Bc])
    S.op("dve", lambda: nc.vector.reduce_sum(out=tmpc[:, 0:1], in_=scr64[:], axis=AX.X), R=[Bc], W=[Bc])
    S.op("dve", lambda: nc.vector.tensor_tensor(out=scr64[:], in0=lamb[:, 2, :], in1=lamb[:, 3, :], op=ALU.mult), R=[Bc], W=[Bc])
    S.op("dve", lambda: nc.vector.reduce_sum(out=tmpc[:, 1:2], in_=scr64[:], axis=AX.X), R=[Bc], W=[Bc])
    S.op("act", lambda: nc.scalar.activation(out=tmpc[:, 2:4], in_=tmpc[:, 0:2], func=AF.Exp), R=[Bc], W=[Bc])
    S.op("dve", lambda: nc.vector.tensor_tensor(out=neglam[:], in0=tmpc[:, 3:4], in1=tmpc[:, 2:3], op=ALU.subtract), R=[Bc], W=[Bc])
    S.op("dve", lambda: nc.vector.tensor_scalar(out=neglam[:], in0=neglam[:], scalar1=-LAM_INIT, scalar2=None, op0=ALU.add), R=[Bc], W=[Bc])
    S.op("dve", lambda: nc.vector.tensor_scalar(out=gsubb[:], in0=gsubb[:], scalar1=1.0 - LAM_INIT, scalar2=None, op0=ALU.mult), R=[Bc], W=[Bc])
    S.op("dve", lambda: nc.vector.scalar_tensor_tensor(out=comb[:], in0=cng[:], scalar=neglam[:, 0:1], in1=comb[:], op0=ALU.mult, op1=ALU.add), R=[Bc], W=[Bc])
    S.barrier()

    uTs = sb("uTs", [128, KC, NTK], BF16)
    qTa = sb("qTa", [128, NH, NTK], BF16)
    kTa = sb("kTa", [128, NH, NTK], BF16)
    Vn = sb("Vn", [8, NSEQ, D], BF16)
    Bu = Buf(); Bq = Buf(); Bk = Buf(); Bvn = Buf()
    with ExitStack() as ph:
        xt = [sb(f"x{i}", [128, D], F32, ph) for i in range(2)]
        xn = [sb(f"xn{i}", [128, D], BF16, ph) for i in range(2)]
        junk = sb("junk", [128, D], F32, ph)
        ss = sb("ss", [128, 4], F32, ph)
        Bxt = [Buf(), Buf()]; Bxn = [Buf(), Buf()]; Bs = Buf()
        psT = [pst(f"psT{i}", [128, KC, 128], BF16, ph) for i in range(2)]
        BpsT = [PBuf(), PBuf()]
        for t in range(NTL):
            i = t % 2
            P = min(128, NTK - t * 128)
            S.dma("sp", lambda e, i=i, P=P, t=t: e.dma_start(out=xt[i][:P, :], in_=xs[t * 128:t * 128 + P, :]), W=[Bxt[i]])
            S.op("act", lambda i=i, P=P: nc.scalar.activation(out=junk[:P, :], in_=xt[i][:P, :], func=AF.Square, accum_out=ss[:P, 0:1]), R=[Bxt[i]], W=[Bs])
            S.op("act", lambda P=P: nc.scalar.activation(out=ss[:P, 1:2], in_=ss[:P, 0:1], func=AF.Sqrt, scale=1.0 / D, bias=epsb[:P, 0:1]), R=[Bs, Bc], W=[Bs])
            S.op("dve", lambda P=P: nc.vector.reciprocal(out=ss[:P, 2:3], in_=ss[:P, 1:2]), R=[Bs], W=[Bs])
            S.op("dve", lambda i=i, P=P: nc.vector.scalar_tensor_tensor(out=xn[i][:P, :], in0=xt[i][:P, :], scalar=ss[:P, 2:3], in1=gb[:P, :], op0=ALU.mult, op1=ALU.mult),
                 R=[Bxt[i], Bs, Bc], W=[Bxn[i]])
            S.mm([lambda kc=kc, i=i, P=P: nc.tensor.transpose(out=psT[i][:, kc, :P], in_=xn[i][:P, kc * 128:(kc + 1) * 128], identity=identb[:P, :P]) for kc in range(KC)],
                 R=[Bxn[i], Bc], W=[BpsT[i]])
            S.op("act", lambda i=i, P=P, t=t: nc.scalar.copy(out=uTs[:, :, t * 128:t * 128 + P], in_=psT[i][:, :, :P]), R=[BpsT[i]], W=[Bu])
        S.barrier()
    with ExitStack() as ph:
        wq = [sb(f"wq{i}", [128, KC, 128], BF16, ph) for i in range(2)]
        wk = [sb(f"wk{i}", [128, KC, 128], BF16, ph) for i in range(2)]
        Bw = [Buf(), Buf()]
        wv = sb("wv", [128, KC, D], BF16, ph)
        Bwv = Buf()
        ps = [pst(f"ps{i}", [128, 512], F32, ph) for i in range(4)]
        Bps = [PBuf() for _ in range(4)]
        pc = 0
        for kc in range(KC):
            for hf in range(2):
                S.dma("pool", lambda e, kc=kc, hf=hf: e.dma_start(out=wv[:, kc, hf * 512:(hf + 1) * 512], in_=w_in[kc * 128:(kc + 1) * 128, 2 * D + hf * 512:2 * D + (hf + 1) * 512]), W=[Bwv])
        for h in range(NH):
            wi = h % 2
            for kc in range(KC):
                S.dma("pool", lambda e, wi=wi, kc=kc, h=h: e.dma_start(out=wq[wi][:, kc, :], in_=w_in[kc * 128:(kc + 1) * 128, h * 128:(h + 1) * 128]), W=[Bw[wi]])
                S.dma("pool", lambda e, wi=wi, kc=kc, h=h: e.dma_start(out=wk[wi][:, kc, :], in_=w_in[kc * 128:(kc + 1) * 128, D + h * 128:D + (h + 1) * 128]), W=[Bw[wi]])
            p = pc % 4; pc += 1
            S.mm([lambda kc=kc, p=p, wi=wi: nc.tensor.matmul(ps[p][:, :NTK], lhsT=wq[wi][:, kc, :], rhs=uTs[:, kc, :], start=(kc == 0), stop=(kc == KC - 1)) for kc in range(KC)],
                 R=[Bw[wi], Bu], W=[Bps[p]])
            S.op("dve", lambda p=p, h=h: nc.vector.tensor_scalar(out=qTa[:, h, :], in0=ps[p][:, :NTK], scalar1=0.125, scalar2=None, op0=ALU.mult), R=[Bps[p]], W=[Bq])
            p = pc % 4; pc += 1
            S.mm([lambda kc=kc, p=p, wi=wi: nc.tensor.matmul(ps[p][:, :NTK], lhsT=wk[wi][:, kc, :], rhs=uTs[:, kc, :], start=(kc == 0), stop=(kc == KC - 1)) for kc in range(KC)],
                 R=[Bw[wi], Bu], W=[Bps[p]])
            S.op("act", lambda p=p, h=h: nc.scalar.copy(out=kTa[:, h, :], in_=ps[p][:, :NTK]), R=[Bps[p]], W=[Bk])
        for s_ in range(NSEQ):
            for hf in range(2):
                p = pc % 4; pc += 1
                S.mm([lambda kc=kc, p=p, s_=s_, hf=hf: nc.tensor.matmul(ps[p][:8, :], lhsT=uTs[:, kc, 8 * s_:8 * s_ + 8], rhs=wv[:, kc, hf * 512:(hf + 1) * 512],
                                                                        start=(kc == 0), stop=(kc == KC - 1)) for kc in range(KC)], R=[Bwv, Bu], W=[Bps[p]])
                S.op("dve" if hf else "act", (lambda p=p, s_=s_, hf=hf: nc.vector.tensor_copy(out=Vn[:8, s_, hf * 512:(hf + 1) * 512], in_=ps[p][:8, :])) if hf else
                     (lambda p=p, s_=s_, hf=hf: nc.scalar.copy(out=Vn[:8, s_, hf * 512:(hf + 1) * 512], in_=ps[p][:8, :])), R=[Bps[p]], W=[Bvn])
        S.barrier()
    with ExitStack() as ph:
        NB3 = 3
        Kp = [sb(f"Kp{i}", [128, D], BF16, ph) for i in range(NB3)]
        Vp = [sb(f"Vp{i}", [128, D], BF16, ph) for i in range(NB3)]
        BKp = [Buf() for _ in range(NB3)]; BVp = [Buf() for _ in range(NB3)]
        KT = [sb(f"KT{i}", [128, NH, 128], BF16, ph) for i in range(2)]
        BKT = [Buf(), Buf()]
        PTs = [sb(f"PTs{i}", [128, 128], BF16, ph) for i in range(2)]
        BPTs = [Buf(), Buf()]
        Qbd = [sb(f"Qbd{i}", [128, NH, 16], BF16, ph) for i in range(2)]
        BQbd = [Buf(), Buf()]
        for i in range(2):
            S.op("pool", lambda i=i: nc.gpsimd.memset(Qbd[i][:], 0.0), W=[BQbd[i]])
        o_n = sb("o_n", [128, 128], F32, ph)
        o_c = sb("o_c", [64, 128], F32, ph)
        o_f = [sb(f"o_f{i}", [64, 128], F32, ph) for i in range(2)]
        rc = sb("rc", [128, 4], F32, ph)
        junk = sb("junk2", [64, 128], F32, ph)
        Bfin = Buf(); Bof = [Buf(), Buf()]
        psT = [pst(f"psK{i}", [128, NH, 128], BF16, ph) for i in range(2)]
        BpsT = [PBuf(), PBuf()]
        pS = [pst(f"pS{i}", [128, 128], F32, ph) for i in range(2)]
        BpS = [PBuf(), PBuf()]
        acc = pst("acc", [128, 2, 512], F32, ph)
        Bacc = PBuf()
        accs = pst("accs", [128, 2], F32, ph)
        Baccs = PBuf()
        pcm = pst("pcm", [64, 128], F32, ph)
        Bpcm = PBuf()
        it = 0
        for s_ in range(NSEQ):
            qb = s_ % 2
            S.op("pool", lambda qb=qb, s_=s_: nc.gpsimd.tensor_copy(out=Qbd[qb][0:64, :, 0:8], in_=qTa[0:64, :, 8 * s_:8 * s_ + 8]), R=[Bq], W=[BQbd[qb]])
            S.op("pool", lambda qb=qb, s_=s_: nc.gpsimd.tensor_copy(out=Qbd[qb][64:128, :, 8:16], in_=qTa[64:128, :, 8 * s_:8 * s_ + 8]), R=[Bq], W=[BQbd[qb]])
            for k in range(NPG + 1):
                new = (k == NPG)
                b3 = it % NB3; b2 = it % 2; it += 1
                if not new:
                    col = s_ * NPG + k
                    S.dma("pool", lambda e, b3=b3, col=col: e.indirect_dma_start(out=Kp[b3][:], out_offset=None, in_=ck,
                                                                                  in_offset=bass.IndirectOffsetOnAxis(ap=idx[:, col:col + 1], axis=0)), R=[Bc], W=[BKp[b3]])
                    S.dma("pool", lambda e, b3=b3, col=col: e.indirect_dma_start(out=Vp[b3][:], out_offset=None, in_=cv,
                                                                                  in_offset=bass.IndirectOffsetOnAxis(ap=idx[:, col:col + 1], axis=0)), R=[Bc], W=[BVp[b3]])
                    S.mm([lambda h=h, b2=b2, b3=b3: nc.tensor.transpose(out=psT[b2][:, h, :], in_=Kp[b3][:, h * 128:(h + 1) * 128], identity=identb[:, :]) for h in range(NH)],
                         R=[BKp[b3], Bc], W=[BpsT[b2]])
                    if it % 2:
                        S.op("act", lambda b2=b2: nc.scalar.copy(out=KT[b2][:], in_=psT[b2][:]), R=[BpsT[b2]], W=[BKT[b2]])
                    else:
                        S.op("dve", lambda b2=b2: nc.vector.tensor_copy(out=KT[b2][:], in_=psT[b2][:]), R=[BpsT[b2]], W=[BKT[b2]])
                    P = 128
                    S.mm([lambda h=h, b2=b2, qb=qb: nc.tensor.matmul(pS[b2][:, h * 16:(h + 1) * 16], lhsT=KT[b2][:, h, :], rhs=Qbd[qb][:, h, :], start=True, stop=True) for h in range(NH)],
                         R=[BKT[b2], BQbd[qb]], W=[BpS[b2]])
                else:
                    P = 8
                    S.mm([lambda h=h, b2=b2, qb=qb, s_=s_: nc.tensor.matmul(pS[b2][:8, h * 16:(h + 1) * 16], lhsT=kTa[:, h, 8 * s_:8 * s_ + 8], rhs=Qbd[qb][:, h, :], start=True, stop=True)
                          for h in range(NH)], R=[Bk, BQbd[qb]], W=[BpS[b2]])
                S.op("act", lambda b2=b2, P=P: nc.scalar.activation(out=PTs[b2][:P, :], in_=pS[b2][:P, :], func=AF.Exp), R=[BpS[b2]], W=[BPTs[b2]])
                if new:
                    S.op("dve", lambda b2=b2: nc.vector.tensor_tensor(out=PTs[b2][:8, :], in0=PTs[b2][:8, :], in1=maskb[:, :], op=ALU.mult), R=[BPTs[b2], Bc], W=[BPTs[b2]])
                    vsrc = [Vn[:8, s_, 0:512], Vn[:8, s_, 512:1024]]
                    Rv = [Bvn]
                else:
                    vsrc = [Vp[b3][:, 0:512], Vp[b3][:, 512:1024]]
                    Rv = [BVp[b3]]
                S.mm([lambda hf=hf, b2=b2, P=P, vsrc=vsrc, k=k, new=new: nc.tensor.matmul(acc[:, hf, :], lhsT=PTs[b2][:P, :], rhs=vsrc[hf], start=(k == 0), stop=new) for hf in range(2)]
                     + [lambda b2=b2, P=P, k=k, new=new: nc.tensor.matmul(accs[:, :], lhsT=PTs[b2][:P, :], rhs=ones2[:P, :], start=(k == 0), stop=new)],
                     R=[BPTs[b2]] + Rv + [Bc], W=[Bacc, Baccs])
            S.op("dve", lambda: nc.vector.tensor_scalar(out=o_n[:, :], in0=acc[:, 0, 0:128], scalar1=hm[:, 0:1], scalar2=None, op0=ALU.mult), R=[Bacc, Bc], W=[Bfin])
            for h2 in range(1, NH):
                S.op("dve", lambda h2=h2: nc.vector.scalar_tensor_tensor(out=o_n[:, :], in0=acc[:, h2 // 4, (h2 % 4) * 128:(h2 % 4 + 1) * 128], scalar=hm[:, h2:h2 + 1], in1=o_n[:, :],
                                                                          op0=ALU.mult, op1=ALU.add), R=[Bacc, Bc, Bfin], W=[Bfin])
            S.op("dve", lambda: nc.vector.reciprocal(out=rc[:, 0:1], in_=accs[:, 0:1]), R=[Baccs], W=[Bfin])
            S.op("dve", lambda: nc.vector.tensor_scalar(out=o_n[:, :], in0=o_n[:, :], scalar1=rc[:, 0:1], scalar2=None, op0=ALU.mult), R=[Bfin], W=[Bfin])
            S.mm([lambda: nc.tensor.matmul(pcm[:, :], lhsT=comb[:, :], rhs=o_n[:, :], start=True, stop=True)], R=[Bfin, Bc], W=[Bpcm])
            fb = s_ % 2
            S.op("dve", lambda: nc.vector.tensor_copy(out=o_c[:, :], in_=pcm[:, :]), R=[Bpcm], W=[Bfin])
            S.op("act", lambda: nc.scalar.activation(out=junk[:, :], in_=o_c[:, :], func=AF.Square, accum_out=rc[:64, 1:2]), R=[Bfin], W=[Bfin])
            S.op("act", lambda: nc.scalar.activation(out=rc[:64, 2:3], in_=rc[:64, 1:2], func=AF.Sqrt, scale=1.0 / 128, bias=epsb[:64, 0:1]), R=[Bfin, Bc], W=[Bfin])
            S.op("dve", lambda: nc.vector.reciprocal(out=rc[:64, 2:3], in_=rc[:64, 2:3]), R=[Bfin], W=[Bfin])
            S.op("dve", lambda fb=fb: nc.vector.scalar_tensor_tensor(out=o_f[fb][:, :], in0=o_c[:, :], scalar=rc[:64, 2:3], in1=gsubb[:64, :], op0=ALU.mult, op1=ALU.mult),
                 R=[Bfin, Bc], W=[Bof[fb]])
            for h in range(NH):
                S.dma("sp", lambda e, fb=fb, h=h, s_=s_: e.dma_start(out=on_all[8 * s_:8 * s_ + 8, h * 128:(h + 1) * 128], in_=o_f[fb][8 * h:8 * h + 8, :]), R=[Bof[fb]], is_out=True)
        S.barrier()
    S.finish()
    es.close()
    return nc


def make_in_maps(inp, cfg):
    TP, TO, NS = cfg["TP"], cfg["TO"], cfg["NS"]
    f32 = np.float32
    xp = np.asarray(inp["x_prompt"], f32)
    B = xp.shape[0]
    vecs = np.stack([np.asarray(inp[k], f32).reshape(-1) for k in
                     ("g_mix", "g_ffn", "g_ple", "g_final", "lru_lambda", "b_conv", "b_gate_a", "b_gate_x")])
    lamv = np.stack([np.asarray(inp[k], f32).reshape(-1) for k in ("lambda_q1", "lambda_k1", "lambda_q2", "lambda_k2")])
    common = {
        "w_in": np.asarray(inp["w_in"], f32)[0], "w_attn": np.asarray(inp["w_attn_br"], f32)[0],
        "w_rec": np.asarray(inp["w_rec_br"], f32)[0], "w_out": np.asarray(inp["w_out"], f32)[0],
        "w_fg": np.asarray(inp["w_ffn_gate"], f32)[0], "w_fu": np.asarray(inp["w_ffn_up"], f32)[0],
        "w_fd": np.asarray(inp["w_ffn_down"], f32)[0], "w_pg": np.asarray(inp["w_ple_gate"], f32)[0],
        "w_pp": np.asarray(inp["w_ple_proj"], f32)[0], "w_ga": np.asarray(inp["w_gate_a"], f32)[0],
        "w_gx": np.asarray(inp["w_gate_x"], f32)[0], "vecs": vecs, "wconv": np.asarray(inp["w_conv"], f32)[0],
        "gsub": np.asarray(inp["g_subln"], f32).reshape(1, 128), "lamv": lamv,
        "ident": np.eye(128, dtype=f32), "tri": np.triu(np.ones((128, 128), f32)),
    }
    maps = []
    xsamp = np.asarray(inp["x_sample"], f32)
    psamp = np.asarray(inp["p_sample"], f32)[0]
    pprm = np.asarray(inp["p_prompt"], f32)[0]
    for c in range(8):
        b, half = c // 2, c % 2
        m = dict(common)
        xfull = np.zeros((TP + TO, D), f32)
        if half == 1:
            xfull[:TP] = xp[b, :TP]
        xfull[TP:] = xp[b, half * TO:(half + 1) * TO]
        m["xf"] = xfull
        m["pown"] = np.ascontiguousarray(pprm[b, half * TO:(half + 1) * TO])
        m["xs"] = np.ascontiguousarray(xsamp[4 * c:4 * c + 4].reshape(NS, D))
        m["psm"] = np.ascontiguousarray(psamp[4 * c:4 * c + 4].reshape(NS, 256))
        m["flag"] = np.full((128, 1), float(half), f32)
        m["sconv"] = np.ascontiguousarray(np.asarray(inp["state_conv"], f32)[0, 4 * c:4 * c + 4].reshape(12, D))
        m["sh"] = np.ascontiguousarray(np.asarray(inp["state_h"], f32)[0, 4 * c:4 * c + 4])
        maps.append(m)
    return maps


_CACHE = {}


def make_sample_map(inp):
    f32 = np.float32
    ck = np.asarray(inp["cache_k"], f32)
    cv = np.asarray(inp["cache_v"], f32)
    nphys = ck.shape[1]
    pt = np.asarray(inp["page_table"], np.int32)
    nseq, npg = pt.shape
    hmask = np.zeros((128, 8), f32)
    cpos = np.zeros((128, 64), f32)
    cneg = np.zeros((128, 64), f32)
    maskn = np.zeros((8, 128), f32)
    for h in range(8):
        for c in range(2):
            for q in range(8):
                r = h * 16 + c * 8 + q
                hmask[r, h] = 1.0
                (cpos if c == 0 else cneg)[r, h * 8 + q] = 1.0
                for j in range(8):
                    if j <= q:
                        maskn[j, r] = 1.0
    lamv = np.stack([np.asarray(inp[k], f32).reshape(-1) for k in ("lambda_q1", "lambda_k1", "lambda_q2", "lambda_k2")])
    m = {
        "xs": np.ascontiguousarray(np.asarray(inp["x_sample"], f32).reshape(nseq * 8, D)),
        "ck": ck.reshape(nphys * 128, D), "cv": cv.reshape(nphys * 128, D),
        "ptab": np.ascontiguousarray(pt.reshape(1, nseq * npg)),
        "iot": np.arange(128, dtype=np.int32).reshape(128, 1),
        "w_in": np.asarray(inp["w_in"], f32)[0], "gmix": np.asarray(inp["g_mix"], f32).reshape(1, D),
        "gsub": np.asarray(inp["g_subln"], f32).reshape(1, 128), "lamv": lamv,
        "ident": np.eye(128, dtype=f32), "maskn": maskn, "hmask": hmask, "cpos": cpos, "cneg": cneg,
    }
    return m, {"NSEQ": nseq, "NPG": npg, "NPHYS": nphys}


def kernel(**inputs):
    import os
    xp = inputs["x_prompt"]
    SEQ = xp.shape[1]
    cfg = {"TP": SEQ // 2, "TO": SEQ // 2, "NS": 32}
    if "KSTOP" in os.environ:
        cfg["stop"] = int(os.environ["KSTOP"])
    mb, cfgb = make_sample_map(inputs)
    keyb = ("B", cfgb["NSEQ"], cfgb["NPG"], cfgb["NPHYS"])
    if keyb not in _CACHE:
        _CACHE[keyb] = build_sample_attn(cfgb)
    resb = run_bass_kernel_spmd(_CACHE[keyb], [mb], core_ids=[0]).results
    on_all = np.asarray(resb[0]["on_all"], np.float32)
    key = ("A", SEQ, cfg.get("stop"))
    if key not in _CACHE:
        _CACHE[key] = build(cfg)
    nc = _CACHE[key]
    maps = make_in_maps(inputs, cfg)
    for c in range(8):
        maps[c]["on_s"] = np.ascontiguousarray(on_all[32 * c:32 * (c + 1)])
    res = run_bass_kernel_spmd(nc, maps, core_ids=list(range(8))).results
    B = xp.shape[0]
    TO = cfg["TO"]
    f32 = np.float32

    def own(name):
        o = np.zeros((B, SEQ, D), f32)
        for c in range(8):
            o[c // 2, (c % 2) * TO:(c % 2 + 1) * TO] = res[c][name]
        return o

    def samp(name, per):
        return np.concatenate([res[c][name].reshape(4, per, -1) for c in range(8)], axis=0)
    y_prompt = own("y_own")
    y_sample = samp("y_s", 8)
    k_prompt = own("k_own").reshape(1, B, SEQ, NH, 2, 64)
    v_prompt = own("v_own").reshape(1, B, SEQ, NH, 128)
    conv_prompt = np.stack([res[2 * b + 1]["conv_p"] for b in range(B)])[None]
    h_prompt = np.stack([res[2 * b + 1]["h_p"][0] for b in range(B)])[None]
    k_sample = samp("k_s", 8).reshape(1, 32, 8, NH, 2, 64)
    v_sample = samp("v_s", 8).reshape(1, 32, 8, NH, 128)
    conv_sample = samp("conv_s", 3)[None]
    h_sample = np.concatenate([res[c]["h_s"] for c in range(8)], axis=0)[None]
    return (y_prompt, y_sample, k_prompt, v_prompt, conv_prompt, h_prompt, k_sample, v_sample, conv_sample, h_sample)
```

```python
import math
from contextlib import ExitStack
import numpy as np
import concourse.bass as bass
import concourse.mybir as mybir
from concourse.bass_utils import run_bass_kernel_spmd

F32 = mybir.dt.float32
BF16 = mybir.dt.bfloat16
I32 = mybir.dt.int32
AF = mybir.ActivationFunctionType
ALU = mybir.AluOpType
AX = mybir.AxisListType

D = 1024
NH = 8
KC = 8
DFF = 2816
FC = 22
EPS = 1e-6
LAM_INIT = 0.8 - 0.6 * math.exp(-0.3 * 0)


class Buf:
    __slots__ = ("w", "r", "excl")

    def __init__(self, excl=False):
        self.w = None
        self.r = {}
        self.excl = excl


def PBuf():
    return Buf(True)


class Sched:
    def __init__(self, nc, es, nds=20):
        self.nc = nc
        self.sems = []
        self.E = {}
        for name, e in (("pe", nc.tensor), ("act", nc.scalar), ("dve", nc.vector),
                        ("pool", nc.gpsimd), ("sp", nc.sync)):
            sid = self._new_sem(es, name)
            self.E[name] = {"e": e, "sid": sid, "cnt": 0, "waited": {}}
        self.dq = {}
        for q in ("sp", "pool", "act"):
            self.dq[q] = {"sids": [self._new_sem(es, f"d{q}{i}") for i in range(nds)], "next": 0}
        self.dval = {}
        self.out_toks = []

    def _new_sem(self, es, name):
        s = es.enter_context(self.nc.semaphore(name))
        self.sems.append(s)
        return len(self.sems) - 1

    def _deps(self, R, W):
        deps = {}

        def add(t):
            if t is None:
                return
            if deps.get(t[0], 0) < t[1]:
                deps[t[0]] = t[1]
        for b in R:
            add(b.w)
            if b.excl:
                for s, v in b.r.items():
                    add((s, v))
        for b in W:
            add(b.w)
            for s, v in b.r.items():
                add((s, v))
        return deps

    def _emit_waits(self, en, deps, skip_own=False):
        E = self.E[en]
        for s, v in deps.items():
            if skip_own and s == E["sid"]:
                continue
            if E["waited"].get(s, 0) < v:
                E["e"].wait_ge(self.sems[s], v)
                E["waited"][s] = v

    def _record(self, tok, R, W):
        for b in R:
            if b.r.get(tok[0], 0) < tok[1]:
                b.r[tok[0]] = tok[1]
        for b in W:
            b.w = tok
            b.r = {}

    def op(self, en, fn, R=(), W=()):
        E = self.E[en]
        self._emit_waits(en, self._deps(R, W), skip_own=(en == "pe"))
        ins = fn()
        E["cnt"] += 1
        ins.then_inc(self.sems[E["sid"]], 1)
        self._record((E["sid"], E["cnt"]), R, W)

    def mm(self, fns, R=(), W=()):
        E = self.E["pe"]
        self._emit_waits("pe", self._deps(R, W), skip_own=True)
        ins = None
        for fn in fns:
            ins = fn()
        E["cnt"] += 1
        ins.then_inc(self.sems[E["sid"]], 1)
        self._record((E["sid"], E["cnt"]), R, W)

    def dma(self, q, fn, R=(), W=(), is_out=False):
        Q = self.dq[q]
        sid = Q["sids"][Q["next"]]
        Q["next"] = (Q["next"] + 1) % len(Q["sids"])
        deps = self._deps(R, W)
        prev = self.dval.get(sid, 0)
        if prev:
            if deps.get(sid, 0) < prev:
                deps[sid] = prev
        self._emit_waits(q, deps)
        ins = fn(self.E[q]["e"])
        val = prev + 16
        self.dval[sid] = val
        ins.then_inc(self.sems[sid], 16)
        tok = (sid, val)
        self._record(tok, R, W)
        if is_out:
            self.out_toks.append(tok)

    def barrier(self):
        toks = {}
        for en, E in self.E.items():
            if E["cnt"]:
                toks[E["sid"]] = E["cnt"]
        for sid, v in self.dval.items():
            toks[sid] = v
        for en in self.E:
            self._emit_waits(en, toks)

    def finish(self):
        toks = {}
        for t in self.out_toks:
            if toks.get(t[0], 0) < t[1]:
                toks[t[0]] = t[1]
        self._emit_waits("sp", toks)


def build(cfg):
    TP, TO, NS = cfg["TP"], cfg["TO"], cfg["NS"]
    TF = TP + TO
    NT = TF + NS
    NTT = TF // 128
    OT0 = TP // 128
    NOT = TO // 128
    NQG = TO // 512
    NOS = TO + NS
    nc = bass.Bass("TRN2", target_bir_lowering=False)

    def din(name, shape, dt=F32):
        return nc.dram_tensor(name, list(shape), dt, kind="ExternalInput").ap()

    def dout(name, shape, dt=F32):
        return nc.dram_tensor(name, list(shape), dt, kind="ExternalOutput").ap()

    xf = din("xf", [TF, D])
    xs = din("xs", [NS, D])
    pown = din("pown", [TO, 256])
    psm = din("psm", [NS, 256])
    flag = din("flag", [128, 1])
    sconv = din("sconv", [12, D])
    sh = din("sh", [4, D])
    w_in = din("w_in", [D, 7 * D])
    w_attn = din("w_attn", [D, D])
    w_rec = din("w_rec", [D, D])
    w_out = din("w_out", [D, D])
    w_fg = din("w_fg", [D, DFF])
    w_fu = din("w_fu", [D, DFF])
    w_fd = din("w_fd", [DFF, D])
    w_pg = din("w_pg", [D, D])
    w_pp = din("w_pp", [256, D])
    w_ga = din("w_ga", [8, 128, 128])
    w_gx = din("w_gx", [8, 128, 128])
    vecs = din("vecs", [8, D])
    wconv = din("wconv", [4, D])
    gsub = din("gsub", [1, 128])
    lamv = din("lamv", [4, 64])
    ident = din("ident", [128, 128])
    tri = din("tri", [128, 128])
    on_s_in = din("on_s", [NS, D])

    y_own = dout("y_own", [TO, D])
    y_s = dout("y_s", [NS, D])
    k_own = dout("k_own", [TO, D])
    v_own = dout("v_own", [TO, D])
    k_s = dout("k_s", [NS, D])
    v_s = dout("v_s", [NS, D])
    conv_p = dout("conv_p", [3, D])
    h_p = dout("h_p", [1, D])
    conv_s = dout("conv_s", [12, D])
    h_s = dout("h_s", [4, D])
    kT_d = nc.dram_tensor("kT_d", [NH, 128, TF], BF16, kind="Internal").ap()
    v_d = nc.dram_tensor("v_d", [NH, 128, NTT, 130], BF16, kind="Internal").ap()

    es = ExitStack()
    S = Sched(nc, es)

    def sb(name, shape, dt=F32, stack=None):
        return (stack or es).enter_context(nc.sbuf_tensor(name, list(shape), dt))

    def pst(name, shape, dt=F32, stack=None):
        return (stack or es).enter_context(nc.psum_tensor(name, list(shape), dt))

    identb = sb("identb", [128, 128], BF16)
    identf = sb("identf", [128, 128], F32)
    trib = sb("trib", [128, 128], BF16)
    gb = sb("gb", [128, 4, D], F32)
    gsubb = sb("gsubb", [128, 128], F32)
    vT = sb("vT", [128, 4, KC], F32)
    wcT = sb("wcT", [128, 4, KC], F32)
    clam = sb("clam", [128, KC], F32)
    clam2 = sb("clam2", [128, KC], F32)
    epsb = sb("epsb", [128, 1], F32)
    flg = sb("flg", [128, 1], F32)
    lamb = sb("lamb", [128, 4, 64], F32)
    neglam = sb("neglam", [128, 1], F32)
    tmpc = sb("tmpc", [128, 4], F32)
    B_const = Buf()
    ld = []
    S.dma("pool", lambda e: e.dma_start(out=identb[:], in_=ident), W=[B_const])
    S.dma("sp", lambda e: e.dma_start(out=identf[:], in_=ident), W=[B_const])
    S.dma("pool", lambda e: e.dma_start(out=trib[:], in_=tri), W=[B_const])
    for i in range(4):
        S.dma("sp", lambda e, i=i: e.dma_start(out=gb[:, i, :], in_=vecs[i:i + 1, :].partition_broadcast(128)), W=[B_const])
    S.dma("sp", lambda e: e.dma_start(out=gsubb[:], in_=gsub.partition_broadcast(128)), W=[B_const])
    with nc.allow_non_contiguous_dma(reason="tiny parameter vectors to feature-major"):
        for i in range(4):
            S.dma("sp", lambda e, i=i: e.dma_start(out=vT[:, i, :], in_=vecs[4 + i, :].rearrange("(c p) -> p c", p=128)), W=[B_const])
            S.dma("sp", lambda e, i=i: e.dma_start(out=wcT[:, i, :], in_=wconv[i, :].rearrange("(c p) -> p c", p=128)), W=[B_const])
    S.dma("sp", lambda e: e.dma_start(out=flg[:], in_=flag), W=[B_const])
    S.dma("sp", lambda e: e.dma_start(out=lamb[:].rearrange("p a b -> p (a b)"),
                                      in_=lamv.rearrange("a b -> (a b)").rearrange("(o n) -> o n", o=1).partition_broadcast(128)), W=[B_const])
    S.op("dve", lambda: nc.vector.memset(epsb[:], EPS), W=[B_const])
    scr64 = sb("scr64", [128, 64], F32)
    S.op("dve", lambda: nc.vector.tensor_tensor(out=scr64[:], in0=lamb[:, 0, :], in1=lamb[:, 1, :], op=ALU.mult), R=[B_const], W=[B_const])
    S.op("dve", lambda: nc.vector.reduce_sum(out=tmpc[:, 0:1], in_=scr64[:], axis=AX.X), R=[B_const], W=[B_const])
    S.op("dve", lambda: nc.vector.tensor_tensor(out=scr64[:], in0=lamb[:, 2, :], in1=lamb[:, 3, :], op=ALU.mult), R=[B_const], W=[B_const])
    S.op("dve", lambda: nc.vector.reduce_sum(out=tmpc[:, 1:2], in_=scr64[:], axis=AX.X), R=[B_const], W=[B_const])
    S.op("act", lambda: nc.scalar.activation(out=tmpc[:, 2:4], in_=tmpc[:, 0:2], func=AF.Exp), R=[B_const], W=[B_const])
    S.op("dve", lambda: nc.vector.tensor_tensor(out=neglam[:], in0=tmpc[:, 3:4], in1=tmpc[:, 2:3], op=ALU.subtract), R=[B_const], W=[B_const])
    S.op("dve", lambda: nc.vector.tensor_scalar(out=neglam[:], in0=neglam[:], scalar1=-LAM_INIT, scalar2=None, op0=ALU.add), R=[B_const], W=[B_const])
    S.op("dve", lambda: nc.vector.tensor_scalar(out=gsubb[:], in0=gsubb[:], scalar1=1.0 - LAM_INIT, scalar2=None, op0=ALU.mult), R=[B_const], W=[B_const])
    S.op("act", lambda: nc.scalar.activation(out=clam[:], in_=vT[:, 0, :], func=AF.Exp, scale=-1.0), R=[B_const], W=[B_const])
    S.op("dve", lambda: nc.vector.tensor_scalar(out=clam[:], in0=clam[:], scalar1=1.0, scalar2=None, op0=ALU.add), R=[B_const], W=[B_const])
    S.op("act", lambda: nc.scalar.activation(out=clam[:], in_=clam[:], func=AF.Ln), R=[B_const], W=[B_const])
    S.op("dve", lambda: nc.vector.tensor_scalar(out=clam2[:], in0=clam[:], scalar1=-16.0, scalar2=None, op0=ALU.mult), R=[B_const], W=[B_const])
    S.op("dve", lambda: nc.vector.tensor_scalar(out=clam[:], in0=clam[:], scalar1=-8.0, scalar2=None, op0=ALU.mult), R=[B_const], W=[B_const])
    S.barrier()
    if cfg.get("stop") == 0:
        S.finish(); es.close(); return nc

    uT = sb("uT", [128, KC, NT], BF16)
    B_uT = [Buf() for _ in range(NTT + 1)]

    def rmsnorm_rows(stk, x_ap, P, gidx, out_ap, Bx, Bo, tag):
        junk = rmsnorm_rows.junk
        ss = rmsnorm_rows.ss
        Bs = rmsnorm_rows.Bs
        S.op("act", lambda: nc.scalar.activation(out=junk[:P, :], in_=x_ap, func=AF.Square, accum_out=ss[:P, 0:1]), R=[Bx], W=[Bs])
        S.op("act", lambda: nc.scalar.activation(out=ss[:P, 1:2], in_=ss[:P, 0:1], func=AF.Sqrt, scale=1.0 / D, bias=epsb[:P, 0:1]), R=[Bs], W=[Bs])
        S.op("dve", lambda: nc.vector.reciprocal(out=ss[:P, 2:3], in_=ss[:P, 1:2]), R=[Bs], W=[Bs])
        S.op("dve", lambda: nc.vector.scalar_tensor_tensor(out=out_ap, in0=x_ap, scalar=ss[:P, 2:3], in1=gb[:P, gidx, :],
                                                             op0=ALU.mult, op1=ALU.mult), R=[Bx, Bs], W=[Bo])
    rmsnorm_rows.junk = sb("nrm_junk", [128, D], F32)
    rmsnorm_rows.ss = sb("nrm_ss", [128, 4], F32)
    rmsnorm_rows.Bs = Buf()

    def transpose_to_uT(stk, xn, P, col0, Bxn, Bu, psT, BpsT):
        S.mm([lambda kc=kc: nc.tensor.transpose(out=psT[:, kc, :P], in_=xn[:P, kc * 128:(kc + 1) * 128], identity=identb[:P, :P])
              for kc in range(KC)], R=[Bxn], W=[BpsT])
        S.op("act", lambda: nc.scalar.copy(out=uT[:, :, col0:col0 + P], in_=psT[:, :, :P]), R=[BpsT], W=[Bu])

    with ExitStack() as ph:
        xt = [sb(f"a0_x{i}", [128, D], F32, ph) for i in range(2)]
        Bxt = [Buf(), Buf()]
        xn = [sb(f"a0_xn{i}", [128, D], BF16, ph) for i in range(2)]
        Bxn = [Buf(), Buf()]
        psT = [pst(f"a0_ps{i}", [128, KC, 128], BF16, ph) for i in range(2)]
        BpsT = [PBuf(), PBuf()]
        for t in range(NTT + 1):
            i = t % 2
            P = 128 if t < NTT else NS
            src = xf[t * 128:(t + 1) * 128, :] if t < NTT else xs
            S.dma("sp", lambda e, i=i, P=P, src=src: e.dma_start(out=xt[i][:P, :], in_=src), W=[Bxt[i]])
            rmsnorm_rows(ph, xt[i][:P, :], P, 0, xn[i][:P, :], Bxt[i], Bxn[i], "a0")
            transpose_to_uT(ph, xn[i], P, t * 128, Bxn[i], B_uT[t], psT[i], BpsT[i])
        S.barrier()

    if cfg.get("stop") == 1:
        S.finish(); es.close(); return nc
    kTs = sb("kTs", [128, NH, NS], BF16)
    B_kTs = Buf()
    with ExitStack() as ph:
        wk = [sb(f"a1_wk{i}", [128, KC, 512], BF16, ph) for i in range(2)]
        Bwk = [Buf(), Buf()]
        wv = [sb(f"a1_wv{i}", [128, KC, 512], BF16, ph) for i in range(2)]
        Bwv = [Buf(), Buf()]
        kst = [sb(f"a1_kst{i}", [128, 512], BF16, ph) for i in range(2)]
        Bkst = [Buf(), Buf()]
        vst = [sb(f"a1_vst{i}", [128, 4, 130], BF16, ph) for i in range(2)]
        Bvst = [Buf(), Buf()]
        fst = [sb(f"a1_fst{i}", [128, 512], F32, ph) for i in range(4)]
        Bfst = [Buf() for _ in range(4)]
        ps = [pst(f"a1_ps{i}", [128, 512], F32, ph) for i in range(4)]
        Bps = [PBuf() for _ in range(4)]
        ones_c = sb("a1_ones", [128, 1], F32, ph)
        B1 = Buf()
        S.op("dve", lambda: nc.vector.memset(ones_c[:], 1.0), W=[B1])
        pi = 0
        fi = 0
        for hh in range(2):
            wi = hh % 2
            for kc in range(KC):
                S.dma("pool", lambda e, wi=wi, hh=hh, kc=kc: e.dma_start(
                    out=wk[wi][:, kc, :], in_=w_in[kc * 128:(kc + 1) * 128, D + hh * 512:D + (hh + 1) * 512]), W=[Bwk[wi]])
                S.dma("pool", lambda e, wi=wi, hh=hh, kc=kc: e.dma_start(
                    out=wv[wi][:, kc, :], in_=w_in[kc * 128:(kc + 1) * 128, 2 * D + hh * 512:2 * D + (hh + 1) * 512]), W=[Bwv[wi]])
            for hl in range(4 if cfg.get("stop") != 2 else 0):
                h = hh * 4 + hl
                c0 = 0
                while c0 < NT:
                    cw = min(512, NT - c0)
                    tiles = sorted(set([min(c // 128, NTT) for c in range(c0, c0 + cw, 128)]))
                    p = pi % 4
                    pi += 1
                    S.mm([lambda kc=kc, p=p, c0=c0, cw=cw, hl=hl, wi=wi: nc.tensor.matmul(
                        ps[p][:, :cw], lhsT=wk[wi][:, kc, hl * 128:(hl + 1) * 128], rhs=uT[:, kc, c0:c0 + cw],
                        start=(kc == 0), stop=(kc == KC - 1)) for kc in range(KC)],
                        R=[Bwk[wi]] + [B_uT[t] for t in tiles], W=[Bps[p]])
                    if c0 < TF:
                        ki = (pi) % 2
                        S.op("act", lambda ki=ki, p=p, cw=cw: nc.scalar.copy(out=kst[ki][:, :cw], in_=ps[p][:, :cw]), R=[Bps[p]], W=[Bkst[ki]])
                        if not cfg.get("noscr"):
                            S.dma("sp", lambda e, ki=ki, h=h, c0=c0, cw=cw: e.dma_start(out=kT_d[h, :, c0:c0 + cw], in_=kst[ki][:, :cw]), R=[Bkst[ki]])
                    else:
                        S.op("act", lambda p=p, cw=cw, h=h: nc.scalar.copy(out=kTs[:, h, :], in_=ps[p][:, :cw]), R=[Bps[p]], W=[B_kTs])
                    c0 += cw
            for t in range(NTT + 1 if cfg.get("stop") not in (2, 3) else 0):
                P = 128 if t < NTT else NS
                col0 = t * 128
                own = t >= OT0
                p = pi % 4
                pi += 1
                S.mm([lambda kc=kc, p=p, P=P, col0=col0, wi=wi: nc.tensor.matmul(
                    ps[p][:P, :], lhsT=uT[:, kc, col0:col0 + P], rhs=wv[wi][:, kc, :],
                    start=(kc == 0), stop=(kc == KC - 1)) for kc in range(KC)], R=[Bwv[wi], B_uT[t]], W=[Bps[p]])
                if t < NTT:
                    vi = t % 2
                    S.op("dve", lambda vi=vi, p=p: nc.vector.tensor_copy(out=vst[vi][:, :, 0:128], in_=ps[p][:].rearrange("p (h d) -> p h d", h=4)),
                         R=[Bps[p]], W=[Bvst[vi]])
                    onesrc = ones_c if own else flg
                    for hl in range(4):
                        S.op("dve", lambda vi=vi, hl=hl, onesrc=onesrc: nc.vector.tensor_copy(out=vst[vi][:, hl, 128:129], in_=onesrc[:, 0:1]),
                             R=[B1, B_const], W=[Bvst[vi]])
                    if not cfg.get("noscr"):
                        S.dma("sp", lambda e, vi=vi, t=t, hh=hh: e.dma_start(out=v_d[hh * 4:(hh + 1) * 4, :, t, :].rearrange("h p d -> p h d"), in_=vst[vi][:]),
                              R=[Bvst[vi]])
                if own and cfg.get("stop") != 4:
                    f = fi % 4
                    fi += 1
                    if cfg.get("stop") == 7 and P == 128:
                        S.op("act", lambda f=f, p=p, P=P: nc.scalar.copy(out=fst[f][:P, :], in_=ps[p][:P, :]), R=[Bps[p]], W=[Bfst[f]])
                    elif cfg.get("stop") in (5, 7):
                        S.op("dve", lambda f=f, p=p, P=P: nc.vector.memset(fst[f][:P, :], 1.0), R=[Bps[p]], W=[Bfst[f]])
                        if P == 128 and cfg.get("stop") == 5:
                            S.op("dve", lambda f=f, col0=col0: nc.vector.tensor_copy(out=fst[f][:, 0:128], in_=uT[:, 0, col0:col0 + 128]), R=[B_uT[t]], W=[Bfst[f]])
                            S.op("dve", lambda f=f, col0=col0, wi=wi: nc.vector.tensor_copy(out=fst[f][:, 128:256], in_=wv[wi][:, 0, 0:128]), R=[Bwv[wi]], W=[Bfst[f]])
                    else:
                        S.op("act", lambda f=f, p=p, P=P: nc.scalar.copy(out=fst[f][:P, :], in_=ps[p][:P, :]), R=[Bps[p]], W=[Bfst[f]])
                    dst = v_own[(t - OT0) * 128:(t - OT0 + 1) * 128, hh * 512:(hh + 1) * 512] if t < NTT else v_s[:, hh * 512:(hh + 1) * 512]
                    S.dma("sp", lambda e, f=f, P=P, dst=dst: e.dma_start(out=dst, in_=fst[f][:P, :]), R=[Bfst[f]], is_out=True)
                    if cfg.get("stop") in (5, 7):
                        continue
                    p = pi % 4
                    pi += 1
                    S.mm([lambda kc=kc, p=p, P=P, col0=col0, wi=wi: nc.tensor.matmul(
                        ps[p][:P, :], lhsT=uT[:, kc, col0:col0 + P], rhs=wk[wi][:, kc, :],
                        start=(kc == 0), stop=(kc == KC - 1)) for kc in range(KC)], R=[Bwk[wi], B_uT[t]], W=[Bps[p]])
                    f = fi % 4
                    fi += 1
                    S.op("dve", lambda f=f, p=p, P=P: nc.vector.tensor_copy(out=fst[f][:P, :], in_=ps[p][:P, :]), R=[Bps[p]], W=[Bfst[f]])
                    dst = k_own[(t - OT0) * 128:(t - OT0 + 1) * 128, hh * 512:(hh + 1) * 512] if t < NTT else k_s[:, hh * 512:(hh + 1) * 512]
                    S.dma("sp", lambda e, f=f, P=P, dst=dst: e.dma_start(out=dst, in_=fst[f][:P, :]), R=[Bfst[f]], is_out=True)
        S.barrier()

    if cfg.get("stop") == 11:
        S.finish(); es.close(); return nc
    mTs = sb("mTs", [128, KC, NS], BF16)
    actTs = sb("actTs", [128, 8, NS], BF16)
    es3 = ExitStack()
    on_tm = sb("on_tm", [128, NOT + 1, D], BF16, es3)
    B_on = [Buf() for _ in range(NOT + 1)]
    qTs = sb("qTs", [128, NH, NS], BF16, es3)
    B_qTs = Buf()
    NPB = TP // 128
    with ExitStack() as ph:
        wq = [sb(f"b_wq{i}", [128, KC, 128], BF16, ph) for i in range(2)]
        Bwq = [Buf(), Buf()]
        kTh = [sb(f"b_kT{i}", [128, TF], BF16, ph) for i in range(2)]
        BkTh = [Buf(), Buf()]
        vh = [sb(f"b_v{i}", [128, NTT, 130], BF16, ph) for i in range(2)]
        Bvh = [Buf(), Buf()]
        qTh = [sb(f"b_q{i}", [128, NOS], BF16, ph) for i in range(2)]
        BqTh = [Buf(), Buf()]
        PT = [sb(f"b_PT{i}", [128, 2, 512], BF16, ph) for i in range(3)]
        BPT = [Buf() for _ in range(3)]
        pS = [pst(f"b_pS{i}", [128, 2, 512], F32, ph) for i in range(2)]
        BpS = [PBuf(), PBuf()]
        acc = [pst(f"b_acc{i}", [128, 512], F32, ph) for i in range(4)]
        Bacc = [PBuf() for _ in range(4)]
        rcp = sb("b_rcp", [128, 4], F32, ph)
        o1 = sb("b_o1", [128, 128], F32, ph)
        o2 = sb("b_o2", [128, 128], F32, ph)
        junk = sb("b_junk", [128, 128], F32, ph)
        Bfin = Buf()
        scnt = 0
        pcnt = 0
        for h in range(NH):
            hb = h % 2
            for kc in range(KC):
                S.dma("pool", lambda e, hb=hb, kc=kc, h=h: e.dma_start(out=wq[hb][:, kc, :], in_=w_in[kc * 128:(kc + 1) * 128, h * 128:(h + 1) * 128]), W=[Bwq[hb]])
            S.dma("sp", lambda e, hb=hb, h=h: e.dma_start(out=kTh[hb][:, :], in_=kT_d[h]), W=[BkTh[hb]])
            S.dma("sp", lambda e, hb=hb, h=h: e.dma_start(out=vh[hb][:, :, :], in_=v_d[h]), W=[Bvh[hb]])
            c0 = 0
            while c0 < NOS:
                cw = min(512, NOS - c0)
                sb_ = scnt % 2; scnt += 1
                tiles = sorted(set(min((TP + c0 + o) // 128, NTT) for o in range(0, cw, 128)))
                S.mm([lambda kc=kc, sb_=sb_, c0=c0, cw=cw, hb=hb: nc.tensor.matmul(pS[sb_][:, 0, :cw], lhsT=wq[hb][:, kc, :], rhs=uT[:, kc, TP + c0:TP + c0 + cw],
                                                                                  start=(kc == 0), stop=(kc == KC - 1)) for kc in range(KC)],
                     R=[Bwq[hb]] + [B_uT[t] for t in tiles], W=[BpS[sb_]])
                S.op("dve", lambda sb_=sb_, c0=c0, cw=cw, hb=hb: nc.vector.tensor_scalar(out=qTh[hb][:, c0:c0 + cw], in0=pS[sb_][:, 0, :cw], scalar1=0.125, scalar2=None, op0=ALU.mult),
                     R=[BpS[sb_]], W=[BqTh[hb]])
                c0 += cw
            S.op("pool", lambda hb=hb, h=h: nc.gpsimd.tensor_copy(out=qTs[:, h, :], in_=qTh[hb][:, TO:TO + NS]), R=[BqTh[hb]], W=[B_qTs])
            for s_ in range(NQG):
                nkb = NPB + 4 * (s_ + 1)

                def emit_qk(j, s_=s_, hb=hb):
                    jj = j - (NPB + 4 * s_)
                    q0 = max(jj, 0) * 128
                    sb_ = j % 2
                    pb = j % 3
                    S.mm([lambda c=c: nc.tensor.matmul(
                        pS[sb_][:, c, q0:512], lhsT=kTh[hb][64 * c:64 * c + 64, j * 128:(j + 1) * 128],
                        rhs=qTh[hb][64 * c:64 * c + 64, s_ * 512 + q0:(s_ + 1) * 512], start=True, stop=True) for c in range(2)],
                        R=[BkTh[hb], BqTh[hb]], W=[BpS[sb_]])
                    S.op("act", lambda: nc.scalar.activation(out=PT[pb][:, :, q0:512], in_=pS[sb_][:, :, q0:512], func=AF.Exp),
                         R=[BpS[sb_]], W=[BPT[pb]])
                    if jj >= 0:
                        for c in range(2):
                            S.op("dve", lambda c=c: nc.vector.tensor_tensor(out=PT[pb][:, c, q0:q0 + 128], in0=PT[pb][:, c, q0:q0 + 128], in1=trib[:, :], op=ALU.mult),
                                 R=[BPT[pb], B_const], W=[BPT[pb]])

                def emit_pv(j, s_=s_, hb=hb):
                    jj = j - (NPB + 4 * s_)
                    qb0 = max(jj, 0)
                    pb = j % 3
                    fns = []
                    for qb in range(qb0, 4):
                        last = (j == NPB + 4 * s_ + qb)
                        for c in range(2):
                            fns.append(lambda qb=qb, c=c, last=last: nc.tensor.matmul(
                                acc[qb][:, c * 256:c * 256 + 129], lhsT=PT[pb][:, c, qb * 128:(qb + 1) * 128], rhs=vh[hb][:, j, 0:129],
                                start=(j == 0 and c == 0), stop=last, skip_group_check=True))
                    S.mm(fns, R=[BPT[pb], Bvh[hb]], W=[Bacc[qb] for qb in range(qb0, 4)])

                emit_qk(0)
                for j in range(nkb):
                    if j + 1 < nkb:
                        emit_qk(j + 1)
                    emit_pv(j)
                for qb in range(4):
                    ti = s_ * 4 + qb
                    S.op("dve", lambda qb=qb: nc.vector.reciprocal(out=rcp[:, 0:1], in_=acc[qb][:, 128:129]), R=[Bacc[qb]], W=[Bfin])
                    S.op("dve", lambda qb=qb: nc.vector.reciprocal(out=rcp[:, 1:2], in_=acc[qb][:, 384:385]), R=[Bacc[qb]], W=[Bfin])
                    S.op("dve", lambda: nc.vector.tensor_tensor(out=rcp[:, 1:2], in0=rcp[:, 1:2], in1=neglam[:, :], op=ALU.mult), R=[Bfin, B_const], W=[Bfin])
                    S.op("dve", lambda qb=qb: nc.vector.tensor_scalar(out=o1[:, :], in0=acc[qb][:, 0:128], scalar1=rcp[:, 0:1], scalar2=None, op0=ALU.mult), R=[Bacc[qb], Bfin], W=[Bfin])
                    S.op("dve", lambda qb=qb: nc.vector.scalar_tensor_tensor(out=o2[:, :], in0=acc[qb][:, 256:384], scalar=rcp[:, 1:2], in1=o1[:, :], op0=ALU.mult, op1=ALU.add),
                         R=[Bacc[qb], Bfin], W=[Bfin])
                    S.op("act", lambda: nc.scalar.activation(out=junk[:, :], in_=o2[:, :], func=AF.Square, accum_out=rcp[:, 2:3]), R=[Bfin], W=[Bfin])
                    S.op("act", lambda: nc.scalar.activation(out=rcp[:, 3:4], in_=rcp[:, 2:3], func=AF.Sqrt, scale=1.0 / 128, bias=epsb[:, 0:1]), R=[Bfin, B_const], W=[Bfin])
                    S.op("dve", lambda: nc.vector.reciprocal(out=rcp[:, 3:4], in_=rcp[:, 3:4]), R=[Bfin], W=[Bfin])
                    S.op("dve", lambda ti=ti, h=h: nc.vector.scalar_tensor_tensor(out=on_tm[:, ti, h * 128:(h + 1) * 128], in0=o2[:, :], scalar=rcp[:, 3:4], in1=gsubb[:, :],
                                                                                op0=ALU.mult, op1=ALU.mult), R=[Bfin, B_const], W=[B_on[ti]])
        S.barrier()
    if cfg.get("stop") == 20:
        with ExitStack() as ph:
            dbg = sb("dbg", [128, D], F32, ph)
            Bd = Buf()
            for ti in range(NOT):
                S.op("dve", lambda ti=ti: nc.vector.tensor_copy(out=dbg[:, :], in_=on_tm[:, ti, :]), R=[B_on[ti]], W=[Bd])
                S.dma("sp", lambda e, ti=ti: e.dma_start(out=y_own[ti * 128:(ti + 1) * 128, :], in_=dbg[:, :]), R=[Bd], is_out=True)
            S.barrier()
        S.finish(); es.close(); return nc

    if cfg.get("stop") == 10:
        S.finish(); es.close(); return nc
    NOS = TO + NS
    es2 = ExitStack()
    hsT = sb("hsT", [128, KC, NOS], BF16, es2)
    B_hsT = [Buf() for _ in range(KC)]
    with ExitStack() as ph:
        PW = min(512, TP)
        NPC = TF // PW
        onesb = sb("c_ones", [128, 1], F32, ph)
        wga = sb("c_wga", [128, 8, 128], BF16, ph)
        wgx = sb("c_wgx", [128, 8, 128], BF16, ph)
        Bwg = Buf()
        S.op("dve", lambda: nc.vector.memset(onesb[:], 1.0), W=[Bwg])
        for n in range(8):
            S.dma("pool", lambda e, n=n: e.dma_start(out=wga[:, n, :], in_=w_ga[n]), W=[Bwg])
            S.dma("pool", lambda e, n=n: e.dma_start(out=wgx[:, n, :], in_=w_gx[n]), W=[Bwg])
        scT = sb("c_scT", [128, KC, 12], F32, ph)
        shT = sb("c_shT", [128, KC, 4], F32, ph)
        Bst = Buf()
        with ExitStack() as ph0:
            sc_tm = sb("c_sctm", [12, D], F32, ph0)
            sh_tm = sb("c_shtm", [4, D], F32, ph0)
            S.dma("sp", lambda e: e.dma_start(out=sc_tm[:], in_=sconv), W=[Bst])
            S.dma("sp", lambda e: e.dma_start(out=sh_tm[:], in_=sh), W=[Bst])
            pss = pst("c_pss", [128, 16], F32, ph0)
            Bpss = PBuf()
            for kc in range(KC):
                S.mm([lambda kc=kc: nc.tensor.transpose(out=pss[:, 0:12], in_=sc_tm[:12, kc * 128:(kc + 1) * 128], identity=identf[:12, :12])], R=[Bst, B_const], W=[Bpss])
                S.op("dve", lambda kc=kc: nc.vector.tensor_copy(out=scT[:, kc, :], in_=pss[:, 0:12]), R=[Bpss], W=[Bst])
                S.mm([lambda kc=kc: nc.tensor.transpose(out=pss[:, 12:16], in_=sh_tm[:4, kc * 128:(kc + 1) * 128], identity=identf[:4, :4])], R=[Bst, B_const], W=[Bpss])
                S.op("dve", lambda kc=kc: nc.vector.tensor_copy(out=shT[:, kc, :], in_=pss[:, 12:16]), R=[Bpss], W=[Bst])
            S.barrier()
        convT = sb("c_convT", [128, KC, 3], F32, ph)
        hlT = sb("c_hlT", [128, KC, 1], F32, ph)
        convsT = sb("c_convsT", [128, KC, 12], F32, ph)
        hsl = sb("c_hsl", [128, KC, 4], F32, ph)
        Bcol = Buf()
        wx = [sb(f"c_wx{i}", [128, KC, 128], BF16, ph) for i in range(2)]
        Bwx = [Buf(), Buf()]
        NB = 2
        xp = [sb(f"c_xp{i}", [128, 3 + PW], F32, ph) for i in range(NB)]
        xc = [sb(f"c_xc{i}", [128, PW], F32, ph) for i in range(NB)]
        xcb = [sb(f"c_xcb{i}", [128, PW], BF16, ph) for i in range(NB)]
        rr = [sb(f"c_r{i}", [128, PW], F32, ph) for i in range(NB)]
        ig = [sb(f"c_ig{i}", [128, PW], F32, ph) for i in range(NB)]
        am = [sb(f"c_am{i}", [128, PW], F32, ph) for i in range(NB)]
        aa = [sb(f"c_aa{i}", [128, PW], F32, ph) for i in range(NB)]
        hh_ = [sb(f"c_h{i}", [128, PW], F32, ph) for i in range(NB)]
        carry = sb("c_carry", [128, 1], F32, ph)
        Bxp = [Buf() for _ in range(NB)]; Bxc = [Buf() for _ in range(NB)]; Bxcb = [Buf() for _ in range(NB)]
        Br = [Buf() for _ in range(NB)]; Big = [Buf() for _ in range(NB)]; Bam = [Buf() for _ in range(NB)]
        Baa = [Buf() for _ in range(NB)]; Bh = [Buf() for _ in range(NB)]; Bcarry = Buf()
        psx = [pst(f"c_psx{i}", [128, 512], F32, ph) for i in range(2)]
        Bpsx = [PBuf() for _ in range(2)]
        psg = [pst(f"c_psg{i}", [128, 512], F32, ph) for i in range(4)]
        Bpsg = [PBuf() for _ in range(4)]
        xps = sb("c_xps", [128, 4, 11], F32, ph)
        xcs = sb("c_xcs", [128, 4, 8], F32, ph)
        xcsb = sb("c_xcsb", [128, 32], BF16, ph)
        rs_ = sb("c_rs", [128, 32], F32, ph)
        igs = sb("c_igs", [128, 32], F32, ph)
        ams = sb("c_ams", [128, 32], F32, ph)
        aas = sb("c_aas", [128, 32], F32, ph)
        hss = sb("c_hss", [128, 4, 8], F32, ph)
        Bsm = Buf()
        gcnt = 0
        xcnt = 0
        it = 0

        def lru_core(j, xp_ap3, xc_t, xcb_t, r_t, ig_t, am_t, aa_t, W_, Bxp_, Bxc_, Bxcb_, Br_, Big_, Bam_, Baa_, shape3=None):
            nonlocal gcnt
            S.op("pool", lambda: nc.gpsimd.tensor_scalar(out=xc_t, in0=xp_ap3(0), scalar1=wcT[:, 0, j:j + 1], scalar2=vT[:, 1, j:j + 1],
                                                          op0=ALU.mult, op1=ALU.add), R=[Bxp_, B_const], W=[Bxc_])
            for jj in range(1, 4):
                S.op("dve", lambda jj=jj: nc.vector.scalar_tensor_tensor(out=xc_t, in0=xp_ap3(jj), scalar=wcT[:, jj, j:j + 1], in1=xc_t,
                                                                            op0=ALU.mult, op1=ALU.add), R=[Bxp_, Bxc_], W=[Bxc_])
            xc2 = xc_t if shape3 is None else xc_t.rearrange("p a b -> p (a b)")
            S.op("pool", lambda: nc.gpsimd.tensor_copy(out=xcb_t, in_=xc2), R=[Bxc_], W=[Bxcb_])
            c0 = 0
            while c0 < W_:
                cw = min(512, W_ - c0)
                g0 = gcnt % 4; g1 = (gcnt + 1) % 4; gcnt += 2
                S.mm([lambda g0=g0, c0=c0, cw=cw: nc.tensor.matmul(psg[g0][:, :cw], lhsT=wga[:, j, :], rhs=xcb_t[:, c0:c0 + cw], start=True, stop=True)],
                     R=[Bwg, Bxcb_], W=[Bpsg[g0]])
                S.mm([lambda g1=g1, c0=c0, cw=cw: nc.tensor.matmul(psg[g1][:, :cw], lhsT=wgx[:, j, :], rhs=xcb_t[:, c0:c0 + cw], start=True, stop=True)],
                     R=[Bwg, Bxcb_], W=[Bpsg[g1]])
                S.op("act", lambda g0=g0, c0=c0, cw=cw: nc.scalar.activation(out=r_t[:, c0:c0 + cw], in_=psg[g0][:, :cw], func=AF.Sigmoid, bias=vT[:, 2, j:j + 1]),
                     R=[Bpsg[g0], B_const], W=[Br_])
                S.op("act", lambda g1=g1, c0=c0, cw=cw: nc.scalar.activation(out=ig_t[:, c0:c0 + cw], in_=psg[g1][:, :cw], func=AF.Sigmoid, bias=vT[:, 3, j:j + 1]),
                     R=[Bpsg[g1], B_const], W=[Big_])
                c0 += cw
            S.op("act", lambda: nc.scalar.activation(out=aa_t, in_=r_t, func=AF.Exp, scale=clam[:, j:j + 1]), R=[Br_, B_const], W=[Baa_])
            S.op("act", lambda: nc.scalar.activation(out=am_t, in_=r_t, func=AF.Exp, scale=clam2[:, j:j + 1]), R=[Br_, B_const], W=[Bam_])
            S.op("act", lambda: nc.scalar.activation(out=am_t, in_=am_t, func=AF.Sqrt, scale=-1.0, bias=onesb[:, 0:1]), R=[Bam_, Bwg], W=[Bam_])
            S.op("dve", lambda: nc.vector.tensor_tensor(out=ig_t, in0=ig_t, in1=xc2, op=ALU.mult), R=[Big_, Bxc_], W=[Big_])
            S.op("dve", lambda: nc.vector.tensor_tensor(out=ig_t, in0=ig_t, in1=am_t, op=ALU.mult), R=[Big_, Bam_], W=[Big_])

        for j in range(KC):
            wi = j % 2
            for kc in range(KC):
                S.dma("pool", lambda e, wi=wi, kc=kc, j=j: e.dma_start(out=wx[wi][:, kc, :], in_=w_in[kc * 128:(kc + 1) * 128, 3 * D + j * 128:3 * D + (j + 1) * 128]), W=[Bwx[wi]])
            for pc in range(NPC):
                b = it % NB
                it += 1
                col0 = pc * PW
                if pc == 0:
                    S.op("pool", lambda b=b: nc.gpsimd.memset(xp[b][:, 0:3], 0.0), W=[Bxp[b]])
                else:
                    pb = (it - 2) % NB
                    S.op("pool", lambda b=b, pb=pb: nc.gpsimd.tensor_copy(out=xp[b][:, 0:3], in_=xp[pb][:, PW:PW + 3]), R=[Bxp[pb]], W=[Bxp[b]])
                c0 = 0
                while c0 < PW:
                    x_ = xcnt % 2; xcnt += 1
                    tiles = sorted(set((col0 + c0 + o) // 128 for o in range(0, 512, 128)))
                    S.mm([lambda kc=kc, x_=x_, c0=c0, col0=col0, wi=wi: nc.tensor.matmul(psx[x_][:, :], lhsT=wx[wi][:, kc, :], rhs=uT[:, kc, col0 + c0:col0 + c0 + 512],
                                                                                       start=(kc == 0), stop=(kc == KC - 1)) for kc in range(KC)],
                         R=[Bwx[wi]] + [B_uT[t] for t in tiles], W=[Bpsx[x_]])
                    S.op("act", lambda x_=x_, b=b, c0=c0: nc.scalar.copy(out=xp[b][:, 3 + c0:3 + c0 + 512], in_=psx[x_][:, :]), R=[Bpsx[x_]], W=[Bxp[b]])
                    c0 += 512
                lru_core(j, lambda jj, b=b: xp[b][:, jj:jj + PW], xc[b][:, :], xcb[b][:, :], rr[b][:, :], ig[b][:, :], am[b][:, :], aa[b][:, :], PW,
                         Bxp[b], Bxc[b], Bxcb[b], Br[b], Big[b], Bam[b], Baa[b])
                if pc == 0:
                    init = 0.0
                    Rc = []
                else:
                    pb = (it - 2) % NB
                    if col0 == TP:
                        S.op("dve", lambda pb=pb: nc.vector.tensor_tensor(out=carry[:], in0=hh_[pb][:, PW - 1:PW], in1=flg[:], op=ALU.mult), R=[Bh[pb], B_const], W=[Bcarry])
                        init = carry[:, 0:1]
                        Rc = [Bcarry]
                    else:
                        init = hh_[pb][:, PW - 1:PW]
                        Rc = [Bh[pb]]
                S.op("dve", lambda b=b, init=init: nc.vector.tensor_tensor_scan(out=hh_[b][:, :], data0=aa[b][:, :], data1=ig[b][:, :], initial=init,
                                                                                 op0=ALU.mult, op1=ALU.add), R=[Baa[b], Big[b]] + Rc, W=[Bh[b]])
                if col0 >= TP:
                    o0 = col0 - TP
                    S.op("pool", lambda b=b, o0=o0, j=j: nc.gpsimd.tensor_copy(out=hsT[:, j, o0:o0 + PW], in_=hh_[b][:, :]), R=[Bh[b]], W=[B_hsT[j]])
                if pc == NPC - 1:
                    S.op("pool", lambda b=b, j=j: nc.gpsimd.tensor_copy(out=convT[:, j, :], in_=xp[b][:, PW:PW + 3]), R=[Bxp[b]], W=[Bcol])
                    S.op("pool", lambda b=b, j=j: nc.gpsimd.tensor_copy(out=hlT[:, j, :], in_=hh_[b][:, PW - 1:PW]), R=[Bh[b]], W=[Bcol])
            x_ = xcnt % 2; xcnt += 1
            S.mm([lambda kc=kc, x_=x_, wi=wi: nc.tensor.matmul(psx[x_][:, :NS], lhsT=wx[wi][:, kc, :], rhs=uT[:, kc, TF:TF + NS],
                                                              start=(kc == 0), stop=(kc == KC - 1)) for kc in range(KC)],
                 R=[Bwx[wi], B_uT[NTT]], W=[Bpsx[x_]])
            S.op("act", lambda x_=x_: nc.scalar.copy(out=xps[:, :, 3:11], in_=psx[x_][:, :NS].rearrange("p (a b) -> p a b", a=4)), R=[Bpsx[x_]], W=[Bsm])
            S.op("pool", lambda j=j: nc.gpsimd.tensor_copy(out=xps[:, :, 0:3], in_=scT[:, j, :].rearrange("p (a b) -> p a b", a=4)), R=[Bst], W=[Bsm])
            lru_core(j, lambda jj: xps[:, :, jj:jj + 8], xcs[:, :, :], xcsb[:, :], rs_[:, :], igs[:, :], ams[:, :], aas[:, :], NS,
                     Bsm, Bsm, Bsm, Bsm, Bsm, Bsm, Bsm, shape3=True)
            for sq in range(4):
                S.op("dve", lambda sq=sq, j=j: nc.vector.tensor_tensor_scan(out=hss[:, sq, :], data0=aas[:, sq * 8:(sq + 1) * 8], data1=igs[:, sq * 8:(sq + 1) * 8],
                                                                              initial=shT[:, j, sq:sq + 1], op0=ALU.mult, op1=ALU.add), R=[Bsm, Bst], W=[Bsm])
            S.op("pool", lambda j=j: nc.gpsimd.tensor_copy(out=hsT[:, j, TO:TO + NS], in_=hss[:].rearrange("p a b -> p (a b)")), R=[Bsm], W=[B_hsT[j]])
            S.op("pool", lambda j=j: nc.gpsimd.tensor_copy(out=convsT[:, j, :].rearrange("p (a b) -> p a b", a=4), in_=xps[:, :, 8:11]), R=[Bsm], W=[Bcol])
            S.op("pool", lambda j=j: nc.gpsimd.tensor_copy(out=hsl[:, j, :], in_=hss[:, :, 7]), R=[Bsm], W=[Bcol])
        otm = sb("c_otm", [12, D], F32, ph)
        Botm = Buf()
        pso = pst("c_pso", [12, 512], F32, ph)
        Bpso = PBuf()
        for (src, n, k, dst) in ((convT, 3, 0, conv_p), (hlT, 1, 1, h_p), (convsT, 12, 2, conv_s), (hsl, 4, 3, h_s)):
            for half in range(2):
                S.mm([lambda kc=kc, src=src, n=n, half=half: nc.tensor.transpose(out=pso[:n, (kc % 4) * 128:(kc % 4 + 1) * 128], in_=src[:, kc, :], identity=identf[:, :])
                      for kc in range(half * 4, half * 4 + 4)], R=[Bcol, B_const], W=[Bpso])
                S.op("dve", lambda n=n, k=k, half=half: nc.vector.tensor_copy(out=otm[:n, half * 512:(half + 1) * 512], in_=pso[:n, :]), R=[Bpso], W=[Botm])
            S.dma("sp", lambda e, n=n, k=k, dst=dst: e.dma_start(out=dst, in_=otm[:n, :]), R=[Botm], is_out=True)
        S.barrier()

    if cfg.get("stop") == 12:
        S.finish(); es.close(); return nc
    S.dma("pool", lambda e: e.dma_start(out=on_tm[:NS, NOT, :], in_=on_s_in), W=[B_on[NOT]])

    def col_chunks():
        c0 = 0
        while c0 < NOS:
            cw = min(512, NOS - c0)
            tiles = sorted(set(min((TP + c0 + o) // 128, NTT) for o in range(0, cw, 128)))
            yield c0, cw, tiles
            c0 += cw

    with ExitStack() as ph:
        wy = [sb(f"d1_wy{i}", [128, KC, 128], BF16, ph) for i in range(2)]
        Bwy = [Buf(), Buf()]
        ps = [pst(f"d1_ps{i}", [128, 512], F32, ph) for i in range(4)]
        Bps = [PBuf() for _ in range(4)]
        NBF = 2
        xs_ = [sb(f"d1_x{i}", [128, 512], F32, ph) for i in range(NBF)]
        t1 = [sb(f"d1_t{i}", [128, 512], F32, ph) for i in range(NBF)]
        sg = [sb(f"d1_s{i}", [128, 512], F32, ph) for i in range(NBF)]
        Bx = [Buf() for _ in range(NBF)]; Bt = [Buf() for _ in range(NBF)]; Bsg = [Buf() for _ in range(NBF)]
        cnt = 0
        for n in range(KC):
            wi = n % 2
            for kc in range(KC):
                S.dma("pool", lambda e, wi=wi, kc=kc, n=n: e.dma_start(out=wy[wi][:, kc, :], in_=w_in[kc * 128:(kc + 1) * 128, 4 * D + n * 128:4 * D + (n + 1) * 128]), W=[Bwy[wi]])
            for c0, cw, tiles in col_chunks():
                p = cnt % 4; b = cnt % NBF; cnt += 1
                S.mm([lambda kc=kc, p=p, c0=c0, cw=cw, wi=wi: nc.tensor.matmul(ps[p][:, :cw], lhsT=wy[wi][:, kc, :], rhs=uT[:, kc, TP + c0:TP + c0 + cw],
                                                                              start=(kc == 0), stop=(kc == KC - 1)) for kc in range(KC)],
                     R=[Bwy[wi]] + [B_uT[t] for t in tiles], W=[Bps[p]])
                S.op("act", lambda p=p, b=b, cw=cw: nc.scalar.copy(out=xs_[b][:, :cw], in_=ps[p][:, :cw]), R=[Bps[p]], W=[Bx[b]])
                S.op("pool", lambda b=b, cw=cw: nc.gpsimd.tensor_tensor(out=t1[b][:, :cw], in0=xs_[b][:, :cw], in1=xs_[b][:, :cw], op=ALU.mult), R=[Bx[b]], W=[Bt[b]])
                S.op("pool", lambda b=b, cw=cw: nc.gpsimd.tensor_scalar(out=t1[b][:, :cw], in0=t1[b][:, :cw], scalar1=0.044715, scalar2=1.0, op0=ALU.mult, op1=ALU.add), R=[Bt[b]], W=[Bt[b]])
                S.op("pool", lambda b=b, cw=cw: nc.gpsimd.tensor_tensor(out=t1[b][:, :cw], in0=t1[b][:, :cw], in1=xs_[b][:, :cw], op=ALU.mult), R=[Bt[b], Bx[b]], W=[Bt[b]])
                S.op("act", lambda b=b, cw=cw: nc.scalar.activation(out=sg[b][:, :cw], in_=t1[b][:, :cw], func=AF.Sigmoid, scale=1.5957691216057308), R=[Bt[b]], W=[Bsg[b]])
                S.op("dve", lambda b=b, cw=cw: nc.vector.tensor_tensor(out=sg[b][:, :cw], in0=sg[b][:, :cw], in1=xs_[b][:, :cw], op=ALU.mult), R=[Bsg[b], Bx[b]], W=[Bsg[b]])
                S.op("dve", lambda b=b, cw=cw, n=n, c0=c0: nc.vector.tensor_tensor(out=hsT[:, n, c0:c0 + cw], in0=hsT[:, n, c0:c0 + cw], in1=sg[b][:, :cw], op=ALU.mult),
                     R=[Bsg[b], B_hsT[n]], W=[B_hsT[n]])
        S.barrier()

    B_mT = [Buf() for _ in range(NOT + 1)]

    def mT_cols(kc_or_all, c0, cw):
        if c0 < TO:
            return uT[:, kc_or_all, c0:c0 + cw]
        return mTs[:, kc_or_all, 0:cw]

    with ExitStack() as ph:
        wr = [sb(f"d2_wr{i}", [128, KC, 128], BF16, ph) for i in range(2)]
        wgr = [sb(f"d2_wgr{i}", [128, KC, 128], BF16, ph) for i in range(2)]
        Bw = [Buf(), Buf()]
        ps = [pst(f"d2_ps{i}", [128, 512], F32, ph) for i in range(4)]
        Bps = [PBuf() for _ in range(4)]
        sgr = [sb(f"d2_sgr{i}", [128, 512], F32, ph) for i in range(2)]
        Bsgr = [Buf(), Buf()]
        cnt = 0
        for m in range(KC):
            wi = m % 2
            for kc in range(KC):
                r0, r1 = kc * 128, (kc + 1) * 128
                S.dma("pool", lambda e, wi=wi, kc=kc, m=m, r0=r0, r1=r1: e.dma_start(out=wr[wi][:, kc, :], in_=w_rec[r0:r1, m * 128:(m + 1) * 128]), W=[Bw[wi]])
                S.dma("pool", lambda e, wi=wi, kc=kc, m=m, r0=r0, r1=r1: e.dma_start(out=wgr[wi][:, kc, :], in_=w_in[r0:r1, 6 * D + m * 128:6 * D + (m + 1) * 128]), W=[Bw[wi]])
            for c0, cw, tiles in col_chunks():
                b = cnt % 2; cnt += 1
                pA, pB = (2 * b, 2 * b + 1)
                otiles = sorted(set(min((c0 + o) // 128, NOT) for o in range(0, cw, 128)))
                S.mm([lambda kc=kc: nc.tensor.matmul(ps[pA][:, :cw], lhsT=wr[wi][:, kc, :], rhs=hsT[:, kc, c0:c0 + cw], start=(kc == 0), stop=(kc == KC - 1)) for kc in range(KC)],
                     R=[Bw[wi]] + B_hsT, W=[Bps[pA]])
                S.mm([lambda kc=kc: nc.tensor.matmul(ps[pB][:, :cw], lhsT=wgr[wi][:, kc, :], rhs=uT[:, kc, TP + c0:TP + c0 + cw], start=(kc == 0), stop=(kc == KC - 1)) for kc in range(KC)],
                     R=[Bw[wi]] + [B_uT[t] for t in tiles], W=[Bps[pB]])
                S.op("act", lambda: nc.scalar.activation(out=sgr[b][:, :cw], in_=ps[pB][:, :cw], func=AF.Sigmoid), R=[Bps[pB]], W=[Bsgr[b]])
                S.op("dve", lambda: nc.vector.tensor_tensor(out=mT_cols(m, c0, cw), in0=ps[pA][:, :cw], in1=sgr[b][:, :cw], op=ALU.mult),
                     R=[Bps[pA], Bsgr[b]], W=[B_mT[t] for t in otiles])
        S.barrier()
    es2.close()

    with ExitStack() as ph:
        wa = sb("d2_wa", [128, KC, D], BF16, ph)
        wga_ = sb("d2_wga", [128, KC, D], BF16, ph)
        Bw = Buf()
        for kc in range(KC):
            for hf in range(2):
                S.dma("pool", lambda e, kc=kc, hf=hf: e.dma_start(out=wa[:, kc, hf * 512:(hf + 1) * 512], in_=w_attn[kc * 128:(kc + 1) * 128, hf * 512:(hf + 1) * 512]), W=[Bw])
                S.dma("pool", lambda e, kc=kc, hf=hf: e.dma_start(out=wga_[:, kc, hf * 512:(hf + 1) * 512],
                                                                   in_=w_in[kc * 128:(kc + 1) * 128, 5 * D + hf * 512:5 * D + (hf + 1) * 512]), W=[Bw])
        oaT = [sb(f"d2_oaT{i}", [128, KC, 512], BF16, ph) for i in range(2)]
        BoaT = [Buf(), Buf()]
        psT = [pst(f"d2_psT{i}", [128, KC, 128], BF16, ph) for i in range(2)]
        BpsT = [PBuf(), PBuf()]
        ps = [pst(f"d2b_ps{i}", [128, 512], F32, ph) for i in range(4)]
        Bps = [PBuf() for _ in range(4)]
        sga = [sb(f"d2_sga{i}", [128, 512], F32, ph) for i in range(2)]
        ta_ = [sb(f"d2_ta{i}", [128, 512], F32, ph) for i in range(2)]
        Bsga = [Buf(), Buf()]; Bta = [Buf(), Buf()]
        cnt = 0
        tcnt = 0
        for ci, (c0, cw, tiles) in enumerate(col_chunks()):
            ob = ci % 2
            otiles = sorted(set(min((c0 + o) // 128, NOT) for o in range(0, cw, 128)))
            for t in otiles:
                P = 128 if t < NOT else NS
                i = tcnt % 2; tcnt += 1
                lc = t * 128 - c0
                S.mm([lambda kc=kc, i=i, t=t, P=P: nc.tensor.transpose(out=psT[i][:, kc, :P], in_=on_tm[:P, t, kc * 128:(kc + 1) * 128], identity=identb[:P, :P])
                      for kc in range(KC)], R=[B_on[t], B_const], W=[BpsT[i]])
                S.op("act", lambda i=i, ob=ob, lc=lc, P=P: nc.scalar.copy(out=oaT[ob][:, :, lc:lc + P], in_=psT[i][:, :, :P]), R=[BpsT[i]], W=[BoaT[ob]])
            for m in range(KC):
                b = cnt % 2; cnt += 1
                pC, pD = (2 * b, 2 * b + 1)
                S.mm([lambda kc=kc: nc.tensor.matmul(ps[pC][:, :cw], lhsT=wa[:, kc, m * 128:(m + 1) * 128], rhs=oaT[ob][:, kc, :cw], start=(kc == 0), stop=(kc == KC - 1)) for kc in range(KC)],
                     R=[Bw, BoaT[ob]], W=[Bps[pC]])
                S.mm([lambda kc=kc: nc.tensor.matmul(ps[pD][:, :cw], lhsT=wga_[:, kc, m * 128:(m + 1) * 128], rhs=uT[:, kc, TP + c0:TP + c0 + cw], start=(kc == 0), stop=(kc == KC - 1)) for kc in range(KC)],
                     R=[Bw] + [B_uT[t] for t in tiles], W=[Bps[pD]])
                S.op("act", lambda: nc.scalar.activation(out=sga[b][:, :cw], in_=ps[pD][:, :cw], func=AF.Sigmoid), R=[Bps[pD]], W=[Bsga[b]])
                S.op("dve", lambda: nc.vector.tensor_tensor(out=ta_[b][:, :cw], in0=ps[pC][:, :cw], in1=sga[b][:, :cw], op=ALU.mult), R=[Bps[pC], Bsga[b]], W=[Bta[b]])
                S.op("pool", lambda: nc.gpsimd.tensor_tensor(out=mT_cols(m, c0, cw), in0=mT_cols(m, c0, cw), in1=ta_[b][:, :cw], op=ALU.add),
                     R=[Bta[b]] + [B_mT[t] for t in otiles], W=[B_mT[t] for t in otiles])
        S.barrier()
    es3.close()

    hres = sb("hres", [128, NOT + 1, D], F32)
    B_h = [Buf() for _ in range(NOT + 1)]

    def tile_rows(t):
        return 128 if t < NOT else NS

    def mT_tile(kc, t):
        if t < NOT:
            return uT[:, kc, t * 128:(t + 1) * 128]
        return mTs[:, kc, :]

    def norm_to_uT(ph_, t, gidx, xnb, Bxnb, psT, BpsT):
        P = tile_rows(t)
        i = t % 2
        rmsnorm_rows(ph_, hres[:P, t, :], P, gidx, xnb[i][:P, :], B_h[t], Bxnb[i], "d")
        transpose_to_uT(ph_, xnb[i], P, TP + t * 128, Bxnb[i], B_uT[OT0 + t], psT[i], BpsT[i])

    with ExitStack() as ph:
        wo = sb("d3_wo", [128, KC, D], BF16, ph)
        Bwo = Buf()
        for kc in range(KC):
            for hf in range(2):
                S.dma("pool", lambda e, kc=kc, hf=hf: e.dma_start(out=wo[:, kc, hf * 512:(hf + 1) * 512], in_=w_out[kc * 128:(kc + 1) * 128, hf * 512:(hf + 1) * 512]), W=[Bwo])
        xt = [sb(f"d3_x{i}", [128, D], F32, ph) for i in range(2)]
        Bxt = [Buf(), Buf()]
        xnb = [sb(f"d3_xn{i}", [128, D], BF16, ph) for i in range(2)]
        Bxnb = [Buf(), Buf()]
        psT = [pst(f"d3_psT{i}", [128, KC, 128], BF16, ph) for i in range(2)]
        BpsT = [PBuf(), PBuf()]
        ps = [pst(f"d3_ps{i}", [128, 512], F32, ph) for i in range(4)]
        Bps = [PBuf() for _ in range(4)]
        cnt = 0
        for t in range(NOT + 1):
            P = tile_rows(t)
            i = t % 2
            src = xf[TP + t * 128:TP + (t + 1) * 128, :] if t < NOT else xs
            S.dma("sp", lambda e, i=i, P=P, src=src: e.dma_start(out=xt[i][:P, :], in_=src), W=[Bxt[i]])
            for hf in range(2):
                p = cnt % 4; cnt += 1
                S.mm([lambda kc=kc, p=p, t=t, P=P, hf=hf: nc.tensor.matmul(ps[p][:P, :], lhsT=mT_tile(kc, t), rhs=wo[:, kc, hf * 512:(hf + 1) * 512],
                                                                          start=(kc == 0), stop=(kc == KC - 1)) for kc in range(KC)], R=[Bwo, B_mT[t]], W=[Bps[p]])
                S.op("dve", lambda p=p, t=t, P=P, hf=hf, i=i: nc.vector.tensor_tensor(out=hres[:P, t, hf * 512:(hf + 1) * 512], in0=ps[p][:P, :], in1=xt[i][:P, hf * 512:(hf + 1) * 512], op=ALU.add),
                     R=[Bps[p], Bxt[i]], W=[B_h[t]])
            norm_to_uT(ph, t, 1, xnb, Bxnb, psT, BpsT)
        S.barrier()

    B_act = [Buf() for _ in range(NOT + 1)]

    def act_cols(fl, c0, cw):
        if c0 < TO:
            return uT[:, fl, c0:c0 + cw]
        return actTs[:, fl, 0:cw]

    def act_tile(fl, t):
        if t < NOT:
            return uT[:, fl, t * 128:(t + 1) * 128]
        return actTs[:, fl, :]

    with ExitStack() as ph:
        wg = [sb(f"d4_wg{i}", [128, KC, 128], BF16, ph) for i in range(2)]
        wu = [sb(f"d4_wu{i}", [128, KC, 128], BF16, ph) for i in range(2)]
        Bwgu = [Buf(), Buf()]
        wd = sb("d4_wd", [128, 8, D], BF16, ph)
        Bwd = Buf()
        psg = [pst(f"d4_pg{i}", [128, 512], F32, ph) for i in range(2)]
        psu = [pst(f"d4_pu{i}", [128, 512], F32, ph) for i in range(2)]
        psd = [pst(f"d4_pd{i}", [128, 512], F32, ph) for i in range(4)]
        Bpsg = [PBuf(), PBuf()]; Bpsu = [PBuf(), PBuf()]; Bpsd = [PBuf() for _ in range(4)]
        sl = [sb(f"d4_sl{i}", [128, 512], F32, ph) for i in range(2)]
        Bsl = [Buf(), Buf()]
        parts = [(0, 8), (8, 15), (15, 22)]
        cnt = 0
        dcnt = 0
        wcnt = 0
        for (f0, f1) in parts:
            nf = f1 - f0
            for fl in range(nf):
                for hf in range(2):
                    S.dma("pool", lambda e, fl=fl, f0=f0, hf=hf: e.dma_start(out=wd[:, fl, hf * 512:(hf + 1) * 512],
                                                                              in_=w_fd[(f0 + fl) * 128:(f0 + fl + 1) * 128, hf * 512:(hf + 1) * 512]), W=[Bwd])
            for fl in range(nf):
                fc = f0 + fl
                wi = wcnt % 2; wcnt += 1
                for kc in range(KC):
                    S.dma("pool", lambda e, wi=wi, kc=kc, fc=fc: e.dma_start(out=wg[wi][:, kc, :], in_=w_fg[kc * 128:(kc + 1) * 128, fc * 128:(fc + 1) * 128]), W=[Bwgu[wi]])
                    S.dma("pool", lambda e, wi=wi, kc=kc, fc=fc: e.dma_start(out=wu[wi][:, kc, :], in_=w_fu[kc * 128:(kc + 1) * 128, fc * 128:(fc + 1) * 128]), W=[Bwgu[wi]])
                for c0, cw, tiles in col_chunks():
                    b = cnt % 2; cnt += 1
                    otiles = sorted(set(min((c0 + o) // 128, NOT) for o in range(0, cw, 128)))
                    S.mm([lambda kc=kc: nc.tensor.matmul(psg[b][:, :cw], lhsT=wg[wi][:, kc, :], rhs=uT[:, kc, TP + c0:TP + c0 + cw], start=(kc == 0), stop=(kc == KC - 1)) for kc in range(KC)],
                         R=[Bwgu[wi]] + [B_uT[t] for t in tiles], W=[Bpsg[b]])
                    S.mm([lambda kc=kc: nc.tensor.matmul(psu[b][:, :cw], lhsT=wu[wi][:, kc, :], rhs=uT[:, kc, TP + c0:TP + c0 + cw], start=(kc == 0), stop=(kc == KC - 1)) for kc in range(KC)],
                         R=[Bwgu[wi]] + [B_uT[t] for t in tiles], W=[Bpsu[b]])
                    S.op("act", lambda: nc.scalar.activation(out=sl[b][:, :cw], in_=psg[b][:, :cw], func=AF.Silu), R=[Bpsg[b]], W=[Bsl[b]])
                    S.op("dve", lambda: nc.vector.tensor_tensor(out=act_cols(fl, c0, cw), in0=psu[b][:, :cw], in1=sl[b][:, :cw], op=ALU.mult),
                         R=[Bpsu[b], Bsl[b]], W=[B_act[t] for t in otiles])
            for t in range(NOT + 1):
                P = tile_rows(t)
                for hf in range(2):
                    p = dcnt % 4; dcnt += 1
                    S.mm([lambda fl=fl, p=p, t=t, P=P, hf=hf: nc.tensor.matmul(psd[p][:P, :], lhsT=act_tile(fl, t), rhs=wd[:, fl, hf * 512:(hf + 1) * 512],
                                                                              start=(fl == 0), stop=(fl == nf - 1)) for fl in range(nf)], R=[Bwd, B_act[t]], W=[Bpsd[p]])
                    S.op("dve", lambda p=p, t=t, P=P, hf=hf: nc.vector.tensor_tensor(out=hres[:P, t, hf * 512:(hf + 1) * 512], in0=psd[p][:P, :], in1=hres[:P, t, hf * 512:(hf + 1) * 512], op=ALU.add),
                         R=[Bpsd[p], B_h[t]], W=[B_h[t]])
        S.barrier()

    with ExitStack() as ph:
        wpg = sb("d5_wpg", [128, KC, D], BF16, ph)
        wpp = sb("d5_wpp", [128, 2, D], BF16, ph)
        Bwp = Buf()
        for kc in range(KC):
            for hf in range(2):
                S.dma("pool", lambda e, kc=kc, hf=hf: e.dma_start(out=wpg[:, kc, hf * 512:(hf + 1) * 512], in_=w_pg[kc * 128:(kc + 1) * 128, hf * 512:(hf + 1) * 512]), W=[Bwp])
        for c in range(2):
            for hf in range(2):
                S.dma("pool", lambda e, c=c, hf=hf: e.dma_start(out=wpp[:, c, hf * 512:(hf + 1) * 512], in_=w_pp[c * 128:(c + 1) * 128, hf * 512:(hf + 1) * 512]), W=[Bwp])
        xnb = [sb(f"d5_xn{i}", [128, D], BF16, ph) for i in range(2)]
        Bxnb = [Buf(), Buf()]
        psT = [pst(f"d5_psT{i}", [128, KC, 128], BF16, ph) for i in range(2)]
        BpsT = [PBuf(), PBuf()]
        pt_ = [sb(f"d5_p{i}", [128, 256], F32, ph) for i in range(2)]
        ptb = [sb(f"d5_pb{i}", [128, 256], BF16, ph) for i in range(2)]
        pT = [sb(f"d5_pT{i}", [128, 2, 128], BF16, ph) for i in range(2)]
        Bpt = [Buf(), Buf()]; Bptb = [Buf(), Buf()]; BpT = [Buf(), Buf()]
        psg = [pst(f"d5_pg{i}", [128, 512], F32, ph) for i in range(2)]
        psp = [pst(f"d5_pp{i}", [128, 512], F32, ph) for i in range(2)]
        Bpsg = [PBuf(), PBuf()]; Bpsp = [PBuf(), PBuf()]
        sg = [sb(f"d5_sg{i}", [128, 512], F32, ph) for i in range(2)]
        Bsg = [Buf(), Buf()]
        yt = [sb(f"d5_y{i}", [128, D], F32, ph) for i in range(2)]
        Byt = [Buf(), Buf()]
        cnt = 0
        for t in range(NOT + 1):
            P = tile_rows(t)
            i = t % 2
            norm_to_uT(ph, t, 2, xnb, Bxnb, psT, BpsT)
            src = pown[t * 128:(t + 1) * 128, :] if t < NOT else psm
            S.dma("sp", lambda e, i=i, P=P, src=src: e.dma_start(out=pt_[i][:P, :], in_=src), W=[Bpt[i]])
            S.op("pool", lambda i=i, P=P: nc.gpsimd.tensor_copy(out=ptb[i][:P, :], in_=pt_[i][:P, :]), R=[Bpt[i]], W=[Bptb[i]])
            S.mm([lambda c=c, i=i, P=P: nc.tensor.transpose(out=psT[i][:, c, :P], in_=ptb[i][:P, c * 128:(c + 1) * 128], identity=identb[:P, :P]) for c in range(2)],
                 R=[Bptb[i], B_const], W=[BpsT[i]])
            S.op("act", lambda i=i, P=P: nc.scalar.copy(out=pT[i][:, :, :P], in_=psT[i][:, 0:2, :P]), R=[BpsT[i]], W=[BpT[i]])
            col0 = TP + t * 128
            for hf in range(2):
                b = cnt % 2; cnt += 1
                S.mm([lambda kc=kc, b=b, P=P, hf=hf, col0=col0: nc.tensor.matmul(psg[b][:P, :], lhsT=uT[:, kc, col0:col0 + P], rhs=wpg[:, kc, hf * 512:(hf + 1) * 512],
                                                                                start=(kc == 0), stop=(kc == KC - 1)) for kc in range(KC)], R=[Bwp, B_uT[OT0 + t]], W=[Bpsg[b]])
                S.mm([lambda c=c, b=b, P=P, hf=hf, i=i: nc.tensor.matmul(psp[b][:P, :], lhsT=pT[i][:, c, :P], rhs=wpp[:, c, hf * 512:(hf + 1) * 512],
                                                                        start=(c == 0), stop=(c == 1)) for c in range(2)], R=[Bwp, BpT[i]], W=[Bpsp[b]])
                S.op("act", lambda b=b, P=P: nc.scalar.activation(out=sg[b][:P, :], in_=psg[b][:P, :], func=AF.Sigmoid), R=[Bpsg[b]], W=[Bsg[b]])
                S.op("dve", lambda b=b, P=P: nc.vector.tensor_tensor(out=sg[b][:P, :], in0=psp[b][:P, :], in1=sg[b][:P, :], op=ALU.mult), R=[Bpsp[b], Bsg[b]], W=[Bsg[b]])
                S.op("pool", lambda b=b, P=P, t=t, hf=hf: nc.gpsimd.tensor_tensor(out=hres[:P, t, hf * 512:(hf + 1) * 512], in0=hres[:P, t, hf * 512:(hf + 1) * 512], in1=sg[b][:P, :], op=ALU.add),
                     R=[Bsg[b], B_h[t]], W=[B_h[t]])
            rmsnorm_rows(ph, hres[:P, t, :], P, 3, yt[i][:P, :], B_h[t], Byt[i], "y")
            dst = y_own[t * 128:(t + 1) * 128, :] if t < NOT else y_s
            S.dma("sp", lambda e, i=i, P=P, dst=dst: e.dma_start(out=dst, in_=yt[i][:P, :]), R=[Byt[i]], is_out=True)
        S.barrier()

    S.finish()
    es.close()
    return nc


def build_sample_attn(cfgb):
    NSEQ, NPG, NPHYS = cfgb["NSEQ"], cfgb["NPG"], cfgb["NPHYS"]
    NTK = NSEQ * 8
    NTL = (NTK + 127) // 128
    nc = bass.Bass("TRN2", target_bir_lowering=False)

    def din(name, shape, dt=F32):
        return nc.dram_tensor(name, list(shape), dt, kind="ExternalInput").ap()

    xs = din("xs", [NTK, D])
    ck = din("ck", [NPHYS * 128, D])
    cv = din("cv", [NPHYS * 128, D])
    ptab = din("ptab", [1, NSEQ * NPG], I32)
    iot = din("iot", [128, 1], I32)
    w_in = din("w_in", [D, 7 * D])
    gmix = din("gmix", [1, D])
    gsub = din("gsub", [1, 128])
    lamv = din("lamv", [4, 64])
    ident = din("ident", [128, 128])
    maskn = din("maskn", [8, 128])
    hmask = din("hmask", [128, 8])
    cpos = din("cpos", [128, 64])
    cneg = din("cneg", [128, 64])
    on_all = nc.dram_tensor("on_all", [NTK, D], F32, kind="ExternalOutput").ap()

    es = ExitStack()
    S = Sched(nc, es)

    def sb(name, shape, dt=F32, stack=None):
        return (stack or es).enter_context(nc.sbuf_tensor(name, list(shape), dt))

    def pst(name, shape, dt=F32, stack=None):
        return (stack or es).enter_context(nc.psum_tensor(name, list(shape), dt))

    identb = sb("identb", [128, 128], BF16)
    gb = sb("gb", [128, D], F32)
    gsubb = sb("gsubb", [128, 128], F32)
    epsb = sb("epsb", [128, 1], F32)
    lamb = sb("lamb", [128, 4, 64], F32)
    neglam = sb("neglam", [128, 1], F32)
    tmpc = sb("tmpc", [128, 4], F32)
    scr64 = sb("scr64", [128, 64], F32)
    maskb = sb("maskb", [8, 128], BF16)
    hm = sb("hm", [128, 8], F32)
    comb = sb("comb", [128, 64], F32)
    cng = sb("cng", [128, 64], F32)
    ones2 = sb("ones2", [128, 2], BF16)
    ptb = sb("ptb", [128, NSEQ * NPG], I32)
    iott = sb("iott", [128, 1], I32)
    idx = sb("idx", [128, NSEQ * NPG], I32)
    Bc = Buf()
    S.dma("pool", lambda e: e.dma_start(out=identb[:], in_=ident), W=[Bc])
    S.dma("pool", lambda e: e.dma_start(out=maskb[:], in_=maskn), W=[Bc])
    S.dma("sp", lambda e: e.dma_start(out=gb[:], in_=gmix.partition_broadcast(128)), W=[Bc])
    S.dma("sp", lambda e: e.dma_start(out=gsubb[:], in_=gsub.partition_broadcast(128)), W=[Bc])
    S.dma("sp", lambda e: e.dma_start(out=hm[:], in_=hmask), W=[Bc])
    S.dma("sp", lambda e: e.dma_start(out=comb[:], in_=cpos), W=[Bc])
    S.dma("sp", lambda e: e.dma_start(out=cng[:], in_=cneg), W=[Bc])
    S.dma("sp", lambda e: e.dma_start(out=ptb[:], in_=ptab.partition_broadcast(128)), W=[Bc])
    S.dma("sp", lambda e: e.dma_start(out=iott[:], in_=iot), W=[Bc])
    S.dma("sp", lambda e: e.dma_start(out=lamb[:].rearrange("p a b -> p (a b)"),
                                      in_=lamv.rearrange("a b -> (a b)").rearrange("(o n) -> o n", o=1).partition_broadcast(128)), W=[Bc])
    S.op("dve", lambda: nc.vector.memset(epsb[:], EPS), W=[Bc])
    S.op("dve", lambda: nc.vector.memset(ones2[:], 1.0), W=[Bc])
    S.op("dve", lambda: nc.vector.tensor_scalar(out=idx[:], in0=ptb[:], scalar1=128, scalar2=iott[:, 0:1], op0=ALU.mult, op1=ALU.add), R=[Bc], W=[Bc])
    S.op("dve", lambda: nc.vector.tensor_tensor(out=scr64[:], in0=lamb[:, 0, :], in1=lamb[:, 1, :], op=ALU.mult), R=[Bc], W=[Bc])
    S.op("dve", lambda: nc.vector.reduce_sum(out=tmpc[:, 0:1], in_=scr64[:], axis=AX.X), R=[Bc], W=[Bc])
    S.op("dve", lambda: nc.vector.tensor_tensor(out=scr64[:], in0=lamb[:, 2, :], in1=lamb[:, 3, :], op=ALU.mult), R=[Bc], W=[Bc])
    S.op("dve", lambda: nc.vector.reduce_sum(out=tmpc[:, 1:2], in_=scr64[:], axis=AX.X), R=[Bc], W=[Bc])
    S.op("act", lambda: nc.scalar.activation(out=tmpc[:, 2:4], in_=tmpc[:, 0:2], func=AF.Exp), R=[Bc], W=[Bc])
    S.op("dve", lambda: nc.vector.tensor_tensor(out=neglam[:], in0=tmpc[:, 3:4], in1=tmpc[:, 2:3], op=ALU.subtract), R=[Bc], W=[Bc])
    S.op("dve", lambda: nc.vector.tensor_scalar(out=neglam[:], in0=neglam[:], scalar1=-LAM_INIT, scalar2=None, op0=ALU.add), R=[Bc], W=[Bc])
    S.op("dve", lambda: nc.vector.tensor_scalar(out=gsubb[:], in0=gsubb[:], scalar1=1.0 - LAM_INIT, scalar2=None, op0=ALU.mult), R=[Bc], W=[Bc])
    S.op("dve", lambda: nc.vector.scalar_tensor_tensor(out=comb[:], in0=cng[:], scalar=neglam[:, 0:1], in1=comb[:], op0=ALU.mult, op1=ALU.add), R=[Bc], W=[Bc])
    S.barrier()

    uTs = sb("uTs", [128, KC, NTK], BF16)
    qTa = sb("qTa", [128, NH, NTK], BF16)
    kTa = sb("kTa", [128, NH, NTK], BF16)
    Vn = sb("Vn", [8, NSEQ, D], BF16)
    Bu = Buf(); Bq = Buf(); Bk = Buf(); Bvn = Buf()
    with ExitStack() as ph:
        xt = [sb(f"x{i}", [128, D], F32, ph) for i in range(2)]
        xn = [sb(f"xn{i}", [128, D], BF16, ph) for i in range(2)]
        junk = sb("junk", [128, D], F32, ph)
        ss = sb("ss", [128, 4], F32, ph)
        Bxt = [Buf(), Buf()]; Bxn = [Buf(), Buf()]; Bs = Buf()
        psT = [pst(f"psT{i}", [128, KC, 128], BF16, ph) for i in range(2)]
        BpsT = [PBuf(), PBuf()]
        for t in range(NTL):
            i = t % 2
            P = min(128, NTK - t * 128)
            S.dma("sp", lambda e, i=i, P=P, t=t: e.dma_start(out=xt[i][:P, :], in_=xs[t * 128:t * 128 + P, :]), W=[Bxt[i]])
            S.op("act", lambda i=i, P=P: nc.scalar.activation(out=junk[:P, :], in_=xt[i][:P, :], func=AF.Square, accum_out=ss[:P, 0:1]), R=[Bxt[i]], W=[Bs])
            S.op("act", lambda P=P: nc.scalar.activation(out=ss[:P, 1:2], in_=ss[:P, 0:1], func=AF.Sqrt, scale=1.0 / D, bias=epsb[:P, 0:1]), R=[Bs, Bc], W=[Bs])
            S.op("dve", lambda P=P: nc.vector.reciprocal(out=ss[:P, 2:3], in_=ss[:P, 1:2]), R=[Bs], W=[Bs])
            S.op("dve", lambda i=i, P=P: nc.vector.scalar_tensor_tensor(out=xn[i][:P, :], in0=xt[i][:P, :], scalar=ss[:P, 2:3], in1=gb[:P, :], op0=ALU.mult, op1=ALU.mult),
                 R=[Bxt[i], Bs, Bc], W=[Bxn[i]])
            S.mm([lambda kc=kc, i=i, P=P: nc.tensor.transpose(out=psT[i][:, kc, :P], in_=xn[i][:P, kc * 128:(kc + 1) * 128], identity=identb[:P, :P]) for kc in range(KC)],
                 R=[Bxn[i], Bc], W=[BpsT[i]])
            S.op("act", lambda i=i, P=P, t=t: nc.scalar.copy(out=uTs[:, :, t * 128:t * 128 + P], in_=psT[i][:, :, :P]), R=[BpsT[i]], W=[Bu])
        S.barrier()
    with ExitStack() as ph:
        wq = [sb(f"wq{i}", [128, KC, 128], BF16, ph) for i in range(2)]
        wk = [sb(f"wk{i}", [128, KC, 128], BF16, ph) for i in range(2)]
        Bw = [Buf(), Buf()]
        wv = sb("wv", [128, KC, D], BF16, ph)
        Bwv = Buf()
        ps = [pst(f"ps{i}", [128, 512], F32, ph) for i in range(4)]
        Bps = [PBuf() for _ in range(4)]
        pc = 0
        for kc in range(KC):
            for hf in range(2):
                S.dma("pool", lambda e, kc=kc, hf=hf: e.dma_start(out=wv[:, kc, hf * 512:(hf + 1) * 512], in_=w_in[kc * 128:(kc + 1) * 128, 2 * D + hf * 512:2 * D + (hf + 1) * 512]), W=[Bwv])
        for h in range(NH):
            wi = h % 2
            for kc in range(KC):
                S.dma("pool", lambda e, wi=wi, kc=kc, h=h: e.dma_start(out=wq[wi][:, kc, :], in_=w_in[kc * 128:(kc + 1) * 128, h * 128:(h + 1) * 128]), W=[Bw[wi]])
                S.dma("pool", lambda e, wi=wi, kc=kc, h=h: e.dma_start(out=wk[wi][:, kc, :], in_=w_in[kc * 128:(kc + 1) * 128, D + h * 128:D + (h + 1) * 128]), W=[Bw[wi]])
            p = pc % 4; pc += 1
            S.mm([lambda kc=kc, p=p, wi=wi: nc.tensor.matmul(ps[p][:, :NTK], lhsT=wq[wi][:, kc, :], rhs=uTs[:, kc, :], start=(kc == 0), stop=(kc == KC - 1)) for kc in range(KC)],
                 R=[Bw[wi], Bu], W=[Bps[p]])
            S.op("dve", lambda p=p, h=h: nc.vector.tensor_scalar(out=qTa[:, h, :], in0=ps[p][:, :NTK], scalar1=0.125, scalar2=None, op0=ALU.mult), R=[Bps[p]], W=[Bq])
            p = pc % 4; pc += 1
            S.mm([lambda kc=kc, p=p, wi=wi: nc.tensor.matmul(ps[p][:, :NTK], lhsT=wk[wi][:, kc, :], rhs=uTs[:, kc, :], start=(kc == 0), stop=(kc == KC - 1)) for kc in range(KC)],
                 R=[Bw[wi], Bu], W=[Bps[p]])
            S.op("act", lambda p=p, h=h: nc.scalar.copy(out=kTa[:, h, :], in_=ps[p][:, :NTK]), R=[Bps[p]], W=[Bk])
        for s_ in range(NSEQ):
            for hf in range(2):
                p = pc % 4; pc += 1
                S.mm([lambda kc=kc, p=p, s_=s_, hf=hf: nc.tensor.matmul(ps[p][:8, :], lhsT=uTs[:, kc, 8 * s_:8 * s_ + 8], rhs=wv[:, kc, hf * 512:(hf + 1) * 512],
                                                                        start=(kc == 0), stop=(kc == KC - 1)) for kc in range(KC)], R=[Bwv, Bu], W=[Bps[p]])
                S.op("dve" if hf else "act", (lambda p=p, s_=s_, hf=hf: nc.vector.tensor_copy(out=Vn[:8, s_, hf * 512:(hf + 1) * 512], in_=ps[p][:8, :])) if hf else
                     (lambda p=p, s_=s_, hf=hf: nc.scalar.copy(out=Vn[:8, s_, hf * 512:(hf + 1) * 512], in_=ps[p][:8, :])), R=[Bps[p]], W=[Bvn])
        S.barrier()
    with ExitStack() as ph:
        NB3 = 4
        Kp = [sb(f"Kp{i}", [128, D], BF16, ph) for i in range(NB3)]
        Vp = [sb(f"Vp{i}", [128, D], BF16, ph) for i in range(NB3)]
        BKp = [Buf() for _ in range(NB3)]; BVp = [Buf() for _ in range(NB3)]
        KT = [sb(f"KT{i}", [128, NH, 128], BF16, ph) for i in range(3)]
        BKT = [Buf(), Buf(), Buf()]
        PTs = [sb(f"PTs{i}", [128, 128], BF16, ph) for i in range(3)]
        BPTs = [Buf(), Buf(), Buf()]
        Qbd = [sb(f"Qbd{i}", [128, NH, 16], BF16, ph) for i in range(2)]
        BQbd = [Buf(), Buf()]
        for i in range(2):
            S.op("pool", lambda i=i: nc.gpsimd.memset(Qbd[i][:], 0.0), W=[BQbd[i]])
        o_n = sb("o_n", [128, 128], F32, ph)
        o_c = sb("o_c", [64, 128], F32, ph)
        o_f = [sb(f"o_f{i}", [64, 128], F32, ph) for i in range(2)]
        rc = sb("rc", [128, 4], F32, ph)
        junk = sb("junk2", [64, 128], F32, ph)
        Bfin = Buf(); Bof = [Buf(), Buf()]
        psT = [pst(f"psK{i}", [128, NH, 128], BF16, ph) for i in range(2)]
        BpsT = [PBuf(), PBuf()]
        pS = [pst(f"pS{i}", [128, 128], F32, ph) for i in range(2)]
        BpS = [PBuf(), PBuf()]
        acc = pst("acc", [128, 2, 512], F32, ph)
        Bacc = PBuf()
        accs = pst("accs", [128, 2], F32, ph)
        Baccs = PBuf()
        pcm = pst("pcm", [64, 128], F32, ph)
        Bpcm = PBuf()
        steps = [(s_, k) for s_ in range(NSEQ) for k in range(NPG + 1)]
        NST = len(steps)

        def bufs(g):
            return g % NB3, g % 2, g % 3

        def stage_T(g):
            s_, k = steps[g]
            b3, b2, bk = bufs(g)
            qb = s_ % 2
            if k == 0:
                S.op("pool", lambda: nc.gpsimd.tensor_copy(out=Qbd[qb][0:64, :, 0:8], in_=qTa[0:64, :, 8 * s_:8 * s_ + 8]), R=[Bq], W=[BQbd[qb]])
                S.op("pool", lambda: nc.gpsimd.tensor_copy(out=Qbd[qb][64:128, :, 8:16], in_=qTa[64:128, :, 8 * s_:8 * s_ + 8]), R=[Bq], W=[BQbd[qb]])
            if k == NPG:
                return
            col = s_ * NPG + k
            S.dma("pool", lambda e: e.indirect_dma_start(out=Kp[b3][:], out_offset=None, in_=ck,
                                                          in_offset=bass.IndirectOffsetOnAxis(ap=idx[:, col:col + 1], axis=0)), R=[Bc], W=[BKp[b3]])
            S.dma("pool", lambda e: e.indirect_dma_start(out=Vp[b3][:], out_offset=None, in_=cv,
                                                          in_offset=bass.IndirectOffsetOnAxis(ap=idx[:, col:col + 1], axis=0)), R=[Bc], W=[BVp[b3]])
            S.mm([lambda h=h: nc.tensor.transpose(out=psT[b2][:, h, :], in_=Kp[b3][:, h * 128:(h + 1) * 128], identity=identb[:, :]) for h in range(NH)],
                 R=[BKp[b3], Bc], W=[BpsT[b2]])
            if g % 2:
                S.op("act", lambda: nc.scalar.copy(out=KT[bk][:], in_=psT[b2][:]), R=[BpsT[b2]], W=[BKT[bk]])
            else:
                S.op("dve", lambda: nc.vector.tensor_copy(out=KT[bk][:], in_=psT[b2][:]), R=[BpsT[b2]], W=[BKT[bk]])

        def stage_S(g):
            s_, k = steps[g]
            b3, b2, bk = bufs(g)
            qb = s_ % 2
            if k < NPG:
                P = 128
                S.mm([lambda h=h: nc.tensor.matmul(pS[b2][:, h * 16:(h + 1) * 16], lhsT=KT[bk][:, h, :], rhs=Qbd[qb][:, h, :], start=True, stop=True) for h in range(NH)],
                     R=[BKT[bk], BQbd[qb]], W=[BpS[b2]])
            else:
                P = 8
                S.mm([lambda h=h: nc.tensor.matmul(pS[b2][:8, h * 16:(h + 1) * 16], lhsT=kTa[:, h, 8 * s_:8 * s_ + 8], rhs=Qbd[qb][:, h, :], start=True, stop=True)
                      for h in range(NH)], R=[Bk, BQbd[qb]], W=[BpS[b2]])
            S.op("act", lambda: nc.scalar.activation(out=PTs[bk][:P, :], in_=pS[b2][:P, :], func=AF.Exp), R=[BpS[b2]], W=[BPTs[bk]])
            if k == NPG:
                S.op("dve", lambda: nc.vector.tensor_tensor(out=PTs[bk][:8, :], in0=PTs[bk][:8, :], in1=maskb[:, :], op=ALU.mult), R=[BPTs[bk], Bc], W=[BPTs[bk]])

        def stage_PV(g):
            s_, k = steps[g]
            b3, b2, bk = bufs(g)
            new = (k == NPG)
            if new:
                P = 8
                vsrc = [Vn[:8, s_, 0:512], Vn[:8, s_, 512:1024]]
                Rv = [Bvn]
            else:
                P = 128
                vsrc = [Vp[b3][:, 0:512], Vp[b3][:, 512:1024]]
                Rv = [BVp[b3]]
            S.mm([lambda hf=hf: nc.tensor.matmul(acc[:, hf, :], lhsT=PTs[bk][:P, :], rhs=vsrc[hf], start=(k == 0), stop=new) for hf in range(2)]
                 + [lambda: nc.tensor.matmul(accs[:, :], lhsT=PTs[bk][:P, :], rhs=ones2[:P, :], start=(k == 0), stop=new)],
                 R=[BPTs[bk]] + Rv + [Bc], W=[Bacc, Baccs])
            if new:
                finalize(s_)

        def finalize(s_):
            S.op("dve", lambda: nc.vector.tensor_scalar(out=o_n[:, :], in0=acc[:, 0, 0:128], scalar1=hm[:, 0:1], scalar2=None, op0=ALU.mult), R=[Bacc, Bc], W=[Bfin])
            for h2 in range(1, NH):
                S.op("dve", lambda h2=h2: nc.vector.scalar_tensor_tensor(out=o_n[:, :], in0=acc[:, h2 // 4, (h2 % 4) * 128:(h2 % 4 + 1) * 128], scalar=hm[:, h2:h2 + 1], in1=o_n[:, :],
                                                                          op0=ALU.mult, op1=ALU.add), R=[Bacc, Bc, Bfin], W=[Bfin])
            S.op("dve", lambda: nc.vector.reciprocal(out=rc[:, 0:1], in_=accs[:, 0:1]), R=[Baccs], W=[Bfin])
            S.op("dve", lambda: nc.vector.tensor_scalar(out=o_n[:, :], in0=o_n[:, :], scalar1=rc[:, 0:1], scalar2=None, op0=ALU.mult), R=[Bfin], W=[Bfin])
            S.mm([lambda: nc.tensor.matmul(pcm[:, :], lhsT=comb[:, :], rhs=o_n[:, :], start=True, stop=True)], R=[Bfin, Bc], W=[Bpcm])
            fb = s_ % 2
            S.op("dve", lambda: nc.vector.tensor_copy(out=o_c[:, :], in_=pcm[:, :]), R=[Bpcm], W=[Bfin])
            S.op("act", lambda: nc.scalar.activation(out=junk[:, :], in_=o_c[:, :], func=AF.Square, accum_out=rc[:64, 1:2]), R=[Bfin], W=[Bfin])
            S.op("act", lambda: nc.scalar.activation(out=rc[:64, 2:3], in_=rc[:64, 1:2], func=AF.Sqrt, scale=1.0 / 128, bias=epsb[:64, 0:1]), R=[Bfin, Bc], W=[Bfin])
            S.op("dve", lambda: nc.vector.reciprocal(out=rc[:64, 2:3], in_=rc[:64, 2:3]), R=[Bfin], W=[Bfin])
            S.op("dve", lambda: nc.vector.scalar_tensor_tensor(out=o_f[fb][:, :], in0=o_c[:, :], scalar=rc[:64, 2:3], in1=gsubb[:64, :], op0=ALU.mult, op1=ALU.mult),
                 R=[Bfin, Bc], W=[Bof[fb]])
            for h in range(NH):
                S.dma("sp", lambda e, h=h: e.dma_start(out=on_all[8 * s_:8 * s_ + 8, h * 128:(h + 1) * 128], in_=o_f[fb][8 * h:8 * h + 8, :]), R=[Bof[fb]], is_out=True)

        for i in range(NST + 2):
            if i < NST:
                stage_T(i)
            if 0 <= i - 1 < NST:
                stage_S(i - 1)
            if 0 <= i - 2 < NST:
                stage_PV(i - 2)
        S.barrier()
    S.finish()
    es.close()
    return nc


def make_in_maps(inp, cfg):
    TP, TO, NS = cfg["TP"], cfg["TO"], cfg["NS"]
    f32 = np.float32
    xp = np.asarray(inp["x_prompt"], f32)
    B = xp.shape[0]
    vecs = np.stack([np.asarray(inp[k], f32).reshape(-1) for k in
                     ("g_mix", "g_ffn", "g_ple", "g_final", "lru_lambda", "b_conv", "b_gate_a", "b_gate_x")])
    lamv = np.stack([np.asarray(inp[k], f32).reshape(-1) for k in ("lambda_q1", "lambda_k1", "lambda_q2", "lambda_k2")])
    common = {
        "w_in": np.asarray(inp["w_in"], f32)[0], "w_attn": np.asarray(inp["w_attn_br"], f32)[0],
        "w_rec": np.asarray(inp["w_rec_br"], f32)[0], "w_out": np.asarray(inp["w_out"], f32)[0],
        "w_fg": np.asarray(inp["w_ffn_gate"], f32)[0], "w_fu": np.asarray(inp["w_ffn_up"], f32)[0],
        "w_fd": np.asarray(inp["w_ffn_down"], f32)[0], "w_pg": np.asarray(inp["w_ple_gate"], f32)[0],
        "w_pp": np.asarray(inp["w_ple_proj"], f32)[0], "w_ga": np.asarray(inp["w_gate_a"], f32)[0],
        "w_gx": np.asarray(inp["w_gate_x"], f32)[0], "vecs": vecs, "wconv": np.asarray(inp["w_conv"], f32)[0],
        "gsub": np.asarray(inp["g_subln"], f32).reshape(1, 128), "lamv": lamv,
        "ident": np.eye(128, dtype=f32), "tri": np.triu(np.ones((128, 128), f32)),
    }
    maps = []
    xsamp = np.asarray(inp["x_sample"], f32)
    psamp = np.asarray(inp["p_sample"], f32)[0]
    pprm = np.asarray(inp["p_prompt"], f32)[0]
    for c in range(8):
        b, half = c // 2, c % 2
        m = dict(common)
        xfull = np.zeros((TP + TO, D), f32)
        if half == 1:
            xfull[:TP] = xp[b, :TP]
        xfull[TP:] = xp[b, half * TO:(half + 1) * TO]
        m["xf"] = xfull
        m["pown"] = np.ascontiguousarray(pprm[b, half * TO:(half + 1) * TO])
        m["xs"] = np.ascontiguousarray(xsamp[4 * c:4 * c + 4].reshape(NS, D))
        m["psm"] = np.ascontiguousarray(psamp[4 * c:4 * c + 4].reshape(NS, 256))
        m["flag"] = np.full((128, 1), float(half), f32)
        m["sconv"] = np.ascontiguousarray(np.asarray(inp["state_conv"], f32)[0, 4 * c:4 * c + 4].reshape(12, D))
        m["sh"] = np.ascontiguousarray(np.asarray(inp["state_h"], f32)[0, 4 * c:4 * c + 4])
        maps.append(m)
    return maps


_CACHE = {}


def make_sample_map(inp):
    f32 = np.float32
    ck = np.asarray(inp["cache_k"], f32)
    cv = np.asarray(inp["cache_v"], f32)
    nphys = ck.shape[1]
    pt = np.asarray(inp["page_table"], np.int32)
    nseq, npg = pt.shape
    hmask = np.zeros((128, 8), f32)
    cpos = np.zeros((128, 64), f32)
    cneg = np.zeros((128, 64), f32)
    maskn = np.zeros((8, 128), f32)
    for h in range(8):
        for c in range(2):
            for q in range(8):
                r = h * 16 + c * 8 + q
                hmask[r, h] = 1.0
                (cpos if c == 0 else cneg)[r, h * 8 + q] = 1.0
                for j in range(8):
                    if j <= q:
                        maskn[j, r] = 1.0
    lamv = np.stack([np.asarray(inp[k], f32).reshape(-1) for k in ("lambda_q1", "lambda_k1", "lambda_q2", "lambda_k2")])
    m = {
        "xs": np.ascontiguousarray(np.asarray(inp["x_sample"], f32).reshape(nseq * 8, D)),
        "ck": ck.reshape(nphys * 128, D), "cv": cv.reshape(nphys * 128, D),
        "ptab": np.ascontiguousarray(pt.reshape(1, nseq * npg)),
        "iot": np.arange(128, dtype=np.int32).reshape(128, 1),
        "w_in": np.asarray(inp["w_in"], f32)[0], "gmix": np.asarray(inp["g_mix"], f32).reshape(1, D),
        "gsub": np.asarray(inp["g_subln"], f32).reshape(1, 128), "lamv": lamv,
        "ident": np.eye(128, dtype=f32), "maskn": maskn, "hmask": hmask, "cpos": cpos, "cneg": cneg,
    }
    return m, {"NSEQ": nseq, "NPG": npg, "NPHYS": nphys}


def kernel(**inputs):
    import os
    xp = inputs["x_prompt"]
    SEQ = xp.shape[1]
    cfg = {"TP": SEQ // 2, "TO": SEQ // 2, "NS": 32}
    if "KSTOP" in os.environ:
        cfg["stop"] = int(os.environ["KSTOP"])
    mb, cfgb = make_sample_map(inputs)
    keyb = ("B", cfgb["NSEQ"], cfgb["NPG"], cfgb["NPHYS"])
    if keyb not in _CACHE:
        _CACHE[keyb] = build_sample_attn(cfgb)
    resb = run_bass_kernel_spmd(_CACHE[keyb], [mb], core_ids=[0]).results
    on_all = np.asarray(resb[0]["on_all"], np.float32)
    key = ("A", SEQ, cfg.get("stop"))
    if key not in _CACHE:
        _CACHE[key] = build(cfg)
    nc = _CACHE[key]
    maps = make_in_maps(inputs, cfg)
    for c in range(8):
        maps[c]["on_s"] = np.ascontiguousarray(on_all[32 * c:32 * (c + 1)])
    res = run_bass_kernel_spmd(nc, maps, core_ids=list(range(8))).results
    B = xp.shape[0]
    TO = cfg["TO"]
    f32 = np.float32

    def own(name):
        o = np.zeros((B, SEQ, D), f32)
        for c in range(8):
            o[c // 2, (c % 2) * TO:(c % 2 + 1) * TO] = res[c][name]
        return o

    def samp(name, per):
        return np.concatenate([res[c][name].reshape(4, per, -1) for c in range(8)], axis=0)
    y_prompt = own("y_own")
    y_sample = samp("y_s", 8)
    k_prompt = own("k_own").reshape(1, B, SEQ, NH, 2, 64)
    v_prompt = own("v_own").reshape(1, B, SEQ, NH, 128)
    conv_prompt = np.stack([res[2 * b + 1]["conv_p"] for b in range(B)])[None]
    h_prompt = np.stack([res[2 * b + 1]["h_p"][0] for b in range(B)])[None]
    k_sample = samp("k_s", 8).reshape(1, 32, 8, NH, 2, 64)
    v_sample = samp("v_s", 8).reshape(1, 32, 8, NH, 128)
    conv_sample = samp("conv_s", 3)[None]
    h_sample = np.concatenate([res[c]["h_s"] for c in range(8)], axis=0)[None]
    return (y_prompt, y_sample, k_prompt, v_prompt, conv_prompt, h_prompt, k_sample, v_sample, conv_sample, h_sample)
```

```python
import math
from contextlib import ExitStack
import numpy as np
import concourse.bass as bass
import concourse.mybir as mybir
from concourse.bass_utils import run_bass_kernel_spmd

F32 = mybir.dt.float32
BF16 = mybir.dt.bfloat16
I32 = mybir.dt.int32
AF = mybir.ActivationFunctionType
ALU = mybir.AluOpType
AX = mybir.AxisListType

D = 1024
NH = 8
KC = 8
DFF = 2816
FC = 22
EPS = 1e-6
LAM_INIT = 0.8 - 0.6 * math.exp(-0.3 * 0)


class Buf:
    __slots__ = ("w", "r", "excl")

    def __init__(self, excl=False):
        self.w = None
        self.r = {}
        self.excl = excl


def PBuf():
    return Buf(True)


class Sched:
    def __init__(self, nc, es, nds=20):
        self.nc = nc
        self.sems = []
        self.E = {}
        for name, e in (("pe", nc.tensor), ("act", nc.scalar), ("dve", nc.vector),
                        ("pool", nc.gpsimd), ("sp", nc.sync)):
            sid = self._new_sem(es, name)
            self.E[name] = {"e": e, "sid": sid, "cnt": 0, "waited": {}}
        self.dq = {}
        for q in ("sp", "pool", "act"):
            self.dq[q] = {"sids": [self._new_sem(es, f"d{q}{i}") for i in range(nds)], "next": 0}
        self.dval = {}
        self.out_toks = []

    def _new_sem(self, es, name):
        s = es.enter_context(self.nc.semaphore(name))
        self.sems.append(s)
        return len(self.sems) - 1

    def _deps(self, R, W):
        deps = {}

        def add(t):
            if t is None:
                return
            if deps.get(t[0], 0) < t[1]:
                deps[t[0]] = t[1]
        for b in R:
            add(b.w)
            if b.excl:
                for s, v in b.r.items():
                    add((s, v))
        for b in W:
            add(b.w)
            for s, v in b.r.items():
                add((s, v))
        return deps

    def _emit_waits(self, en, deps, skip_own=False):
        E = self.E[en]
        for s, v in deps.items():
            if skip_own and s == E["sid"]:
                continue
            if E["waited"].get(s, 0) < v:
                E["e"].wait_ge(self.sems[s], v)
                E["waited"][s] = v

    def _record(self, tok, R, W):
        for b in R:
            if b.r.get(tok[0], 0) < tok[1]:
                b.r[tok[0]] = tok[1]
        for b in W:
            b.w = tok
            b.r = {}

    def op(self, en, fn, R=(), W=()):
        E = self.E[en]
        self._emit_waits(en, self._deps(R, W), skip_own=(en == "pe"))
        ins = fn()
        E["cnt"] += 1
        ins.then_inc(self.sems[E["sid"]], 1)
        self._record((E["sid"], E["cnt"]), R, W)

    def mm(self, fns, R=(), W=()):
        E = self.E["pe"]
        self._emit_waits("pe", self._deps(R, W), skip_own=True)
        ins = None
        for fn in fns:
            ins = fn()
        E["cnt"] += 1
        ins.then_inc(self.sems[E["sid"]], 1)
        self._record((E["sid"], E["cnt"]), R, W)

    def dma(self, q, fn, R=(), W=(), is_out=False):
        Q = self.dq[q]
        sid = Q["sids"][Q["next"]]
        Q["next"] = (Q["next"] + 1) % len(Q["sids"])
        deps = self._deps(R, W)
        prev = self.dval.get(sid, 0)
        if prev:
            if deps.get(sid, 0) < prev:
                deps[sid] = prev
        self._emit_waits(q, deps)
        ins = fn(self.E[q]["e"])
        val = prev + 16
        self.dval[sid] = val
        ins.then_inc(self.sems[sid], 16)
        tok = (sid, val)
        self._record(tok, R, W)
        if is_out:
            self.out_toks.append(tok)

    def barrier(self):
        toks = {}
        for en, E in self.E.items():
            if E["cnt"]:
                toks[E["sid"]] = E["cnt"]
        for sid, v in self.dval.items():
            toks[sid] = v
        for en in self.E:
            self._emit_waits(en, toks)

    def finish(self):
        toks = {}
        for t in self.out_toks:
            if toks.get(t[0], 0) < t[1]:
                toks[t[0]] = t[1]
        self._emit_waits("sp", toks)


def build(cfg):
    TP, TO, NS = cfg["TP"], cfg["TO"], cfg["NS"]
    TF = TP + TO
    NT = TF + NS
    NTT = TF // 128
    OT0 = TP // 128
    NOT = TO // 128
    NQG = TO // 512
    NOS = TO + NS
    nc = bass.Bass("TRN2", target_bir_lowering=False)

    def din(name, shape, dt=F32):
        return nc.dram_tensor(name, list(shape), dt, kind="ExternalInput").ap()

    def dout(name, shape, dt=F32):
        return nc.dram_tensor(name, list(shape), dt, kind="ExternalOutput").ap()

    xf = din("xf", [TF, D])
    xs = din("xs", [NS, D])
    pown = din("pown", [TO, 256])
    psm = din("psm", [NS, 256])
    flag = din("flag", [128, 1])
    sconv = din("sconv", [12, D])
    sh = din("sh", [4, D])
    w_in = din("w_in", [D, 7 * D])
    w_attn = din("w_attn", [D, D])
    w_rec = din("w_rec", [D, D])
    w_out = din("w_out", [D, D])
    w_fg = din("w_fg", [D, DFF])
    w_fu = din("w_fu", [D, DFF])
    w_fd = din("w_fd", [DFF, D])
    w_pg = din("w_pg", [D, D])
    w_pp = din("w_pp", [256, D])
    w_ga = din("w_ga", [8, 128, 128])
    w_gx = din("w_gx", [8, 128, 128])
    vecs = din("vecs", [8, D])
    wconv = din("wconv", [4, D])
    gsub = din("gsub", [1, 128])
    lamv = din("lamv", [4, 64])
    ident = din("ident", [128, 128])
    tri = din("tri", [128, 128])
    on_s_in = din("on_s", [NS, D])

    y_own = dout("y_own", [TO, D])
    y_s = dout("y_s", [NS, D])
    k_own = dout("k_own", [TO, D])
    v_own = dout("v_own", [TO, D])
    k_s = dout("k_s", [NS, D])
    v_s = dout("v_s", [NS, D])
    conv_p = dout("conv_p", [3, D])
    h_p = dout("h_p", [1, D])
    conv_s = dout("conv_s", [12, D])
    h_s = dout("h_s", [4, D])
    kT_d = nc.dram_tensor("kT_d", [NH, 128, TF], BF16, kind="Internal").ap()
    v_d = nc.dram_tensor("v_d", [NH, 128, NTT, 130], BF16, kind="Internal").ap()

    es = ExitStack()
    S = Sched(nc, es)

    def sb(name, shape, dt=F32, stack=None):
        return (stack or es).enter_context(nc.sbuf_tensor(name, list(shape), dt))

    def pst(name, shape, dt=F32, stack=None):
        return (stack or es).enter_context(nc.psum_tensor(name, list(shape), dt))

    identb = sb("identb", [128, 128], BF16)
    identf = sb("identf", [128, 128], F32)
    trib = sb("trib", [128, 128], BF16)
    gb = sb("gb", [128, 4, D], F32)
    gsubb = sb("gsubb", [128, 128], F32)
    vT = sb("vT", [128, 4, KC], F32)
    wcT = sb("wcT", [128, 4, KC], F32)
    clam = sb("clam", [128, KC], F32)
    clam2 = sb("clam2", [128, KC], F32)
    epsb = sb("epsb", [128, 1], F32)
    flg = sb("flg", [128, 1], F32)
    lamb = sb("lamb", [128, 4, 64], F32)
    neglam = sb("neglam", [128, 1], F32)
    tmpc = sb("tmpc", [128, 4], F32)
    B_const = Buf()
    ld = []
    S.dma("pool", lambda e: e.dma_start(out=identb[:], in_=ident), W=[B_const])
    S.dma("sp", lambda e: e.dma_start(out=identf[:], in_=ident), W=[B_const])
    S.dma("pool", lambda e: e.dma_start(out=trib[:], in_=tri), W=[B_const])
    for i in range(4):
        S.dma("sp", lambda e, i=i: e.dma_start(out=gb[:, i, :], in_=vecs[i:i + 1, :].partition_broadcast(128)), W=[B_const])
    S.dma("sp", lambda e: e.dma_start(out=gsubb[:], in_=gsub.partition_broadcast(128)), W=[B_const])
    with nc.allow_non_contiguous_dma(reason="tiny parameter vectors to feature-major"):
        for i in range(4):
            S.dma("sp", lambda e, i=i: e.dma_start(out=vT[:, i, :], in_=vecs[4 + i, :].rearrange("(c p) -> p c", p=128)), W=[B_const])
            S.dma("sp", lambda e, i=i: e.dma_start(out=wcT[:, i, :], in_=wconv[i, :].rearrange("(c p) -> p c", p=128)), W=[B_const])
    S.dma("sp", lambda e: e.dma_start(out=flg[:], in_=flag), W=[B_const])
    S.dma("sp", lambda e: e.dma_start(out=lamb[:].rearrange("p a b -> p (a b)"),
                                      in_=lamv.rearrange("a b -> (a b)").rearrange("(o n) -> o n", o=1).partition_broadcast(128)), W=[B_const])
    S.op("dve", lambda: nc.vector.memset(epsb[:], EPS), W=[B_const])
    scr64 = sb("scr64", [128, 64], F32)
    S.op("dve", lambda: nc.vector.tensor_tensor(out=scr64[:], in0=lamb[:, 0, :], in1=lamb[:, 1, :], op=ALU.mult), R=[B_const], W=[B_const])
    S.op("dve", lambda: nc.vector.reduce_sum(out=tmpc[:, 0:1], in_=scr64[:], axis=AX.X), R=[B_const], W=[B_const])
    S.op("dve", lambda: nc.vector.tensor_tensor(out=scr64[:], in0=lamb[:, 2, :], in1=lamb[:, 3, :], op=ALU.mult), R=[B_const], W=[B_const])
    S.op("dve", lambda: nc.vector.reduce_sum(out=tmpc[:, 1:2], in_=scr64[:], axis=AX.X), R=[B_const], W=[B_const])
    S.op("act", lambda: nc.scalar.activation(out=tmpc[:, 2:4], in_=tmpc[:, 0:2], func=AF.Exp), R=[B_const], W=[B_const])
    S.op("dve", lambda: nc.vector.tensor_tensor(out=neglam[:], in0=tmpc[:, 3:4], in1=tmpc[:, 2:3], op=ALU.subtract), R=[B_const], W=[B_const])
    S.op("dve", lambda: nc.vector.tensor_scalar(out=neglam[:], in0=neglam[:], scalar1=-LAM_INIT, scalar2=None, op0=ALU.add), R=[B_const], W=[B_const])
    S.op("dve", lambda: nc.vector.tensor_scalar(out=gsubb[:], in0=gsubb[:], scalar1=1.0 - LAM_INIT, scalar2=None, op0=ALU.mult), R=[B_const], W=[B_const])
    S.op("act", lambda: nc.scalar.activation(out=clam[:], in_=vT[:, 0, :], func=AF.Exp, scale=-1.0), R=[B_const], W=[B_const])
    S.op("dve", lambda: nc.vector.tensor_scalar(out=clam[:], in0=clam[:], scalar1=1.0, scalar2=None, op0=ALU.add), R=[B_const], W=[B_const])
    S.op("act", lambda: nc.scalar.activation(out=clam[:], in_=clam[:], func=AF.Ln), R=[B_const], W=[B_const])
    S.op("dve", lambda: nc.vector.tensor_scalar(out=clam2[:], in0=clam[:], scalar1=-16.0, scalar2=None, op0=ALU.mult), R=[B_const], W=[B_const])
    S.op("dve", lambda: nc.vector.tensor_scalar(out=clam[:], in0=clam[:], scalar1=-8.0, scalar2=None, op0=ALU.mult), R=[B_const], W=[B_const])
    S.barrier()
    if cfg.get("stop") == 0:
        S.finish(); es.close(); return nc

    uT = sb("uT", [128, KC, NT], BF16)
    B_uT = [Buf() for _ in range(NTT + 1)]

    def rmsnorm_rows(stk, x_ap, P, gidx, out_ap, Bx, Bo, tag):
        junk = rmsnorm_rows.junk
        ss = rmsnorm_rows.ss
        Bs = rmsnorm_rows.Bs
        S.op("act", lambda: nc.scalar.activation(out=junk[:P, :], in_=x_ap, func=AF.Square, accum_out=ss[:P, 0:1]), R=[Bx], W=[Bs])
        S.op("act", lambda: nc.scalar.activation(out=ss[:P, 1:2], in_=ss[:P, 0:1], func=AF.Sqrt, scale=1.0 / D, bias=epsb[:P, 0:1]), R=[Bs], W=[Bs])
        S.op("dve", lambda: nc.vector.reciprocal(out=ss[:P, 2:3], in_=ss[:P, 1:2]), R=[Bs], W=[Bs])
        S.op("dve", lambda: nc.vector.scalar_tensor_tensor(out=out_ap, in0=x_ap, scalar=ss[:P, 2:3], in1=gb[:P, gidx, :],
                                                             op0=ALU.mult, op1=ALU.mult), R=[Bx, Bs], W=[Bo])
    rmsnorm_rows.junk = sb("nrm_junk", [128, D], F32)
    rmsnorm_rows.ss = sb("nrm_ss", [128, 4], F32)
    rmsnorm_rows.Bs = Buf()

    def transpose_to_uT(stk, xn, P, col0, Bxn, Bu, psT, BpsT):
        S.mm([lambda kc=kc: nc.tensor.transpose(out=psT[:, kc, :P], in_=xn[:P, kc * 128:(kc + 1) * 128], identity=identb[:P, :P])
              for kc in range(KC)], R=[Bxn], W=[BpsT])
        S.op("act", lambda: nc.scalar.copy(out=uT[:, :, col0:col0 + P], in_=psT[:, :, :P]), R=[BpsT], W=[Bu])

    with ExitStack() as ph:
        xt = [sb(f"a0_x{i}", [128, D], F32, ph) for i in range(2)]
        Bxt = [Buf(), Buf()]
        xn = [sb(f"a0_xn{i}", [128, D], BF16, ph) for i in range(2)]
        Bxn = [Buf(), Buf()]
        psT = [pst(f"a0_ps{i}", [128, KC, 128], BF16, ph) for i in range(2)]
        BpsT = [PBuf(), PBuf()]
        for t in range(NTT + 1):
            i = t % 2
            P = 128 if t < NTT else NS
            src = xf[t * 128:(t + 1) * 128, :] if t < NTT else xs
            S.dma("sp", lambda e, i=i, P=P, src=src: e.dma_start(out=xt[i][:P, :], in_=src), W=[Bxt[i]])
            rmsnorm_rows(ph, xt[i][:P, :], P, 0, xn[i][:P, :], Bxt[i], Bxn[i], "a0")
            transpose_to_uT(ph, xn[i], P, t * 128, Bxn[i], B_uT[t], psT[i], BpsT[i])
        S.barrier()

    if cfg.get("stop") == 1:
        S.finish(); es.close(); return nc
    kTs = sb("kTs", [128, NH, NS], BF16)
    B_kTs = Buf()
    with ExitStack() as ph:
        wk = [sb(f"a1_wk{i}", [128, KC, 512], BF16, ph) for i in range(2)]
        Bwk = [Buf(), Buf()]
        wv = [sb(f"a1_wv{i}", [128, KC, 512], BF16, ph) for i in range(2)]
        Bwv = [Buf(), Buf()]
        kst = [sb(f"a1_kst{i}", [128, 512], BF16, ph) for i in range(2)]
        Bkst = [Buf(), Buf()]
        vst = [sb(f"a1_vst{i}", [128, 4, 130], BF16, ph) for i in range(2)]
        Bvst = [Buf(), Buf()]
        fst = [sb(f"a1_fst{i}", [128, 512], F32, ph) for i in range(4)]
        Bfst = [Buf() for _ in range(4)]
        ps = [pst(f"a1_ps{i}", [128, 512], F32, ph) for i in range(4)]
        Bps = [PBuf() for _ in range(4)]
        ones_c = sb("a1_ones", [128, 1], F32, ph)
        B1 = Buf()
        S.op("dve", lambda: nc.vector.memset(ones_c[:], 1.0), W=[B1])
        pi = 0
        fi = 0
        for hh in range(2):
            wi = hh % 2
            for kc in range(KC):
                S.dma("pool", lambda e, wi=wi, hh=hh, kc=kc: e.dma_start(
                    out=wk[wi][:, kc, :], in_=w_in[kc * 128:(kc + 1) * 128, D + hh * 512:D + (hh + 1) * 512]), W=[Bwk[wi]])
                S.dma("pool", lambda e, wi=wi, hh=hh, kc=kc: e.dma_start(
                    out=wv[wi][:, kc, :], in_=w_in[kc * 128:(kc + 1) * 128, 2 * D + hh * 512:2 * D + (hh + 1) * 512]), W=[Bwv[wi]])
            for hl in range(4 if cfg.get("stop") != 2 else 0):
                h = hh * 4 + hl
                c0 = 0
                while c0 < NT:
                    cw = min(512, NT - c0)
                    tiles = sorted(set([min(c // 128, NTT) for c in range(c0, c0 + cw, 128)]))
                    p = pi % 4
                    pi += 1
                    S.mm([lambda kc=kc, p=p, c0=c0, cw=cw, hl=hl, wi=wi: nc.tensor.matmul(
                        ps[p][:, :cw], lhsT=wk[wi][:, kc, hl * 128:(hl + 1) * 128], rhs=uT[:, kc, c0:c0 + cw],
                        start=(kc == 0), stop=(kc == KC - 1)) for kc in range(KC)],
                        R=[Bwk[wi]] + [B_uT[t] for t in tiles], W=[Bps[p]])
                    if c0 < TF:
                        ki = (pi) % 2
                        S.op("act", lambda ki=ki, p=p, cw=cw: nc.scalar.copy(out=kst[ki][:, :cw], in_=ps[p][:, :cw]), R=[Bps[p]], W=[Bkst[ki]])
                        if not cfg.get("noscr"):
                            S.dma("sp", lambda e, ki=ki, h=h, c0=c0, cw=cw: e.dma_start(out=kT_d[h, :, c0:c0 + cw], in_=kst[ki][:, :cw]), R=[Bkst[ki]])
                    else:
                        S.op("act", lambda p=p, cw=cw, h=h: nc.scalar.copy(out=kTs[:, h, :], in_=ps[p][:, :cw]), R=[Bps[p]], W=[B_kTs])
                    c0 += cw
            for t in range(NTT + 1 if cfg.get("stop") not in (2, 3) else 0):
                P = 128 if t < NTT else NS
                col0 = t * 128
                own = t >= OT0
                p = pi % 4
                pi += 1
                S.mm([lambda kc=kc, p=p, P=P, col0=col0, wi=wi: nc.tensor.matmul(
                    ps[p][:P, :], lhsT=uT[:, kc, col0:col0 + P], rhs=wv[wi][:, kc, :],
                    start=(kc == 0), stop=(kc == KC - 1)) for kc in range(KC)], R=[Bwv[wi], B_uT[t]], W=[Bps[p]])
                if t < NTT:
                    vi = t % 2
                    S.op("dve", lambda vi=vi, p=p: nc.vector.tensor_copy(out=vst[vi][:, :, 0:128], in_=ps[p][:].rearrange("p (h d) -> p h d", h=4)),
                         R=[Bps[p]], W=[Bvst[vi]])
                    onesrc = ones_c if own else flg
                    for hl in range(4):
                        S.op("dve", lambda vi=vi, hl=hl, onesrc=onesrc: nc.vector.tensor_copy(out=vst[vi][:, hl, 128:129], in_=onesrc[:, 0:1]),
                             R=[B1, B_const], W=[Bvst[vi]])
                    if not cfg.get("noscr"):
                        S.dma("sp", lambda e, vi=vi, t=t, hh=hh: e.dma_start(out=v_d[hh * 4:(hh + 1) * 4, :, t, :].rearrange("h p d -> p h d"), in_=vst[vi][:]),
                              R=[Bvst[vi]])
                if own and cfg.get("stop") != 4:
                    f = fi % 4
                    fi += 1
                    if cfg.get("stop") == 7 and P == 128:
                        S.op("act", lambda f=f, p=p, P=P: nc.scalar.copy(out=fst[f][:P, :], in_=ps[p][:P, :]), R=[Bps[p]], W=[Bfst[f]])
                    elif cfg.get("stop") in (5, 7):
                        S.op("dve", lambda f=f, p=p, P=P: nc.vector.memset(fst[f][:P, :], 1.0), R=[Bps[p]], W=[Bfst[f]])
                        if P == 128 and cfg.get("stop") == 5:
                            S.op("dve", lambda f=f, col0=col0: nc.vector.tensor_copy(out=fst[f][:, 0:128], in_=uT[:, 0, col0:col0 + 128]), R=[B_uT[t]], W=[Bfst[f]])
                            S.op("dve", lambda f=f, col0=col0, wi=wi: nc.vector.tensor_copy(out=fst[f][:, 128:256], in_=wv[wi][:, 0, 0:128]), R=[Bwv[wi]], W=[Bfst[f]])
                    else:
                        S.op("act", lambda f=f, p=p, P=P: nc.scalar.copy(out=fst[f][:P, :], in_=ps[p][:P, :]), R=[Bps[p]], W=[Bfst[f]])
                    dst = v_own[(t - OT0) * 128:(t - OT0 + 1) * 128, hh * 512:(hh + 1) * 512] if t < NTT else v_s[:, hh * 512:(hh + 1) * 512]
                    S.dma("sp", lambda e, f=f, P=P, dst=dst: e.dma_start(out=dst, in_=fst[f][:P, :]), R=[Bfst[f]], is_out=True)
                    if cfg.get("stop") in (5, 7):
                        continue
                    p = pi % 4
                    pi += 1
                    S.mm([lambda kc=kc, p=p, P=P, col0=col0, wi=wi: nc.tensor.matmul(
                        ps[p][:P, :], lhsT=uT[:, kc, col0:col0 + P], rhs=wk[wi][:, kc, :],
                        start=(kc == 0), stop=(kc == KC - 1)) for kc in range(KC)], R=[Bwk[wi], B_uT[t]], W=[Bps[p]])
                    f = fi % 4
                    fi += 1
                    S.op("dve", lambda f=f, p=p, P=P: nc.vector.tensor_copy(out=fst[f][:P, :], in_=ps[p][:P, :]), R=[Bps[p]], W=[Bfst[f]])
                    dst = k_own[(t - OT0) * 128:(t - OT0 + 1) * 128, hh * 512:(hh + 1) * 512] if t < NTT else k_s[:, hh * 512:(hh + 1) * 512]
                    S.dma("sp", lambda e, f=f, P=P, dst=dst: e.dma_start(out=dst, in_=fst[f][:P, :]), R=[Bfst[f]], is_out=True)
        S.barrier()

    if cfg.get("stop") == 11:
        S.finish(); es.close(); return nc
    mTs = sb("mTs", [128, KC, NS], BF16)
    actTs = sb("actTs", [128, 8, NS], BF16)
    es3 = ExitStack()
    on_tm = sb("on_tm", [128, NOT + 1, D], BF16, es3)
    B_on = [Buf() for _ in range(NOT + 1)]
    qTs = sb("qTs", [128, NH, NS], BF16, es3)
    B_qTs = Buf()
    NPB = TP // 128
    with ExitStack() as ph:
        wq = [sb(f"b_wq{i}", [128, KC, 128], BF16, ph) for i in range(2)]
        Bwq = [Buf(), Buf()]
        kTh = [sb(f"b_kT{i}", [128, TF], BF16, ph) for i in range(2)]
        BkTh = [Buf(), Buf()]
        vh = [sb(f"b_v{i}", [128, NTT, 130], BF16, ph) for i in range(2)]
        Bvh = [Buf(), Buf()]
        qTh = [sb(f"b_q{i}", [128, NOS], BF16, ph) for i in range(2)]
        BqTh = [Buf(), Buf()]
        PT = [sb(f"b_PT{i}", [128, 2, 512], BF16, ph) for i in range(3)]
        BPT = [Buf() for _ in range(3)]
        pS = [pst(f"b_pS{i}", [128, 2, 512], F32, ph) for i in range(2)]
        BpS = [PBuf(), PBuf()]
        acc = [pst(f"b_acc{i}", [128, 512], F32, ph) for i in range(4)]
        Bacc = [PBuf() for _ in range(4)]
        rcp = sb("b_rcp", [128, 4], F32, ph)
        o1 = sb("b_o1", [128, 128], F32, ph)
        o2 = sb("b_o2", [128, 128], F32, ph)
        junk = sb("b_junk", [128, 128], F32, ph)
        Bfin = Buf()
        scnt = 0
        pcnt = 0
        for h in range(NH):
            hb = h % 2
            for kc in range(KC):
                S.dma("pool", lambda e, hb=hb, kc=kc, h=h: e.dma_start(out=wq[hb][:, kc, :], in_=w_in[kc * 128:(kc + 1) * 128, h * 128:(h + 1) * 128]), W=[Bwq[hb]])
            S.dma("sp", lambda e, hb=hb, h=h: e.dma_start(out=kTh[hb][:, :], in_=kT_d[h]), W=[BkTh[hb]])
            S.dma("sp", lambda e, hb=hb, h=h: e.dma_start(out=vh[hb][:, :, :], in_=v_d[h]), W=[Bvh[hb]])
            c0 = 0
            while c0 < NOS:
                cw = min(512, NOS - c0)
                sb_ = scnt % 2; scnt += 1
                tiles = sorted(set(min((TP + c0 + o) // 128, NTT) for o in range(0, cw, 128)))
                S.mm([lambda kc=kc, sb_=sb_, c0=c0, cw=cw, hb=hb: nc.tensor.matmul(pS[sb_][:, 0, :cw], lhsT=wq[hb][:, kc, :], rhs=uT[:, kc, TP + c0:TP + c0 + cw],
                                                                                  start=(kc == 0), stop=(kc == KC - 1)) for kc in range(KC)],
                     R=[Bwq[hb]] + [B_uT[t] for t in tiles], W=[BpS[sb_]])
                S.op("dve", lambda sb_=sb_, c0=c0, cw=cw, hb=hb: nc.vector.tensor_scalar(out=qTh[hb][:, c0:c0 + cw], in0=pS[sb_][:, 0, :cw], scalar1=0.125, scalar2=None, op0=ALU.mult),
                     R=[BpS[sb_]], W=[BqTh[hb]])
                c0 += cw
            S.op("pool", lambda hb=hb, h=h: nc.gpsimd.tensor_copy(out=qTs[:, h, :], in_=qTh[hb][:, TO:TO + NS]), R=[BqTh[hb]], W=[B_qTs])
            for s_ in range(NQG):
                nkb = NPB + 4 * (s_ + 1)

                def emit_qk(j, s_=s_, hb=hb):
                    jj = j - (NPB + 4 * s_)
                    q0 = max(jj, 0) * 128
                    sb_ = j % 2
                    pb = j % 3
                    S.mm([lambda c=c: nc.tensor.matmul(
                        pS[sb_][:, c, q0:512], lhsT=kTh[hb][64 * c:64 * c + 64, j * 128:(j + 1) * 128],
                        rhs=qTh[hb][64 * c:64 * c + 64, s_ * 512 + q0:(s_ + 1) * 512], start=True, stop=True) for c in range(2)],
                        R=[BkTh[hb], BqTh[hb]], W=[BpS[sb_]])
                    S.op("act", lambda: nc.scalar.activation(out=PT[pb][:, :, q0:512], in_=pS[sb_][:, :, q0:512], func=AF.Exp),
                         R=[BpS[sb_]], W=[BPT[pb]])
                    if jj >= 0:
                        for c in range(2):
                            S.op("dve", lambda c=c: nc.vector.tensor_tensor(out=PT[pb][:, c, q0:q0 + 128], in0=PT[pb][:, c, q0:q0 + 128], in1=trib[:, :], op=ALU.mult),
                                 R=[BPT[pb], B_const], W=[BPT[pb]])

                def emit_pv(j, s_=s_, hb=hb):
                    jj = j - (NPB + 4 * s_)
                    qb0 = max(jj, 0)
                    pb = j % 3
                    fns = []
                    for qb in range(qb0, 4):
                        last = (j == NPB + 4 * s_ + qb)
                        for c in range(2):
                            fns.append(lambda qb=qb, c=c, last=last: nc.tensor.matmul(
                                acc[qb][:, c * 256:c * 256 + 129], lhsT=PT[pb][:, c, qb * 128:(qb + 1) * 128], rhs=vh[hb][:, j, 0:129],
                                start=(j == 0 and c == 0), stop=last, skip_group_check=True))
                    S.mm(fns, R=[BPT[pb], Bvh[hb]], W=[Bacc[qb] for qb in range(qb0, 4)])

                emit_qk(0)
                for j in range(nkb):
                    if j + 1 < nkb:
                        emit_qk(j + 1)
                    emit_pv(j)
                for qb in range(4):
                    ti = s_ * 4 + qb
                    S.op("dve", lambda qb=qb: nc.vector.reciprocal(out=rcp[:, 0:1], in_=acc[qb][:, 128:129]), R=[Bacc[qb]], W=[Bfin])
                    S.op("dve", lambda qb=qb: nc.vector.reciprocal(out=rcp[:, 1:2], in_=acc[qb][:, 384:385]), R=[Bacc[qb]], W=[Bfin])
                    S.op("dve", lambda: nc.vector.tensor_tensor(out=rcp[:, 1:2], in0=rcp[:, 1:2], in1=neglam[:, :], op=ALU.mult), R=[Bfin, B_const], W=[Bfin])
                    S.op("dve", lambda qb=qb: nc.vector.tensor_scalar(out=o1[:, :], in0=acc[qb][:, 0:128], scalar1=rcp[:, 0:1], scalar2=None, op0=ALU.mult), R=[Bacc[qb], Bfin], W=[Bfin])
                    S.op("dve", lambda qb=qb: nc.vector.scalar_tensor_tensor(out=o2[:, :], in0=acc[qb][:, 256:384], scalar=rcp[:, 1:2], in1=o1[:, :], op0=ALU.mult, op1=ALU.add),
                         R=[Bacc[qb], Bfin], W=[Bfin])
                    S.op("act", lambda: nc.scalar.activation(out=junk[:, :], in_=o2[:, :], func=AF.Square, accum_out=rcp[:, 2:3]), R=[Bfin], W=[Bfin])
                    S.op("act", lambda: nc.scalar.activation(out=rcp[:, 3:4], in_=rcp[:, 2:3], func=AF.Sqrt, scale=1.0 / 128, bias=epsb[:, 0:1]), R=[Bfin, B_const], W=[Bfin])
                    S.op("dve", lambda: nc.vector.reciprocal(out=rcp[:, 3:4], in_=rcp[:, 3:4]), R=[Bfin], W=[Bfin])
                    S.op("dve", lambda ti=ti, h=h: nc.vector.scalar_tensor_tensor(out=on_tm[:, ti, h * 128:(h + 1) * 128], in0=o2[:, :], scalar=rcp[:, 3:4], in1=gsubb[:, :],
                                                                                op0=ALU.mult, op1=ALU.mult), R=[Bfin, B_const], W=[B_on[ti]])
        S.barrier()
    if cfg.get("stop") == 20:
        with ExitStack() as ph:
            dbg = sb("dbg", [128, D], F32, ph)
            Bd = Buf()
            for ti in range(NOT):
                S.op("dve", lambda ti=ti: nc.vector.tensor_copy(out=dbg[:, :], in_=on_tm[:, ti, :]), R=[B_on[ti]], W=[Bd])
                S.dma("sp", lambda e, ti=ti: e.dma_start(out=y_own[ti * 128:(ti + 1) * 128, :], in_=dbg[:, :]), R=[Bd], is_out=True)
            S.barrier()
        S.finish(); es.close(); return nc

    if cfg.get("stop") == 10:
        S.finish(); es.close(); return nc
    NOS = TO + NS
    es2 = ExitStack()
    hsT = sb("hsT", [128, KC, NOS], BF16, es2)
    B_hsT = [Buf() for _ in range(KC)]
    with ExitStack() as ph:
        PW = min(512, TP)
        NPC = TF // PW
        onesb = sb("c_ones", [128, 1], F32, ph)
        wga = sb("c_wga", [128, 8, 128], BF16, ph)
        wgx = sb("c_wgx", [128, 8, 128], BF16, ph)
        Bwg = Buf()
        S.op("dve", lambda: nc.vector.memset(onesb[:], 1.0), W=[Bwg])
        for n in range(8):
            S.dma("pool", lambda e, n=n: e.dma_start(out=wga[:, n, :], in_=w_ga[n]), W=[Bwg])
            S.dma("pool", lambda e, n=n: e.dma_start(out=wgx[:, n, :], in_=w_gx[n]), W=[Bwg])
        scT = sb("c_scT", [128, KC, 12], F32, ph)
        shT = sb("c_shT", [128, KC, 4], F32, ph)
        Bst = Buf()
        with ExitStack() as ph0:
            sc_tm = sb("c_sctm", [12, D], F32, ph0)
            sh_tm = sb("c_shtm", [4, D], F32, ph0)
            S.dma("sp", lambda e: e.dma_start(out=sc_tm[:], in_=sconv), W=[Bst])
            S.dma("sp", lambda e: e.dma_start(out=sh_tm[:], in_=sh), W=[Bst])
            pss = pst("c_pss", [128, 16], F32, ph0)
            Bpss = PBuf()
            for kc in range(KC):
                S.mm([lambda kc=kc: nc.tensor.transpose(out=pss[:, 0:12], in_=sc_tm[:12, kc * 128:(kc + 1) * 128], identity=identf[:12, :12])], R=[Bst, B_const], W=[Bpss])
                S.op("dve", lambda kc=kc: nc.vector.tensor_copy(out=scT[:, kc, :], in_=pss[:, 0:12]), R=[Bpss], W=[Bst])
                S.mm([lambda kc=kc: nc.tensor.transpose(out=pss[:, 12:16], in_=sh_tm[:4, kc * 128:(kc + 1) * 128], identity=identf[:4, :4])], R=[Bst, B_const], W=[Bpss])
                S.op("dve", lambda kc=kc: nc.vector.tensor_copy(out=shT[:, kc, :], in_=pss[:, 12:16]), R=[Bpss], W=[Bst])
            S.barrier()
        convT = sb("c_convT", [128, KC, 3], F32, ph)
        hlT = sb("c_hlT", [128, KC, 1], F32, ph)
        convsT = sb("c_convsT", [128, KC, 12], F32, ph)
        hsl = sb("c_hsl", [128, KC, 4], F32, ph)
        Bcol = Buf()
        wx = [sb(f"c_wx{i}", [128, KC, 128], BF16, ph) for i in range(2)]
        Bwx = [Buf(), Buf()]
        NB = 2
        xp = [sb(f"c_xp{i}", [128, 3 + PW], F32, ph) for i in range(NB)]
        xc = [sb(f"c_xc{i}", [128, PW], F32, ph) for i in range(NB)]
        xcb = [sb(f"c_xcb{i}", [128, PW], BF16, ph) for i in range(NB)]
        rr = [sb(f"c_r{i}", [128, PW], F32, ph) for i in range(NB)]
        ig = [sb(f"c_ig{i}", [128, PW], F32, ph) for i in range(NB)]
        am = [sb(f"c_am{i}", [128, PW], F32, ph) for i in range(NB)]
        aa = [sb(f"c_aa{i}", [128, PW], F32, ph) for i in range(NB)]
        hh_ = [sb(f"c_h{i}", [128, PW], F32, ph) for i in range(NB)]
        carry = sb("c_carry", [128, 1], F32, ph)
        Bxp = [Buf() for _ in range(NB)]; Bxc = [Buf() for _ in range(NB)]; Bxcb = [Buf() for _ in range(NB)]
        Br = [Buf() for _ in range(NB)]; Big = [Buf() for _ in range(NB)]; Bam = [Buf() for _ in range(NB)]
        Baa = [Buf() for _ in range(NB)]; Bh = [Buf() for _ in range(NB)]; Bcarry = Buf()
        psx = [pst(f"c_psx{i}", [128, 512], F32, ph) for i in range(2)]
        Bpsx = [PBuf() for _ in range(2)]
        psg = [pst(f"c_psg{i}", [128, 512], F32, ph) for i in range(4)]
        Bpsg = [PBuf() for _ in range(4)]
        xps = sb("c_xps", [128, 4, 11], F32, ph)
        xcs = sb("c_xcs", [128, 4, 8], F32, ph)
        xcsb = sb("c_xcsb", [128, 32], BF16, ph)
        rs_ = sb("c_rs", [128, 32], F32, ph)
        igs = sb("c_igs", [128, 32], F32, ph)
        ams = sb("c_ams", [128, 32], F32, ph)
        aas = sb("c_aas", [128, 32], F32, ph)
        hss = sb("c_hss", [128, 4, 8], F32, ph)
        Bsm = Buf()
        gcnt = 0
        xcnt = 0
        it = 0

        def lru_core(j, xp_ap3, xc_t, xcb_t, r_t, ig_t, am_t, aa_t, W_, Bxp_, Bxc_, Bxcb_, Br_, Big_, Bam_, Baa_, shape3=None):
            nonlocal gcnt
            S.op("pool", lambda: nc.gpsimd.tensor_scalar(out=xc_t, in0=xp_ap3(0), scalar1=wcT[:, 0, j:j + 1], scalar2=vT[:, 1, j:j + 1],
                                                          op0=ALU.mult, op1=ALU.add), R=[Bxp_, B_const], W=[Bxc_])
            for jj in range(1, 4):
                S.op("dve", lambda jj=jj: nc.vector.scalar_tensor_tensor(out=xc_t, in0=xp_ap3(jj), scalar=wcT[:, jj, j:j + 1], in1=xc_t,
                                                                            op0=ALU.mult, op1=ALU.add), R=[Bxp_, Bxc_], W=[Bxc_])
            xc2 = xc_t if shape3 is None else xc_t.rearrange("p a b -> p (a b)")
            S.op("pool", lambda: nc.gpsimd.tensor_copy(out=xcb_t, in_=xc2), R=[Bxc_], W=[Bxcb_])
            c0 = 0
            while c0 < W_:
                cw = min(512, W_ - c0)
                g0 = gcnt % 4; g1 = (gcnt + 1) % 4; gcnt += 2
                S.mm([lambda g0=g0, c0=c0, cw=cw: nc.tensor.matmul(psg[g0][:, :cw], lhsT=wga[:, j, :], rhs=xcb_t[:, c0:c0 + cw], start=True, stop=True)],
                     R=[Bwg, Bxcb_], W=[Bpsg[g0]])
                S.mm([lambda g1=g1, c0=c0, cw=cw: nc.tensor.matmul(psg[g1][:, :cw], lhsT=wgx[:, j, :], rhs=xcb_t[:, c0:c0 + cw], start=True, stop=True)],
                     R=[Bwg, Bxcb_], W=[Bpsg[g1]])
                S.op("act", lambda g0=g0, c0=c0, cw=cw: nc.scalar.activation(out=r_t[:, c0:c0 + cw], in_=psg[g0][:, :cw], func=AF.Sigmoid, bias=vT[:, 2, j:j + 1]),
                     R=[Bpsg[g0], B_const], W=[Br_])
                S.op("act", lambda g1=g1, c0=c0, cw=cw: nc.scalar.activation(out=ig_t[:, c0:c0 + cw], in_=psg[g1][:, :cw], func=AF.Sigmoid, bias=vT[:, 3, j:j + 1]),
                     R=[Bpsg[g1], B_const], W=[Big_])
                c0 += cw
            S.op("act", lambda: nc.scalar.activation(out=aa_t, in_=r_t, func=AF.Exp, scale=clam[:, j:j + 1]), R=[Br_, B_const], W=[Baa_])
            S.op("act", lambda: nc.scalar.activation(out=am_t, in_=r_t, func=AF.Exp, scale=clam2[:, j:j + 1]), R=[Br_, B_const], W=[Bam_])
            S.op("act", lambda: nc.scalar.activation(out=am_t, in_=am_t, func=AF.Sqrt, scale=-1.0, bias=onesb[:, 0:1]), R=[Bam_, Bwg], W=[Bam_])
            S.op("dve", lambda: nc.vector.tensor_tensor(out=ig_t, in0=ig_t, in1=xc2, op=ALU.mult), R=[Big_, Bxc_], W=[Big_])
            S.op("dve", lambda: nc.vector.tensor_tensor(out=ig_t, in0=ig_t, in1=am_t, op=ALU.mult), R=[Big_, Bam_], W=[Big_])

        for j in range(KC):
            wi = j % 2
            for kc in range(KC):
                S.dma("pool", lambda e, wi=wi, kc=kc, j=j: e.dma_start(out=wx[wi][:, kc, :], in_=w_in[kc * 128:(kc + 1) * 128, 3 * D + j * 128:3 * D + (j + 1) * 128]), W=[Bwx[wi]])
            for pc in range(NPC):
                b = it % NB
                it += 1
                col0 = pc * PW
                if pc == 0:
                    S.op("pool", lambda b=b: nc.gpsimd.memset(xp[b][:, 0:3], 0.0), W=[Bxp[b]])
                else:
                    pb = (it - 2) % NB
                    S.op("pool", lambda b=b, pb=pb: nc.gpsimd.tensor_copy(out=xp[b][:, 0:3], in_=xp[pb][:, PW:PW + 3]), R=[Bxp[pb]], W=[Bxp[b]])
                c0 = 0
                while c0 < PW:
                    x_ = xcnt % 2; xcnt += 1
                    tiles = sorted(set((col0 + c0 + o) // 128 for o in range(0, 512, 128)))
                    S.mm([lambda kc=kc, x_=x_, c0=c0, col0=col0, wi=wi: nc.tensor.matmul(psx[x_][:, :], lhsT=wx[wi][:, kc, :], rhs=uT[:, kc, col0 + c0:col0 + c0 + 512],
                                                                                       start=(kc == 0), stop=(kc == KC - 1)) for kc in range(KC)],
                         R=[Bwx[wi]] + [B_uT[t] for t in tiles], W=[Bpsx[x_]])
                    S.op("act", lambda x_=x_, b=b, c0=c0: nc.scalar.copy(out=xp[b][:, 3 + c0:3 + c0 + 512], in_=psx[x_][:, :]), R=[Bpsx[x_]], W=[Bxp[b]])
                    c0 += 512
                lru_core(j, lambda jj, b=b: xp[b][:, jj:jj + PW], xc[b][:, :], xcb[b][:, :], rr[b][:, :], ig[b][:, :], am[b][:, :], aa[b][:, :], PW,
                         Bxp[b], Bxc[b], Bxcb[b], Br[b], Big[b], Bam[b], Baa[b])
                if pc == 0:
                    init = 0.0
                    Rc = []
                else:
                    pb = (it - 2) % NB
                    if col0 == TP:
                        S.op("dve", lambda pb=pb: nc.vector.tensor_tensor(out=carry[:], in0=hh_[pb][:, PW - 1:PW], in1=flg[:], op=ALU.mult), R=[Bh[pb], B_const], W=[Bcarry])
                        init = carry[:, 0:1]
                        Rc = [Bcarry]
                    else:
                        init = hh_[pb][:, PW - 1:PW]
                        Rc = [Bh[pb]]
                S.op("dve", lambda b=b, init=init: nc.vector.tensor_tensor_scan(out=hh_[b][:, :], data0=aa[b][:, :], data1=ig[b][:, :], initial=init,
                                                                                 op0=ALU.mult, op1=ALU.add), R=[Baa[b], Big[b]] + Rc, W=[Bh[b]])
                if col0 >= TP:
                    o0 = col0 - TP
                    S.op("pool", lambda b=b, o0=o0, j=j: nc.gpsimd.tensor_copy(out=hsT[:, j, o0:o0 + PW], in_=hh_[b][:, :]), R=[Bh[b]], W=[B_hsT[j]])
                if pc == NPC - 1:
                    S.op("pool", lambda b=b, j=j: nc.gpsimd.tensor_copy(out=convT[:, j, :], in_=xp[b][:, PW:PW + 3]), R=[Bxp[b]], W=[Bcol])
                    S.op("pool", lambda b=b, j=j: nc.gpsimd.tensor_copy(out=hlT[:, j, :], in_=hh_[b][:, PW - 1:PW]), R=[Bh[b]], W=[Bcol])
            x_ = xcnt % 2; xcnt += 1
            S.mm([lambda kc=kc, x_=x_, wi=wi: nc.tensor.matmul(psx[x_][:, :NS], lhsT=wx[wi][:, kc, :], rhs=uT[:, kc, TF:TF + NS],
                                                              start=(kc == 0), stop=(kc == KC - 1)) for kc in range(KC)],
                 R=[Bwx[wi], B_uT[NTT]], W=[Bpsx[x_]])
            S.op("act", lambda x_=x_: nc.scalar.copy(out=xps[:, :, 3:11], in_=psx[x_][:, :NS].rearrange("p (a b) -> p a b", a=4)), R=[Bpsx[x_]], W=[Bsm])
            S.op("pool", lambda j=j: nc.gpsimd.tensor_copy(out=xps[:, :, 0:3], in_=scT[:, j, :].rearrange("p (a b) -> p a b", a=4)), R=[Bst], W=[Bsm])
            lru_core(j, lambda jj: xps[:, :, jj:jj + 8], xcs[:, :, :], xcsb[:, :], rs_[:, :], igs[:, :], ams[:, :], aas[:, :], NS,
                     Bsm, Bsm, Bsm, Bsm, Bsm, Bsm, Bsm, shape3=True)
            for sq in range(4):
                S.op("dve", lambda sq=sq, j=j: nc.vector.tensor_tensor_scan(out=hss[:, sq, :], data0=aas[:, sq * 8:(sq + 1) * 8], data1=igs[:, sq * 8:(sq + 1) * 8],
                                                                              initial=shT[:, j, sq:sq + 1], op0=ALU.mult, op1=ALU.add), R=[Bsm, Bst], W=[Bsm])
            S.op("pool", lambda j=j: nc.gpsimd.tensor_copy(out=hsT[:, j, TO:TO + NS], in_=hss[:].rearrange("p a b -> p (a b)")), R=[Bsm], W=[B_hsT[j]])
            S.op("pool", lambda j=j: nc.gpsimd.tensor_copy(out=convsT[:, j, :].rearrange("p (a b) -> p a b", a=4), in_=xps[:, :, 8:11]), R=[Bsm], W=[Bcol])
            S.op("pool", lambda j=j: nc.gpsimd.tensor_copy(out=hsl[:, j, :], in_=hss[:, :, 7]), R=[Bsm], W=[Bcol])
        otm = sb("c_otm", [12, D], F32, ph)
        Botm = Buf()
        pso = pst("c_pso", [12, 512], F32, ph)
        Bpso = PBuf()
        for (src, n, k, dst) in ((convT, 3, 0, conv_p), (hlT, 1, 1, h_p), (convsT, 12, 2, conv_s), (hsl, 4, 3, h_s)):
            for half in range(2):
                S.mm([lambda kc=kc, src=src, n=n, half=half: nc.tensor.transpose(out=pso[:n, (kc % 4) * 128:(kc % 4 + 1) * 128], in_=src[:, kc, :], identity=identf[:, :])
                      for kc in range(half * 4, half * 4 + 4)], R=[Bcol, B_const], W=[Bpso])
                S.op("dve", lambda n=n, k=k, half=half: nc.vector.tensor_copy(out=otm[:n, half * 512:(half + 1) * 512], in_=pso[:n, :]), R=[Bpso], W=[Botm])
            S.dma("sp", lambda e, n=n, k=k, dst=dst: e.dma_start(out=dst, in_=otm[:n, :]), R=[Botm], is_out=True)
        S.barrier()

    if cfg.get("stop") == 12:
        S.finish(); es.close(); return nc
    S.dma("pool", lambda e: e.dma_start(out=on_tm[:NS, NOT, :], in_=on_s_in), W=[B_on[NOT]])

    def col_chunks():
        c0 = 0
        while c0 < NOS:
            cw = min(512, NOS - c0)
            tiles = sorted(set(min((TP + c0 + o) // 128, NTT) for o in range(0, cw, 128)))
            yield c0, cw, tiles
            c0 += cw

    with ExitStack() as ph:
        wy = [sb(f"d1_wy{i}", [128, KC, 128], BF16, ph) for i in range(2)]
        Bwy = [Buf(), Buf()]
        ps = [pst(f"d1_ps{i}", [128, 512], F32, ph) for i in range(4)]
        Bps = [PBuf() for _ in range(4)]
        NBF = 2
        xs_ = [sb(f"d1_x{i}", [128, 512], F32, ph) for i in range(NBF)]
        t1 = [sb(f"d1_t{i}", [128, 512], F32, ph) for i in range(NBF)]
        sg = [sb(f"d1_s{i}", [128, 512], F32, ph) for i in range(NBF)]
        Bx = [Buf() for _ in range(NBF)]; Bt = [Buf() for _ in range(NBF)]; Bsg = [Buf() for _ in range(NBF)]
        cnt = 0
        for n in range(KC):
            wi = n % 2
            for kc in range(KC):
                S.dma("pool", lambda e, wi=wi, kc=kc, n=n: e.dma_start(out=wy[wi][:, kc, :], in_=w_in[kc * 128:(kc + 1) * 128, 4 * D + n * 128:4 * D + (n + 1) * 128]), W=[Bwy[wi]])
            for c0, cw, tiles in col_chunks():
                p = cnt % 4; b = cnt % NBF; cnt += 1
                S.mm([lambda kc=kc, p=p, c0=c0, cw=cw, wi=wi: nc.tensor.matmul(ps[p][:, :cw], lhsT=wy[wi][:, kc, :], rhs=uT[:, kc, TP + c0:TP + c0 + cw],
                                                                              start=(kc == 0), stop=(kc == KC - 1)) for kc in range(KC)],
                     R=[Bwy[wi]] + [B_uT[t] for t in tiles], W=[Bps[p]])
                S.op("act", lambda p=p, b=b, cw=cw: nc.scalar.copy(out=xs_[b][:, :cw], in_=ps[p][:, :cw]), R=[Bps[p]], W=[Bx[b]])
                S.op("pool", lambda b=b, cw=cw: nc.gpsimd.tensor_tensor(out=t1[b][:, :cw], in0=xs_[b][:, :cw], in1=xs_[b][:, :cw], op=ALU.mult), R=[Bx[b]], W=[Bt[b]])
                S.op("pool", lambda b=b, cw=cw: nc.gpsimd.tensor_scalar(out=t1[b][:, :cw], in0=t1[b][:, :cw], scalar1=0.044715, scalar2=1.0, op0=ALU.mult, op1=ALU.add), R=[Bt[b]], W=[Bt[b]])
                S.op("pool", lambda b=b, cw=cw: nc.gpsimd.tensor_tensor(out=t1[b][:, :cw], in0=t1[b][:, :cw], in1=xs_[b][:, :cw], op=ALU.mult), R=[Bt[b], Bx[b]], W=[Bt[b]])
                S.op("act", lambda b=b, cw=cw: nc.scalar.activation(out=sg[b][:, :cw], in_=t1[b][:, :cw], func=AF.Sigmoid, scale=1.5957691216057308), R=[Bt[b]], W=[Bsg[b]])
                S.op("dve", lambda b=b, cw=cw: nc.vector.tensor_tensor(out=sg[b][:, :cw], in0=sg[b][:, :cw], in1=xs_[b][:, :cw], op=ALU.mult), R=[Bsg[b], Bx[b]], W=[Bsg[b]])
                S.op("dve", lambda b=b, cw=cw, n=n, c0=c0: nc.vector.tensor_tensor(out=hsT[:, n, c0:c0 + cw], in0=hsT[:, n, c0:c0 + cw], in1=sg[b][:, :cw], op=ALU.mult),
                     R=[Bsg[b], B_hsT[n]], W=[B_hsT[n]])
        S.barrier()

    B_mT = [Buf() for _ in range(NOT + 1)]

    def mT_cols(kc_or_all, c0, cw):
        if c0 < TO:
            return uT[:, kc_or_all, c0:c0 + cw]
        return mTs[:, kc_or_all, 0:cw]

    with ExitStack() as ph:
        wr = [sb(f"d2_wr{i}", [128, KC, 128], BF16, ph) for i in range(2)]
        wgr = [sb(f"d2_wgr{i}", [128, KC, 128], BF16, ph) for i in range(2)]
        Bw = [Buf(), Buf()]
        ps = [pst(f"d2_ps{i}", [128, 512], F32, ph) for i in range(4)]
        Bps = [PBuf() for _ in range(4)]
        sgr = [sb(f"d2_sgr{i}", [128, 512], F32, ph) for i in range(2)]
        Bsgr = [Buf(), Buf()]
        cnt = 0
        for m in range(KC):
            wi = m % 2
            for kc in range(KC):
                r0, r1 = kc * 128, (kc + 1) * 128
                S.dma("pool", lambda e, wi=wi, kc=kc, m=m, r0=r0, r1=r1: e.dma_start(out=wr[wi][:, kc, :], in_=w_rec[r0:r1, m * 128:(m + 1) * 128]), W=[Bw[wi]])
                S.dma("pool", lambda e, wi=wi, kc=kc, m=m, r0=r0, r1=r1: e.dma_start(out=wgr[wi][:, kc, :], in_=w_in[r0:r1, 6 * D + m * 128:6 * D + (m + 1) * 128]), W=[Bw[wi]])
            for c0, cw, tiles in col_chunks():
                b = cnt % 2; cnt += 1
                pA, pB = (2 * b, 2 * b + 1)
                otiles = sorted(set(min((c0 + o) // 128, NOT) for o in range(0, cw, 128)))
                S.mm([lambda kc=kc: nc.tensor.matmul(ps[pA][:, :cw], lhsT=wr[wi][:, kc, :], rhs=hsT[:, kc, c0:c0 + cw], start=(kc == 0), stop=(kc == KC - 1)) for kc in range(KC)],
                     R=[Bw[wi]] + B_hsT, W=[Bps[pA]])
                S.mm([lambda kc=kc: nc.tensor.matmul(ps[pB][:, :cw], lhsT=wgr[wi][:, kc, :], rhs=uT[:, kc, TP + c0:TP + c0 + cw], start=(kc == 0), stop=(kc == KC - 1)) for kc in range(KC)],
                     R=[Bw[wi]] + [B_uT[t] for t in tiles], W=[Bps[pB]])
                S.op("act", lambda: nc.scalar.activation(out=sgr[b][:, :cw], in_=ps[pB][:, :cw], func=AF.Sigmoid), R=[Bps[pB]], W=[Bsgr[b]])
                S.op("dve", lambda: nc.vector.tensor_tensor(out=mT_cols(m, c0, cw), in0=ps[pA][:, :cw], in1=sgr[b][:, :cw], op=ALU.mult),
                     R=[Bps[pA], Bsgr[b]], W=[B_mT[t] for t in otiles])
        S.barrier()
    es2.close()

    with ExitStack() as ph:
        wa = sb("d2_wa", [128, KC, D], BF16, ph)
        wga_ = sb("d2_wga", [128, KC, D], BF16, ph)
        Bw = Buf()
        for kc in range(KC):
            for hf in range(2):
                S.dma("pool", lambda e, kc=kc, hf=hf: e.dma_start(out=wa[:, kc, hf * 512:(hf + 1) * 512], in_=w_attn[kc * 128:(kc + 1) * 128, hf * 512:(hf + 1) * 512]), W=[Bw])
                S.dma("pool", lambda e, kc=kc, hf=hf: e.dma_start(out=wga_[:, kc, hf * 512:(hf + 1) * 512],
                                                                   in_=w_in[kc * 128:(kc + 1) * 128, 5 * D + hf * 512:5 * D + (hf + 1) * 512]), W=[Bw])
        oaT = [sb(f"d2_oaT{i}", [128, KC, 512], BF16, ph) for i in range(2)]
        BoaT = [Buf(), Buf()]
        psT = [pst(f"d2_psT{i}", [128, KC, 128], BF16, ph) for i in range(2)]
        BpsT = [PBuf(), PBuf()]
        ps = [pst(f"d2b_ps{i}", [128, 512], F32, ph) for i in range(4)]
        Bps = [PBuf() for _ in range(4)]
        sga = [sb(f"d2_sga{i}", [128, 512], F32, ph) for i in range(2)]
        ta_ = [sb(f"d2_ta{i}", [128, 512], F32, ph) for i in range(2)]
        Bsga = [Buf(), Buf()]; Bta = [Buf(), Buf()]
        cnt = 0
        tcnt = 0
        for ci, (c0, cw, tiles) in enumerate(col_chunks()):
            ob = ci % 2
            otiles = sorted(set(min((c0 + o) // 128, NOT) for o in range(0, cw, 128)))
            for t in otiles:
                P = 128 if t < NOT else NS
                i = tcnt % 2; tcnt += 1
                lc = t * 128 - c0
                S.mm([lambda kc=kc, i=i, t=t, P=P: nc.tensor.transpose(out=psT[i][:, kc, :P], in_=on_tm[:P, t, kc * 128:(kc + 1) * 128], identity=identb[:P, :P])
                      for kc in range(KC)], R=[B_on[t], B_const], W=[BpsT[i]])
                S.op("act", lambda i=i, ob=ob, lc=lc, P=P: nc.scalar.copy(out=oaT[ob][:, :, lc:lc + P], in_=psT[i][:, :, :P]), R=[BpsT[i]], W=[BoaT[ob]])
            for m in range(KC):
                b = cnt % 2; cnt += 1
                pC, pD = (2 * b, 2 * b + 1)
                S.mm([lambda kc=kc: nc.tensor.matmul(ps[pC][:, :cw], lhsT=wa[:, kc, m * 128:(m + 1) * 128], rhs=oaT[ob][:, kc, :cw], start=(kc == 0), stop=(kc == KC - 1)) for kc in range(KC)],
                     R=[Bw, BoaT[ob]], W=[Bps[pC]])
                S.mm([lambda kc=kc: nc.tensor.matmul(ps[pD][:, :cw], lhsT=wga_[:, kc, m * 128:(m + 1) * 128], rhs=uT[:, kc, TP + c0:TP + c0 + cw], start=(kc == 0), stop=(kc == KC - 1)) for kc in range(KC)],
                     R=[Bw] + [B_uT[t] for t in tiles], W=[Bps[pD]])
                S.op("act", lambda: nc.scalar.activation(out=sga[b][:, :cw], in_=ps[pD][:, :cw], func=AF.Sigmoid), R=[Bps[pD]], W=[Bsga[b]])
                S.op("dve", lambda: nc.vector.tensor_tensor(out=ta_[b][:, :cw], in0=ps[pC][:, :cw], in1=sga[b][:, :cw], op=ALU.mult), R=[Bps[pC], Bsga[b]], W=[Bta[b]])
                S.op("pool", lambda: nc.gpsimd.tensor_tensor(out=mT_cols(m, c0, cw), in0=mT_cols(m, c0, cw), in1=ta_[b][:, :cw], op=ALU.add),
                     R=[Bta[b]] + [B_mT[t] for t in otiles], W=[B_mT[t] for t in otiles])
        S.barrier()
    es3.close()

    hres = sb("hres", [128, NOT + 1, D], F32)
    B_h = [Buf() for _ in range(NOT + 1)]

    def tile_rows(t):
        return 128 if t < NOT else NS

    def mT_tile(kc, t):
        if t < NOT:
            return uT[:, kc, t * 128:(t + 1) * 128]
        return mTs[:, kc, :]

    def norm_to_uT(ph_, t, gidx, xnb, Bxnb, psT, BpsT):
        P = tile_rows(t)
        i = t % 2
        rmsnorm_rows(ph_, hres[:P, t, :], P, gidx, xnb[i][:P, :], B_h[t], Bxnb[i], "d")
        transpose_to_uT(ph_, xnb[i], P, TP + t * 128, Bxnb[i], B_uT[OT0 + t], psT[i], BpsT[i])

    with ExitStack() as ph:
        wo = sb("d3_wo", [128, KC, D], BF16, ph)
        Bwo = Buf()
        for kc in range(KC):
            for hf in range(2):
                S.dma("pool", lambda e, kc=kc, hf=hf: e.dma_start(out=wo[:, kc, hf * 512:(hf + 1) * 512], in_=w_out[kc * 128:(kc + 1) * 128, hf * 512:(hf + 1) * 512]), W=[Bwo])
        xt = [sb(f"d3_x{i}", [128, D], F32, ph) for i in range(2)]
        Bxt = [Buf(), Buf()]
        xnb = [sb(f"d3_xn{i}", [128, D], BF16, ph) for i in range(2)]
        Bxnb = [Buf(), Buf()]
        psT = [pst(f"d3_psT{i}", [128, KC, 128], BF16, ph) for i in range(2)]
        BpsT = [PBuf(), PBuf()]
        ps = [pst(f"d3_ps{i}", [128, 512], F32, ph) for i in range(4)]
        Bps = [PBuf() for _ in range(4)]
        cnt = 0
        for t in range(NOT + 1):
            P = tile_rows(t)
            i = t % 2
            src = xf[TP + t * 128:TP + (t + 1) * 128, :] if t < NOT else xs
            S.dma("sp", lambda e, i=i, P=P, src=src: e.dma_start(out=xt[i][:P, :], in_=src), W=[Bxt[i]])
            for hf in range(2):
                p = cnt % 4; cnt += 1
                S.mm([lambda kc=kc, p=p, t=t, P=P, hf=hf: nc.tensor.matmul(ps[p][:P, :], lhsT=mT_tile(kc, t), rhs=wo[:, kc, hf * 512:(hf + 1) * 512],
                                                                          start=(kc == 0), stop=(kc == KC - 1)) for kc in range(KC)], R=[Bwo, B_mT[t]], W=[Bps[p]])
                S.op("dve", lambda p=p, t=t, P=P, hf=hf, i=i: nc.vector.tensor_tensor(out=hres[:P, t, hf * 512:(hf + 1) * 512], in0=ps[p][:P, :], in1=xt[i][:P, hf * 512:(hf + 1) * 512], op=ALU.add),
                     R=[Bps[p], Bxt[i]], W=[B_h[t]])
            norm_to_uT(ph, t, 1, xnb, Bxnb, psT, BpsT)
        S.barrier()

    B_act = [Buf() for _ in range(NOT + 1)]

    def act_cols(fl, c0, cw):
        if c0 < TO:
            return uT[:, fl, c0:c0 + cw]
        return actTs[:, fl, 0:cw]

    def act_tile(fl, t):
        if t < NOT:
            return uT[:, fl, t * 128:(t + 1) * 128]
        return actTs[:, fl, :]

    with ExitStack() as ph:
        wg = [sb(f"d4_wg{i}", [128, KC, 128], BF16, ph) for i in range(2)]
        wu = [sb(f"d4_wu{i}", [128, KC, 128], BF16, ph) for i in range(2)]
        Bwgu = [Buf(), Buf()]
        wd = sb("d4_wd", [128, 8, D], BF16, ph)
        Bwd = Buf()
        psg = [pst(f"d4_pg{i}", [128, 512], F32, ph) for i in range(2)]
        psu = [pst(f"d4_pu{i}", [128, 512], F32, ph) for i in range(2)]
        psd = [pst(f"d4_pd{i}", [128, 512], F32, ph) for i in range(4)]
        Bpsg = [PBuf(), PBuf()]; Bpsu = [PBuf(), PBuf()]; Bpsd = [PBuf() for _ in range(4)]
        sl = [sb(f"d4_sl{i}", [128, 512], F32, ph) for i in range(2)]
        Bsl = [Buf(), Buf()]
        parts = [(0, 8), (8, 15), (15, 22)]
        cnt = 0
        dcnt = 0
        wcnt = 0
        for (f0, f1) in parts:
            nf = f1 - f0
            for fl in range(nf):
                for hf in range(2):
                    S.dma("pool", lambda e, fl=fl, f0=f0, hf=hf: e.dma_start(out=wd[:, fl, hf * 512:(hf + 1) * 512],
                                                                              in_=w_fd[(f0 + fl) * 128:(f0 + fl + 1) * 128, hf * 512:(hf + 1) * 512]), W=[Bwd])
            for fl in range(nf):
                fc = f0 + fl
                wi = wcnt % 2; wcnt += 1
                for kc in range(KC):
                    S.dma("pool", lambda e, wi=wi, kc=kc, fc=fc: e.dma_start(out=wg[wi][:, kc, :], in_=w_fg[kc * 128:(kc + 1) * 128, fc * 128:(fc + 1) * 128]), W=[Bwgu[wi]])
                    S.dma("pool", lambda e, wi=wi, kc=kc, fc=fc: e.dma_start(out=wu[wi][:, kc, :], in_=w_fu[kc * 128:(kc + 1) * 128, fc * 128:(fc + 1) * 128]), W=[Bwgu[wi]])
                for c0, cw, tiles in col_chunks():
                    b = cnt % 2; cnt += 1
                    otiles = sorted(set(min((c0 + o) // 128, NOT) for o in range(0, cw, 128)))
                    S.mm([lambda kc=kc: nc.tensor.matmul(psg[b][:, :cw], lhsT=wg[wi][:, kc, :], rhs=uT[:, kc, TP + c0:TP + c0 + cw], start=(kc == 0), stop=(kc == KC - 1)) for kc in range(KC)],
                         R=[Bwgu[wi]] + [B_uT[t] for t in tiles], W=[Bpsg[b]])
                    S.mm([lambda kc=kc: nc.tensor.matmul(psu[b][:, :cw], lhsT=wu[wi][:, kc, :], rhs=uT[:, kc, TP + c0:TP + c0 + cw], start=(kc == 0), stop=(kc == KC - 1)) for kc in range(KC)],
                         R=[Bwgu[wi]] + [B_uT[t] for t in tiles], W=[Bpsu[b]])
                    S.op("act", lambda: nc.scalar.activation(out=sl[b][:, :cw], in_=psg[b][:, :cw], func=AF.Silu), R=[Bpsg[b]], W=[Bsl[b]])
                    S.op("dve", lambda: nc.vector.tensor_tensor(out=act_cols(fl, c0, cw), in0=psu[b][:, :cw], in1=sl[b][:, :cw], op=ALU.mult),
                         R=[Bpsu[b], Bsl[b]], W=[B_act[t] for t in otiles])
            for t in range(NOT + 1):
                P = tile_rows(t)
                for hf in range(2):
                    p = dcnt % 4; dcnt += 1
                    S.mm([lambda fl=fl, p=p, t=t, P=P, hf=hf: nc.tensor.matmul(psd[p][:P, :], lhsT=act_tile(fl, t), rhs=wd[:, fl, hf * 512:(hf + 1) * 512],
                                                                              start=(fl == 0), stop=(fl == nf - 1)) for fl in range(nf)], R=[Bwd, B_act[t]], W=[Bpsd[p]])
                    S.op("dve", lambda p=p, t=t, P=P, hf=hf: nc.vector.tensor_tensor(out=hres[:P, t, hf * 512:(hf + 1) * 512], in0=psd[p][:P, :], in1=hres[:P, t, hf * 512:(hf + 1) * 512], op=ALU.add),
                         R=[Bpsd[p], B_h[t]], W=[B_h[t]])
        S.barrier()

    with ExitStack() as ph:
        wpg = sb("d5_wpg", [128, KC, D], BF16, ph)
        wpp = sb("d5_wpp", [128, 2, D], BF16, ph)
        Bwp = Buf()
        for kc in range(KC):
            for hf in range(2):
                S.dma("pool", lambda e, kc=kc, hf=hf: e.dma_start(out=wpg[:, kc, hf * 512:(hf + 1) * 512], in_=w_pg[kc * 128:(kc + 1) * 128, hf * 512:(hf + 1) * 512]), W=[Bwp])
        for c in range(2):
            for hf in range(2):
                S.dma("pool", lambda e, c=c, hf=hf: e.dma_start(out=wpp[:, c, hf * 512:(hf + 1) * 512], in_=w_pp[c * 128:(c + 1) * 128, hf * 512:(hf + 1) * 512]), W=[Bwp])
        xnb = [sb(f"d5_xn{i}", [128, D], BF16, ph) for i in range(2)]
        Bxnb = [Buf(), Buf()]
        psT = [pst(f"d5_psT{i}", [128, KC, 128], BF16, ph) for i in range(2)]
        BpsT = [PBuf(), PBuf()]
        pt_ = [sb(f"d5_p{i}", [128, 256], F32, ph) for i in range(2)]
        ptb = [sb(f"d5_pb{i}", [128, 256], BF16, ph) for i in range(2)]
        pT = [sb(f"d5_pT{i}", [128, 2, 128], BF16, ph) for i in range(2)]
        Bpt = [Buf(), Buf()]; Bptb = [Buf(), Buf()]; BpT = [Buf(), Buf()]
        psg = [pst(f"d5_pg{i}", [128, 512], F32, ph) for i in range(2)]
        psp = [pst(f"d5_pp{i}", [128, 512], F32, ph) for i in range(2)]
        Bpsg = [PBuf(), PBuf()]; Bpsp = [PBuf(), PBuf()]
        sg = [sb(f"d5_sg{i}", [128, 512], F32, ph) for i in range(2)]
        Bsg = [Buf(), Buf()]
        yt = [sb(f"d5_y{i}", [128, D], F32, ph) for i in range(2)]
        Byt = [Buf(), Buf()]
        cnt = 0
        for t in range(NOT + 1):
            P = tile_rows(t)
            i = t % 2
            norm_to_uT(ph, t, 2, xnb, Bxnb, psT, BpsT)
            src = pown[t * 128:(t + 1) * 128, :] if t < NOT else psm
            S.dma("sp", lambda e, i=i, P=P, src=src: e.dma_start(out=pt_[i][:P, :], in_=src), W=[Bpt[i]])
            S.op("pool", lambda i=i, P=P: nc.gpsimd.tensor_copy(out=ptb[i][:P, :], in_=pt_[i][:P, :]), R=[Bpt[i]], W=[Bptb[i]])
            S.mm([lambda c=c, i=i, P=P: nc.tensor.transpose(out=psT[i][:, c, :P], in_=ptb[i][:P, c * 128:(c + 1) * 128], identity=identb[:P, :P]) for c in range(2)],
                 R=[Bptb[i], B_const], W=[BpsT[i]])
            S.op("act", lambda i=i, P=P: nc.scalar.copy(out=pT[i][:, :, :P], in_=psT[i][:, 0:2, :P]), R=[BpsT[i]], W=[BpT[i]])
            col0 = TP + t * 128
            for hf in range(2):
                b = cnt % 2; cnt += 1
                S.mm([lambda kc=kc, b=b, P=P, hf=hf, col0=col0: nc.tensor.matmul(psg[b][:P, :], lhsT=uT[:, kc, col0:col0 + P], rhs=wpg[:, kc, hf * 512:(hf + 1) * 512],
                                                                                start=(kc == 0), stop=(kc == KC - 1)) for kc in range(KC)], R=[Bwp, B_uT[OT0 + t]], W=[Bpsg[b]])
                S.mm([lambda c=c, b=b, P=P, hf=hf, i=i: nc.tensor.matmul(psp[b][:P, :], lhsT=pT[i][:, c, :P], rhs=wpp[:, c, hf * 512:(hf + 1) * 512],
                                                                        start=(c == 0), stop=(c == 1)) for c in range(2)], R=[Bwp, BpT[i]], W=[Bpsp[b]])
                S.op("act", lambda b=b, P=P: nc.scalar.activation(out=sg[b][:P, :], in_=psg[b][:P, :], func=AF.Sigmoid), R=[Bpsg[b]], W=[Bsg[b]])
                S.op("dve", lambda b=b, P=P: nc.vector.tensor_tensor(out=sg[b][:P, :], in0=psp[b][:P, :], in1=sg[b][:P, :], op=ALU.mult), R=[Bpsp[b], Bsg[b]], W=[Bsg[b]])
                S.op("pool", lambda b=b, P=P, t=t, hf=hf: nc.gpsimd.tensor_tensor(out=hres[:P, t, hf * 512:(hf + 1) * 512], in0=hres[:P, t, hf * 512:(hf + 1) * 512], in1=sg[b][:P, :], op=ALU.add),
                     R=[Bsg[b], B_h[t]], W=[B_h[t]])
            rmsnorm_rows(ph, hres[:P, t, :], P, 3, yt[i][:P, :], B_h[t], Byt[i], "y")
            dst = y_own[t * 128:(t + 1) * 128, :] if t < NOT else y_s
            S.dma("sp", lambda e, i=i, P=P, dst=dst: e.dma_start(out=dst, in_=yt[i][:P, :]), R=[Byt[i]], is_out=True)
        S.barrier()

    S.finish()
    es.close()
    return nc


def build_sample_attn(cfgb):
    NSEQ, NPG, NPHYS = cfgb["NSEQ"], cfgb["NPG"], cfgb["NPHYS"]
    NTK = NSEQ * 8
    NTL = (NTK + 127) // 128
    nc = bass.Bass("TRN2", target_bir_lowering=False)

    def din(name, shape, dt=F32):
        return nc.dram_tensor(name, list(shape), dt, kind="ExternalInput").ap()

    xs = din("xs", [NTK, D])
    ckv = din("ckv", [NPHYS * 128, 2 * D])
    ptab = din("ptab", [1, NSEQ * NPG], I32)
    iot = din("iot", [128, 1], I32)
    w_in = din("w_in", [D, 7 * D])
    gmix = din("gmix", [1, D])
    gsub = din("gsub", [1, 128])
    lamv = din("lamv", [4, 64])
    ident = din("ident", [128, 128])
    maskn = din("maskn", [8, 128])
    hmask = din("hmask", [128, 8])
    cpos = din("cpos", [128, 64])
    cneg = din("cneg", [128, 64])
    on_all = nc.dram_tensor("on_all", [NTK, D], F32, kind="ExternalOutput").ap()

    es = ExitStack()
    S = Sched(nc, es)

    def sb(name, shape, dt=F32, stack=None):
        return (stack or es).enter_context(nc.sbuf_tensor(name, list(shape), dt))

    def pst(name, shape, dt=F32, stack=None):
        return (stack or es).enter_context(nc.psum_tensor(name, list(shape), dt))

    identb = sb("identb", [128, 128], BF16)
    gb = sb("gb", [128, D], F32)
    gsubb = sb("gsubb", [128, 128], F32)
    epsb = sb("epsb", [128, 1], F32)
    lamb = sb("lamb", [128, 4, 64], F32)
    neglam = sb("neglam", [128, 1], F32)
    tmpc = sb("tmpc", [128, 4], F32)
    scr64 = sb("scr64", [128, 64], F32)
    maskb = sb("maskb", [8, 128], BF16)
    hm = sb("hm", [128, 8], F32)
    comb = sb("comb", [128, 64], F32)
    cng = sb("cng", [128, 64], F32)
    ones2 = sb("ones2", [128, 2], BF16)
    ptb = sb("ptb", [128, NSEQ * NPG], I32)
    iott = sb("iott", [128, 1], I32)
    idx = sb("idx", [128, NSEQ * NPG], I32)
    Bc = Buf()
    S.dma("pool", lambda e: e.dma_start(out=identb[:], in_=ident), W=[Bc])
    S.dma("pool", lambda e: e.dma_start(out=maskb[:], in_=maskn), W=[Bc])
    S.dma("sp", lambda e: e.dma_start(out=gb[:], in_=gmix.partition_broadcast(128)), W=[Bc])
    S.dma("sp", lambda e: e.dma_start(out=gsubb[:], in_=gsub.partition_broadcast(128)), W=[Bc])
    S.dma("sp", lambda e: e.dma_start(out=hm[:], in_=hmask), W=[Bc])
    S.dma("sp", lambda e: e.dma_start(out=comb[:], in_=cpos), W=[Bc])
    S.dma("sp", lambda e: e.dma_start(out=cng[:], in_=cneg), W=[Bc])
    S.dma("sp", lambda e: e.dma_start(out=ptb[:], in_=ptab.partition_broadcast(128)), W=[Bc])
    S.dma("sp", lambda e: e.dma_start(out=iott[:], in_=iot), W=[Bc])
    S.dma("sp", lambda e: e.dma_start(out=lamb[:].rearrange("p a b -> p (a b)"),
                                      in_=lamv.rearrange("a b -> (a b)").rearrange("(o n) -> o n", o=1).partition_broadcast(128)), W=[Bc])
    S.op("dve", lambda: nc.vector.memset(epsb[:], EPS), W=[Bc])
    S.op("dve", lambda: nc.vector.memset(ones2[:], 1.0), W=[Bc])
    S.op("dve", lambda: nc.vector.tensor_scalar(out=idx[:], in0=ptb[:], scalar1=128, scalar2=iott[:, 0:1], op0=ALU.mult, op1=ALU.add), R=[Bc], W=[Bc])
    S.op("dve", lambda: nc.vector.tensor_tensor(out=scr64[:], in0=lamb[:, 0, :], in1=lamb[:, 1, :], op=ALU.mult), R=[Bc], W=[Bc])
    S.op("dve", lambda: nc.vector.reduce_sum(out=tmpc[:, 0:1], in_=scr64[:], axis=AX.X), R=[Bc], W=[Bc])
    S.op("dve", lambda: nc.vector.tensor_tensor(out=scr64[:], in0=lamb[:, 2, :], in1=lamb[:, 3, :], op=ALU.mult), R=[Bc], W=[Bc])
    S.op("dve", lambda: nc.vector.reduce_sum(out=tmpc[:, 1:2], in_=scr64[:], axis=AX.X), R=[Bc], W=[Bc])
    S.op("act", lambda: nc.scalar.activation(out=tmpc[:, 2:4], in_=tmpc[:, 0:2], func=AF.Exp), R=[Bc], W=[Bc])
    S.op("dve", lambda: nc.vector.tensor_tensor(out=neglam[:], in0=tmpc[:, 3:4], in1=tmpc[:, 2:3], op=ALU.subtract), R=[Bc], W=[Bc])
    S.op("dve", lambda: nc.vector.tensor_scalar(out=neglam[:], in0=neglam[:], scalar1=-LAM_INIT, scalar2=None, op0=ALU.add), R=[Bc], W=[Bc])
    S.op("dve", lambda: nc.vector.tensor_scalar(out=gsubb[:], in0=gsubb[:], scalar1=1.0 - LAM_INIT, scalar2=None, op0=ALU.mult), R=[Bc], W=[Bc])
    S.op("dve", lambda: nc.vector.scalar_tensor_tensor(out=comb[:], in0=cng[:], scalar=neglam[:, 0:1], in1=comb[:], op0=ALU.mult, op1=ALU.add), R=[Bc], W=[Bc])
    S.barrier()

    uTs = sb("uTs", [128, KC, NTK], BF16)
    qTa = sb("qTa", [128, NH, NTK], BF16)
    kTa = sb("kTa", [128, NH, NTK], BF16)
    Vn = sb("Vn", [8, NSEQ, D], BF16)
    Bu = Buf(); Bq = Buf(); Bk = Buf(); Bvn = Buf()
    with ExitStack() as ph:
        xt = [sb(f"x{i}", [128, D], F32, ph) for i in range(2)]
        xn = [sb(f"xn{i}", [128, D], BF16, ph) for i in range(2)]
        junk = sb("junk", [128, D], F32, ph)
        ss = sb("ss", [128, 4], F32, ph)
        Bxt = [Buf(), Buf()]; Bxn = [Buf(), Buf()]; Bs = Buf()
        psT = [pst(f"psT{i}", [128, KC, 128], BF16, ph) for i in range(2)]
        BpsT = [PBuf(), PBuf()]
        for t in range(NTL):
            i = t % 2
            P = min(128, NTK - t * 128)
            S.dma("sp", lambda e, i=i, P=P, t=t: e.dma_start(out=xt[i][:P, :], in_=xs[t * 128:t * 128 + P, :]), W=[Bxt[i]])
            S.op("act", lambda i=i, P=P: nc.scalar.activation(out=junk[:P, :], in_=xt[i][:P, :], func=AF.Square, accum_out=ss[:P, 0:1]), R=[Bxt[i]], W=[Bs])
            S.op("act", lambda P=P: nc.scalar.activation(out=ss[:P, 1:2], in_=ss[:P, 0:1], func=AF.Sqrt, scale=1.0 / D, bias=epsb[:P, 0:1]), R=[Bs, Bc], W=[Bs])
            S.op("dve", lambda P=P: nc.vector.reciprocal(out=ss[:P, 2:3], in_=ss[:P, 1:2]), R=[Bs], W=[Bs])
            S.op("dve", lambda i=i, P=P: nc.vector.scalar_tensor_tensor(out=xn[i][:P, :], in0=xt[i][:P, :], scalar=ss[:P, 2:3], in1=gb[:P, :], op0=ALU.mult, op1=ALU.mult),
                 R=[Bxt[i], Bs, Bc], W=[Bxn[i]])
            S.mm([lambda kc=kc, i=i, P=P: nc.tensor.transpose(out=psT[i][:, kc, :P], in_=xn[i][:P, kc * 128:(kc + 1) * 128], identity=identb[:P, :P]) for kc in range(KC)],
                 R=[Bxn[i], Bc], W=[BpsT[i]])
            S.op("act", lambda i=i, P=P, t=t: nc.scalar.copy(out=uTs[:, :, t * 128:t * 128 + P], in_=psT[i][:, :, :P]), R=[BpsT[i]], W=[Bu])
        S.barrier()
    with ExitStack() as ph:
        wq = [sb(f"wq{i}", [128, KC, 128], BF16, ph) for i in range(2)]
        wk = [sb(f"wk{i}", [128, KC, 128], BF16, ph) for i in range(2)]
        Bw = [Buf(), Buf()]
        wv = sb("wv", [128, KC, D], BF16, ph)
        Bwv = Buf()
        ps = [pst(f"ps{i}", [128, 512], F32, ph) for i in range(4)]
        Bps = [PBuf() for _ in range(4)]
        pc = 0
        for kc in range(KC):
            for hf in range(2):
                S.dma("pool", lambda e, kc=kc, hf=hf: e.dma_start(out=wv[:, kc, hf * 512:(hf + 1) * 512], in_=w_in[kc * 128:(kc + 1) * 128, 2 * D + hf * 512:2 * D + (hf + 1) * 512]), W=[Bwv])
        for h in range(NH):
            wi = h % 2
            for kc in range(KC):
                S.dma("pool", lambda e, wi=wi, kc=kc, h=h: e.dma_start(out=wq[wi][:, kc, :], in_=w_in[kc * 128:(kc + 1) * 128, h * 128:(h + 1) * 128]), W=[Bw[wi]])
                S.dma("pool", lambda e, wi=wi, kc=kc, h=h: e.dma_start(out=wk[wi][:, kc, :], in_=w_in[kc * 128:(kc + 1) * 128, D + h * 128:D + (h + 1) * 128]), W=[Bw[wi]])
            p = pc % 4; pc += 1
            S.mm([lambda kc=kc, p=p, wi=wi: nc.tensor.matmul(ps[p][:, :NTK], lhsT=wq[wi][:, kc, :], rhs=uTs[:, kc, :], start=(kc == 0), stop=(kc == KC - 1)) for kc in range(KC)],
                 R=[Bw[wi], Bu], W=[Bps[p]])
            S.op("dve", lambda p=p, h=h: nc.vector.tensor_scalar(out=qTa[:, h, :], in0=ps[p][:, :NTK], scalar1=0.125, scalar2=None, op0=ALU.mult), R=[Bps[p]], W=[Bq])
            p = pc % 4; pc += 1
            S.mm([lambda kc=kc, p=p, wi=wi: nc.tensor.matmul(ps[p][:, :NTK], lhsT=wk[wi][:, kc, :], rhs=uTs[:, kc, :], start=(kc == 0), stop=(kc == KC - 1)) for kc in range(KC)],
                 R=[Bw[wi], Bu], W=[Bps[p]])
            S.op("act", lambda p=p, h=h: nc.scalar.copy(out=kTa[:, h, :], in_=ps[p][:, :NTK]), R=[Bps[p]], W=[Bk])
        for s_ in range(NSEQ):
            for hf in range(2):
                p = pc % 4; pc += 1
                S.mm([lambda kc=kc, p=p, s_=s_, hf=hf: nc.tensor.matmul(ps[p][:8, :], lhsT=uTs[:, kc, 8 * s_:8 * s_ + 8], rhs=wv[:, kc, hf * 512:(hf + 1) * 512],
                                                                        start=(kc == 0), stop=(kc == KC - 1)) for kc in range(KC)], R=[Bwv, Bu], W=[Bps[p]])
                S.op("dve" if hf else "act", (lambda p=p, s_=s_, hf=hf: nc.vector.tensor_copy(out=Vn[:8, s_, hf * 512:(hf + 1) * 512], in_=ps[p][:8, :])) if hf else
                     (lambda p=p, s_=s_, hf=hf: nc.scalar.copy(out=Vn[:8, s_, hf * 512:(hf + 1) * 512], in_=ps[p][:8, :])), R=[Bps[p]], W=[Bvn])
        S.barrier()
    with ExitStack() as ph:
        NB3 = 4
        KVp = [sb(f"KVp{i}", [128, 2 * D], BF16, ph) for i in range(NB3)]
        BKp = [Buf() for _ in range(NB3)]
        BVp = BKp
        KT = [sb(f"KT{i}", [128, NH, 128], BF16, ph) for i in range(3)]
        BKT = [Buf(), Buf(), Buf()]
        PTs = [sb(f"PTs{i}", [128, 128], BF16, ph) for i in range(3)]
        BPTs = [Buf(), Buf(), Buf()]
        Qbd = [sb(f"Qbd{i}", [128, NH, 16], BF16, ph) for i in range(2)]
        BQbd = [Buf(), Buf()]
        for i in range(2):
            S.op("pool", lambda i=i: nc.gpsimd.memset(Qbd[i][:], 0.0), W=[BQbd[i]])
        o_n = sb("o_n", [128, 128], F32, ph)
        o_c = sb("o_c", [64, 128], F32, ph)
        o_f = [sb(f"o_f{i}", [64, 128], F32, ph) for i in range(2)]
        rc = sb("rc", [128, 4], F32, ph)
        junk = sb("junk2", [64, 128], F32, ph)
        Bfin = Buf(); Bof = [Buf(), Buf()]
        psT = [pst(f"psK{i}", [128, NH, 128], BF16, ph) for i in range(2)]
        BpsT = [PBuf(), PBuf()]
        pS = [pst(f"pS{i}", [128, 128], F32, ph) for i in range(2)]
        BpS = [PBuf(), PBuf()]
        acc = pst("acc", [128, 2, 512], F32, ph)
        Bacc = PBuf()
        accs = pst("accs", [128, 2], F32, ph)
        Baccs = PBuf()
        pcm = pst("pcm", [64, 128], F32, ph)
        Bpcm = PBuf()
        steps = [(s_, k) for s_ in range(NSEQ) for k in range(NPG + 1)]
        NST = len(steps)

        def bufs(g):
            return g % NB3, g % 2, g % 3

        def stage_T(g):
            s_, k = steps[g]
            b3, b2, bk = bufs(g)
            qb = s_ % 2
            if k == 0:
                S.op("pool", lambda: nc.gpsimd.tensor_copy(out=Qbd[qb][0:64, :, 0:8], in_=qTa[0:64, :, 8 * s_:8 * s_ + 8]), R=[Bq], W=[BQbd[qb]])
                S.op("pool", lambda: nc.gpsimd.tensor_copy(out=Qbd[qb][64:128, :, 8:16], in_=qTa[64:128, :, 8 * s_:8 * s_ + 8]), R=[Bq], W=[BQbd[qb]])
            if k == NPG:
                return
            col = s_ * NPG + k
            S.dma("pool", lambda e: e.indirect_dma_start(out=KVp[b3][:], out_offset=None, in_=ckv,
                                                          in_offset=bass.IndirectOffsetOnAxis(ap=idx[:, col:col + 1], axis=0)), R=[Bc], W=[BKp[b3]])
            S.mm([lambda h=h: nc.tensor.transpose(out=psT[b2][:, h, :], in_=KVp[b3][:, h * 128:(h + 1) * 128], identity=identb[:, :]) for h in range(NH)],
                 R=[BKp[b3], Bc], W=[BpsT[b2]])
            if g % 2:
                S.op("act", lambda: nc.scalar.copy(out=KT[bk][:], in_=psT[b2][:]), R=[BpsT[b2]], W=[BKT[bk]])
            else:
                S.op("dve", lambda: nc.vector.tensor_copy(out=KT[bk][:], in_=psT[b2][:]), R=[BpsT[b2]], W=[BKT[bk]])

        def stage_S(g):
            s_, k = steps[g]
            b3, b2, bk = bufs(g)
            qb = s_ % 2
            if k < NPG:
                P = 128
                S.mm([lambda h=h: nc.tensor.matmul(pS[b2][:, h * 16:(h + 1) * 16], lhsT=KT[bk][:, h, :], rhs=Qbd[qb][:, h, :], start=True, stop=True) for h in range(NH)],
                     R=[BKT[bk], BQbd[qb]], W=[BpS[b2]])
            else:
                P = 8
                S.mm([lambda h=h: nc.tensor.matmul(pS[b2][:8, h * 16:(h + 1) * 16], lhsT=kTa[:, h, 8 * s_:8 * s_ + 8], rhs=Qbd[qb][:, h, :], start=True, stop=True)
                      for h in range(NH)], R=[Bk, BQbd[qb]], W=[BpS[b2]])
            S.op("act", lambda: nc.scalar.activation(out=PTs[bk][:P, :], in_=pS[b2][:P, :], func=AF.Exp), R=[BpS[b2]], W=[BPTs[bk]])
            if k == NPG:
                S.op("dve", lambda: nc.vector.tensor_tensor(out=PTs[bk][:8, :], in0=PTs[bk][:8, :], in1=maskb[:, :], op=ALU.mult), R=[BPTs[bk], Bc], W=[BPTs[bk]])

        def stage_PV(g):
            s_, k = steps[g]
            b3, b2, bk = bufs(g)
            new = (k == NPG)
            if new:
                P = 8
                vsrc = [Vn[:8, s_, 0:512], Vn[:8, s_, 512:1024]]
                Rv = [Bvn]
            else:
                P = 128
                vsrc = [KVp[b3][:, D:D + 512], KVp[b3][:, D + 512:2 * D]]
                Rv = [BVp[b3]]
            S.mm([lambda hf=hf: nc.tensor.matmul(acc[:, hf, :], lhsT=PTs[bk][:P, :], rhs=vsrc[hf], start=(k == 0), stop=new) for hf in range(2)]
                 + [lambda: nc.tensor.matmul(accs[:, :], lhsT=PTs[bk][:P, :], rhs=ones2[:P, :], start=(k == 0), stop=new)],
                 R=[BPTs[bk]] + Rv + [Bc], W=[Bacc, Baccs])
            if new:
                finalize(s_)

        def finalize(s_):
            S.op("dve", lambda: nc.vector.tensor_scalar(out=o_n[:, :], in0=acc[:, 0, 0:128], scalar1=hm[:, 0:1], scalar2=None, op0=ALU.mult), R=[Bacc, Bc], W=[Bfin])
            for h2 in range(1, NH):
                S.op("dve", lambda h2=h2: nc.vector.scalar_tensor_tensor(out=o_n[:, :], in0=acc[:, h2 // 4, (h2 % 4) * 128:(h2 % 4 + 1) * 128], scalar=hm[:, h2:h2 + 1], in1=o_n[:, :],
                                                                          op0=ALU.mult, op1=ALU.add), R=[Bacc, Bc, Bfin], W=[Bfin])
            S.op("dve", lambda: nc.vector.reciprocal(out=rc[:, 0:1], in_=accs[:, 0:1]), R=[Baccs], W=[Bfin])
            S.op("dve", lambda: nc.vector.tensor_scalar(out=o_n[:, :], in0=o_n[:, :], scalar1=rc[:, 0:1], scalar2=None, op0=ALU.mult), R=[Bfin], W=[Bfin])
            S.mm([lambda: nc.tensor.matmul(pcm[:, :], lhsT=comb[:, :], rhs=o_n[:, :], start=True, stop=True)], R=[Bfin, Bc], W=[Bpcm])
            fb = s_ % 2
            S.op("dve", lambda: nc.vector.tensor_copy(out=o_c[:, :], in_=pcm[:, :]), R=[Bpcm], W=[Bfin])
            S.op("act", lambda: nc.scalar.activation(out=junk[:, :], in_=o_c[:, :], func=AF.Square, accum_out=rc[:64, 1:2]), R=[Bfin], W=[Bfin])
            S.op("act", lambda: nc.scalar.activation(out=rc[:64, 2:3], in_=rc[:64, 1:2], func=AF.Sqrt, scale=1.0 / 128, bias=epsb[:64, 0:1]), R=[Bfin, Bc], W=[Bfin])
            S.op("dve", lambda: nc.vector.reciprocal(out=rc[:64, 2:3], in_=rc[:64, 2:3]), R=[Bfin], W=[Bfin])
            S.op("dve", lambda: nc.vector.scalar_tensor_tensor(out=o_f[fb][:, :], in0=o_c[:, :], scalar=rc[:64, 2:3], in1=gsubb[:64, :], op0=ALU.mult, op1=ALU.mult),
                 R=[Bfin, Bc], W=[Bof[fb]])
            for h in range(NH):
                S.dma("sp", lambda e, h=h: e.dma_start(out=on_all[8 * s_:8 * s_ + 8, h * 128:(h + 1) * 128], in_=o_f[fb][8 * h:8 * h + 8, :]), R=[Bof[fb]], is_out=True)

        for i in range(NST + 2):
            if i < NST:
                stage_T(i)
            if 0 <= i - 1 < NST:
                stage_S(i - 1)
            if 0 <= i - 2 < NST:
                stage_PV(i - 2)
        S.barrier()
    S.finish()
    es.close()
    return nc


def make_in_maps(inp, cfg):
    TP, TO, NS = cfg["TP"], cfg["TO"], cfg["NS"]
    f32 = np.float32
    xp = np.asarray(inp["x_prompt"], f32)
    B = xp.shape[0]
    vecs = np.stack([np.asarray(inp[k], f32).reshape(-1) for k in
                     ("g_mix", "g_ffn", "g_ple", "g_final", "lru_lambda", "b_conv", "b_gate_a", "b_gate_x")])
    lamv = np.stack([np.asarray(inp[k], f32).reshape(-1) for k in ("lambda_q1", "lambda_k1", "lambda_q2", "lambda_k2")])
    common = {
        "w_in": np.asarray(inp["w_in"], f32)[0], "w_attn": np.asarray(inp["w_attn_br"], f32)[0],
        "w_rec": np.asarray(inp["w_rec_br"], f32)[0], "w_out": np.asarray(inp["w_out"], f32)[0],
        "w_fg": np.asarray(inp["w_ffn_gate"], f32)[0], "w_fu": np.asarray(inp["w_ffn_up"], f32)[0],
        "w_fd": np.asarray(inp["w_ffn_down"], f32)[0], "w_pg": np.asarray(inp["w_ple_gate"], f32)[0],
        "w_pp": np.asarray(inp["w_ple_proj"], f32)[0], "w_ga": np.asarray(inp["w_gate_a"], f32)[0],
        "w_gx": np.asarray(inp["w_gate_x"], f32)[0], "vecs": vecs, "wconv": np.asarray(inp["w_conv"], f32)[0],
        "gsub": np.asarray(inp["g_subln"], f32).reshape(1, 128), "lamv": lamv,
        "ident": np.eye(128, dtype=f32), "tri": np.triu(np.ones((128, 128), f32)),
    }
    maps = []
    xsamp = np.asarray(inp["x_sample"], f32)
    psamp = np.asarray(inp["p_sample"], f32)[0]
    pprm = np.asarray(inp["p_prompt"], f32)[0]
    for c in range(8):
        b, half = c // 2, c % 2
        m = dict(common)
        xfull = np.zeros((TP + TO, D), f32)
        if half == 1:
            xfull[:TP] = xp[b, :TP]
        xfull[TP:] = xp[b, half * TO:(half + 1) * TO]
        m["xf"] = xfull
        m["pown"] = np.ascontiguousarray(pprm[b, half * TO:(half + 1) * TO])
        m["xs"] = np.ascontiguousarray(xsamp[4 * c:4 * c + 4].reshape(NS, D))
        m["psm"] = np.ascontiguousarray(psamp[4 * c:4 * c + 4].reshape(NS, 256))
        m["flag"] = np.full((128, 1), float(half), f32)
        m["sconv"] = np.ascontiguousarray(np.asarray(inp["state_conv"], f32)[0, 4 * c:4 * c + 4].reshape(12, D))
        m["sh"] = np.ascontiguousarray(np.asarray(inp["state_h"], f32)[0, 4 * c:4 * c + 4])
        maps.append(m)
    return maps


_CACHE = {}


NCB = 2


def make_sample_maps(inp):
    f32 = np.float32
    ck = np.asarray(inp["cache_k"], f32)
    cv = np.asarray(inp["cache_v"], f32)
    nphys = ck.shape[1]
    ckv = np.concatenate([ck.reshape(nphys * 128, D), cv.reshape(nphys * 128, D)], axis=1)
    pt = np.asarray(inp["page_table"], np.int32)
    nseq, npg = pt.shape
    spc = nseq // NCB
    hmask = np.zeros((128, 8), f32)
    cpos = np.zeros((128, 64), f32)
    cneg = np.zeros((128, 64), f32)
    maskn = np.zeros((8, 128), f32)
    for h in range(8):
        for c in range(2):
            for q in range(8):
                r = h * 16 + c * 8 + q
                hmask[r, h] = 1.0
                (cpos if c == 0 else cneg)[r, h * 8 + q] = 1.0
                for j in range(8):
                    if j <= q:
                        maskn[j, r] = 1.0
    lamv = np.stack([np.asarray(inp[k], f32).reshape(-1) for k in ("lambda_q1", "lambda_k1", "lambda_q2", "lambda_k2")])
    xs = np.asarray(inp["x_sample"], f32)
    maps = []
    for c in range(NCB):
        maps.append({
            "xs": np.ascontiguousarray(xs[c * spc:(c + 1) * spc].reshape(spc * 8, D)),
            "ckv": ckv,
            "ptab": np.ascontiguousarray(pt[c * spc:(c + 1) * spc].reshape(1, spc * npg)),
            "iot": np.arange(128, dtype=np.int32).reshape(128, 1),
            "w_in": np.asarray(inp["w_in"], f32)[0], "gmix": np.asarray(inp["g_mix"], f32).reshape(1, D),
            "gsub": np.asarray(inp["g_subln"], f32).reshape(1, 128), "lamv": lamv,
            "ident": np.eye(128, dtype=f32), "maskn": maskn, "hmask": hmask, "cpos": cpos, "cneg": cneg,
        })
    return maps, {"NSEQ": spc, "NPG": npg, "NPHYS": nphys}


def kernel(**inputs):
    import os
    xp = inputs["x_prompt"]
    SEQ = xp.shape[1]
    cfg = {"TP": SEQ // 2, "TO": SEQ // 2, "NS": 32}
    if "KSTOP" in os.environ:
        cfg["stop"] = int(os.environ["KSTOP"])
    mbs, cfgb = make_sample_maps(inputs)
    keyb = ("B", cfgb["NSEQ"], cfgb["NPG"], cfgb["NPHYS"])
    if keyb not in _CACHE:
        _CACHE[keyb] = build_sample_attn(cfgb)
    resb = run_bass_kernel_spmd(_CACHE[keyb], mbs, core_ids=list(range(NCB))).results
    del mbs
    on_all = np.concatenate([np.asarray(r["on_all"], np.float32) for r in resb], axis=0)
    key = ("A", SEQ, cfg.get("stop"))
    if key not in _CACHE:
        _CACHE[key] = build(cfg)
    nc = _CACHE[key]
    maps = make_in_maps(inputs, cfg)
    for c in range(8):
        maps[c]["on_s"] = np.ascontiguousarray(on_all[32 * c:32 * (c + 1)])
    res = run_bass_kernel_spmd(nc, maps, core_ids=list(range(8))).results
    B = xp.shape[0]
    TO = cfg["TO"]
    f32 = np.float32

    def own(name):
        o = np.zeros((B, SEQ, D), f32)
        for c in range(8):
            o[c // 2, (c % 2) * TO:(c % 2 + 1) * TO] = res[c][name]
        return o

    def samp(name, per):
        return np.concatenate([res[c][name].reshape(4, per, -1) for c in range(8)], axis=0)
    y_prompt = own("y_own")
    y_sample = samp("y_s", 8)
    k_prompt = own("k_own").reshape(1, B, SEQ, NH, 2, 64)
    v_prompt = own("v_own").reshape(1, B, SEQ, NH, 128)
    conv_prompt = np.stack([res[2 * b + 1]["conv_p"] for b in range(B)])[None]
    h_prompt = np.stack([res[2 * b + 1]["h_p"][0] for b in range(B)])[None]
    k_sample = samp("k_s", 8).reshape(1, 32, 8, NH, 2, 64)
    v_sample = samp("v_s", 8).reshape(1, 32, 8, NH, 128)
    conv_sample = samp("conv_s", 3)[None]
    h_sample = np.concatenate([res[c]["h_s"] for c in range(8)], axis=0)[None]
    return (y_prompt, y_sample, k_prompt, v_prompt, conv_prompt, h_prompt, k_sample, v_sample, conv_sample, h_sample)
```

```python
import math
from contextlib import ExitStack
import numpy as np
import concourse.bass as bass
import concourse.mybir as mybir
from concourse.bass_utils import run_bass_kernel_spmd

F32 = mybir.dt.float32
BF16 = mybir.dt.bfloat16
I32 = mybir.dt.int32
AF = mybir.ActivationFunctionType
ALU = mybir.AluOpType
AX = mybir.AxisListType

D = 1024
NH = 8
KC = 8
DFF = 2816
FC = 22
EPS = 1e-6
LAM_INIT = 0.8 - 0.6 * math.exp(-0.3 * 0)


class Buf:
    __slots__ = ("w", "r", "excl")

    def __init__(self, excl=False):
        self.w = None
        self.r = {}
        self.excl = excl


def PBuf():
    return Buf(True)


class Sched:
    def __init__(self, nc, es, nds=20):
        self.nc = nc
        self.sems = []
        self.E = {}
        for name, e in (("pe", nc.tensor), ("act", nc.scalar), ("dve", nc.vector),
                        ("pool", nc.gpsimd), ("sp", nc.sync)):
            sid = self._new_sem(es, name)
            self.E[name] = {"e": e, "sid": sid, "cnt": 0, "waited": {}}
        self.dq = {}
        for q in ("sp", "pool", "act"):
            self.dq[q] = {"sids": [self._new_sem(es, f"d{q}{i}") for i in range(nds)], "next": 0}
        self.dval = {}
        self.out_toks = []

    def _new_sem(self, es, name):
        s = es.enter_context(self.nc.semaphore(name))
        self.sems.append(s)
        return len(self.sems) - 1

    def _deps(self, R, W):
        deps = {}

        def add(t):
            if t is None:
                return
            if deps.get(t[0], 0) < t[1]:
                deps[t[0]] = t[1]
        for b in R:
            add(b.w)
            if b.excl:
                for s, v in b.r.items():
                    add((s, v))
        for b in W:
            add(b.w)
            for s, v in b.r.items():
                add((s, v))
        return deps

    def _emit_waits(self, en, deps, skip_own=False):
        E = self.E[en]
        for s, v in deps.items():
            if skip_own and s == E["sid"]:
                continue
            if E["waited"].get(s, 0) < v:
                E["e"].wait_ge(self.sems[s], v)
                E["waited"][s] = v

    def _record(self, tok, R, W):
        for b in R:
            if b.r.get(tok[0], 0) < tok[1]:
                b.r[tok[0]] = tok[1]
        for b in W:
            b.w = tok
            b.r = {}

    def op(self, en, fn, R=(), W=()):
        E = self.E[en]
        self._emit_waits(en, self._deps(R, W), skip_own=(en == "pe"))
        ins = fn()
        E["cnt"] += 1
        ins.then_inc(self.sems[E["sid"]], 1)
        self._record((E["sid"], E["cnt"]), R, W)

    def mm(self, fns, R=(), W=()):
        E = self.E["pe"]
        self._emit_waits("pe", self._deps(R, W), skip_own=True)
        ins = None
        for fn in fns:
            ins = fn()
        E["cnt"] += 1
        ins.then_inc(self.sems[E["sid"]], 1)
        self._record((E["sid"], E["cnt"]), R, W)

    def dma(self, q, fn, R=(), W=(), is_out=False):
        Q = self.dq[q]
        sid = Q["sids"][Q["next"]]
        Q["next"] = (Q["next"] + 1) % len(Q["sids"])
        deps = self._deps(R, W)
        prev = self.dval.get(sid, 0)
        if prev:
            if deps.get(sid, 0) < prev:
                deps[sid] = prev
        self._emit_waits(q, deps)
        ins = fn(self.E[q]["e"])
        val = prev + 16
        self.dval[sid] = val
        ins.then_inc(self.sems[sid], 16)
        tok = (sid, val)
        self._record(tok, R, W)
        if is_out:
            self.out_toks.append(tok)

    def barrier(self):
        toks = {}
        for en, E in self.E.items():
            if E["cnt"]:
                toks[E["sid"]] = E["cnt"]
        for sid, v in self.dval.items():
            toks[sid] = v
        for en in self.E:
            self._emit_waits(en, toks)

    def finish(self):
        toks = {}
        for t in self.out_toks:
            if toks.get(t[0], 0) < t[1]:
                toks[t[0]] = t[1]
        self._emit_waits("sp", toks)


def build(cfg):
    TP, TO, NS = cfg["TP"], cfg["TO"], cfg["NS"]
    TF = TP + TO
    NT = TF + NS
    NTT = TF // 128
    OT0 = TP // 128
    NOT = TO // 128
    NQG = TO // 512
    NOS = TO + NS
    nc = bass.Bass("TRN2", target_bir_lowering=False)

    def din(name, shape, dt=F32):
        return nc.dram_tensor(name, list(shape), dt, kind="ExternalInput").ap()

    def dout(name, shape, dt=F32):
        return nc.dram_tensor(name, list(shape), dt, kind="ExternalOutput").ap()

    xf = din("xf", [TF, D])
    xs = din("xs", [NS, D])
    pown = din("pown", [TO, 256])
    psm = din("psm", [NS, 256])
    flag = din("flag", [128, 1])
    sconv = din("sconv", [12, D])
    sh = din("sh", [4, D])
    w_in = din("w_in", [D, 7 * D])
    w_attn = din("w_attn", [D, D])
    w_rec = din("w_rec", [D, D])
    w_out = din("w_out", [D, D])
    w_fg = din("w_fg", [D, DFF])
    w_fu = din("w_fu", [D, DFF])
    w_fd = din("w_fd", [DFF, D])
    w_pg = din("w_pg", [D, D])
    w_pp = din("w_pp", [256, D])
    w_ga = din("w_ga", [8, 128, 128])
    w_gx = din("w_gx", [8, 128, 128])
    vecs = din("vecs", [8, D])
    wconv = din("wconv", [4, D])
    gsub = din("gsub", [1, 128])
    lamv = din("lamv", [4, 64])
    ident = din("ident", [128, 128])
    tri = din("tri", [128, 128])
    on_s_in = din("on_s", [NS, D])

    y_own = dout("y_own", [TO, D])
    y_s = dout("y_s", [NS, D])
    k_own = dout("k_own", [TO, D])
    v_own = dout("v_own", [TO, D])
    k_s = dout("k_s", [NS, D])
    v_s = dout("v_s", [NS, D])
    conv_p = dout("conv_p", [3, D])
    h_p = dout("h_p", [1, D])
    conv_s = dout("conv_s", [12, D])
    h_s = dout("h_s", [4, D])
    kT_d = nc.dram_tensor("kT_d", [NH, 128, TF], BF16, kind="Internal").ap()
    v_d = nc.dram_tensor("v_d", [NH, 128, NTT, 130], BF16, kind="Internal").ap()

    es = ExitStack()
    S = Sched(nc, es)

    def sb(name, shape, dt=F32, stack=None):
        return (stack or es).enter_context(nc.sbuf_tensor(name, list(shape), dt))

    def pst(name, shape, dt=F32, stack=None):
        return (stack or es).enter_context(nc.psum_tensor(name, list(shape), dt))

    identb = sb("identb", [128, 128], BF16)
    identf = sb("identf", [128, 128], F32)
    trib = sb("trib", [128, 128], BF16)
    gb = sb("gb", [128, 4, D], F32)
    gsubb = sb("gsubb", [128, 128], F32)
    vT = sb("vT", [128, 4, KC], F32)
    wcT = sb("wcT", [128, 4, KC], F32)
    clam = sb("clam", [128, KC], F32)
    clam2 = sb("clam2", [128, KC], F32)
    epsb = sb("epsb", [128, 1], F32)
    flg = sb("flg", [128, 1], F32)
    lamb = sb("lamb", [128, 4, 64], F32)
    neglam = sb("neglam", [128, 1], F32)
    tmpc = sb("tmpc", [128, 4], F32)
    B_const = Buf()
    ld = []
    S.dma("pool", lambda e: e.dma_start(out=identb[:], in_=ident), W=[B_const])
    S.dma("sp", lambda e: e.dma_start(out=identf[:], in_=ident), W=[B_const])
    S.dma("pool", lambda e: e.dma_start(out=trib[:], in_=tri), W=[B_const])
    for i in range(4):
        S.dma("sp", lambda e, i=i: e.dma_start(out=gb[:, i, :], in_=vecs[i:i + 1, :].partition_broadcast(128)), W=[B_const])
    S.dma("sp", lambda e: e.dma_start(out=gsubb[:], in_=gsub.partition_broadcast(128)), W=[B_const])
    with nc.allow_non_contiguous_dma(reason="tiny parameter vectors to feature-major"):
        for i in range(4):
            S.dma("sp", lambda e, i=i: e.dma_start(out=vT[:, i, :], in_=vecs[4 + i, :].rearrange("(c p) -> p c", p=128)), W=[B_const])
            S.dma("sp", lambda e, i=i: e.dma_start(out=wcT[:, i, :], in_=wconv[i, :].rearrange("(c p) -> p c", p=128)), W=[B_const])
    S.dma("sp", lambda e: e.dma_start(out=flg[:], in_=flag), W=[B_const])
    S.dma("sp", lambda e: e.dma_start(out=lamb[:].rearrange("p a b -> p (a b)"),
                                      in_=lamv.rearrange("a b -> (a b)").rearrange("(o n) -> o n", o=1).partition_broadcast(128)), W=[B_const])
    S.op("dve", lambda: nc.vector.memset(epsb[:], EPS), W=[B_const])
    scr64 = sb("scr64", [128, 64], F32)
    S.op("dve", lambda: nc.vector.tensor_tensor(out=scr64[:], in0=lamb[:, 0, :], in1=lamb[:, 1, :], op=ALU.mult), R=[B_const], W=[B_const])
    S.op("dve", lambda: nc.vector.reduce_sum(out=tmpc[:, 0:1], in_=scr64[:], axis=AX.X), R=[B_const], W=[B_const])
    S.op("dve", lambda: nc.vector.tensor_tensor(out=scr64[:], in0=lamb[:, 2, :], in1=lamb[:, 3, :], op=ALU.mult), R=[B_const], W=[B_const])
    S.op("dve", lambda: nc.vector.reduce_sum(out=tmpc[:, 1:2], in_=scr64[:], axis=AX.X), R=[B_const], W=[B_const])
    S.op("act", lambda: nc.scalar.activation(out=tmpc[:, 2:4], in_=tmpc[:, 0:2], func=AF.Exp), R=[B_const], W=[B_const])
    S.op("dve", lambda: nc.vector.tensor_tensor(out=neglam[:], in0=tmpc[:, 3:4], in1=tmpc[:, 2:3], op=ALU.subtract), R=[B_const], W=[B_const])
    S.op("dve", lambda: nc.vector.tensor_scalar(out=neglam[:], in0=neglam[:], scalar1=-LAM_INIT, scalar2=None, op0=ALU.add), R=[B_const], W=[B_const])
    S.op("dve", lambda: nc.vector.tensor_scalar(out=gsubb[:], in0=gsubb[:], scalar1=1.0 - LAM_INIT, scalar2=None, op0=ALU.mult), R=[B_const], W=[B_const])
    S.op("act", lambda: nc.scalar.activation(out=clam[:], in_=vT[:, 0, :], func=AF.Exp, scale=-1.0), R=[B_const], W=[B_const])
    S.op("dve", lambda: nc.vector.tensor_scalar(out=clam[:], in0=clam[:], scalar1=1.0, scalar2=None, op0=ALU.add), R=[B_const], W=[B_const])
    S.op("act", lambda: nc.scalar.activation(out=clam[:], in_=clam[:], func=AF.Ln), R=[B_const], W=[B_const])
    S.op("dve", lambda: nc.vector.tensor_scalar(out=clam2[:], in0=clam[:], scalar1=-16.0, scalar2=None, op0=ALU.mult), R=[B_const], W=[B_const])
    S.op("dve", lambda: nc.vector.tensor_scalar(out=clam[:], in0=clam[:], scalar1=-8.0, scalar2=None, op0=ALU.mult), R=[B_const], W=[B_const])
    S.barrier()
    if cfg.get("stop") == 0:
        S.finish(); es.close(); return nc

    uT = sb("uT", [128, KC, NT], BF16)
    B_uT = [Buf() for _ in range(NTT + 1)]

    def rmsnorm_rows(stk, x_ap, P, gidx, out_ap, Bx, Bo, tag):
        junk = rmsnorm_rows.junk
        ss = rmsnorm_rows.ss
        Bs = rmsnorm_rows.Bs
        S.op("act", lambda: nc.scalar.activation(out=junk[:P, :], in_=x_ap, func=AF.Square, accum_out=ss[:P, 0:1]), R=[Bx], W=[Bs])
        S.op("act", lambda: nc.scalar.activation(out=ss[:P, 1:2], in_=ss[:P, 0:1], func=AF.Sqrt, scale=1.0 / D, bias=epsb[:P, 0:1]), R=[Bs], W=[Bs])
        S.op("dve", lambda: nc.vector.reciprocal(out=ss[:P, 2:3], in_=ss[:P, 1:2]), R=[Bs], W=[Bs])
        S.op("dve", lambda: nc.vector.scalar_tensor_tensor(out=out_ap, in0=x_ap, scalar=ss[:P, 2:3], in1=gb[:P, gidx, :],
                                                             op0=ALU.mult, op1=ALU.mult), R=[Bx, Bs], W=[Bo])
    rmsnorm_rows.junk = sb("nrm_junk", [128, D], F32)
    rmsnorm_rows.ss = sb("nrm_ss", [128, 4], F32)
    rmsnorm_rows.Bs = Buf()

    def transpose_to_uT(stk, xn, P, col0, Bxn, Bu, psT, BpsT):
        S.mm([lambda kc=kc: nc.tensor.transpose(out=psT[:, kc, :P], in_=xn[:P, kc * 128:(kc + 1) * 128], identity=identb[:P, :P])
              for kc in range(KC)], R=[Bxn], W=[BpsT])
        S.op("act", lambda: nc.scalar.copy(out=uT[:, :, col0:col0 + P], in_=psT[:, :, :P]), R=[BpsT], W=[Bu])

    with ExitStack() as ph:
        xt = [sb(f"a0_x{i}", [128, D], F32, ph) for i in range(2)]
        Bxt = [Buf(), Buf()]
        xn = [sb(f"a0_xn{i}", [128, D], BF16, ph) for i in range(2)]
        Bxn = [Buf(), Buf()]
        psT = [pst(f"a0_ps{i}", [128, KC, 128], BF16, ph) for i in range(2)]
        BpsT = [PBuf(), PBuf()]
        for t in range(NTT + 1):
            i = t % 2
            P = 128 if t < NTT else NS
            src = xf[t * 128:(t + 1) * 128, :] if t < NTT else xs
            S.dma("sp", lambda e, i=i, P=P, src=src: e.dma_start(out=xt[i][:P, :], in_=src), W=[Bxt[i]])
            rmsnorm_rows(ph, xt[i][:P, :], P, 0, xn[i][:P, :], Bxt[i], Bxn[i], "a0")
            transpose_to_uT(ph, xn[i], P, t * 128, Bxn[i], B_uT[t], psT[i], BpsT[i])
        S.barrier()

    if cfg.get("stop") == 1:
        S.finish(); es.close(); return nc
    kTs = sb("kTs", [128, NH, NS], BF16)
    B_kTs = Buf()
    with ExitStack() as ph:
        wk = [sb(f"a1_wk{i}", [128, KC, 512], BF16, ph) for i in range(2)]
        Bwk = [Buf(), Buf()]
        wv = [sb(f"a1_wv{i}", [128, KC, 512], BF16, ph) for i in range(2)]
        Bwv = [Buf(), Buf()]
        kst = [sb(f"a1_kst{i}", [128, 512], BF16, ph) for i in range(2)]
        Bkst = [Buf(), Buf()]
        vst = [sb(f"a1_vst{i}", [128, 4, 130], BF16, ph) for i in range(2)]
        Bvst = [Buf(), Buf()]
        fst = [sb(f"a1_fst{i}", [128, 512], F32, ph) for i in range(4)]
        Bfst = [Buf() for _ in range(4)]
        ps = [pst(f"a1_ps{i}", [128, 512], F32, ph) for i in range(4)]
        Bps = [PBuf() for _ in range(4)]
        ones_c = sb("a1_ones", [128, 1], F32, ph)
        B1 = Buf()
        S.op("dve", lambda: nc.vector.memset(ones_c[:], 1.0), W=[B1])
        pi = 0
        fi = 0
        for hh in range(2):
            wi = hh % 2
            for kc in range(KC):
                S.dma("pool", lambda e, wi=wi, hh=hh, kc=kc: e.dma_start(
                    out=wk[wi][:, kc, :], in_=w_in[kc * 128:(kc + 1) * 128, D + hh * 512:D + (hh + 1) * 512]), W=[Bwk[wi]])
                S.dma("pool", lambda e, wi=wi, hh=hh, kc=kc: e.dma_start(
                    out=wv[wi][:, kc, :], in_=w_in[kc * 128:(kc + 1) * 128, 2 * D + hh * 512:2 * D + (hh + 1) * 512]), W=[Bwv[wi]])
            for hl in range(4 if cfg.get("stop") != 2 else 0):
                h = hh * 4 + hl
                c0 = 0
                while c0 < NT:
                    cw = min(512, NT - c0)
                    tiles = sorted(set([min(c // 128, NTT) for c in range(c0, c0 + cw, 128)]))
                    p = pi % 4
                    pi += 1
                    S.mm([lambda kc=kc, p=p, c0=c0, cw=cw, hl=hl, wi=wi: nc.tensor.matmul(
                        ps[p][:, :cw], lhsT=wk[wi][:, kc, hl * 128:(hl + 1) * 128], rhs=uT[:, kc, c0:c0 + cw],
                        start=(kc == 0), stop=(kc == KC - 1)) for kc in range(KC)],
                        R=[Bwk[wi]] + [B_uT[t] for t in tiles], W=[Bps[p]])
                    if c0 < TF:
                        ki = (pi) % 2
                        S.op("act", lambda ki=ki, p=p, cw=cw: nc.scalar.copy(out=kst[ki][:, :cw], in_=ps[p][:, :cw]), R=[Bps[p]], W=[Bkst[ki]])
                        if not cfg.get("noscr"):
                            S.dma("sp", lambda e, ki=ki, h=h, c0=c0, cw=cw: e.dma_start(out=kT_d[h, :, c0:c0 + cw], in_=kst[ki][:, :cw]), R=[Bkst[ki]])
                    else:
                        S.op("act", lambda p=p, cw=cw, h=h: nc.scalar.copy(out=kTs[:, h, :], in_=ps[p][:, :cw]), R=[Bps[p]], W=[B_kTs])
                    c0 += cw
            for t in range(NTT + 1 if cfg.get("stop") not in (2, 3) else 0):
                P = 128 if t < NTT else NS
                col0 = t * 128
                own = t >= OT0
                p = pi % 4
                pi += 1
                S.mm([lambda kc=kc, p=p, P=P, col0=col0, wi=wi: nc.tensor.matmul(
                    ps[p][:P, :], lhsT=uT[:, kc, col0:col0 + P], rhs=wv[wi][:, kc, :],
                    start=(kc == 0), stop=(kc == KC - 1)) for kc in range(KC)], R=[Bwv[wi], B_uT[t]], W=[Bps[p]])
                if t < NTT:
                    vi = t % 2
                    S.op("dve", lambda vi=vi, p=p: nc.vector.tensor_copy(out=vst[vi][:, :, 0:128], in_=ps[p][:].rearrange("p (h d) -> p h d", h=4)),
                         R=[Bps[p]], W=[Bvst[vi]])
                    onesrc = ones_c if own else flg
                    for hl in range(4):
                        S.op("dve", lambda vi=vi, hl=hl, onesrc=onesrc: nc.vector.tensor_copy(out=vst[vi][:, hl, 128:129], in_=onesrc[:, 0:1]),
                             R=[B1, B_const], W=[Bvst[vi]])
                    if not cfg.get("noscr"):
                        S.dma("sp", lambda e, vi=vi, t=t, hh=hh: e.dma_start(out=v_d[hh * 4:(hh + 1) * 4, :, t, :].rearrange("h p d -> p h d"), in_=vst[vi][:]),
                              R=[Bvst[vi]])
                if own and cfg.get("stop") != 4:
                    f = fi % 4
                    fi += 1
                    if cfg.get("stop") == 7 and P == 128:
                        S.op("act", lambda f=f, p=p, P=P: nc.scalar.copy(out=fst[f][:P, :], in_=ps[p][:P, :]), R=[Bps[p]], W=[Bfst[f]])
                    elif cfg.get("stop") in (5, 7):
                        S.op("dve", lambda f=f, p=p, P=P: nc.vector.memset(fst[f][:P, :], 1.0), R=[Bps[p]], W=[Bfst[f]])
                        if P == 128 and cfg.get("stop") == 5:
                            S.op("dve", lambda f=f, col0=col0: nc.vector.tensor_copy(out=fst[f][:, 0:128], in_=uT[:, 0, col0:col0 + 128]), R=[B_uT[t]], W=[Bfst[f]])
                            S.op("dve", lambda f=f, col0=col0, wi=wi: nc.vector.tensor_copy(out=fst[f][:, 128:256], in_=wv[wi][:, 0, 0:128]), R=[Bwv[wi]], W=[Bfst[f]])
                    else:
                        S.op("act", lambda f=f, p=p, P=P: nc.scalar.copy(out=fst[f][:P, :], in_=ps[p][:P, :]), R=[Bps[p]], W=[Bfst[f]])
                    dst = v_own[(t - OT0) * 128:(t - OT0 + 1) * 128, hh * 512:(hh + 1) * 512] if t < NTT else v_s[:, hh * 512:(hh + 1) * 512]
                    S.dma("sp", lambda e, f=f, P=P, dst=dst: e.dma_start(out=dst, in_=fst[f][:P, :]), R=[Bfst[f]], is_out=True)
                    if cfg.get("stop") in (5, 7):
                        continue
                    p = pi % 4
                    pi += 1
                    S.mm([lambda kc=kc, p=p, P=P, col0=col0, wi=wi: nc.tensor.matmul(
                        ps[p][:P, :], lhsT=uT[:, kc, col0:col0 + P], rhs=wk[wi][:, kc, :],
                        start=(kc == 0), stop=(kc == KC - 1)) for kc in range(KC)], R=[Bwk[wi], B_uT[t]], W=[Bps[p]])
                    f = fi % 4
                    fi += 1
                    S.op("dve", lambda f=f, p=p, P=P: nc.vector.tensor_copy(out=fst[f][:P, :], in_=ps[p][:P, :]), R=[Bps[p]], W=[Bfst[f]])
                    dst = k_own[(t - OT0) * 128:(t - OT0 + 1) * 128, hh * 512:(hh + 1) * 512] if t < NTT else k_s[:, hh * 512:(hh + 1) * 512]
                    S.dma("sp", lambda e, f=f, P=P, dst=dst: e.dma_start(out=dst, in_=fst[f][:P, :]), R=[Bfst[f]], is_out=True)
        S.barrier()

    if cfg.get("stop") == 11:
        S.finish(); es.close(); return nc
    mTs = sb("mTs", [128, KC, NS], BF16)
    actTs = sb("actTs", [128, 8, NS], BF16)
    es3 = ExitStack()
    on_tm = sb("on_tm", [128, NOT + 1, D], BF16, es3)
    B_on = [Buf() for _ in range(NOT + 1)]
    qTs = sb("qTs", [128, NH, NS], BF16, es3)
    B_qTs = Buf()
    NPB = TP // 128
    with ExitStack() as ph:
        wq = [sb(f"b_wq{i}", [128, KC, 128], BF16, ph) for i in range(2)]
        Bwq = [Buf(), Buf()]
        kTh = [sb(f"b_kT{i}", [128, TF], BF16, ph) for i in range(2)]
        BkTh = [Buf(), Buf()]
        vh = [sb(f"b_v{i}", [128, NTT, 130], BF16, ph) for i in range(2)]
        Bvh = [Buf(), Buf()]
        qTh = [sb(f"b_q{i}", [128, NOS], BF16, ph) for i in range(2)]
        BqTh = [Buf(), Buf()]
        PT = [sb(f"b_PT{i}", [128, 2, 512], BF16, ph) for i in range(3)]
        BPT = [Buf() for _ in range(3)]
        pS = [pst(f"b_pS{i}", [128, 2, 512], F32, ph) for i in range(2)]
        BpS = [PBuf(), PBuf()]
        acc = [pst(f"b_acc{i}", [128, 512], F32, ph) for i in range(4)]
        Bacc = [PBuf() for _ in range(4)]
        rcp = sb("b_rcp", [128, 4], F32, ph)
        o1 = sb("b_o1", [128, 128], F32, ph)
        o2 = sb("b_o2", [128, 128], F32, ph)
        junk = sb("b_junk", [128, 128], F32, ph)
        Bfin = Buf()
        scnt = 0
        pcnt = 0
        for h in range(NH):
            hb = h % 2
            for kc in range(KC):
                S.dma("pool", lambda e, hb=hb, kc=kc, h=h: e.dma_start(out=wq[hb][:, kc, :], in_=w_in[kc * 128:(kc + 1) * 128, h * 128:(h + 1) * 128]), W=[Bwq[hb]])
            S.dma("sp", lambda e, hb=hb, h=h: e.dma_start(out=kTh[hb][:, :], in_=kT_d[h]), W=[BkTh[hb]])
            S.dma("sp", lambda e, hb=hb, h=h: e.dma_start(out=vh[hb][:, :, :], in_=v_d[h]), W=[Bvh[hb]])
            c0 = 0
            while c0 < NOS:
                cw = min(512, NOS - c0)
                sb_ = scnt % 2; scnt += 1
                tiles = sorted(set(min((TP + c0 + o) // 128, NTT) for o in range(0, cw, 128)))
                S.mm([lambda kc=kc, sb_=sb_, c0=c0, cw=cw, hb=hb: nc.tensor.matmul(pS[sb_][:, 0, :cw], lhsT=wq[hb][:, kc, :], rhs=uT[:, kc, TP + c0:TP + c0 + cw],
                                                                                  start=(kc == 0), stop=(kc == KC - 1)) for kc in range(KC)],
                     R=[Bwq[hb]] + [B_uT[t] for t in tiles], W=[BpS[sb_]])
                S.op("dve", lambda sb_=sb_, c0=c0, cw=cw, hb=hb: nc.vector.tensor_scalar(out=qTh[hb][:, c0:c0 + cw], in0=pS[sb_][:, 0, :cw], scalar1=0.125, scalar2=None, op0=ALU.mult),
                     R=[BpS[sb_]], W=[BqTh[hb]])
                c0 += cw
            S.op("pool", lambda hb=hb, h=h: nc.gpsimd.tensor_copy(out=qTs[:, h, :], in_=qTh[hb][:, TO:TO + NS]), R=[BqTh[hb]], W=[B_qTs])
            for s_ in range(NQG):
                nkb = NPB + 4 * (s_ + 1)

                def emit_qk(j, s_=s_, hb=hb):
                    jj = j - (NPB + 4 * s_)
                    q0 = max(jj, 0) * 128
                    sb_ = j % 2
                    pb = j % 3
                    S.mm([lambda c=c: nc.tensor.matmul(
                        pS[sb_][:, c, q0:512], lhsT=kTh[hb][64 * c:64 * c + 64, j * 128:(j + 1) * 128],
                        rhs=qTh[hb][64 * c:64 * c + 64, s_ * 512 + q0:(s_ + 1) * 512], start=True, stop=True) for c in range(2)],
                        R=[BkTh[hb], BqTh[hb]], W=[BpS[sb_]])
                    S.op("act", lambda: nc.scalar.activation(out=PT[pb][:, :, q0:512], in_=pS[sb_][:, :, q0:512], func=AF.Exp),
                         R=[BpS[sb_]], W=[BPT[pb]])
                    if jj >= 0:
                        for c in range(2):
                            S.op("dve", lambda c=c: nc.vector.tensor_tensor(out=PT[pb][:, c, q0:q0 + 128], in0=PT[pb][:, c, q0:q0 + 128], in1=trib[:, :], op=ALU.mult),
                                 R=[BPT[pb], B_const], W=[BPT[pb]])

                def emit_pv(j, s_=s_, hb=hb):
                    jj = j - (NPB + 4 * s_)
                    qb0 = max(jj, 0)
                    pb = j % 3
                    fns = []
                    for qb in range(qb0, 4):
                        last = (j == NPB + 4 * s_ + qb)
                        for c in range(2):
                            fns.append(lambda qb=qb, c=c, last=last: nc.tensor.matmul(
                                acc[qb][:, c * 256:c * 256 + 129], lhsT=PT[pb][:, c, qb * 128:(qb + 1) * 128], rhs=vh[hb][:, j, 0:129],
                                start=(j == 0 and c == 0), stop=last, skip_group_check=True))
                    S.mm(fns, R=[BPT[pb], Bvh[hb]], W=[Bacc[qb] for qb in range(qb0, 4)])

                emit_qk(0)
                for j in range(nkb):
                    if j + 1 < nkb:
                        emit_qk(j + 1)
                    emit_pv(j)
                for qb in range(4):
                    ti = s_ * 4 + qb
                    S.op("dve", lambda qb=qb: nc.vector.reciprocal(out=rcp[:, 0:1], in_=acc[qb][:, 128:129]), R=[Bacc[qb]], W=[Bfin])
                    S.op("dve", lambda qb=qb: nc.vector.reciprocal(out=rcp[:, 1:2], in_=acc[qb][:, 384:385]), R=[Bacc[qb]], W=[Bfin])
                    S.op("dve", lambda: nc.vector.tensor_tensor(out=rcp[:, 1:2], in0=rcp[:, 1:2], in1=neglam[:, :], op=ALU.mult), R=[Bfin, B_const], W=[Bfin])
                    S.op("dve", lambda qb=qb: nc.vector.tensor_scalar(out=o1[:, :], in0=acc[qb][:, 0:128], scalar1=rcp[:, 0:1], scalar2=None, op0=ALU.mult), R=[Bacc[qb], Bfin], W=[Bfin])
                    S.op("dve", lambda qb=qb: nc.vector.scalar_tensor_tensor(out=o2[:, :], in0=acc[qb][:, 256:384], scalar=rcp[:, 1:2], in1=o1[:, :], op0=ALU.mult, op1=ALU.add),
                         R=[Bacc[qb], Bfin], W=[Bfin])
                    S.op("act", lambda: nc.scalar.activation(out=junk[:, :], in_=o2[:, :], func=AF.Square, accum_out=rcp[:, 2:3]), R=[Bfin], W=[Bfin])
                    S.op("act", lambda: nc.scalar.activation(out=rcp[:, 3:4], in_=rcp[:, 2:3], func=AF.Sqrt, scale=1.0 / 128, bias=epsb[:, 0:1]), R=[Bfin, B_const], W=[Bfin])
                    S.op("dve", lambda: nc.vector.reciprocal(out=rcp[:, 3:4], in_=rcp[:, 3:4]), R=[Bfin], W=[Bfin])
                    S.op("dve", lambda ti=ti, h=h: nc.vector.scalar_tensor_tensor(out=on_tm[:, ti, h * 128:(h + 1) * 128], in0=o2[:, :], scalar=rcp[:, 3:4], in1=gsubb[:, :],
                                                                                op0=ALU.mult, op1=ALU.mult), R=[Bfin, B_const], W=[B_on[ti]])
        S.barrier()
    if cfg.get("stop") == 20:
        with ExitStack() as ph:
            dbg = sb("dbg", [128, D], F32, ph)
            Bd = Buf()
            for ti in range(NOT):
                S.op("dve", lambda ti=ti: nc.vector.tensor_copy(out=dbg[:, :], in_=on_tm[:, ti, :]), R=[B_on[ti]], W=[Bd])
                S.dma("sp", lambda e, ti=ti: e.dma_start(out=y_own[ti * 128:(ti + 1) * 128, :], in_=dbg[:, :]), R=[Bd], is_out=True)
            S.barrier()
        S.finish(); es.close(); return nc

    if cfg.get("stop") == 10:
        S.finish(); es.close(); return nc
    NOS = TO + NS
    es2 = ExitStack()
    hsT = sb("hsT", [128, KC, NOS], BF16, es2)
    B_hsT = [Buf() for _ in range(KC)]
    with ExitStack() as ph:
        PW = min(512, TP)
        NPC = TF // PW
        onesb = sb("c_ones", [128, 1], F32, ph)
        wga = sb("c_wga", [128, 8, 128], BF16, ph)
        wgx = sb("c_wgx", [128, 8, 128], BF16, ph)
        Bwg = Buf()
        S.op("dve", lambda: nc.vector.memset(onesb[:], 1.0), W=[Bwg])
        for n in range(8):
            S.dma("pool", lambda e, n=n: e.dma_start(out=wga[:, n, :], in_=w_ga[n]), W=[Bwg])
            S.dma("pool", lambda e, n=n: e.dma_start(out=wgx[:, n, :], in_=w_gx[n]), W=[Bwg])
        scT = sb("c_scT", [128, KC, 12], F32, ph)
        shT = sb("c_shT", [128, KC, 4], F32, ph)
        Bst = Buf()
        with ExitStack() as ph0:
            sc_tm = sb("c_sctm", [12, D], F32, ph0)
            sh_tm = sb("c_shtm", [4, D], F32, ph0)
            S.dma("sp", lambda e: e.dma_start(out=sc_tm[:], in_=sconv), W=[Bst])
            S.dma("sp", lambda e: e.dma_start(out=sh_tm[:], in_=sh), W=[Bst])
            pss = pst("c_pss", [128, 16], F32, ph0)
            Bpss = PBuf()
            for kc in range(KC):
                S.mm([lambda kc=kc: nc.tensor.transpose(out=pss[:, 0:12], in_=sc_tm[:12, kc * 128:(kc + 1) * 128], identity=identf[:12, :12])], R=[Bst, B_const], W=[Bpss])
                S.op("dve", lambda kc=kc: nc.vector.tensor_copy(out=scT[:, kc, :], in_=pss[:, 0:12]), R=[Bpss], W=[Bst])
                S.mm([lambda kc=kc: nc.tensor.transpose(out=pss[:, 12:16], in_=sh_tm[:4, kc * 128:(kc + 1) * 128], identity=identf[:4, :4])], R=[Bst, B_const], W=[Bpss])
                S.op("dve", lambda kc=kc: nc.vector.tensor_copy(out=shT[:, kc, :], in_=pss[:, 12:16]), R=[Bpss], W=[Bst])
            S.barrier()
        convT = sb("c_convT", [128, KC, 3], F32, ph)
        hlT = sb("c_hlT", [128, KC, 1], F32, ph)
        convsT = sb("c_convsT", [128, KC, 12], F32, ph)
        hsl = sb("c_hsl", [128, KC, 4], F32, ph)
        Bcol = Buf()
        wx = [sb(f"c_wx{i}", [128, KC, 128], BF16, ph) for i in range(2)]
        Bwx = [Buf(), Buf()]
        NB = 2
        xp = [sb(f"c_xp{i}", [128, 3 + PW], F32, ph) for i in range(NB)]
        xc = [sb(f"c_xc{i}", [128, PW], F32, ph) for i in range(NB)]
        xcb = [sb(f"c_xcb{i}", [128, PW], BF16, ph) for i in range(NB)]
        rr = [sb(f"c_r{i}", [128, PW], F32, ph) for i in range(NB)]
        ig = [sb(f"c_ig{i}", [128, PW], F32, ph) for i in range(NB)]
        am = [sb(f"c_am{i}", [128, PW], F32, ph) for i in range(NB)]
        aa = [sb(f"c_aa{i}", [128, PW], F32, ph) for i in range(NB)]
        hh_ = [sb(f"c_h{i}", [128, PW], F32, ph) for i in range(NB)]
        carry = sb("c_carry", [128, 1], F32, ph)
        Bxp = [Buf() for _ in range(NB)]; Bxc = [Buf() for _ in range(NB)]; Bxcb = [Buf() for _ in range(NB)]
        Br = [Buf() for _ in range(NB)]; Big = [Buf() for _ in range(NB)]; Bam = [Buf() for _ in range(NB)]
        Baa = [Buf() for _ in range(NB)]; Bh = [Buf() for _ in range(NB)]; Bcarry = Buf()
        psx = [pst(f"c_psx{i}", [128, 512], F32, ph) for i in range(2)]
        Bpsx = [PBuf() for _ in range(2)]
        psg = [pst(f"c_psg{i}", [128, 512], F32, ph) for i in range(4)]
        Bpsg = [PBuf() for _ in range(4)]
        xps = sb("c_xps", [128, 4, 11], F32, ph)
        xcs = sb("c_xcs", [128, 4, 8], F32, ph)
        xcsb = sb("c_xcsb", [128, 32], BF16, ph)
        rs_ = sb("c_rs", [128, 32], F32, ph)
        igs = sb("c_igs", [128, 32], F32, ph)
        ams = sb("c_ams", [128, 32], F32, ph)
        aas = sb("c_aas", [128, 32], F32, ph)
        hss = sb("c_hss", [128, 4, 8], F32, ph)
        Bsm = Buf()
        gcnt = 0
        xcnt = 0
        it = 0

        def lru_core(j, xp_ap3, xc_t, xcb_t, r_t, ig_t, am_t, aa_t, W_, Bxp_, Bxc_, Bxcb_, Br_, Big_, Bam_, Baa_, shape3=None):
            nonlocal gcnt
            S.op("pool", lambda: nc.gpsimd.tensor_scalar(out=xc_t, in0=xp_ap3(0), scalar1=wcT[:, 0, j:j + 1], scalar2=vT[:, 1, j:j + 1],
                                                          op0=ALU.mult, op1=ALU.add), R=[Bxp_, B_const], W=[Bxc_])
            for jj in range(1, 4):
                S.op("dve", lambda jj=jj: nc.vector.scalar_tensor_tensor(out=xc_t, in0=xp_ap3(jj), scalar=wcT[:, jj, j:j + 1], in1=xc_t,
                                                                            op0=ALU.mult, op1=ALU.add), R=[Bxp_, Bxc_], W=[Bxc_])
            xc2 = xc_t if shape3 is None else xc_t.rearrange("p a b -> p (a b)")
            S.op("pool", lambda: nc.gpsimd.tensor_copy(out=xcb_t, in_=xc2), R=[Bxc_], W=[Bxcb_])
            c0 = 0
            while c0 < W_:
                cw = min(512, W_ - c0)
                g0 = gcnt % 4; g1 = (gcnt + 1) % 4; gcnt += 2
                S.mm([lambda g0=g0, c0=c0, cw=cw: nc.tensor.matmul(psg[g0][:, :cw], lhsT=wga[:, j, :], rhs=xcb_t[:, c0:c0 + cw], start=True, stop=True)],
                     R=[Bwg, Bxcb_], W=[Bpsg[g0]])
                S.mm([lambda g1=g1, c0=c0, cw=cw: nc.tensor.matmul(psg[g1][:, :cw], lhsT=wgx[:, j, :], rhs=xcb_t[:, c0:c0 + cw], start=True, stop=True)],
                     R=[Bwg, Bxcb_], W=[Bpsg[g1]])
                S.op("act", lambda g0=g0, c0=c0, cw=cw: nc.scalar.activation(out=r_t[:, c0:c0 + cw], in_=psg[g0][:, :cw], func=AF.Sigmoid, bias=vT[:, 2, j:j + 1]),
                     R=[Bpsg[g0], B_const], W=[Br_])
                S.op("act", lambda g1=g1, c0=c0, cw=cw: nc.scalar.activation(out=ig_t[:, c0:c0 + cw], in_=psg[g1][:, :cw], func=AF.Sigmoid, bias=vT[:, 3, j:j + 1]),
                     R=[Bpsg[g1], B_const], W=[Big_])
                c0 += cw
            S.op("act", lambda: nc.scalar.activation(out=aa_t, in_=r_t, func=AF.Exp, scale=clam[:, j:j + 1]), R=[Br_, B_const], W=[Baa_])
            S.op("act", lambda: nc.scalar.activation(out=am_t, in_=r_t, func=AF.Exp, scale=clam2[:, j:j + 1]), R=[Br_, B_const], W=[Bam_])
            S.op("act", lambda: nc.scalar.activation(out=am_t, in_=am_t, func=AF.Sqrt, scale=-1.0, bias=onesb[:, 0:1]), R=[Bam_, Bwg], W=[Bam_])
            S.op("dve", lambda: nc.vector.tensor_tensor(out=ig_t, in0=ig_t, in1=xc2, op=ALU.mult), R=[Big_, Bxc_], W=[Big_])
            S.op("dve", lambda: nc.vector.tensor_tensor(out=ig_t, in0=ig_t, in1=am_t, op=ALU.mult), R=[Big_, Bam_], W=[Big_])

        for j in range(KC):
            wi = j % 2
            for kc in range(KC):
                S.dma("pool", lambda e, wi=wi, kc=kc, j=j: e.dma_start(out=wx[wi][:, kc, :], in_=w_in[kc * 128:(kc + 1) * 128, 3 * D + j * 128:3 * D + (j + 1) * 128]), W=[Bwx[wi]])
            for pc in range(NPC):
                b = it % NB
                it += 1
                col0 = pc * PW
                if pc == 0:
                    S.op("pool", lambda b=b: nc.gpsimd.memset(xp[b][:, 0:3], 0.0), W=[Bxp[b]])
                else:
                    pb = (it - 2) % NB
                    S.op("pool", lambda b=b, pb=pb: nc.gpsimd.tensor_copy(out=xp[b][:, 0:3], in_=xp[pb][:, PW:PW + 3]), R=[Bxp[pb]], W=[Bxp[b]])
                c0 = 0
                while c0 < PW:
                    x_ = xcnt % 2; xcnt += 1
                    tiles = sorted(set((col0 + c0 + o) // 128 for o in range(0, 512, 128)))
                    S.mm([lambda kc=kc, x_=x_, c0=c0, col0=col0, wi=wi: nc.tensor.matmul(psx[x_][:, :], lhsT=wx[wi][:, kc, :], rhs=uT[:, kc, col0 + c0:col0 + c0 + 512],
                                                                                       start=(kc == 0), stop=(kc == KC - 1)) for kc in range(KC)],
                         R=[Bwx[wi]] + [B_uT[t] for t in tiles], W=[Bpsx[x_]])
                    S.op("act", lambda x_=x_, b=b, c0=c0: nc.scalar.copy(out=xp[b][:, 3 + c0:3 + c0 + 512], in_=psx[x_][:, :]), R=[Bpsx[x_]], W=[Bxp[b]])
                    c0 += 512
                lru_core(j, lambda jj, b=b: xp[b][:, jj:jj + PW], xc[b][:, :], xcb[b][:, :], rr[b][:, :], ig[b][:, :], am[b][:, :], aa[b][:, :], PW,
                         Bxp[b], Bxc[b], Bxcb[b], Br[b], Big[b], Bam[b], Baa[b])
                if pc == 0:
                    init = 0.0
                    Rc = []
                else:
                    pb = (it - 2) % NB
                    if col0 == TP:
                        S.op("dve", lambda pb=pb: nc.vector.tensor_tensor(out=carry[:], in0=hh_[pb][:, PW - 1:PW], in1=flg[:], op=ALU.mult), R=[Bh[pb], B_const], W=[Bcarry])
                        init = carry[:, 0:1]
                        Rc = [Bcarry]
                    else:
                        init = hh_[pb][:, PW - 1:PW]
                        Rc = [Bh[pb]]
                S.op("dve", lambda b=b, init=init: nc.vector.tensor_tensor_scan(out=hh_[b][:, :], data0=aa[b][:, :], data1=ig[b][:, :], initial=init,
                                                                                 op0=ALU.mult, op1=ALU.add), R=[Baa[b], Big[b]] + Rc, W=[Bh[b]])
                if col0 >= TP:
                    o0 = col0 - TP
                    S.op("pool", lambda b=b, o0=o0, j=j: nc.gpsimd.tensor_copy(out=hsT[:, j, o0:o0 + PW], in_=hh_[b][:, :]), R=[Bh[b]], W=[B_hsT[j]])
                if pc == NPC - 1:
                    S.op("pool", lambda b=b, j=j: nc.gpsimd.tensor_copy(out=convT[:, j, :], in_=xp[b][:, PW:PW + 3]), R=[Bxp[b]], W=[Bcol])
                    S.op("pool", lambda b=b, j=j: nc.gpsimd.tensor_copy(out=hlT[:, j, :], in_=hh_[b][:, PW - 1:PW]), R=[Bh[b]], W=[Bcol])
            x_ = xcnt % 2; xcnt += 1
            S.mm([lambda kc=kc, x_=x_, wi=wi: nc.tensor.matmul(psx[x_][:, :NS], lhsT=wx[wi][:, kc, :], rhs=uT[:, kc, TF:TF + NS],
                                                              start=(kc == 0), stop=(kc == KC - 1)) for kc in range(KC)],
                 R=[Bwx[wi], B_uT[NTT]], W=[Bpsx[x_]])
            S.op("act", lambda x_=x_: nc.scalar.copy(out=xps[:, :, 3:11], in_=psx[x_][:, :NS].rearrange("p (a b) -> p a b", a=4)), R=[Bpsx[x_]], W=[Bsm])
            S.op("pool", lambda j=j: nc.gpsimd.tensor_copy(out=xps[:, :, 0:3], in_=scT[:, j, :].rearrange("p (a b) -> p a b", a=4)), R=[Bst], W=[Bsm])
            lru_core(j, lambda jj: xps[:, :, jj:jj + 8], xcs[:, :, :], xcsb[:, :], rs_[:, :], igs[:, :], ams[:, :], aas[:, :], NS,
                     Bsm, Bsm, Bsm, Bsm, Bsm, Bsm, Bsm, shape3=True)
            for sq in range(4):
                S.op("dve", lambda sq=sq, j=j: nc.vector.tensor_tensor_scan(out=hss[:, sq, :], data0=aas[:, sq * 8:(sq + 1) * 8], data1=igs[:, sq * 8:(sq + 1) * 8],
                                                                              initial=shT[:, j, sq:sq + 1], op0=ALU.mult, op1=ALU.add), R=[Bsm, Bst], W=[Bsm])
            S.op("pool", lambda j=j: nc.gpsimd.tensor_copy(out=hsT[:, j, TO:TO + NS], in_=hss[:].rearrange("p a b -> p (a b)")), R=[Bsm], W=[B_hsT[j]])
            S.op("pool", lambda j=j: nc.gpsimd.tensor_copy(out=convsT[:, j, :].rearrange("p (a b) -> p a b", a=4), in_=xps[:, :, 8:11]), R=[Bsm], W=[Bcol])
            S.op("pool", lambda j=j: nc.gpsimd.tensor_copy(out=hsl[:, j, :], in_=hss[:, :, 7]), R=[Bsm], W=[Bcol])
        otm = sb("c_otm", [12, D], F32, ph)
        Botm = Buf()
        pso = pst("c_pso", [12, 512], F32, ph)
        Bpso = PBuf()
        for (src, n, k, dst) in ((convT, 3, 0, conv_p), (hlT, 1, 1, h_p), (convsT, 12, 2, conv_s), (hsl, 4, 3, h_s)):
            for half in range(2):
                S.mm([lambda kc=kc, src=src, n=n, half=half: nc.tensor.transpose(out=pso[:n, (kc % 4) * 128:(kc % 4 + 1) * 128], in_=src[:, kc, :], identity=identf[:, :])
                      for kc in range(half * 4, half * 4 + 4)], R=[Bcol, B_const], W=[Bpso])
                S.op("dve", lambda n=n, k=k, half=half: nc.vector.tensor_copy(out=otm[:n, half * 512:(half + 1) * 512], in_=pso[:n, :]), R=[Bpso], W=[Botm])
            S.dma("sp", lambda e, n=n, k=k, dst=dst: e.dma_start(out=dst, in_=otm[:n, :]), R=[Botm], is_out=True)
        S.barrier()

    if cfg.get("stop") == 12:
        S.finish(); es.close(); return nc
    S.dma("pool", lambda e: e.dma_start(out=on_tm[:NS, NOT, :], in_=on_s_in), W=[B_on[NOT]])

    def col_chunks():
        c0 = 0
        while c0 < NOS:
            cw = min(512, NOS - c0)
            tiles = sorted(set(min((TP + c0 + o) // 128, NTT) for o in range(0, cw, 128)))
            yield c0, cw, tiles
            c0 += cw

    with ExitStack() as ph:
        wy = [sb(f"d1_wy{i}", [128, KC, 128], BF16, ph) for i in range(2)]
        Bwy = [Buf(), Buf()]
        ps = [pst(f"d1_ps{i}", [128, 512], F32, ph) for i in range(4)]
        Bps = [PBuf() for _ in range(4)]
        NBF = 2
        xs_ = [sb(f"d1_x{i}", [128, 512], F32, ph) for i in range(NBF)]
        t1 = [sb(f"d1_t{i}", [128, 512], F32, ph) for i in range(NBF)]
        sg = [sb(f"d1_s{i}", [128, 512], F32, ph) for i in range(NBF)]
        Bx = [Buf() for _ in range(NBF)]; Bt = [Buf() for _ in range(NBF)]; Bsg = [Buf() for _ in range(NBF)]
        cnt = 0
        for n in range(KC):
            wi = n % 2
            for kc in range(KC):
                S.dma("pool", lambda e, wi=wi, kc=kc, n=n: e.dma_start(out=wy[wi][:, kc, :], in_=w_in[kc * 128:(kc + 1) * 128, 4 * D + n * 128:4 * D + (n + 1) * 128]), W=[Bwy[wi]])
            for c0, cw, tiles in col_chunks():
                p = cnt % 4; b = cnt % NBF; cnt += 1
                S.mm([lambda kc=kc, p=p, c0=c0, cw=cw, wi=wi: nc.tensor.matmul(ps[p][:, :cw], lhsT=wy[wi][:, kc, :], rhs=uT[:, kc, TP + c0:TP + c0 + cw],
                                                                              start=(kc == 0), stop=(kc == KC - 1)) for kc in range(KC)],
                     R=[Bwy[wi]] + [B_uT[t] for t in tiles], W=[Bps[p]])
                S.op("act", lambda p=p, b=b, cw=cw: nc.scalar.copy(out=xs_[b][:, :cw], in_=ps[p][:, :cw]), R=[Bps[p]], W=[Bx[b]])
                S.op("dve", lambda b=b, cw=cw: nc.vector.tensor_tensor(out=t1[b][:, :cw], in0=xs_[b][:, :cw], in1=xs_[b][:, :cw], op=ALU.mult), R=[Bx[b]], W=[Bt[b]])
                S.op("dve", lambda b=b, cw=cw: nc.vector.tensor_scalar(out=t1[b][:, :cw], in0=t1[b][:, :cw], scalar1=0.044715, scalar2=1.0, op0=ALU.mult, op1=ALU.add), R=[Bt[b]], W=[Bt[b]])
                S.op("dve", lambda b=b, cw=cw: nc.vector.tensor_tensor(out=t1[b][:, :cw], in0=t1[b][:, :cw], in1=xs_[b][:, :cw], op=ALU.mult), R=[Bt[b], Bx[b]], W=[Bt[b]])
                S.op("act", lambda b=b, cw=cw: nc.scalar.activation(out=sg[b][:, :cw], in_=t1[b][:, :cw], func=AF.Sigmoid, scale=1.5957691216057308), R=[Bt[b]], W=[Bsg[b]])
                S.op("dve", lambda b=b, cw=cw: nc.vector.tensor_tensor(out=sg[b][:, :cw], in0=sg[b][:, :cw], in1=xs_[b][:, :cw], op=ALU.mult), R=[Bsg[b], Bx[b]], W=[Bsg[b]])
                S.op("dve", lambda b=b, cw=cw, n=n, c0=c0: nc.vector.tensor_tensor(out=hsT[:, n, c0:c0 + cw], in0=hsT[:, n, c0:c0 + cw], in1=sg[b][:, :cw], op=ALU.mult),
                     R=[Bsg[b], B_hsT[n]], W=[B_hsT[n]])
        S.barrier()

    B_mT = [Buf() for _ in range(NOT + 1)]

    def mT_cols(kc_or_all, c0, cw):
        if c0 < TO:
            return uT[:, kc_or_all, c0:c0 + cw]
        return mTs[:, kc_or_all, 0:cw]

    with ExitStack() as ph:
        wr = [sb(f"d2_wr{i}", [128, KC, 128], BF16, ph) for i in range(2)]
        wgr = [sb(f"d2_wgr{i}", [128, KC, 128], BF16, ph) for i in range(2)]
        Bw = [Buf(), Buf()]
        ps = [pst(f"d2_ps{i}", [128, 512], F32, ph) for i in range(4)]
        Bps = [PBuf() for _ in range(4)]
        sgr = [sb(f"d2_sgr{i}", [128, 512], F32, ph) for i in range(2)]
        Bsgr = [Buf(), Buf()]
        cnt = 0
        for m in range(KC):
            wi = m % 2
            for kc in range(KC):
                r0, r1 = kc * 128, (kc + 1) * 128
                S.dma("pool", lambda e, wi=wi, kc=kc, m=m, r0=r0, r1=r1: e.dma_start(out=wr[wi][:, kc, :], in_=w_rec[r0:r1, m * 128:(m + 1) * 128]), W=[Bw[wi]])
                S.dma("pool", lambda e, wi=wi, kc=kc, m=m, r0=r0, r1=r1: e.dma_start(out=wgr[wi][:, kc, :], in_=w_in[r0:r1, 6 * D + m * 128:6 * D + (m + 1) * 128]), W=[Bw[wi]])
            for c0, cw, tiles in col_chunks():
                b = cnt % 2; cnt += 1
                pA, pB = (2 * b, 2 * b + 1)
                otiles = sorted(set(min((c0 + o) // 128, NOT) for o in range(0, cw, 128)))
                S.mm([lambda kc=kc: nc.tensor.matmul(ps[pA][:, :cw], lhsT=wr[wi][:, kc, :], rhs=hsT[:, kc, c0:c0 + cw], start=(kc == 0), stop=(kc == KC - 1)) for kc in range(KC)],
                     R=[Bw[wi]] + B_hsT, W=[Bps[pA]])
                S.mm([lambda kc=kc: nc.tensor.matmul(ps[pB][:, :cw], lhsT=wgr[wi][:, kc, :], rhs=uT[:, kc, TP + c0:TP + c0 + cw], start=(kc == 0), stop=(kc == KC - 1)) for kc in range(KC)],
                     R=[Bw[wi]] + [B_uT[t] for t in tiles], W=[Bps[pB]])
                S.op("act", lambda: nc.scalar.activation(out=sgr[b][:, :cw], in_=ps[pB][:, :cw], func=AF.Sigmoid), R=[Bps[pB]], W=[Bsgr[b]])
                S.op("dve", lambda: nc.vector.tensor_tensor(out=mT_cols(m, c0, cw), in0=ps[pA][:, :cw], in1=sgr[b][:, :cw], op=ALU.mult),
                     R=[Bps[pA], Bsgr[b]], W=[B_mT[t] for t in otiles])
        S.barrier()
    es2.close()

    with ExitStack() as ph:
        wa = sb("d2_wa", [128, KC, D], BF16, ph)
        wga_ = sb("d2_wga", [128, KC, D], BF16, ph)
        Bw = Buf()
        for kc in range(KC):
            for hf in range(2):
                S.dma("pool", lambda e, kc=kc, hf=hf: e.dma_start(out=wa[:, kc, hf * 512:(hf + 1) * 512], in_=w_attn[kc * 128:(kc + 1) * 128, hf * 512:(hf + 1) * 512]), W=[Bw])
                S.dma("pool", lambda e, kc=kc, hf=hf: e.dma_start(out=wga_[:, kc, hf * 512:(hf + 1) * 512],
                                                                   in_=w_in[kc * 128:(kc + 1) * 128, 5 * D + hf * 512:5 * D + (hf + 1) * 512]), W=[Bw])
        oaT = [sb(f"d2_oaT{i}", [128, KC, 512], BF16, ph) for i in range(2)]
        BoaT = [Buf(), Buf()]
        psT = [pst(f"d2_psT{i}", [128, KC, 128], BF16, ph) for i in range(2)]
        BpsT = [PBuf(), PBuf()]
        ps = [pst(f"d2b_ps{i}", [128, 512], F32, ph) for i in range(4)]
        Bps = [PBuf() for _ in range(4)]
        sga = [sb(f"d2_sga{i}", [128, 512], F32, ph) for i in range(2)]
        ta_ = [sb(f"d2_ta{i}", [128, 512], F32, ph) for i in range(2)]
        Bsga = [Buf(), Buf()]; Bta = [Buf(), Buf()]
        cnt = 0
        tcnt = 0
        for ci, (c0, cw, tiles) in enumerate(col_chunks()):
            ob = ci % 2
            otiles = sorted(set(min((c0 + o) // 128, NOT) for o in range(0, cw, 128)))
            for t in otiles:
                P = 128 if t < NOT else NS
                i = tcnt % 2; tcnt += 1
                lc = t * 128 - c0
                S.mm([lambda kc=kc, i=i, t=t, P=P: nc.tensor.transpose(out=psT[i][:, kc, :P], in_=on_tm[:P, t, kc * 128:(kc + 1) * 128], identity=identb[:P, :P])
                      for kc in range(KC)], R=[B_on[t], B_const], W=[BpsT[i]])
                S.op("act", lambda i=i, ob=ob, lc=lc, P=P: nc.scalar.copy(out=oaT[ob][:, :, lc:lc + P], in_=psT[i][:, :, :P]), R=[BpsT[i]], W=[BoaT[ob]])
            for m in range(KC):
                b = cnt % 2; cnt += 1
                pC, pD = (2 * b, 2 * b + 1)
                S.mm([lambda kc=kc: nc.tensor.matmul(ps[pC][:, :cw], lhsT=wa[:, kc, m * 128:(m + 1) * 128], rhs=oaT[ob][:, kc, :cw], start=(kc == 0), stop=(kc == KC - 1)) for kc in range(KC)],
                     R=[Bw, BoaT[ob]], W=[Bps[pC]])
                S.mm([lambda kc=kc: nc.tensor.matmul(ps[pD][:, :cw], lhsT=wga_[:, kc, m * 128:(m + 1) * 128], rhs=uT[:, kc, TP + c0:TP + c0 + cw], start=(kc == 0), stop=(kc == KC - 1)) for kc in range(KC)],
                     R=[Bw] + [B_uT[t] for t in tiles], W=[Bps[pD]])
                S.op("act", lambda: nc.scalar.activation(out=sga[b][:, :cw], in_=ps[pD][:, :cw], func=AF.Sigmoid), R=[Bps[pD]], W=[Bsga[b]])
                S.op("dve", lambda: nc.vector.tensor_tensor(out=ta_[b][:, :cw], in0=ps[pC][:, :cw], in1=sga[b][:, :cw], op=ALU.mult), R=[Bps[pC], Bsga[b]], W=[Bta[b]])
                S.op("pool", lambda: nc.gpsimd.tensor_tensor(out=mT_cols(m, c0, cw), in0=mT_cols(m, c0, cw), in1=ta_[b][:, :cw], op=ALU.add),
                     R=[Bta[b]] + [B_mT[t] for t in otiles], W=[B_mT[t] for t in otiles])
        S.barrier()
    es3.close()

    hres = sb("hres", [128, NOT + 1, D], F32)
    B_h = [Buf() for _ in range(NOT + 1)]

    def tile_rows(t):
        return 128 if t < NOT else NS

    def mT_tile(kc, t):
        if t < NOT:
            return uT[:, kc, t * 128:(t + 1) * 128]
        return mTs[:, kc, :]

    def norm_to_uT(ph_, t, gidx, xnb, Bxnb, psT, BpsT):
        P = tile_rows(t)
        i = t % 2
        rmsnorm_rows(ph_, hres[:P, t, :], P, gidx, xnb[i][:P, :], B_h[t], Bxnb[i], "d")
        transpose_to_uT(ph_, xnb[i], P, TP + t * 128, Bxnb[i], B_uT[OT0 + t], psT[i], BpsT[i])

    with ExitStack() as ph:
        wo = sb("d3_wo", [128, KC, D], BF16, ph)
        Bwo = Buf()
        for kc in range(KC):
            for hf in range(2):
                S.dma("pool", lambda e, kc=kc, hf=hf: e.dma_start(out=wo[:, kc, hf * 512:(hf + 1) * 512], in_=w_out[kc * 128:(kc + 1) * 128, hf * 512:(hf + 1) * 512]), W=[Bwo])
        xt = [sb(f"d3_x{i}", [128, D], F32, ph) for i in range(2)]
        Bxt = [Buf(), Buf()]
        xnb = [sb(f"d3_xn{i}", [128, D], BF16, ph) for i in range(2)]
        Bxnb = [Buf(), Buf()]
        psT = [pst(f"d3_psT{i}", [128, KC, 128], BF16, ph) for i in range(2)]
        BpsT = [PBuf(), PBuf()]
        ps = [pst(f"d3_ps{i}", [128, 512], F32, ph) for i in range(4)]
        Bps = [PBuf() for _ in range(4)]
        cnt = 0
        for t in range(NOT + 1):
            P = tile_rows(t)
            i = t % 2
            src = xf[TP + t * 128:TP + (t + 1) * 128, :] if t < NOT else xs
            S.dma("sp", lambda e, i=i, P=P, src=src: e.dma_start(out=xt[i][:P, :], in_=src), W=[Bxt[i]])
            for hf in range(2):
                p = cnt % 4; cnt += 1
                S.mm([lambda kc=kc, p=p, t=t, P=P, hf=hf: nc.tensor.matmul(ps[p][:P, :], lhsT=mT_tile(kc, t), rhs=wo[:, kc, hf * 512:(hf + 1) * 512],
                                                                          start=(kc == 0), stop=(kc == KC - 1)) for kc in range(KC)], R=[Bwo, B_mT[t]], W=[Bps[p]])
                S.op("dve", lambda p=p, t=t, P=P, hf=hf, i=i: nc.vector.tensor_tensor(out=hres[:P, t, hf * 512:(hf + 1) * 512], in0=ps[p][:P, :], in1=xt[i][:P, hf * 512:(hf + 1) * 512], op=ALU.add),
                     R=[Bps[p], Bxt[i]], W=[B_h[t]])
            norm_to_uT(ph, t, 1, xnb, Bxnb, psT, BpsT)
        S.barrier()

    B_act = [Buf() for _ in range(NOT + 1)]

    def act_cols(fl, c0, cw):
        if c0 < TO:
            return uT[:, fl, c0:c0 + cw]
        return actTs[:, fl, 0:cw]

    def act_tile(fl, t):
        if t < NOT:
            return uT[:, fl, t * 128:(t + 1) * 128]
        return actTs[:, fl, :]

    with ExitStack() as ph:
        wg = [sb(f"d4_wg{i}", [128, KC, 256], BF16, ph) for i in range(2)]
        wu = [sb(f"d4_wu{i}", [128, KC, 256], BF16, ph) for i in range(2)]
        Bwgu = [Buf(), Buf()]
        wd = sb("d4_wd", [128, 8, D], BF16, ph)
        Bwd = Buf()
        psg = [pst(f"d4_pg{i}", [128, 512], F32, ph) for i in range(2)]
        psu = [pst(f"d4_pu{i}", [128, 512], F32, ph) for i in range(2)]
        psd = [pst(f"d4_pd{i}", [128, 512], F32, ph) for i in range(4)]
        Bpsg = [PBuf(), PBuf()]; Bpsu = [PBuf(), PBuf()]; Bpsd = [PBuf() for _ in range(4)]
        sl = [sb(f"d4_sl{i}", [128, 512], F32, ph) for i in range(2)]
        Bsl = [Buf(), Buf()]
        parts = [(0, 8), (8, 16), (16, 22)]
        cnt = 0
        dcnt = 0
        wcnt = 0
        for (f0, f1) in parts:
            nf = f1 - f0
            for fl in range(nf):
                for hf in range(2):
                    S.dma("pool", lambda e, fl=fl, f0=f0, hf=hf: e.dma_start(out=wd[:, fl, hf * 512:(hf + 1) * 512],
                                                                              in_=w_fd[(f0 + fl) * 128:(f0 + fl + 1) * 128, hf * 512:(hf + 1) * 512]), W=[Bwd])
            assert nf % 2 == 0
            for fl in range(nf):
                fc = f0 + fl
                sub = fl % 2
                if sub == 0:
                    wi = wcnt % 2; wcnt += 1
                    for kc in range(KC):
                        S.dma("pool", lambda e, wi=wi, kc=kc, fc=fc: e.dma_start(out=wg[wi][:, kc, :], in_=w_fg[kc * 128:(kc + 1) * 128, fc * 128:(fc + 2) * 128]), W=[Bwgu[wi]])
                        S.dma("pool", lambda e, wi=wi, kc=kc, fc=fc: e.dma_start(out=wu[wi][:, kc, :], in_=w_fu[kc * 128:(kc + 1) * 128, fc * 128:(fc + 2) * 128]), W=[Bwgu[wi]])
                for c0, cw, tiles in col_chunks():
                    b = cnt % 2; cnt += 1
                    otiles = sorted(set(min((c0 + o) // 128, NOT) for o in range(0, cw, 128)))
                    S.mm([lambda kc=kc: nc.tensor.matmul(psg[b][:, :cw], lhsT=wg[wi][:, kc, sub * 128:(sub + 1) * 128], rhs=uT[:, kc, TP + c0:TP + c0 + cw], start=(kc == 0), stop=(kc == KC - 1)) for kc in range(KC)],
                         R=[Bwgu[wi]] + [B_uT[t] for t in tiles], W=[Bpsg[b]])
                    S.mm([lambda kc=kc: nc.tensor.matmul(psu[b][:, :cw], lhsT=wu[wi][:, kc, sub * 128:(sub + 1) * 128], rhs=uT[:, kc, TP + c0:TP + c0 + cw], start=(kc == 0), stop=(kc == KC - 1)) for kc in range(KC)],
                         R=[Bwgu[wi]] + [B_uT[t] for t in tiles], W=[Bpsu[b]])
                    S.op("act", lambda: nc.scalar.activation(out=sl[b][:, :cw], in_=psg[b][:, :cw], func=AF.Silu), R=[Bpsg[b]], W=[Bsl[b]])
                    S.op("dve", lambda: nc.vector.tensor_tensor(out=act_cols(fl, c0, cw), in0=psu[b][:, :cw], in1=sl[b][:, :cw], op=ALU.mult),
                         R=[Bpsu[b], Bsl[b]], W=[B_act[t] for t in otiles])
            for t in range(NOT + 1):
                P = tile_rows(t)
                for hf in range(2):
                    p = dcnt % 4; dcnt += 1
                    S.mm([lambda fl=fl, p=p, t=t, P=P, hf=hf: nc.tensor.matmul(psd[p][:P, :], lhsT=act_tile(fl, t), rhs=wd[:, fl, hf * 512:(hf + 1) * 512],
                                                                              start=(fl == 0), stop=(fl == nf - 1)) for fl in range(nf)], R=[Bwd, B_act[t]], W=[Bpsd[p]])
                    S.op("dve", lambda p=p, t=t, P=P, hf=hf: nc.vector.tensor_tensor(out=hres[:P, t, hf * 512:(hf + 1) * 512], in0=psd[p][:P, :], in1=hres[:P, t, hf * 512:(hf + 1) * 512], op=ALU.add),
                         R=[Bpsd[p], B_h[t]], W=[B_h[t]])
        S.barrier()

    with ExitStack() as ph:
        wpg = sb("d5_wpg", [128, KC, D], BF16, ph)
        wpp = sb("d5_wpp", [128, 2, D], BF16, ph)
        Bwp = Buf()
        for kc in range(KC):
            for hf in range(2):
                S.dma("pool", lambda e, kc=kc, hf=hf: e.dma_start(out=wpg[:, kc, hf * 512:(hf + 1) * 512], in_=w_pg[kc * 128:(kc + 1) * 128, hf * 512:(hf + 1) * 512]), W=[Bwp])
        for c in range(2):
            for hf in range(2):
                S.dma("pool", lambda e, c=c, hf=hf: e.dma_start(out=wpp[:, c, hf * 512:(hf + 1) * 512], in_=w_pp[c * 128:(c + 1) * 128, hf * 512:(hf + 1) * 512]), W=[Bwp])
        xnb = [sb(f"d5_xn{i}", [128, D], BF16, ph) for i in range(2)]
        Bxnb = [Buf(), Buf()]
        psT = [pst(f"d5_psT{i}", [128, KC, 128], BF16, ph) for i in range(2)]
        BpsT = [PBuf(), PBuf()]
        pt_ = [sb(f"d5_p{i}", [128, 256], F32, ph) for i in range(2)]
        ptb = [sb(f"d5_pb{i}", [128, 256], BF16, ph) for i in range(2)]
        pT = [sb(f"d5_pT{i}", [128, 2, 128], BF16, ph) for i in range(2)]
        Bpt = [Buf(), Buf()]; Bptb = [Buf(), Buf()]; BpT = [Buf(), Buf()]
        psg = [pst(f"d5_pg{i}", [128, 512], F32, ph) for i in range(2)]
        psp = [pst(f"d5_pp{i}", [128, 512], F32, ph) for i in range(2)]
        Bpsg = [PBuf(), PBuf()]; Bpsp = [PBuf(), PBuf()]
        sg = [sb(f"d5_sg{i}", [128, 512], F32, ph) for i in range(2)]
        Bsg = [Buf(), Buf()]
        yt = [sb(f"d5_y{i}", [128, D], F32, ph) for i in range(2)]
        Byt = [Buf(), Buf()]
        cnt = 0
        for t in range(NOT + 1):
            P = tile_rows(t)
            i = t % 2
            norm_to_uT(ph, t, 2, xnb, Bxnb, psT, BpsT)
            src = pown[t * 128:(t + 1) * 128, :] if t < NOT else psm
            S.dma("sp", lambda e, i=i, P=P, src=src: e.dma_start(out=pt_[i][:P, :], in_=src), W=[Bpt[i]])
            S.op("pool", lambda i=i, P=P: nc.gpsimd.tensor_copy(out=ptb[i][:P, :], in_=pt_[i][:P, :]), R=[Bpt[i]], W=[Bptb[i]])
            S.mm([lambda c=c, i=i, P=P: nc.tensor.transpose(out=psT[i][:, c, :P], in_=ptb[i][:P, c * 128:(c + 1) * 128], identity=identb[:P, :P]) for c in range(2)],
                 R=[Bptb[i], B_const], W=[BpsT[i]])
            S.op("act", lambda i=i, P=P: nc.scalar.copy(out=pT[i][:, :, :P], in_=psT[i][:, 0:2, :P]), R=[BpsT[i]], W=[BpT[i]])
            col0 = TP + t * 128
            for hf in range(2):
                b = cnt % 2; cnt += 1
                S.mm([lambda kc=kc, b=b, P=P, hf=hf, col0=col0: nc.tensor.matmul(psg[b][:P, :], lhsT=uT[:, kc, col0:col0 + P], rhs=wpg[:, kc, hf * 512:(hf + 1) * 512],
                                                                                start=(kc == 0), stop=(kc == KC - 1)) for kc in range(KC)], R=[Bwp, B_uT[OT0 + t]], W=[Bpsg[b]])
                S.mm([lambda c=c, b=b, P=P, hf=hf, i=i: nc.tensor.matmul(psp[b][:P, :], lhsT=pT[i][:, c, :P], rhs=wpp[:, c, hf * 512:(hf + 1) * 512],
                                                                        start=(c == 0), stop=(c == 1)) for c in range(2)], R=[Bwp, BpT[i]], W=[Bpsp[b]])
                S.op("act", lambda b=b, P=P: nc.scalar.activation(out=sg[b][:P, :], in_=psg[b][:P, :], func=AF.Sigmoid), R=[Bpsg[b]], W=[Bsg[b]])
                S.op("dve", lambda b=b, P=P: nc.vector.tensor_tensor(out=sg[b][:P, :], in0=psp[b][:P, :], in1=sg[b][:P, :], op=ALU.mult), R=[Bpsp[b], Bsg[b]], W=[Bsg[b]])
                S.op("pool", lambda b=b, P=P, t=t, hf=hf: nc.gpsimd.tensor_tensor(out=hres[:P, t, hf * 512:(hf + 1) * 512], in0=hres[:P, t, hf * 512:(hf + 1) * 512], in1=sg[b][:P, :], op=ALU.add),
                     R=[Bsg[b], B_h[t]], W=[B_h[t]])
            rmsnorm_rows(ph, hres[:P, t, :], P, 3, yt[i][:P, :], B_h[t], Byt[i], "y")
            dst = y_own[t * 128:(t + 1) * 128, :] if t < NOT else y_s
            S.dma("sp", lambda e, i=i, P=P, dst=dst: e.dma_start(out=dst, in_=yt[i][:P, :]), R=[Byt[i]], is_out=True)
        S.barrier()

    S.finish()
    es.close()
    return nc


def build_sample_attn(cfgb):
    NSEQ, NPG, NPHYS = cfgb["NSEQ"], cfgb["NPG"], cfgb["NPHYS"]
    NTK = NSEQ * 8
    NTL = (NTK + 127) // 128
    nc = bass.Bass("TRN2", target_bir_lowering=False)

    def din(name, shape, dt=F32):
        return nc.dram_tensor(name, list(shape), dt, kind="ExternalInput").ap()

    xs = din("xs", [NTK, D])
    ckv = din("ckv", [NPHYS * 128, 2 * D])
    ptab = din("ptab", [1, NSEQ * NPG], I32)
    iot = din("iot", [128, 1], I32)
    w_in = din("w_in", [D, 7 * D])
    gmix = din("gmix", [1, D])
    gsub = din("gsub", [1, 128])
    lamv = din("lamv", [4, 64])
    ident = din("ident", [128, 128])
    maskn = din("maskn", [8, 128])
    hmask = din("hmask", [128, 8])
    cpos = din("cpos", [128, 64])
    cneg = din("cneg", [128, 64])
    on_all = nc.dram_tensor("on_all", [NTK, D], F32, kind="ExternalOutput").ap()

    es = ExitStack()
    S = Sched(nc, es)

    def sb(name, shape, dt=F32, stack=None):
        return (stack or es).enter_context(nc.sbuf_tensor(name, list(shape), dt))

    def pst(name, shape, dt=F32, stack=None):
        return (stack or es).enter_context(nc.psum_tensor(name, list(shape), dt))

    identb = sb("identb", [128, 128], BF16)
    gb = sb("gb", [128, D], F32)
    gsubb = sb("gsubb", [128, 128], F32)
    epsb = sb("epsb", [128, 1], F32)
    lamb = sb("lamb", [128, 4, 64], F32)
    neglam = sb("neglam", [128, 1], F32)
    tmpc = sb("tmpc", [128, 4], F32)
    scr64 = sb("scr64", [128, 64], F32)
    maskb = sb("maskb", [8, 128], BF16)
    hm = sb("hm", [128, 8], F32)
    comb = sb("comb", [128, 64], F32)
    cng = sb("cng", [128, 64], F32)
    ones2 = sb("ones2", [128, 2], BF16)
    ptb = sb("ptb", [128, NSEQ * NPG], I32)
    iott = sb("iott", [128, 1], I32)
    idx = sb("idx", [128, NSEQ * NPG], I32)
    Bc = Buf()
    S.dma("pool", lambda e: e.dma_start(out=identb[:], in_=ident), W=[Bc])
    S.dma("pool", lambda e: e.dma_start(out=maskb[:], in_=maskn), W=[Bc])
    S.dma("sp", lambda e: e.dma_start(out=gb[:], in_=gmix.partition_broadcast(128)), W=[Bc])
    S.dma("sp", lambda e: e.dma_start(out=gsubb[:], in_=gsub.partition_broadcast(128)), W=[Bc])
    S.dma("sp", lambda e: e.dma_start(out=hm[:], in_=hmask), W=[Bc])
    S.dma("sp", lambda e: e.dma_start(out=comb[:], in_=cpos), W=[Bc])
    S.dma("sp", lambda e: e.dma_start(out=cng[:], in_=cneg), W=[Bc])
    S.dma("sp", lambda e: e.dma_start(out=ptb[:], in_=ptab.partition_broadcast(128)), W=[Bc])
    S.dma("sp", lambda e: e.dma_start(out=iott[:], in_=iot), W=[Bc])
    S.dma("sp", lambda e: e.dma_start(out=lamb[:].rearrange("p a b -> p (a b)"),
                                      in_=lamv.rearrange("a b -> (a b)").rearrange("(o n) -> o n", o=1).partition_broadcast(128)), W=[Bc])
    S.op("dve", lambda: nc.vector.memset(epsb[:], EPS), W=[Bc])
    S.op("dve", lambda: nc.vector.memset(ones2[:], 1.0), W=[Bc])
    S.op("dve", lambda: nc.vector.tensor_scalar(out=idx[:], in0=ptb[:], scalar1=128, scalar2=iott[:, 0:1], op0=ALU.mult, op1=ALU.add), R=[Bc], W=[Bc])
    S.op("dve", lambda: nc.vector.tensor_tensor(out=scr64[:], in0=lamb[:, 0, :], in1=lamb[:, 1, :], op=ALU.mult), R=[Bc], W=[Bc])
    S.op("dve", lambda: nc.vector.reduce_sum(out=tmpc[:, 0:1], in_=scr64[:], axis=AX.X), R=[Bc], W=[Bc])
    S.op("dve", lambda: nc.vector.tensor_tensor(out=scr64[:], in0=lamb[:, 2, :], in1=lamb[:, 3, :], op=ALU.mult), R=[Bc], W=[Bc])
    S.op("dve", lambda: nc.vector.reduce_sum(out=tmpc[:, 1:2], in_=scr64[:], axis=AX.X), R=[Bc], W=[Bc])
    S.op("act", lambda: nc.scalar.activation(out=tmpc[:, 2:4], in_=tmpc[:, 0:2], func=AF.Exp), R=[Bc], W=[Bc])
    S.op("dve", lambda: nc.vector.tensor_tensor(out=neglam[:], in0=tmpc[:, 3:4], in1=tmpc[:, 2:3], op=ALU.subtract), R=[Bc], W=[Bc])
    S.op("dve", lambda: nc.vector.tensor_scalar(out=neglam[:], in0=neglam[:], scalar1=-LAM_INIT, scalar2=None, op0=ALU.add), R=[Bc], W=[Bc])
    S.op("dve", lambda: nc.vector.tensor_scalar(out=gsubb[:], in0=gsubb[:], scalar1=1.0 - LAM_INIT, scalar2=None, op0=ALU.mult), R=[Bc], W=[Bc])
    S.op("dve", lambda: nc.vector.scalar_tensor_tensor(out=comb[:], in0=cng[:], scalar=neglam[:, 0:1], in1=comb[:], op0=ALU.mult, op1=ALU.add), R=[Bc], W=[Bc])
    S.barrier()

    uTs = sb("uTs", [128, KC, NTK], BF16)
    qTa = sb("qTa", [128, NH, NTK], BF16)
    kTa = sb("kTa", [128, NH, NTK], BF16)
    Vn = sb("Vn", [8, NSEQ, D], BF16)
    Bu = Buf(); Bq = Buf(); Bk = Buf(); Bvn = Buf()
    with ExitStack() as ph:
        xt = [sb(f"x{i}", [128, D], F32, ph) for i in range(2)]
        xn = [sb(f"xn{i}", [128, D], BF16, ph) for i in range(2)]
        junk = sb("junk", [128, D], F32, ph)
        ss = sb("ss", [128, 4], F32, ph)
        Bxt = [Buf(), Buf()]; Bxn = [Buf(), Buf()]; Bs = Buf()
        psT = [pst(f"psT{i}", [128, KC, 128], BF16, ph) for i in range(2)]
        BpsT = [PBuf(), PBuf()]
        for t in range(NTL):
            i = t % 2
            P = min(128, NTK - t * 128)
            S.dma("sp", lambda e, i=i, P=P, t=t: e.dma_start(out=xt[i][:P, :], in_=xs[t * 128:t * 128 + P, :]), W=[Bxt[i]])
            S.op("act", lambda i=i, P=P: nc.scalar.activation(out=junk[:P, :], in_=xt[i][:P, :], func=AF.Square, accum_out=ss[:P, 0:1]), R=[Bxt[i]], W=[Bs])
            S.op("act", lambda P=P: nc.scalar.activation(out=ss[:P, 1:2], in_=ss[:P, 0:1], func=AF.Sqrt, scale=1.0 / D, bias=epsb[:P, 0:1]), R=[Bs, Bc], W=[Bs])
            S.op("dve", lambda P=P: nc.vector.reciprocal(out=ss[:P, 2:3], in_=ss[:P, 1:2]), R=[Bs], W=[Bs])
            S.op("dve", lambda i=i, P=P: nc.vector.scalar_tensor_tensor(out=xn[i][:P, :], in0=xt[i][:P, :], scalar=ss[:P, 2:3], in1=gb[:P, :], op0=ALU.mult, op1=ALU.mult),
                 R=[Bxt[i], Bs, Bc], W=[Bxn[i]])
            S.mm([lambda kc=kc, i=i, P=P: nc.tensor.transpose(out=psT[i][:, kc, :P], in_=xn[i][:P, kc * 128:(kc + 1) * 128], identity=identb[:P, :P]) for kc in range(KC)],
                 R=[Bxn[i], Bc], W=[BpsT[i]])
            S.op("act", lambda i=i, P=P, t=t: nc.scalar.copy(out=uTs[:, :, t * 128:t * 128 + P], in_=psT[i][:, :, :P]), R=[BpsT[i]], W=[Bu])
        S.barrier()
    with ExitStack() as ph:
        wq = [sb(f"wq{i}", [128, KC, 128], BF16, ph) for i in range(2)]
        wk = [sb(f"wk{i}", [128, KC, 128], BF16, ph) for i in range(2)]
        Bw = [Buf(), Buf()]
        wv = sb("wv", [128, KC, D], BF16, ph)
        Bwv = Buf()
        ps = [pst(f"ps{i}", [128, 512], F32, ph) for i in range(4)]
        Bps = [PBuf() for _ in range(4)]
        pc = 0
        for kc in range(KC):
            for hf in range(2):
                S.dma("pool", lambda e, kc=kc, hf=hf: e.dma_start(out=wv[:, kc, hf * 512:(hf + 1) * 512], in_=w_in[kc * 128:(kc + 1) * 128, 2 * D + hf * 512:2 * D + (hf + 1) * 512]), W=[Bwv])
        for h in range(NH):
            wi = h % 2
            for kc in range(KC):
                S.dma("pool", lambda e, wi=wi, kc=kc, h=h: e.dma_start(out=wq[wi][:, kc, :], in_=w_in[kc * 128:(kc + 1) * 128, h * 128:(h + 1) * 128]), W=[Bw[wi]])
                S.dma("pool", lambda e, wi=wi, kc=kc, h=h: e.dma_start(out=wk[wi][:, kc, :], in_=w_in[kc * 128:(kc + 1) * 128, D + h * 128:D + (h + 1) * 128]), W=[Bw[wi]])
            p = pc % 4; pc += 1
            S.mm([lambda kc=kc, p=p, wi=wi: nc.tensor.matmul(ps[p][:, :NTK], lhsT=wq[wi][:, kc, :], rhs=uTs[:, kc, :], start=(kc == 0), stop=(kc == KC - 1)) for kc in range(KC)],
                 R=[Bw[wi], Bu], W=[Bps[p]])
            S.op("dve", lambda p=p, h=h: nc.vector.tensor_scalar(out=qTa[:, h, :], in0=ps[p][:, :NTK], scalar1=0.125, scalar2=None, op0=ALU.mult), R=[Bps[p]], W=[Bq])
            p = pc % 4; pc += 1
            S.mm([lambda kc=kc, p=p, wi=wi: nc.tensor.matmul(ps[p][:, :NTK], lhsT=wk[wi][:, kc, :], rhs=uTs[:, kc, :], start=(kc == 0), stop=(kc == KC - 1)) for kc in range(KC)],
                 R=[Bw[wi], Bu], W=[Bps[p]])
            S.op("act", lambda p=p, h=h: nc.scalar.copy(out=kTa[:, h, :], in_=ps[p][:, :NTK]), R=[Bps[p]], W=[Bk])
        for s_ in range(NSEQ):
            for hf in range(2):
                p = pc % 4; pc += 1
                S.mm([lambda kc=kc, p=p, s_=s_, hf=hf: nc.tensor.matmul(ps[p][:8, :], lhsT=uTs[:, kc, 8 * s_:8 * s_ + 8], rhs=wv[:, kc, hf * 512:(hf + 1) * 512],
                                                                        start=(kc == 0), stop=(kc == KC - 1)) for kc in range(KC)], R=[Bwv, Bu], W=[Bps[p]])
                S.op("dve" if hf else "act", (lambda p=p, s_=s_, hf=hf: nc.vector.tensor_copy(out=Vn[:8, s_, hf * 512:(hf + 1) * 512], in_=ps[p][:8, :])) if hf else
                     (lambda p=p, s_=s_, hf=hf: nc.scalar.copy(out=Vn[:8, s_, hf * 512:(hf + 1) * 512], in_=ps[p][:8, :])), R=[Bps[p]], W=[Bvn])
        S.barrier()
    with ExitStack() as ph:
        NB3 = 4
        KVp = [sb(f"KVp{i}", [128, 2 * D], BF16, ph) for i in range(NB3)]
        BKp = [Buf() for _ in range(NB3)]
        BVp = BKp
        KT = [sb(f"KT{i}", [128, NH, 128], BF16, ph) for i in range(3)]
        BKT = [Buf(), Buf(), Buf()]
        PTs = [sb(f"PTs{i}", [128, 128], BF16, ph) for i in range(3)]
        BPTs = [Buf(), Buf(), Buf()]
        Qbd = [sb(f"Qbd{i}", [128, NH, 16], BF16, ph) for i in range(2)]
        BQbd = [Buf(), Buf()]
        for i in range(2):
            S.op("pool", lambda i=i: nc.gpsimd.memset(Qbd[i][:], 0.0), W=[BQbd[i]])
        o_n = sb("o_n", [128, 128], F32, ph)
        o_c = sb("o_c", [64, 128], F32, ph)
        o_f = [sb(f"o_f{i}", [64, 128], F32, ph) for i in range(2)]
        rc = sb("rc", [128, 4], F32, ph)
        junk = sb("junk2", [64, 128], F32, ph)
        Bfin = Buf(); Bof = [Buf(), Buf()]
        psT = [pst(f"psK{i}", [128, NH, 128], BF16, ph) for i in range(2)]
        BpsT = [PBuf(), PBuf()]
        pS = [pst(f"pS{i}", [128, 128], F32, ph) for i in range(2)]
        BpS = [PBuf(), PBuf()]
        acc = pst("acc", [128, 2, 512], F32, ph)
        Bacc = PBuf()
        accs = pst("accs", [128, 2], F32, ph)
        Baccs = PBuf()
        pcm = pst("pcm", [64, 128], F32, ph)
        Bpcm = PBuf()
        steps = [(s_, k) for s_ in range(NSEQ) for k in range(NPG + 1)]
        NST = len(steps)

        def bufs(g):
            return g % NB3, g % 2, g % 3

        def stage_T(g):
            s_, k = steps[g]
            b3, b2, bk = bufs(g)
            qb = s_ % 2
            if k == 0:
                S.op("pool", lambda: nc.gpsimd.tensor_copy(out=Qbd[qb][0:64, :, 0:8], in_=qTa[0:64, :, 8 * s_:8 * s_ + 8]), R=[Bq], W=[BQbd[qb]])
                S.op("pool", lambda: nc.gpsimd.tensor_copy(out=Qbd[qb][64:128, :, 8:16], in_=qTa[64:128, :, 8 * s_:8 * s_ + 8]), R=[Bq], W=[BQbd[qb]])
            if k == NPG:
                return
            col = s_ * NPG + k
            S.dma("pool", lambda e: e.indirect_dma_start(out=KVp[b3][:], out_offset=None, in_=ckv,
                                                          in_offset=bass.IndirectOffsetOnAxis(ap=idx[:, col:col + 1], axis=0)), R=[Bc], W=[BKp[b3]])
            S.mm([lambda h=h: nc.tensor.transpose(out=psT[b2][:, h, :], in_=KVp[b3][:, h * 128:(h + 1) * 128], identity=identb[:, :]) for h in range(NH)],
                 R=[BKp[b3], Bc], W=[BpsT[b2]])
            if g % 2:
                S.op("act", lambda: nc.scalar.copy(out=KT[bk][:], in_=psT[b2][:]), R=[BpsT[b2]], W=[BKT[bk]])
            else:
                S.op("dve", lambda: nc.vector.tensor_copy(out=KT[bk][:], in_=psT[b2][:]), R=[BpsT[b2]], W=[BKT[bk]])

        def stage_S(g):
            s_, k = steps[g]
            b3, b2, bk = bufs(g)
            qb = s_ % 2
            if k < NPG:
                P = 128
                S.mm([lambda h=h: nc.tensor.matmul(pS[b2][:, h * 16:(h + 1) * 16], lhsT=KT[bk][:, h, :], rhs=Qbd[qb][:, h, :], start=True, stop=True) for h in range(NH)],
                     R=[BKT[bk], BQbd[qb]], W=[BpS[b2]])
            else:
                P = 8
                S.mm([lambda h=h: nc.tensor.matmul(pS[b2][:8, h * 16:(h + 1) * 16], lhsT=kTa[:, h, 8 * s_:8 * s_ + 8], rhs=Qbd[qb][:, h, :], start=True, stop=True)
                      for h in range(NH)], R=[Bk, BQbd[qb]], W=[BpS[b2]])
            S.op("act", lambda: nc.scalar.activation(out=PTs[bk][:P, :], in_=pS[b2][:P, :], func=AF.Exp), R=[BpS[b2]], W=[BPTs[bk]])
            if k == NPG:
                S.op("dve", lambda: nc.vector.tensor_tensor(out=PTs[bk][:8, :], in0=PTs[bk][:8, :], in1=maskb[:, :], op=ALU.mult), R=[BPTs[bk], Bc], W=[BPTs[bk]])

        def stage_PV(g):
            s_, k = steps[g]
            b3, b2, bk = bufs(g)
            new = (k == NPG)
            if new:
                P = 8
                vsrc = [Vn[:8, s_, 0:512], Vn[:8, s_, 512:1024]]
                Rv = [Bvn]
            else:
                P = 128
                vsrc = [KVp[b3][:, D:D + 512], KVp[b3][:, D + 512:2 * D]]
                Rv = [BVp[b3]]
            S.mm([lambda hf=hf: nc.tensor.matmul(acc[:, hf, :], lhsT=PTs[bk][:P, :], rhs=vsrc[hf], start=(k == 0), stop=new) for hf in range(2)]
                 + [lambda: nc.tensor.matmul(accs[:, :], lhsT=PTs[bk][:P, :], rhs=ones2[:P, :], start=(k == 0), stop=new)],
                 R=[BPTs[bk]] + Rv + [Bc], W=[Bacc, Baccs])
            if new:
                finalize(s_)

        def finalize(s_):
            S.op("dve", lambda: nc.vector.tensor_scalar(out=o_n[:, :], in0=acc[:, 0, 0:128], scalar1=hm[:, 0:1], scalar2=None, op0=ALU.mult), R=[Bacc, Bc], W=[Bfin])
            for h2 in range(1, NH):
                S.op("dve", lambda h2=h2: nc.vector.scalar_tensor_tensor(out=o_n[:, :], in0=acc[:, h2 // 4, (h2 % 4) * 128:(h2 % 4 + 1) * 128], scalar=hm[:, h2:h2 + 1], in1=o_n[:, :],
                                                                          op0=ALU.mult, op1=ALU.add), R=[Bacc, Bc, Bfin], W=[Bfin])
            S.op("dve", lambda: nc.vector.reciprocal(out=rc[:, 0:1], in_=accs[:, 0:1]), R=[Baccs], W=[Bfin])
            S.op("dve", lambda: nc.vector.tensor_scalar(out=o_n[:, :], in0=o_n[:, :], scalar1=rc[:, 0:1], scalar2=None, op0=ALU.mult), R=[Bfin], W=[Bfin])
            S.mm([lambda: nc.tensor.matmul(pcm[:, :], lhsT=comb[:, :], rhs=o_n[:, :], start=True, stop=True)], R=[Bfin, Bc], W=[Bpcm])
            fb = s_ % 2
            S.op("dve", lambda: nc.vector.tensor_copy(out=o_c[:, :], in_=pcm[:, :]), R=[Bpcm], W=[Bfin])
            S.op("act", lambda: nc.scalar.activation(out=junk[:, :], in_=o_c[:, :], func=AF.Square, accum_out=rc[:64, 1:2]), R=[Bfin], W=[Bfin])
            S.op("act", lambda: nc.scalar.activation(out=rc[:64, 2:3], in_=rc[:64, 1:2], func=AF.Sqrt, scale=1.0 / 128, bias=epsb[:64, 0:1]), R=[Bfin, Bc], W=[Bfin])
            S.op("dve", lambda: nc.vector.reciprocal(out=rc[:64, 2:3], in_=rc[:64, 2:3]), R=[Bfin], W=[Bfin])
            S.op("dve", lambda: nc.vector.scalar_tensor_tensor(out=o_f[fb][:, :], in0=o_c[:, :], scalar=rc[:64, 2:3], in1=gsubb[:64, :], op0=ALU.mult, op1=ALU.mult),
                 R=[Bfin, Bc], W=[Bof[fb]])
            for h in range(NH):
                S.dma("sp", lambda e, h=h: e.dma_start(out=on_all[8 * s_:8 * s_ + 8, h * 128:(h + 1) * 128], in_=o_f[fb][8 * h:8 * h + 8, :]), R=[Bof[fb]], is_out=True)

        for i in range(NST + 2):
            if i < NST:
                stage_T(i)
            if 0 <= i - 1 < NST:
                stage_S(i - 1)
            if 0 <= i - 2 < NST:
                stage_PV(i - 2)
        S.barrier()
    S.finish()
    es.close()
    return nc


def make_in_maps(inp, cfg):
    TP, TO, NS = cfg["TP"], cfg["TO"], cfg["NS"]
    f32 = np.float32
    xp = np.asarray(inp["x_prompt"], f32)
    B = xp.shape[0]
    vecs = np.stack([np.asarray(inp[k], f32).reshape(-1) for k in
                     ("g_mix", "g_ffn", "g_ple", "g_final", "lru_lambda", "b_conv", "b_gate_a", "b_gate_x")])
    lamv = np.stack([np.asarray(inp[k], f32).reshape(-1) for k in ("lambda_q1", "lambda_k1", "lambda_q2", "lambda_k2")])
    common = {
        "w_in": np.asarray(inp["w_in"], f32)[0], "w_attn": np.asarray(inp["w_attn_br"], f32)[0],
        "w_rec": np.asarray(inp["w_rec_br"], f32)[0], "w_out": np.asarray(inp["w_out"], f32)[0],
        "w_fg": np.asarray(inp["w_ffn_gate"], f32)[0], "w_fu": np.asarray(inp["w_ffn_up"], f32)[0],
        "w_fd": np.asarray(inp["w_ffn_down"], f32)[0], "w_pg": np.asarray(inp["w_ple_gate"], f32)[0],
        "w_pp": np.asarray(inp["w_ple_proj"], f32)[0], "w_ga": np.asarray(inp["w_gate_a"], f32)[0],
        "w_gx": np.asarray(inp["w_gate_x"], f32)[0], "vecs": vecs, "wconv": np.asarray(inp["w_conv"], f32)[0],
        "gsub": np.asarray(inp["g_subln"], f32).reshape(1, 128), "lamv": lamv,
        "ident": np.eye(128, dtype=f32), "tri": np.triu(np.ones((128, 128), f32)),
    }
    maps = []
    xsamp = np.asarray(inp["x_sample"], f32)
    psamp = np.asarray(inp["p_sample"], f32)[0]
    pprm = np.asarray(inp["p_prompt"], f32)[0]
    for c in range(8):
        b, half = c // 2, c % 2
        m = dict(common)
        xfull = np.zeros((TP + TO, D), f32)
        if half == 1:
            xfull[:TP] = xp[b, :TP]
        xfull[TP:] = xp[b, half * TO:(half + 1) * TO]
        m["xf"] = xfull
        m["pown"] = np.ascontiguousarray(pprm[b, half * TO:(half + 1) * TO])
        m["xs"] = np.ascontiguousarray(xsamp[4 * c:4 * c + 4].reshape(NS, D))
        m["psm"] = np.ascontiguousarray(psamp[4 * c:4 * c + 4].reshape(NS, 256))
        m["flag"] = np.full((128, 1), float(half), f32)
        m["sconv"] = np.ascontiguousarray(np.asarray(inp["state_conv"], f32)[0, 4 * c:4 * c + 4].reshape(12, D))
        m["sh"] = np.ascontiguousarray(np.asarray(inp["state_h"], f32)[0, 4 * c:4 * c + 4])
        maps.append(m)
    return maps


_CACHE = {}


NCB = 2


def make_sample_maps(inp):
    f32 = np.float32
    ck = np.asarray(inp["cache_k"], f32)
    cv = np.asarray(inp["cache_v"], f32)
    nphys = ck.shape[1]
    ckv = np.concatenate([ck.reshape(nphys * 128, D), cv.reshape(nphys * 128, D)], axis=1)
    pt = np.asarray(inp["page_table"], np.int32)
    nseq, npg = pt.shape
    spc = nseq // NCB
    hmask = np.zeros((128, 8), f32)
    cpos = np.zeros((128, 64), f32)
    cneg = np.zeros((128, 64), f32)
    maskn = np.zeros((8, 128), f32)
    for h in range(8):
        for c in range(2):
            for q in range(8):
                r = h * 16 + c * 8 + q
                hmask[r, h] = 1.0
                (cpos if c == 0 else cneg)[r, h * 8 + q] = 1.0
                for j in range(8):
                    if j <= q:
                        maskn[j, r] = 1.0
    lamv = np.stack([np.asarray(inp[k], f32).reshape(-1) for k in ("lambda_q1", "lambda_k1", "lambda_q2", "lambda_k2")])
    xs = np.asarray(inp["x_sample"], f32)
    maps = []
    for c in range(NCB):
        maps.append({
            "xs": np.ascontiguousarray(xs[c * spc:(c + 1) * spc].reshape(spc * 8, D)),
            "ckv": ckv,
            "ptab": np.ascontiguousarray(pt[c * spc:(c + 1) * spc].reshape(1, spc * npg)),
            "iot": np.arange(128, dtype=np.int32).reshape(128, 1),
            "w_in": np.asarray(inp["w_in"], f32)[0], "gmix": np.asarray(inp["g_mix"], f32).reshape(1, D),
            "gsub": np.asarray(inp["g_subln"], f32).reshape(1, 128), "lamv": lamv,
            "ident": np.eye(128, dtype=f32), "maskn": maskn, "hmask": hmask, "cpos": cpos, "cneg": cneg,
        })
    return maps, {"NSEQ": spc, "NPG": npg, "NPHYS": nphys}


def kernel(**inputs):
    import os
    xp = inputs["x_prompt"]
    SEQ = xp.shape[1]
    cfg = {"TP": SEQ // 2, "TO": SEQ // 2, "NS": 32}
    if "KSTOP" in os.environ:
        cfg["stop"] = int(os.environ["KSTOP"])
    mbs, cfgb = make_sample_maps(inputs)
    keyb = ("B", cfgb["NSEQ"], cfgb["NPG"], cfgb["NPHYS"])
    if keyb not in _CACHE:
        _CACHE[keyb] = build_sample_attn(cfgb)
    resb = run_bass_kernel_spmd(_CACHE[keyb], mbs, core_ids=list(range(NCB))).results
    del mbs
    on_all = np.concatenate([np.asarray(r["on_all"], np.float32) for r in resb], axis=0)
    key = ("A", SEQ, cfg.get("stop"))
    if key not in _CACHE:
        _CACHE[key] = build(cfg)
    nc = _CACHE[key]
    maps = make_in_maps(inputs, cfg)
    for c in range(8):
        maps[c]["on_s"] = np.ascontiguousarray(on_all[32 * c:32 * (c + 1)])
    res = run_bass_kernel_spmd(nc, maps, core_ids=list(range(8))).results
    B = xp.shape[0]
    TO = cfg["TO"]
    f32 = np.float32

    def own(name):
        o = np.zeros((B, SEQ, D), f32)
        for c in range(8):
            o[c // 2, (c % 2) * TO:(c % 2 + 1) * TO] = res[c][name]
        return o

    def samp(name, per):
        return np.concatenate([res[c][name].reshape(4, per, -1) for c in range(8)], axis=0)
    y_prompt = own("y_own")
    y_sample = samp("y_s", 8)
    k_prompt = own("k_own").reshape(1, B, SEQ, NH, 2, 64)
    v_prompt = own("v_own").reshape(1, B, SEQ, NH, 128)
    conv_prompt = np.stack([res[2 * b + 1]["conv_p"] for b in range(B)])[None]
    h_prompt = np.stack([res[2 * b + 1]["h_p"][0] for b in range(B)])[None]
    k_sample = samp("k_s", 8).reshape(1, 32, 8, NH, 2, 64)
    v_sample = samp("v_s", 8).reshape(1, 32, 8, NH, 128)
    conv_sample = samp("conv_s", 3)[None]
    h_sample = np.concatenate([res[c]["h_s"] for c in range(8)], axis=0)[None]
    return (y_prompt, y_sample, k_prompt, v_prompt, conv_prompt, h_prompt, k_sample, v_sample, conv_sample, h_sample)
```
